# Optimizing a Trainium2 kernel written in Bass

```python
import jax, jax.numpy as jnp
from jax import lax
import numpy as np


D_MODEL = 1024
BATCH = 16
SEQ = 2048
DEPTH = 2
DEC_BATCH = 8
DEC_SEQ = 64
PAST_LEN = 4096

CHUNK = 64
POOL_WIDTH = 512
POOL_GROUPS = 4
POOL_GROUP_WIDTH = POOL_WIDTH // POOL_GROUPS
POOL_WINDOWS = (2, 4, 8, 16)
POOL_STATE = max(POOL_WINDOWS) - 1
N_HEADS = 8
HEAD_DIM = 64
ATTN_WIDTH = N_HEADS * HEAD_DIM
IDX_HEADS = 8
IDX_DIM = 32
TOPK_MAX = 256
QUERY_BLOCK = 128
ROPE_THETA = 10000.0
EPS = 1e-6
IN_SIZES = (POOL_WIDTH, POOL_WIDTH, ATTN_WIDTH, ATTN_WIDTH, ATTN_WIDTH,
            IDX_HEADS * IDX_DIM, IDX_DIM, IDX_HEADS, ATTN_WIDTH, D_MODEL, D_MODEL)
IN_COLS = sum(IN_SIZES)

kernel_name = 'streaming_pool_dsa_hybrid_step'


def rms_norm(x, g):
    xf = x.astype(jnp.float32)
    r = lax.rsqrt(jnp.mean(xf * xf, axis=-1, keepdims=True) + EPS)
    return (xf * r * g.astype(jnp.float32)).astype(x.dtype)


def rope(x, pos):
    d = x.shape[-1]
    inv = ROPE_THETA ** (-jnp.arange(0, d, 2, dtype=jnp.float32) / d)
    ang = pos.astype(jnp.float32)[:, None] * inv[None, :]
    cos = jnp.cos(ang)[:, None, :]
    sin = jnp.sin(ang)[:, None, :]
    xf = x.astype(jnp.float32)
    x1, x2 = xf[..., : d // 2], xf[..., d // 2:]
    return jnp.concatenate([x1 * cos - x2 * sin, x2 * cos + x1 * sin], axis=-1).astype(x.dtype)


def pool_mix(u, left, pos, w_mix, scale):
    B, T, P = u.shape
    up = jnp.concatenate([left, u], axis=1).astype(jnp.float32)
    c = jnp.concatenate([jnp.zeros((B, 1, P), jnp.float32), jnp.cumsum(up, axis=1)], axis=1)
    end = c[:, POOL_STATE + 1:]
    means = []
    for gi, w in enumerate(POOL_WINDOWS):
        sl = slice(gi * POOL_GROUP_WIDTH, (gi + 1) * POOL_GROUP_WIDTH)
        start = c[:, POOL_STATE + 1 - w: POOL_STATE + 1 - w + T, sl]
        cnt = jnp.minimum(pos + 1, w).astype(jnp.float32)[None, :, None]
        means.append((end[..., sl] - start) / cnt)
    pooled = jnp.concatenate(means, axis=-1) - up[:, POOL_STATE:]
    mixed = jnp.einsum('btgc,gcd->btgd',
                       pooled.reshape(B, T, POOL_GROUPS, POOL_GROUP_WIDTH),
                       w_mix.astype(jnp.float32)).reshape(B, T, P)
    mixed = mixed * scale.astype(jnp.float32)
    new_state = up[:, -POOL_STATE:].astype(u.dtype)
    return mixed.astype(u.dtype), new_state


def dsa_attend(q, qi, wi, k, v, ki, q_pos, k_pos, k_sel):
    B, T, H, Dh = q.shape
    qb = min(QUERY_BLOCK, T)
    nb = T // qb

    def blk(r):
        return r.reshape((B * nb, qb) + r.shape[2:])

    xs = (blk(q), blk(qi), blk(wi),
          jnp.tile(q_pos.reshape(nb, qb), (B, 1)),
          jnp.repeat(jnp.arange(B, dtype=jnp.int32), nb))
    k_chunk = k_pos // CHUNK

    def one(args):
        qB, qiB, wB, pB, b = args
        kB, vB, kiB = k[b], v[b], ki[b]
        s = jnp.einsum('thd,sd->ths', qiB.astype(jnp.float32), kiB.astype(jnp.float32)) * IDX_DIM ** -0.5
        score = jnp.einsum('th,ths->ts', wB.astype(jnp.float32), jax.nn.relu(s)) * IDX_HEADS ** -0.5
        adm = k_chunk[None, :] <= (pB // CHUNK)[:, None]
        score = jnp.where(adm, score, -jnp.inf)
        _, sel = lax.top_k(score, k_sel)
        ok = jnp.take_along_axis(adm, sel, axis=1)
        kg = kB[sel].astype(jnp.float32)
        vg = vB[sel].astype(jnp.float32)
        logit = jnp.einsum('thd,tkhd->thk', qB.astype(jnp.float32), kg) * Dh ** -0.5
        logit = jnp.where(ok[:, None, :], logit, -jnp.inf)
        p = jax.nn.softmax(logit, axis=-1)
        return jnp.einsum('thk,tkhd->thd', p, vg).astype(q.dtype)

    out = lax.map(one, xs)
    return out.reshape(B, T, H * Dh)


def layer(x, pos, k_pos, past_k, past_v, past_ki, pool_left, k_sel,
          norm_g, w_in, w_pool_mix, pool_scale, w_pool_out, w_attn_out, w_o):
    B, T, _ = x.shape
    h = rms_norm(x, norm_g)
    proj = h @ w_in
    split_points = np.cumsum(IN_SIZES)[:-1].tolist()
    u_p, z_p, q, k, v, qi, ki, wi, z_a, g_p, g_a = jnp.split(proj, split_points, axis=-1)
    pooled, pool_state = pool_mix(u_p, pool_left, pos, w_pool_mix, pool_scale)
    br_pool = (pooled * jax.nn.silu(z_p)) @ w_pool_out
    q = rope(q.reshape(B, T, N_HEADS, HEAD_DIM), pos)
    k = rope(k.reshape(B, T, N_HEADS, HEAD_DIM), pos)
    v = v.reshape(B, T, N_HEADS, HEAD_DIM)
    qi = rope(qi.reshape(B, T, IDX_HEADS, IDX_DIM), pos)
    ki = rope(ki[:, :, None, :], pos)[:, :, 0]
    if past_k is None:
        k_all, v_all, ki_all = k, v, ki
    else:
        k_all = jnp.concatenate([past_k, k], axis=1)
        v_all = jnp.concatenate([past_v, v], axis=1)
        ki_all = jnp.concatenate([past_ki, ki], axis=1)
    o = dsa_attend(q, qi, wi, k_all, v_all, ki_all, pos, k_pos, k_sel)
    br_attn = (o * jax.nn.silu(z_a)) @ w_attn_out
    merged = jax.nn.sigmoid(g_p) * br_pool + jax.nn.sigmoid(g_a) * br_attn
    x = x + merged @ w_o
    return x, k, v, ki, pool_state


def setup_inputs(seed: int = 0) -> dict:
    key = jax.random.key(seed)
    ks = jax.random.split(key, 16)
    f32 = jnp.float32
    return {
        'x_prompt': jax.random.normal(ks[0], (BATCH, SEQ, D_MODEL), f32),
        'x_sample': jax.random.normal(ks[1], (DEC_BATCH, DEC_SEQ, D_MODEL), f32),
        'cache_k': jax.random.normal(ks[2], (DEPTH, DEC_BATCH, PAST_LEN, N_HEADS, HEAD_DIM), f32),
        'cache_v': jax.random.normal(ks[3], (DEPTH, DEC_BATCH, PAST_LEN, N_HEADS, HEAD_DIM), f32),
        'cache_kidx': jax.random.normal(ks[4], (DEPTH, DEC_BATCH, PAST_LEN, IDX_DIM), f32),
        'state_pool': jax.random.normal(ks[5], (DEPTH, DEC_BATCH, POOL_STATE, POOL_WIDTH), f32),
        'norm_g': 1.0 + 0.02 * jax.random.normal(ks[6], (DEPTH, D_MODEL), f32),
        'w_in': jax.random.normal(ks[7], (DEPTH, D_MODEL, IN_COLS), f32) * D_MODEL ** -0.5,
        'w_pool_mix': jax.random.normal(ks[8], (DEPTH, POOL_GROUPS, POOL_GROUP_WIDTH, POOL_GROUP_WIDTH), f32) * POOL_GROUP_WIDTH ** -0.5,
        'pool_scale': 1.0 + 0.02 * jax.random.normal(ks[9], (DEPTH, POOL_WIDTH), f32),
        'w_pool_out': jax.random.normal(ks[10], (DEPTH, POOL_WIDTH, D_MODEL), f32) * POOL_WIDTH ** -0.5,
        'w_attn_out': jax.random.normal(ks[11], (DEPTH, ATTN_WIDTH, D_MODEL), f32) * ATTN_WIDTH ** -0.5,
        'w_o': jax.random.normal(ks[12], (DEPTH, D_MODEL, D_MODEL), f32) * D_MODEL ** -0.5,
        'final_norm_g': 1.0 + 0.02 * jax.random.normal(ks[13], (D_MODEL,), f32),
    }


def reference(x_prompt, x_sample, cache_k, cache_v, cache_kidx, state_pool,
              norm_g, w_in, w_pool_mix, pool_scale, w_pool_out, w_attn_out, w_o, final_norm_g):
    Bp, Tp, _ = x_prompt.shape
    Bs, Ts, _ = x_sample.shape
    past = cache_k.shape[2]
    pos_p = jnp.arange(Tp, dtype=jnp.int32)
    pos_s = past + jnp.arange(Ts, dtype=jnp.int32)
    kpos_s = jnp.arange(past + Ts, dtype=jnp.int32)
    ksel_p = min(TOPK_MAX, Tp // 4)
    ksel_s = min(TOPK_MAX, (past + Ts) // 4)
    zero_left = jnp.zeros((Bp, POOL_STATE, POOL_WIDTH), x_prompt.dtype)

    xp, xs = x_prompt, x_sample
    kp, vp, kip, pp = [], [], [], []
    ksl, vsl, kisl, psl = [], [], [], []
    for l in range(DEPTH):
        wts = (norm_g[l], w_in[l], w_pool_mix[l], pool_scale[l], w_pool_out[l], w_attn_out[l], w_o[l])
        xp, k1, v1, ki1, p1 = layer(xp, pos_p, pos_p, None, None, None, zero_left, ksel_p, *wts)
        xs, k2, v2, ki2, p2 = layer(xs, pos_s, kpos_s, cache_k[l], cache_v[l], cache_kidx[l],
                                    state_pool[l], ksel_s, *wts)
        kp.append(k1); vp.append(v1); kip.append(ki1); pp.append(p1)
        ksl.append(k2); vsl.append(v2); kisl.append(ki2); psl.append(p2)

    y_prompt = rms_norm(xp, final_norm_g)
    y_sample = rms_norm(xs, final_norm_g)
    return (y_prompt, y_sample,
            jnp.stack(kp), jnp.stack(vp), jnp.stack(kip), jnp.stack(pp),
            jnp.stack(ksl), jnp.stack(vsl), jnp.stack(kisl), jnp.stack(psl))
```

```python
from contextlib import ExitStack
import os
import numpy as np
import concourse.bass as bass
import concourse.mybir as mybir
from concourse.bass_utils import run_bass_kernel_spmd

F32 = mybir.dt.float32
BF16 = mybir.dt.bfloat16
ALU = mybir.AluOpType
AF = mybir.ActivationFunctionType

ENGS = ("pe", "act", "dve", "pool", "sp")

D = 1024
NCOL = 5416
C1 = 3368
C2 = NCOL - C1
NH = 8
NEG = -1.0e30
NI_BISECT = int(os.environ.get('KNI', '24'))
EPS = 1e-6


class Buf:
    __slots__ = ("name", "w", "r", "excl")

    def __init__(self, name):
        self.name = name
        self.excl = False
        self.w = None
        self.r = []


class Prog:
    def __init__(self, nc):
        self.nc = nc
        self.q = {e: [] for e in ENGS}
        self.tick = {e: 0 for e in ENGS}
        self.sem = {e: nc.alloc_semaphore("sem_" + e) for e in ENGS}
        self.seen = {e: {} for e in ENGS}
        self.dsem = {}
        self.dtick = {}
        self.nbuf = 0
        self.count = 0
        self.limit = int(os.environ.get("KLIMIT", "1000000000"))

    def buf(self, name=None):
        self.nbuf += 1
        return Buf(name or "b%d" % self.nbuf)

    def _need(self, reads, writes):
        need = {}
        for b in reads:
            if b.w is not None:
                s, t = b.w
                if need.get(s, 0) < t:
                    need[s] = t
        for b in writes:
            if b.w is not None:
                s, t = b.w
                if need.get(s, 0) < t:
                    need[s] = t
            for s, t in b.r:
                if need.get(s, 0) < t:
                    need[s] = t
        return need

    def _waits(self, eng, need):
        waits = []
        seen = self.seen[eng]
        for s, t in need.items():
            if s == eng and eng in ("pe", "sp"):
                continue
            if seen.get(s, 0) >= t:
                continue
            seen[s] = t
            waits.append((s, t))
        return waits

    def _mark(self, src, reads, writes):
        for b in reads:
            if len(b.r) > 24:
                d = {}
                for s, t in b.r:
                    if d.get(s, 0) < t:
                        d[s] = t
                b.r = list(d.items())
            b.r.append(src)
        for b in writes:
            b.w = src
            b.r = []

    def op(self, eng, fn, reads=(), writes=()):
        self.count += 1
        if self.count > self.limit:
            return
        xr = [b for b in reads if b.excl]
        if xr:
            writes = list(writes) + [b for b in xr if b not in writes]
        waits = self._waits(eng, self._need(reads, writes))
        self.tick[eng] += 1
        my = self.tick[eng]
        self.q[eng].append((waits, fn, ("e", eng)))
        self._mark((eng, my), reads, writes)

    def dma(self, stream, fn, reads=(), writes=(), eng="sp"):
        stream = (writes[0] if writes else reads[0]).name
        self.count += 1
        if self.count > self.limit:
            return
        if stream not in self.dsem:
            self.dsem[stream] = self.nc.alloc_semaphore("dsem_" + stream)
            self.dtick[stream] = 0
        waits = self._waits(eng, self._need(reads, writes))
        self.dtick[stream] += 16
        my = self.dtick[stream]
        self.q[eng].append((waits, fn, ("d", stream)))
        self._mark(("dma:" + stream, my), reads, writes)

    def _semof(self, s):
        if s.startswith("dma:"):
            return self.dsem[s[4:]]
        return self.sem[s]

    def barrier(self):
        need = {}
        for e in ENGS:
            if self.tick[e] > 0:
                need[e] = self.tick[e]
        for s, t in self.dtick.items():
            if t > 0:
                need["dma:" + s] = t
        for e in ENGS:
            waits = self._waits(e, dict(need))
            if waits:
                self.q[e].append((waits, None, None))

    def final_wait(self, eng="sp"):
        need = {}
        for s, t in self.dtick.items():
            if t > 0:
                need["dma:" + s] = t
        for e in ENGS:
            if e != eng and self.tick[e] > 0:
                need[e] = self.tick[e]
        waits = self._waits(eng, need)
        if waits:
            self.q[eng].append((waits, None, None))

    def emit(self):
        nc = self.nc
        engobj = {"pe": "tensor", "act": "scalar", "dve": "vector", "pool": "gpsimd", "sp": "sync"}
        with nc.Block() as block:
            for e in ENGS:
                items = self.q[e]
                if not items:
                    continue

                def body(eo, items=items):
                    for waits, fn, inc in items:
                        for s, t in waits:
                            eo.wait_ge(self._semof(s), t)
                        if fn is None:
                            continue
                        ins = fn(eo)
                        if inc[0] == "e":
                            ins.then_inc(self.sem[inc[1]], 1)
                        else:
                            ins.then_inc(self.dsem[inc[1]], 16)

                getattr(block, engobj[e])(body)


def mk(meth, **kw):
    return lambda e: getattr(e, meth)(**kw)


class Tl:
    def __init__(self, t, b):
        self.t = t
        self.b = b

    def __getitem__(self, k):
        return self.t[k]


def _rope_table(positions):
    pos = np.asarray(positions, dtype=np.float32)
    out = np.zeros((len(pos), 192), np.float32)
    for (d, off) in ((64, 0), (32, 128)):
        inv = (10000.0 ** (-np.arange(0, d, 2, dtype=np.float32) / np.float32(d))).astype(np.float32)
        ang = (pos[:, None] * inv[None, :]).astype(np.float32).astype(np.float64)
        c = np.cos(ang).astype(np.float32)
        s = np.sin(ang).astype(np.float32)
        h = d // 2
        out[:, off:off + h] = c
        out[:, off + h:off + 2 * h] = c
        out[:, off + 2 * h:off + 3 * h] = -s
        out[:, off + 3 * h:off + 4 * h] = s
    return out


def _band_tables():
    B = np.zeros((4, 128, 4, 128), np.float32)
    for g, w in enumerate((2, 4, 8, 16)):
        for t in range(128):
            for tp in range(max(0, t - w + 1), t + 1):
                B[0, tp, g, t] += 1.0 / min(t + 1, w)
                B[1, tp, g, t] += 1.0 / w
            B[0, t, g, t] -= 1.0
            B[1, t, g, t] -= 1.0
            for tp in range(128):
                if t - (tp - 128) < w:
                    B[2, tp, g, t] = 1.0 / w
            for j in range(15):
                if t - (j - 15) < w:
                    B[3, j, g, t] = 1.0 / w
    return B


def build(cfg):
    NP, T, TS, L0, DEPTH = cfg["NP"], cfg["T"], cfg["TS"], cfg["L0"], cfg["DEPTH"]
    KP, KS = cfg["KP"], cfg["KS"]
    assert T % 128 == 0 and TS == 64 and L0 % 128 == 0
    NTP = T // 128
    LS = L0 + TS
    NCS = L0 // 128 + 1
    SMAX = max(T, LS)
    KTW = max(4 * T, 2 * LS)
    VS = int(os.environ.get('KVS', '65'))
    VW = max(NTP * 8 * VS, NCS * 4 * VS)
    NTOK = NP * T + TS
    NTILES = NP * NTP + 1

    nc = bass.Bass("TRN2", target_bir_lowering=False, dynamic_dma_scratch_size=256)
    P = Prog(nc)

    def din(name, shape, dt=F32):
        return nc.dram_tensor(name, list(shape), dt, kind="ExternalInput").ap()

    def dout(name, shape, dt=F32):
        return nc.dram_tensor(name, list(shape), dt, kind="ExternalOutput").ap()

    x_p = din("x_p", [NP, T, D])
    x_s = din("x_s", [TS, D])
    ck = din("ck", [DEPTH, L0, 512])
    cv = din("cv", [DEPTH, L0, 512])
    cki = din("cki", [DEPTH, L0, 32])
    spool = din("spool", [DEPTH, 15, 512])
    norm_g = din("norm_g", [DEPTH, D])
    w_in = din("w_in", [DEPTH, D, NCOL])
    w_mix = din("w_mix", [DEPTH, 4, 128, 128])
    pscale = din("pscale", [DEPTH, 512])
    w_po = din("w_po", [DEPTH, 512, D])
    w_ao = din("w_ao", [DEPTH, 512, D])
    w_o = din("w_o", [DEPTH, D, D])
    gfin = din("gfin", [1, D])
    rope = din("rope", [T + TS, 192])
    bands = din("bands", [4, 128, 512])
    pw2 = din("pw2", [128, NI_BISECT])

    y_p = dout("y_p", [NP, T, D])
    y_s = dout("y_s", [TS, D])
    ok_p = dout("ok_p", [DEPTH, NP, T, 512])
    ov_p = dout("ov_p", [DEPTH, NP, T, 512])
    oki_p = dout("oki_p", [DEPTH, NP, T, 32])
    opl_p = dout("opl_p", [DEPTH, NP, 15, 512])
    ok_s = dout("ok_s", [DEPTH, TS, 512])
    ov_s = dout("ov_s", [DEPTH, TS, 512])
    oki_s = dout("oki_s", [DEPTH, TS, 32])
    opl_s = dout("opl_s", [DEPTH, 15, 512])

    xscr = nc.dram_tensor("xscr", [NTOK, D], F32).ap()
    gp_scr = nc.dram_tensor("gp_scr", [NTILES, 128, 512], BF16).ap()
    ga_scr = nc.dram_tensor("ga_scr", [NTILES, 128, 512], BF16).ap()

    seqs = []
    for s in range(NP):
        seqs.append(dict(kind="p", idx=s, T=T, nt=128, ntiles=NTP, tok0=s * T, tile0=s * NTP,
                         x_in=x_p[s], y_out=y_p[s], K=KP, rope0=0))
    seqs.append(dict(kind="s", idx=0, T=TS, nt=TS, ntiles=1, tok0=NP * T, tile0=NP * NTP,
                     x_in=x_s, y_out=y_s, K=KS, rope0=T))

    stack0 = ExitStack()

    uniq = [0]

    def alloc(st, name, shape, dt):
        uniq[0] += 1
        t = st.enter_context(nc.sbuf_tensor("%s_%d" % (name, uniq[0]), list(shape), dt))
        return Tl(t, P.buf(name))

    def palloc(name, shape, dt):
        t = nc.alloc_psum_tensor(name, list(shape), dt)
        b = P.buf(name)
        b.excl = True
        return Tl(t, b)

    mm = [palloc("mm%d" % i, [128, 512], F32) for i in range(2)]
    tp = palloc("tp", [128, 1024], BF16)
    pp = palloc("pp", [128, 512], F32)
    Lb = [palloc("L%d" % i, [128, 512], F32) for i in range(2)]
    Ob = [palloc("O%d" % i, [128, 4, VS], F32) for i in range(2)]
    mmi = [0]

    def next_mm():
        mmi[0] ^= 1
        return mm[mmi[0]]

    ident = alloc(stack0, "ident", [128, 128], BF16)
    identf = alloc(stack0, "identf", [128, 128], F32)
    ones1 = alloc(stack0, "ones1", [1, 128], F32)
    gfbc = alloc(stack0, "gfbc", [128, D], F32)
    band = alloc(stack0, "band", [128, 4, 512], BF16)
    pw = alloc(stack0, "pw", [128, NI_BISECT], F32)
    xt = [alloc(stack0, "xt%d" % i, [128, D], F32) for i in range(2)]
    xn = alloc(stack0, "xn", [128, D], BF16)
    hT = alloc(stack0, "hT", [128, 8, 128], BF16)
    ssq = alloc(stack0, "ssq", [128, 1], F32)
    rstd = alloc(stack0, "rstd", [128, 1], F32)
    gcol = alloc(stack0, "gcol", [128, 8], F32)
    W1 = alloc(stack0, "W1", [128, 8, C1], BF16)
    ftmp = [alloc(stack0, "ftmp%d" % i, [128, 512], F32) for i in range(2)]
    ftmpi = [0]

    def next_ftmp():
        ftmpi[0] ^= 1
        return ftmp[ftmpi[0]]

    P.op("pool", mk("memset", ap=identf[:], constant=0.0), writes=[identf.b])
    P.op("pool", mk("affine_select", out=identf[:], in_=identf[:], pattern=[[-1, 128]],
                                           compare_op=ALU.not_equal, fill=1.0, base=0,
                                           channel_multiplier=1),
         reads=[identf.b], writes=[identf.b])
    P.op("dve", mk("tensor_copy", out=ident[:], in_=identf[:]), reads=[identf.b], writes=[ident.b])
    P.op("dve", mk("memset", ap=ones1[:], constant=1.0), writes=[ones1.b])
    P.dma("c0", mk("dma_start", out=pw[:], in_=pw2), writes=[pw.b])
    for kind in range(4):
        f = next_ftmp()
        P.dma("c1", mk("dma_start", out=f[:], in_=bands[kind]), writes=[f.b])
        P.op("dve", mk("tensor_copy", out=band[:, kind, :], in_=f[:]),
             reads=[f.b], writes=[band.b])
    grow = alloc(stack0, "grow", [1, D], F32)
    P.dma("c0", mk("dma_start", out=grow[:], in_=gfin), writes=[grow.b])
    for hh in range(2):
        m = next_mm()
        P.op("pe", mk("matmul", out=m[:], lhsT=ones1[0:1, :], rhs=grow[0:1, hh * 512:(hh + 1) * 512],
                                                  start=True, stop=True),
             reads=[ones1.b, grow.b], writes=[m.b])
        P.op("dve", mk("tensor_copy", out=gfbc[:, hh * 512:(hh + 1) * 512], in_=m[:]),
             reads=[m.b], writes=[gfbc.b])

    def load_gcol(l):
        g8 = next_ftmp()
        P.dma("c1", mk("dma_start", out=g8[0:8, 0:128], in_=norm_g[l].rearrange("(k p) -> k p", p=128)),
              writes=[g8.b])
        m = next_mm()
        P.op("pe", mk("transpose", out=m[:, 0:8], in_=g8[0:8, 0:128], identity=identf[0:8, 0:8]),
             reads=[g8.b, identf.b], writes=[m.b])
        P.op("dve", mk("tensor_copy", out=gcol[:], in_=m[:, 0:8]), reads=[m.b], writes=[gcol.b])

    def load_cast(dst_ap_fn, src_rows_fn, nk, ncols, stage, scale_gcol, dstb):
        for k in range(nk):
            s = stage[k % 2]
            P.dma("wst%d" % (k % 2), mk("dma_start", out=s[:, 0:ncols], in_=src_rows_fn(k)),
                  writes=[s.b])
            if scale_gcol:
                if k % 2 == 0:
                    P.op("dve", mk("tensor_scalar", out=dst_ap_fn(k), in0=s[:, 0:ncols],
                                                                    scalar1=gcol[:, k:k + 1], scalar2=None,
                                                                    op0=ALU.mult),
                         reads=[s.b, gcol.b], writes=[dstb])
                else:
                    P.op("act", mk("activation", out=dst_ap_fn(k), in_=s[:, 0:ncols],
                                                                 func=AF.Copy, scale=gcol[:, k:k + 1]),
                         reads=[s.b, gcol.b], writes=[dstb])
            else:
                if k % 2 == 0:
                    P.op("dve", mk("tensor_copy", out=dst_ap_fn(k), in_=s[:, 0:ncols]),
                         reads=[s.b], writes=[dstb])
                else:
                    P.op("act", mk("copy", out=dst_ap_fn(k), in_=s[:, 0:ncols]),
                         reads=[s.b], writes=[dstb])

    def load_x(seq, i, slot, src):
        nt = seq["nt"]
        r0 = i * 128
        P.dma("x%d" % slot, mk("dma_start", out=xt[slot][0:nt, :], in_=src[r0:r0 + nt, :]),
              writes=[xt[slot].b])

    def norm_hT(seq, slot):
        nt = seq["nt"]
        x = xt[slot]
        P.op("act", mk("activation", out=xn[0:nt, 0:D], in_=x[0:nt, :], func=AF.Square,
                                           accum_out=ssq[0:nt, :]),
             reads=[x.b], writes=[xn.b, ssq.b])
        P.op("dve", mk("tensor_scalar", out=rstd[0:nt, :], in0=ssq[0:nt, :], scalar1=1.0 / D, scalar2=EPS,
                                              op0=ALU.mult, op1=ALU.add),
             reads=[ssq.b], writes=[rstd.b])
        P.op("act", mk("activation", out=rstd[0:nt, :], in_=rstd[0:nt, :], func=AF.Sqrt),
             reads=[rstd.b], writes=[rstd.b])
        P.op("dve", mk("reciprocal", out=rstd[0:nt, :], in_=rstd[0:nt, :]),
             reads=[rstd.b], writes=[rstd.b])
        P.op("dve", mk("tensor_scalar", out=xn[0:nt, :], in0=x[0:nt, :], scalar1=rstd[0:nt, 0:1],
                                              scalar2=None, op0=ALU.mult),
             reads=[x.b, rstd.b], writes=[xn.b])
        tv = tp[:].rearrange("p (k t) -> p k t", t=128)
        for k in range(8):
            P.op("pe", mk("transpose", out=tv[:, k, 0:nt], in_=xn[0:nt, k * 128:(k + 1) * 128],
                                                  identity=ident[0:nt, 0:nt]),
                 reads=[xn.b, ident.b], writes=[tp.b])
        P.op("act", mk("copy", out=hT[:, 0:4, 0:nt], in_=tv[:, 0:4, 0:nt]), reads=[tp.b], writes=[hT.b])
        P.op("dve", mk("tensor_copy", out=hT[:, 4:8, 0:nt], in_=tv[:, 4:8, 0:nt]), reads=[tp.b], writes=[hT.b])

    def proj(nt, W, c0, ncols):
        m = next_mm()
        for k in range(8):
            P.op("pe", mk("matmul", out=m[0:nt, 0:ncols], lhsT=hT[:, k, 0:nt], rhs=W[:, k, c0:c0 + ncols],
                                               start=(k == 0), stop=(k == 7)),
                 reads=[hT.b, W.b], writes=[m.b])
        return m

    def transpose_to(src, srcb, nt, ncols_list, dst_fn, dstb, evac_eng="act"):
        tv = tp[:].rearrange("p (k t) -> p k t", t=128)
        for j, (c0, wd) in enumerate(ncols_list):
            P.op("pe", mk("transpose", out=tv[0:wd, j, 0:nt], in_=src[0:nt, c0:c0 + wd],
                                                                identity=ident[0:nt, 0:nt]),
                 reads=[srcb, ident.b], writes=[tp.b])
        wmax = max(w for _, w in ncols_list)
        n = len(ncols_list)
        if evac_eng == "none":
            return tv
        if evac_eng == "actbias":
            P.op("act", mk("activation", out=dst_fn(wmax, n), in_=tv[0:wmax, 0:n, 0:nt], func=AF.Copy,
                           scale=30000.0, bias=-30000.0), reads=[tp.b], writes=[dstb])
        elif evac_eng == "act":
            P.op("act", mk("copy", out=dst_fn(wmax, n), in_=tv[0:wmax, 0:n, 0:nt]), reads=[tp.b], writes=[dstb])
        else:
            P.op("dve", mk("tensor_copy", out=dst_fn(wmax, n), in_=tv[0:wmax, 0:n, 0:nt]),
                 reads=[tp.b], writes=[dstb])

    for l in range(DEPTH):
        x_src_of = (lambda seq: seq["x_in"]) if l == 0 else (lambda seq: xscr[seq["tok0"]:seq["tok0"] + seq["T"], :])
        last = (l == DEPTH - 1)

        st1 = ExitStack()
        Wmix = alloc(st1, "Wmix", [128, 4, 128], BF16)
        with ExitStack() as stl:
            if l == 0:
                stage = [alloc(stl, "stg%d" % i, [128, C1], F32) for i in range(2)]
                load_gcol(l)
                load_cast(lambda k: W1[:, k, :], lambda k: w_in[l, k * 128:(k + 1) * 128, 0:C1], 8, C1, stage, True, W1.b)
            srow = next_ftmp()
            P.dma("c1", mk("dma_start", out=srow[0:1, :], in_=pscale[l:l + 1, :]), writes=[srow.b])
            m = next_mm()
            P.op("pe", mk("matmul", out=m[:], lhsT=ones1[0:1, :], rhs=srow[0:1, :], start=True, stop=True),
                 reads=[ones1.b, srow.b], writes=[m.b])
            s0 = next_ftmp()
            P.dma("wst0", mk("dma_start", out=s0[:, 0:512].rearrange("p (g d) -> p g d", g=4),
                                                in_=w_mix[l].rearrange("g c d -> c g d")), writes=[s0.b])
            P.op("dve", mk("tensor_tensor", out=Wmix[:].rearrange("p g d -> p (g d)"), in0=s0[:, 0:512],
                                                  in1=m[:], op=ALU.mult),
                 reads=[s0.b, m.b], writes=[Wmix.b])
            P.barrier()

        kT = alloc(st1, "kT", [128, KTW], BF16)
        Vg = alloc(st1, "Vg", [128, VW], BF16)
        kiT = alloc(st1, "kiT", [96, SMAX], BF16)
        MH = max(T, (LS + 1) // 2)
        score = alloc(st1, "score", [128, 2 * MH], F32)
        sc_bufs = [score.b, P.buf("score1")]
        maskT = alloc(st1, "maskT", [128, max(NTP * 128, NCS * 64)], BF16)
        utok = [alloc(st1, "utok%d" % i, [128, 512], BF16) for i in range(2)]
        szp = alloc(st1, "szp", [128, 512], BF16)
        szas = [alloc(st1, "sza%d" % i, [128, 512], BF16) for i in range(3)]
        qf = alloc(st1, "qf", [128, 512], F32)
        kf = [alloc(st1, "kf%d" % i, [128, 512], F32) for i in range(2)]
        vf = [alloc(st1, "vf%d" % i, [128, 512], F32) for i in range(2)]
        g5 = [alloc(st1, "g5%d" % i, [128, 296], F32) for i in range(2)]
        uf = ftmp[0]
        rtab = [alloc(st1, "rtab%d" % i, [128, 192], F32) for i in range(2)]
        rtmp = alloc(st1, "rtmp", [128, 512], F32)
        qb = alloc(st1, "qb", [128, 512], BF16)
        kb = alloc(st1, "kb", [128, 512], BF16)
        qsb = alloc(st1, "qsb", [128, 256], BF16)
        ki3 = alloc(st1, "ki3", [128, 96], BF16)
        Dg = alloc(st1, "Dg", [128, 8, 128], BF16)
        plT = alloc(st1, "plT", [128, 4, 128], BF16)
        gpt = alloc(st1, "gpt", [128, 512], BF16)
        gpT = [alloc(st1, "gpT%d" % i, [128, 4, 128], BF16) for i in range(2)]
        gaT = [alloc(st1, "gaT%d" % i, [128, 4, 128], BF16) for i in range(2)]
        qTs = [alloc(st1, "qT%d" % i, [128, 8, 128], BF16) for i in range(3)]
        qiT = alloc(st1, "qiT", [96, 3, 128], BF16)
        rel = [alloc(st1, "rel%d" % i, [128, 512], BF16) for i in range(3)]
        maskb = alloc(st1, "maskb", [128, 2 * MH], BF16)
        mb_bufs = [maskb.b, P.buf("maskb1")]
        praw = [alloc(st1, "praw%d" % i, [128, 512], BF16) for i in range(2)]
        pm = [alloc(st1, "pm%d" % i, [128, 512], BF16) for i in range(2)]
        on = alloc(st1, "on", [128, 512], F32)
        gat = alloc(st1, "gat", [128, 512], BF16)
        rs = alloc(st1, "rs", [128, 8], F32)
        bs_lo = alloc(st1, "bs_lo", [128, 1], F32)
        bs_hi = alloc(st1, "bs_hi", [128, 1], F32)
        bs_mid = [alloc(st1, "bs_mid%d" % i, [128, 1], F32) for i in range(2)]
        bs_cnt = alloc(st1, "bs_cnt", [128, 1], F32)
        bs_cnt2 = alloc(st1, "bs_cnt2", [128, 1], F32)
        bs_u = alloc(st1, "bs_u", [128, 1], F32)
        bs_ht = alloc(st1, "bs_ht", [128, NI_BISECT], F32)
        bs_thr = alloc(st1, "bs_thr", [128, 1], F32)
        cst = [alloc(st1, "cst%d" % i, [128, 2, 256], F32) for i in range(2)]
        cbf = alloc(st1, "cbf", [128, 2, 256], BF16)
        spf = alloc(st1, "spf", [16, 512], F32)
        spb = alloc(st1, "spb", [16, 512], BF16)
        ckif = alloc(st1, "ckif", [128, max(L0 // 128, 1), 32], F32)
        cki3 = alloc(st1, "cki3", [128, 96], BF16)

        rot = {"rel": 0, "mk": 0, "praw": 0, "pm": 0, "L": 0}

        def nxt(name, arr):
            rot[name] ^= 1
            return arr[rot[name]]

        P.op("pool", mk("memset", ap=Vg[:], constant=1.0), writes=[Vg.b])
        for qz in qTs:
            P.op("pool", mk("memset", ap=qz[:], constant=0.0), writes=[qz.b])

        def rope_inplace(t, nt, nh, hd, tab, toff, eng="pool"):
            h2 = hd // 2
            xv = lambda: t.rearrange("p (h d) -> p h d", d=hd)
            tv = lambda: rtmp[0:nt, 0:nh * hd].rearrange("p (h d) -> p h d", d=hd)
            cc = lambda: tab[0:nt, toff:toff + hd].unsqueeze(1).broadcast_to([nt, nh, hd])
            sn = lambda: tab[0:nt, toff + hd:toff + hd + h2].unsqueeze(1).broadcast_to([nt, nh, h2])
            sp_ = lambda: tab[0:nt, toff + hd + h2:toff + 2 * hd].unsqueeze(1).broadcast_to([nt, nh, h2])
            return xv, tv, cc, sn, sp_, h2

        def do_rope(tl, col0, nt, nh, hd, tab, toff, eng):
            t = tl[0:nt, col0:col0 + nh * hd]
            xv, tv, cc, sn, sp_, h2 = rope_inplace(t, nt, nh, hd, tab, toff)
            P.op(eng, mk("tensor_tensor", out=tv()[:, :, 0:h2], in0=xv()[:, :, h2:hd], in1=sn(), op=ALU.mult),
                 reads=[tl.b, tab.b], writes=[rtmp.b])
            P.op(eng, mk("tensor_tensor", out=tv()[:, :, h2:hd], in0=xv()[:, :, 0:h2], in1=sp_(), op=ALU.mult),
                 reads=[tl.b, tab.b], writes=[rtmp.b])
            P.op(eng, mk("tensor_tensor", out=xv(), in0=xv(), in1=cc(), op=ALU.mult),
                 reads=[tl.b, tab.b], writes=[tl.b])
            P.op(eng, mk("tensor_tensor", out=xv(), in0=xv(), in1=tv(), op=ALU.add),
                 reads=[tl.b, rtmp.b], writes=[tl.b])

        gtile = [0]

        for seq in seqs:
            nt = seq["nt"]
            ntl = seq["ntiles"]
            isS = seq["kind"] == "s"
            xsrc = x_src_of(seq)
            Ksel = seq["K"]
            if not isS:
                kTv = kT[:, 0:4 * T].rearrange("p (c s) -> p c s", c=4)
                Vv = Vg[:, 0:NTP * 8 * VS].rearrange("p (t h d) -> p t h d", h=8, d=VS)
                okd, ovd, okid, opld = ok_p[l, seq["idx"]], ov_p[l, seq["idx"]], oki_p[l, seq["idx"]], opl_p[l, seq["idx"]]
            else:
                kTv = kT[:, 0:2 * LS].rearrange("p (c s) -> p c s", c=2)
                Vv = Vg[:, 0:NCS * 4 * VS].rearrange("p (t h d) -> p t h d", h=4, d=VS)
                okd, ovd, okid, opld = ok_s[l], ov_s[l], oki_s[l], opl_s[l]

            if isS and L0 > 0:
                nct = L0 // 128
                P.dma("cki", mk("dma_start", out=ckif[:, 0:nct, :],
                                                   in_=cki[l].rearrange("(t p) d -> p t d", p=128)),
                      writes=[ckif.b])
                for c in range(nct):
                    P.op("dve", mk("tensor_copy",
                        out=cki3[:].rearrange("p (r d) -> p r d", r=3),
                        in_=ckif[:, c, :].unsqueeze(1).broadcast_to([128, 3, 32])),
                        reads=[ckif.b], writes=[cki3.b])
                    transpose_to(cki3, cki3.b, 128, [(0, 96)],
                                 lambda wm, n, c=c: kiT[0:96, c * 128:(c + 1) * 128].unsqueeze(1), kiT.b,
                                 evac_eng="act" if c % 2 else "dve")
                P.dma("spf", mk("dma_start", out=spf[0:15, :], in_=spool[l]), writes=[spf.b])
                P.op("dve", mk("tensor_copy", out=spb[0:15, :], in_=spf[0:15, :]), reads=[spf.b], writes=[spb.b])

            def stageA1(i):
                qT = qTs[gtile[0] % 3]
                sza = szas[gtile[0] % 3]
                gi = gtile[0]
                gtile[0] += 1
                slot = gi % 2
                if i + 1 < ntl:
                    load_x(seq, i + 1, (gi + 1) % 2, xsrc)
                rt = rtab[gi % 2]
                rp0 = seq["rope0"] + i * 128
                P.dma("rt%d" % (gi % 2), mk("dma_start", out=rt[0:nt, :], in_=rope[rp0:rp0 + nt, :]),
                      writes=[rt.b])
                norm_hT(seq, slot)
                key0 = (L0 if isS else 0) + i * 128
                S = key0 + nt
                if isS:
                    sview = score[:, 0:S]
                    sbufs = [sc_bufs[0], sc_bufs[1]]
                else:
                    sview = score[:, (gtile[0] - 1) % 2 * MH:(gtile[0] - 1) % 2 * MH + S]
                    sbufs = [sc_bufs[(gtile[0] - 1) % 2]]
                ucur = utok[gi % 2]
                uprev = utok[(gi + 1) % 2]
                kfi = kf[gi % 2]
                vfi = vf[gi % 2]
                g5i = g5[gi % 2]

                yield
                m = proj(nt, W1, 0, 512)
                P.op("act", mk("copy", out=ucur[0:nt, :], in_=m[0:nt, :]), reads=[m.b], writes=[ucur.b])
                if i == ntl - 1:
                    P.op("dve", mk("tensor_copy", out=uf[0:nt, :], in_=m[0:nt, :]), reads=[m.b], writes=[uf.b])
                    P.dma("opl", mk("dma_start", out=opld, in_=uf[nt - 15:nt, :]), reads=[uf.b])
                yield
                m = proj(nt, W1, 512, 512)
                P.op("act", mk("activation", out=szp[0:nt, :], in_=m[0:nt, :], func=AF.Silu),
                     reads=[m.b], writes=[szp.b])
                yield
                m = proj(nt, W1, 1024, 512)
                P.op("act", mk("copy", out=qf[0:nt, :], in_=m[0:nt, :]), reads=[m.b], writes=[qf.b])
                yield
                m = proj(nt, W1, 1536, 512)
                P.op("act", mk("copy", out=kfi[0:nt, :], in_=m[0:nt, :]), reads=[m.b], writes=[kfi.b])
                yield
                m = proj(nt, W1, 2048, 512)
                P.op("act", mk("copy", out=vfi[0:nt, :], in_=m[0:nt, :]), reads=[m.b], writes=[vfi.b])
                if not isS:
                    P.op("dve", mk("tensor_copy", out=Vv[0:nt, i, :, 0:64],
                                   in_=m[0:nt, :].rearrange("p (h d) -> p h d", d=64)),
                         reads=[m.b], writes=[Vg.b])
                P.dma("ov%d" % (gi % 2), mk("dma_start", out=ovd[i * 128:i * 128 + nt, :], in_=vfi[0:nt, :]),
                      reads=[vfi.b])
                yield
                m = proj(nt, W1, 2560, 296)
                P.op("act", mk("copy", out=g5i[0:nt, :], in_=m[0:nt, 0:296]), reads=[m.b], writes=[g5i.b])
                yield
                m = proj(nt, W1, 2856, 512)
                P.op("act", mk("activation", out=sza[0:nt, :], in_=m[0:nt, :], func=AF.Silu),
                     reads=[m.b], writes=[sza.b])

                yield
                yield
                do_rope(qf, 0, nt, 8, 64, rt, 0, "pool")
                yield
                do_rope(kfi, 0, nt, 8, 64, rt, 0, "pool")
                yield
                do_rope(g5i, 0, nt, 8, 32, rt, 128, "pool")
                do_rope(g5i, 256, nt, 1, 32, rt, 128, "pool")
                P.dma("ok%d" % (gi % 2), mk("dma_start", out=okd[i * 128:i * 128 + nt, :], in_=kfi[0:nt, :]),
                      reads=[kfi.b])
                P.dma("oki%d" % (gi % 2), mk("dma_start", out=okid[i * 128:i * 128 + nt, :], in_=g5i[0:nt, 256:288]),
                      reads=[g5i.b])
                P.op("pool", mk("tensor_copy", out=qb[0:nt, :], in_=qf[0:nt, :]), reads=[qf.b], writes=[qb.b])
                P.op("pool", mk("tensor_copy", out=kb[0:nt, :], in_=kfi[0:nt, :]), reads=[kfi.b], writes=[kb.b])
                for h in range(8):
                    P.op("dve", mk("tensor_scalar", out=Dg[0:nt, h, 0:nt], in0=identf[0:nt, 0:nt],
                                   scalar1=g5i[0:nt, 288 + h:289 + h], scalar2=None, op0=ALU.mult),
                         reads=[identf.b, g5i.b], writes=[Dg.b])
                P.op("pool", mk("tensor_copy", out=qsb[0:nt, :], in_=g5i[0:nt, 0:256]), reads=[g5i.b], writes=[qsb.b])
                P.op("dve", mk("tensor_copy", out=ki3[0:nt, :].rearrange("p (r d) -> p r d", r=3),
                                                    in_=g5i[0:nt, 256:288].unsqueeze(1).broadcast_to([nt, 3, 32])),
                     reads=[g5i.b], writes=[ki3.b])

                yield
                transpose_to(ki3, ki3.b, nt, [(0, 96)],
                             lambda wm, n: kiT[0:96, key0:key0 + nt].unsqueeze(1), kiT.b, "act")
                transpose_to(qsb, qsb.b, nt, [(0, 96), (96, 96), (192, 64)],
                             lambda wm, n: qiT[0:96, 0:3, 0:nt], qiT.b, "dve")
                tvq = transpose_to(qb, qb.b, nt, [(c * 128, 128) for c in range(4)], None, None, "none")
                P.op("act", mk("copy", out=qT[0:64, 0:8:2, 0:nt], in_=tvq[0:64, 0:4, 0:nt]), reads=[tp.b], writes=[qT.b])
                P.op("act", mk("copy", out=qT[64:128, 1:8:2, 0:nt], in_=tvq[64:128, 0:4, 0:nt]), reads=[tp.b], writes=[qT.b])
                if not isS:
                    transpose_to(kb, kb.b, nt, [(c * 128, 128) for c in range(4)],
                                 lambda wm, n: kTv[:, 0:4, key0:key0 + nt], kT.b, "dve")

                yield
                ppv = pp[:].rearrange("p (g t) -> p g t", g=4)
                for g in range(4):
                    if isS:
                        P.op("pe", mk("matmul", out=ppv[:, g, 0:nt], lhsT=ucur[0:nt, g * 128:(g + 1) * 128],
                                                           rhs=band[0:nt, 1, g * 128:g * 128 + nt], start=True, stop=False),
                             reads=[ucur.b, band.b], writes=[pp.b])
                        P.op("pe", mk("matmul", out=ppv[:, g, 0:nt], lhsT=spb[0:15, g * 128:(g + 1) * 128],
                                                           rhs=band[0:15, 3, g * 128:g * 128 + nt], start=False, stop=True),
                             reads=[spb.b, band.b], writes=[pp.b])
                    elif i == 0:
                        P.op("pe", mk("matmul", out=ppv[:, g, 0:nt], lhsT=ucur[0:nt, g * 128:(g + 1) * 128],
                                                           rhs=band[0:nt, 0, g * 128:g * 128 + nt], start=True, stop=True),
                             reads=[ucur.b, band.b], writes=[pp.b])
                    else:
                        P.op("pe", mk("matmul", out=ppv[:, g, 0:nt], lhsT=ucur[0:nt, g * 128:(g + 1) * 128],
                                                           rhs=band[0:nt, 1, g * 128:g * 128 + nt], start=True, stop=False),
                             reads=[ucur.b, band.b], writes=[pp.b])
                        P.op("pe", mk("matmul", out=ppv[:, g, 0:nt], lhsT=uprev[:, g * 128:(g + 1) * 128],
                                                           rhs=band[:, 2, g * 128:g * 128 + nt], start=False, stop=True),
                             reads=[uprev.b, band.b], writes=[pp.b])
                P.op("act", mk("copy", out=plT[:, :, 0:nt], in_=ppv[:, :, 0:nt]), reads=[pp.b], writes=[plT.b])
                m = next_mm()
                for g in range(4):
                    P.op("pe", mk("matmul", out=m[0:nt, g * 128:(g + 1) * 128], lhsT=plT[:, g, 0:nt],
                                                            rhs=Wmix[:, g, :], start=True, stop=True),
                         reads=[plT.b, Wmix.b], writes=[m.b])
                P.op("dve", mk("tensor_tensor", out=gpt[0:nt, :], in0=m[0:nt, :], in1=szp[0:nt, :], op=ALU.mult),
                     reads=[m.b, szp.b], writes=[gpt.b])
                gpo = gpT[gi % 2]
                transpose_to(gpt, gpt.b, nt, [(c * 128, 128) for c in range(4)],
                             lambda wm, n: gpo[:, 0:4, 0:nt], gpo.b, "act")
                tix = seq["tile0"] + i
                P.dma("gp%d" % (gi % 2), mk("dma_start",
                    out=gp_scr[tix].rearrange("p (c t) -> p c t", c=4)[:, :, 0:nt], in_=gpo[:, :, 0:nt]),
                    reads=[gpo.b])

                yield
                ngk = (S + 511) // 512
                for gk in range(ngk):
                    k0 = gk * 512
                    gw = min(512, S - k0)
                    prev = None
                    for h in range(8):
                        if h % 2 == 0:
                            yield
                        m = next_mm()
                        bp = 32 * (h % 3)
                        P.op("pe", mk("matmul", out=
                            m[0:nt, 0:gw], lhsT=qiT[bp:bp + 32, h // 3, 0:nt], rhs=kiT[bp:bp + 32, k0:k0 + gw],
                            start=True, stop=True),
                            reads=[qiT.b, kiT.b], writes=[m.b])
                        rot["rel"] = (rot["rel"] + 1) % 3
                        r = rel[rot["rel"]]
                        P.op("act", mk("activation", out=r[0:nt, 0:gw], in_=m[0:nt, 0:gw], func=AF.Relu),
                             reads=[m.b], writes=[r.b])
                        if prev is not None:
                            ph, prr = prev
                            P.op("pe", mk("matmul", out=pp[0:nt, 0:gw], lhsT=Dg[0:nt, ph, 0:nt], rhs=prr[0:nt, 0:gw],
                                          start=(ph == 0), stop=False),
                                 reads=[Dg.b, prr.b], writes=[pp.b])
                        prev = (h, r)
                    ph, prr = prev
                    P.op("pe", mk("matmul", out=pp[0:nt, 0:gw], lhsT=Dg[0:nt, ph, 0:nt], rhs=prr[0:nt, 0:gw],
                                  start=False, stop=True),
                         reads=[Dg.b, prr.b], writes=[pp.b])
                    P.op("act", mk("copy", out=sview[0:nt, k0:k0 + gw], in_=pp[0:nt, 0:gw]), reads=[pp.b], writes=sbufs)
                return dict(i=i, gi=gi, S=S, ngk=ngk, qT=qT, sza=sza, kfi=kfi, vfi=vfi, tix=tix,
                            sview=sview, sbufs=sbufs)

            def stageA2(c):
                gi, S, sview, sbufs = c["gi"], c["S"], c["sview"], c["sbufs"]
                nch = (S + 127) // 128
                if isS:
                    mview = maskb[:, 0:S]
                    mbufs = [mb_bufs[0], mb_bufs[1]]
                else:
                    mview = maskb[:, (gi % 2) * MH:(gi % 2) * MH + S]
                    mbufs = [mb_bufs[gi % 2]]
                c["nch"], c["mview"], c["mbufs"] = nch, mview, mbufs
                yield
                need_topk = S > Ksel
                if not isS:
                    if need_topk:
                        P.op("dve", mk("tensor_reduce", out=bs_lo[0:nt, :], in_=sview[0:nt, 0:S - 64],
                                       axis=mybir.AxisListType.X, op=ALU.min),
                             reads=sbufs, writes=[bs_lo.b])
                    P.op("dve", mk("memset", ap=sview[0:64, S - 64:S], constant=NEG), writes=sbufs)
                else:
                    if need_topk:
                        P.op("dve", mk("tensor_reduce", out=bs_lo[0:nt, :], in_=sview[0:nt, 0:S],
                                       axis=mybir.AxisListType.X, op=ALU.min),
                             reads=sbufs, writes=[bs_lo.b])
                if need_topk:
                    P.op("dve", mk("tensor_reduce", out=bs_hi[0:nt, :], in_=sview[0:nt, 0:S],
                                   axis=mybir.AxisListType.X, op=ALU.max),
                         reads=sbufs, writes=[bs_hi.b])
                    P.op("dve", mk("tensor_tensor", out=bs_hi[0:nt, :], in0=bs_hi[0:nt, :], in1=bs_lo[0:nt, :],
                                   op=ALU.subtract),
                         reads=[bs_hi.b, bs_lo.b], writes=[bs_hi.b])
                    P.op("dve", mk("tensor_scalar", out=bs_ht[0:nt, :], in0=pw[0:nt, :], scalar1=bs_hi[0:nt, 0:1],
                                   scalar2=None, op0=ALU.mult),
                         reads=[pw.b, bs_hi.b], writes=[bs_ht.b])
                    P.op("dve", mk("tensor_tensor", out=bs_mid[0][0:nt, :], in0=bs_lo[0:nt, :], in1=bs_ht[0:nt, 0:1],
                                   op=ALU.add),
                         reads=[bs_lo.b, bs_ht.b], writes=[bs_mid[0].b])
                    for it in range(NI_BISECT):
                        yield
                        mc = bs_mid[it % 2]
                        mn = bs_mid[(it + 1) % 2]
                        P.op("dve", mk("tensor_scalar", out=mview[0:nt, 0:S], in0=sview[0:nt, 0:S],
                                       scalar1=mc[0:nt, 0:1], scalar2=None, op0=ALU.is_ge, op1=ALU.add,
                                       accum_out=bs_cnt[0:nt, :]),
                             reads=sbufs + [mc.b], writes=mbufs + [bs_cnt.b])
                        P.op("dve", mk("tensor_scalar", out=bs_u[0:nt, :], in0=bs_cnt[0:nt, :],
                                       scalar1=float(Ksel) - 0.5, scalar2=0.5, op0=ALU.is_ge, op1=ALU.subtract),
                             reads=[bs_cnt.b], writes=[bs_u.b])
                        P.op("dve", mk("scalar_tensor_tensor", out=mn[0:nt, :], in0=bs_u[0:nt, :],
                                       scalar=bs_ht[0:nt, it:it + 1], in1=mc[0:nt, :], op0=ALU.mult, op1=ALU.add),
                             reads=[bs_u.b, bs_ht.b, mc.b], writes=[mn.b])
                    mfin = bs_mid[NI_BISECT % 2]
                    P.op("dve", mk("scalar_tensor_tensor", out=bs_thr[0:nt, :],
                                   in0=bs_ht[0:nt, NI_BISECT - 1:NI_BISECT], scalar=-0.5, in1=mfin[0:nt, :],
                                   op0=ALU.mult, op1=ALU.add),
                         reads=[bs_ht.b, mfin.b], writes=[bs_thr.b])
                else:
                    P.op("dve", mk("memset", ap=bs_thr[0:nt, :], constant=-1.0e29), writes=[bs_thr.b])
                yield
                P.op("dve", mk("tensor_scalar", out=mview[0:nt, :], in0=sview[0:nt, 0:S], scalar1=bs_thr[0:nt, 0:1],
                               scalar2=None, op0=ALU.is_ge),
                     reads=sbufs + [bs_thr.b], writes=mbufs)

            def stageB1(c):
                i, gi, S, nch, ngk, qT, mview, mbufs, vfi = (c["i"], c["gi"], c["S"], c["nch"], c["ngk"], c["qT"],
                                                             c["mview"], c["mbufs"], c["vfi"])
                mTv = maskT[:, 0:nch * nt].rearrange("p (c t) -> p c t", t=nt)
                for gk in range(ngk):
                    yield
                    k0 = gk * 512
                    gw = min(512, S - k0)
                    blocks = []
                    cc_ = 0
                    while cc_ * 128 < gw:
                        blocks.append((k0 + cc_ * 128, min(128, gw - cc_ * 128)))
                        cc_ += 1
                    full = [b for b in blocks if b[1] == 128]
                    part = [b for b in blocks if b[1] < 128]
                    if full:
                        transpose_to(mview, mbufs[0], nt, full,
                                     lambda wm, n, k0=k0: mTv[:, k0 // 128:k0 // 128 + n, :], maskT.b, "actbias")
                    if part:
                        pc0, pw_ = part[0]
                        transpose_to(mview, mbufs[-1], nt, [(pc0, pw_)],
                                     lambda wm, n, pc0=pc0: mTv[0:wm, pc0 // 128:pc0 // 128 + 1, :],
                                     maskT.b, "actbias")
                halves = [(0, 8)] if not isS else [(0, 4), (4, 8)]
                for (h0, h1) in halves:
                    if isS:
                        hh = h0 // 4
                        nct = L0 // 128
                        for c4 in range(0, nct, 2):
                            n4 = min(2, nct - c4)
                            stg = cst[(c4 // 2) % 2]
                            P.dma("cst", mk("dma_start",
                                out=stg[:, 0:n4, :],
                                in_=ck[l].rearrange("(t p) f -> p t f", p=128)[:, c4:c4 + n4, hh * 256:(hh + 1) * 256]),
                                writes=[stg.b])
                            P.op("pool", mk("tensor_copy", out=cbf[:, 0:n4, :], in_=stg[:, 0:n4, :]),
                                 reads=[stg.b], writes=[cbf.b])
                            for j in range(n4):
                                c = c4 + j
                                transpose_to(cbf[:, j, :], cbf.b, 128, [(0, 128), (128, 128)],
                                             lambda wm, n, c=c: kTv[:, 0:2, c * 128:(c + 1) * 128], kT.b,
                                             "act" if j % 2 else "dve")
                            stg2 = cst[(c4 // 2 + 1) % 2]
                            P.dma("cst", mk("dma_start",
                                out=stg2[:, 0:n4, :],
                                in_=cv[l].rearrange("(t p) f -> p t f", p=128)[:, c4:c4 + n4, hh * 256:(hh + 1) * 256]),
                                writes=[stg2.b])
                            P.op("pool", mk("tensor_copy",
                                out=Vv[:, c4:c4 + n4, :, 0:64],
                                in_=stg2[:, 0:n4, :].rearrange("p t (h d) -> p t h d", d=64)),
                                reads=[stg2.b], writes=[Vg.b])
                        transpose_to(kb[:, hh * 256:(hh + 1) * 256], kb.b, nt, [(0, 128), (128, 128)],
                                     lambda wm, n: kTv[:, 0:2, L0:L0 + nt], kT.b, "act")
                        P.op("pool", mk("tensor_copy",
                            out=Vv[0:nt, NCS - 1, :, 0:64],
                            in_=vfi[0:nt, hh * 256:(hh + 1) * 256].rearrange("p (h d) -> p h d", d=64)),
                            reads=[vfi.b], writes=[Vg.b])
                    for h in range(h0, h1 if not os.environ.get('KNOATT') else h0 + 1):
                        hl = h - h0
                        pr, po = hl // 2, 64 * (hl % 2)
                        qpr, qpo = h // 2, 64 * (h % 2)
                        O = Ob[h // 4]
                        chunks = [(c, min(128, S - c * 128)) for c in range(nch)]
                        groups = []
                        cur = []
                        for cpair in chunks:
                            if cur and (len(cur) == 4 or cur[-1][1] != cpair[1]):
                                groups.append(cur)
                                cur = []
                            cur.append(cpair)
                        if cur:
                            groups.append(cur)
                        for gidx, grp in enumerate(groups):
                            yield
                            rc = grp[0][1]
                            ng = len(grp)
                            c0 = grp[0][0]
                            Lt = nxt("L", Lb)
                            Lv = Lt[:].rearrange("p (c t) -> p c t", t=128)
                            for j, (c, _) in enumerate(grp):
                                P.op("pe", mk("matmul", out=Lv[0:rc, j, 0:nt], lhsT=ident[0:rc, 0:rc],
                                              rhs=mTv[0:rc, c, :], start=True, stop=False),
                                     reads=[ident.b, maskT.b], writes=[Lt.b])
                                P.op("pe", mk("matmul", out=
                                    Lv[0:rc, j, 0:nt], lhsT=kTv[:, pr, c * 128:c * 128 + rc],
                                    rhs=qT[:, h, 0:nt], start=False, stop=True),
                                    reads=[kT.b, qT.b], writes=[Lt.b])
                            pr_t = nxt("praw", praw)
                            prv = pr_t[:].rearrange("p (c t) -> p c t", t=128)
                            P.op("act", mk("activation",
                                out=prv[0:rc, 0:ng, 0:nt], in_=Lv[0:rc, 0:ng, 0:nt], func=AF.Exp, scale=0.125),
                                reads=[Lt.b], writes=[pr_t.b])
                            pm_t = pr_t
                            pmv = prv
                            for j, (c, _) in enumerate(grp):
                                first = (gidx == 0 and j == 0)
                                lastc = (gidx == len(groups) - 1 and j == ng - 1)
                                P.op("pe", mk("matmul", out=
                                    O[0:nt, h % 4, :], lhsT=pmv[0:rc, j, 0:nt], rhs=Vv[0:rc, c, hl if isS else h, :],
                                    start=first, stop=lastc),
                                    reads=[pm_t.b, Vg.b], writes=[O.b])

            def stageB2(c):
                gi, sza, tix = c["gi"], c["sza"], c["tix"]
                for ob in range(2):
                    O = Ob[ob]
                    P.op("dve", mk("reciprocal", out=rs[0:nt, ob * 4:ob * 4 + 4], in_=O[0:nt, :, 64]),
                         reads=[O.b], writes=[rs.b])
                    P.op("dve", mk("tensor_tensor",
                                   out=on[0:nt, ob * 256:(ob + 1) * 256].rearrange("p (h d) -> p h d", d=64),
                                   in0=O[0:nt, :, 0:64],
                                   in1=rs[0:nt, ob * 4:ob * 4 + 4].unsqueeze(2).broadcast_to([nt, 4, 64]),
                                   op=ALU.mult),
                         reads=[O.b, rs.b], writes=[on.b])
                P.op("pool", mk("tensor_tensor", out=gat[0:nt, :], in0=on[0:nt, :], in1=sza[0:nt, :], op=ALU.mult),
                     reads=[on.b, sza.b], writes=[gat.b])
                gao = gaT[gi % 2]
                transpose_to(gat, gat.b, nt, [(c * 128, 128) for c in range(4)],
                             lambda wm, n: gao[:, 0:4, 0:nt], gao.b, "act")
                P.dma("ga%d" % (gi % 2), mk("dma_start",
                    out=ga_scr[tix].rearrange("p (c t) -> p c t", c=4)[:, :, 0:nt], in_=gao[:, :, 0:nt]),
                    reads=[gao.b])

            load_x(seq, 0, gtile[0] % 2, xsrc)

            def drive(gens):
                res = [None] * len(gens)
                live = [g is not None for g in gens]
                while any(live):
                    for k, g in enumerate(gens):
                        if live[k]:
                            try:
                                next(g)
                            except StopIteration as ex:
                                res[k] = ex.value
                                live[k] = False
                return res

            ctx = {}
            for k in range(ntl + 2):
                gA1 = stageA1(k) if k < ntl else None
                gA2 = stageA2(ctx[k - 1]) if 0 <= k - 1 < ntl else None
                gB1 = stageB1(ctx[k - 2]) if 0 <= k - 2 < ntl else None
                r = drive([gA1, gA2, gB1])
                if gA1 is not None:
                    ctx[k] = r[0]
                if gB1 is not None:
                    stageB2(ctx[k - 2])
        P.barrier()
        st1.close()

        st2 = ExitStack()
        W2 = alloc(st2, "W2", [128, 8, C2], BF16)
        Wpo = alloc(st2, "Wpo", [128, 4, D], BF16)
        Wao = alloc(st2, "Wao", [128, 4, D], BF16)
        Wo = alloc(st2, "Wo", [128, 8, D], BF16)
        with ExitStack() as stl:
            stage = [alloc(stl, "stg%d" % i, [128, C2], F32) for i in range(2)]
            load_cast(lambda k: W2[:, k, :], lambda k: w_in[l, k * 128:(k + 1) * 128, C1:NCOL], 8, C2, stage, True, W2.b)
            load_cast(lambda k: Wpo[:, k, :], lambda k: w_po[l, k * 128:(k + 1) * 128, :], 4, D, stage, False, Wpo.b)
            load_cast(lambda k: Wao[:, k, :], lambda k: w_ao[l, k * 128:(k + 1) * 128, :], 4, D, stage, False, Wao.b)
            load_cast(lambda k: Wo[:, k, :], lambda k: w_o[l, k * 128:(k + 1) * 128, :], 8, D, stage, False, Wo.b)
            P.barrier()
        gpl = [alloc(st2, "gpl%d" % i, [128, 4, 128], BF16) for i in range(3)]
        gal = [alloc(st2, "gal%d" % i, [128, 4, 128], BF16) for i in range(3)]
        sgps = [alloc(st2, "sgp%d" % i, [128, D], BF16) for i in range(2)]
        sgas = [alloc(st2, "sga%d" % i, [128, D], BF16) for i in range(2)]
        xt.append(alloc(st2, "xt2", [128, D], F32))
        m1 = alloc(st2, "m1", [128, D], F32)
        t2 = alloc(st2, "t2", [128, 512], F32)
        mrg = alloc(st2, "mrg", [128, D], BF16)
        mT = alloc(st2, "mT", [128, 8, 128], BF16)
        yt = [alloc(st2, "yt%d" % i, [128, D], F32) for i in range(2)]

        prefetch = []
        if l + 1 < DEPTH:
            pst = [alloc(st2, "pst%d" % i, [128, C1], F32) for i in range(2)]
            load_gcol(l + 1)

            def mk_chunk(k):
                def emit():
                    sgt = pst[k % 2]
                    P.dma("pst", mk("dma_start", out=sgt[:, 0:C1], in_=w_in[l + 1, k * 128:(k + 1) * 128, 0:C1]),
                          writes=[sgt.b])
                    if k % 2 == 0:
                        P.op("dve", mk("tensor_scalar", out=W1[:, k, :], in0=sgt[:, 0:C1], scalar1=gcol[:, k:k + 1],
                                       scalar2=None, op0=ALU.mult), reads=[sgt.b, gcol.b], writes=[W1.b])
                    else:
                        P.op("act", mk("activation", out=W1[:, k, :], in_=sgt[:, 0:C1], func=AF.Copy,
                                       scale=gcol[:, k:k + 1]), reads=[sgt.b, gcol.b], writes=[W1.b])
                return emit
            prefetch = [mk_chunk(k) for k in range(8)]
        gtile2 = [0]
        for seq in seqs:
            nt = seq["nt"]
            ntl = seq["ntiles"]
            xsrc = x_src_of(seq)

            def loads2(i, gi):
                load_x(seq, i, gi % 3, xsrc)
                tix = seq["tile0"] + i
                P.dma("gpl%d" % (gi % 3), mk("dma_start",
                    out=gpl[gi % 3][:, :, 0:nt], in_=gp_scr[tix].rearrange("p (c t) -> p c t", c=4)[:, :, 0:nt]),
                    writes=[gpl[gi % 3].b])
                P.dma("gal%d" % (gi % 3), mk("dma_start",
                    out=gal[gi % 3][:, :, 0:nt], in_=ga_scr[tix].rearrange("p (c t) -> p c t", c=4)[:, :, 0:nt]),
                    writes=[gal[gi % 3].b])

            def stage2A(i):
                gi = gtile2[0]
                gtile2[0] += 1
                if prefetch and gi % 3 == 1:
                    prefetch.pop(0)()
                slot = gi % 3
                sgp = sgps[gi % 2]
                sga = sgas[gi % 2]
                if i + 1 < ntl:
                    loads2(i + 1, gi + 1)
                norm_hT(seq, slot)
                for hh in range(2):
                    m = proj(nt, W2, hh * 512, 512)
                    P.op("act", mk("activation", out=sgp[0:nt, hh * 512:(hh + 1) * 512], in_=m[0:nt, :],
                                                                   func=AF.Sigmoid),
                         reads=[m.b], writes=[sgp.b])
                for hh in range(2):
                    m = proj(nt, W2, 1024 + hh * 512, 512)
                    P.op("act", mk("activation", out=sga[0:nt, hh * 512:(hh + 1) * 512], in_=m[0:nt, :],
                                                                   func=AF.Sigmoid),
                         reads=[m.b], writes=[sga.b])
                return dict(i=i, gi=gi, slot=slot, sgp=sgp, sga=sga)

            def stage2B(c):
                i, gi, slot, sgp, sga = c["i"], c["gi"], c["slot"], c["sgp"], c["sga"]
                x = xt[slot]
                gp_, ga_ = gpl[slot], gal[slot]
                for hh in range(2):
                    m = next_mm()
                    for k in range(4):
                        P.op("pe", mk("matmul", out=m[0:nt, :], lhsT=gp_[:, k, 0:nt],
                                                                       rhs=Wpo[:, k, hh * 512:(hh + 1) * 512],
                                                                       start=(k == 0), stop=(k == 3)),
                             reads=[gp_.b, Wpo.b], writes=[m.b])
                    P.op("dve", mk("tensor_tensor", out=m1[0:nt, hh * 512:(hh + 1) * 512], in0=m[0:nt, :],
                                                                      in1=sgp[0:nt, hh * 512:(hh + 1) * 512], op=ALU.mult),
                         reads=[m.b, sgp.b], writes=[m1.b])
                for hh in range(2):
                    m = next_mm()
                    for k in range(4):
                        P.op("pe", mk("matmul", out=m[0:nt, :], lhsT=ga_[:, k, 0:nt],
                                                                       rhs=Wao[:, k, hh * 512:(hh + 1) * 512],
                                                                       start=(k == 0), stop=(k == 3)),
                             reads=[ga_.b, Wao.b], writes=[m.b])
                    P.op("dve", mk("tensor_tensor", out=t2[0:nt, :], in0=m[0:nt, :],
                                                                      in1=sga[0:nt, hh * 512:(hh + 1) * 512], op=ALU.mult),
                         reads=[m.b, sga.b], writes=[t2.b])
                    P.op("pool", mk("tensor_tensor", out=mrg[0:nt, hh * 512:(hh + 1) * 512],
                                                                  in0=m1[0:nt, hh * 512:(hh + 1) * 512], in1=t2[0:nt, :],
                                                                  op=ALU.add),
                         reads=[m1.b, t2.b], writes=[mrg.b])
                transpose_to(mrg, mrg.b, nt, [(c * 128, 128) for c in range(8)],
                             lambda wm, n: mT[:, 0:8, 0:nt], mT.b, "act")
                for hh in range(2):
                    m = next_mm()
                    for k in range(8):
                        P.op("pe", mk("matmul", out=m[0:nt, :], lhsT=mT[:, k, 0:nt],
                                                                       rhs=Wo[:, k, hh * 512:(hh + 1) * 512],
                                                                       start=(k == 0), stop=(k == 7)),
                             reads=[mT.b, Wo.b], writes=[m.b])
                    P.op("dve", mk("tensor_tensor", out=x[0:nt, hh * 512:(hh + 1) * 512],
                                                                      in0=x[0:nt, hh * 512:(hh + 1) * 512], in1=m[0:nt, :],
                                                                      op=ALU.add),
                         reads=[m.b, x.b], writes=[x.b])
                r0 = seq["tok0"] + i * 128
                if not last:
                    P.dma("x%d" % slot, mk("dma_start", out=xscr[r0:r0 + nt, :], in_=x[0:nt, :]), reads=[x.b])
                else:
                    y = yt[gi % 2]
                    P.op("act", mk("activation", out=xn[0:nt, 0:D], in_=x[0:nt, :], func=AF.Square,
                                                       accum_out=ssq[0:nt, :]),
                         reads=[x.b], writes=[xn.b, ssq.b])
                    P.op("dve", mk("tensor_scalar", out=rstd[0:nt, :], in0=ssq[0:nt, :], scalar1=1.0 / D, scalar2=EPS,
                                                          op0=ALU.mult, op1=ALU.add),
                         reads=[ssq.b], writes=[rstd.b])
                    P.op("act", mk("activation", out=rstd[0:nt, :], in_=rstd[0:nt, :], func=AF.Sqrt),
                         reads=[rstd.b], writes=[rstd.b])
                    P.op("dve", mk("reciprocal", out=rstd[0:nt, :], in_=rstd[0:nt, :]),
                         reads=[rstd.b], writes=[rstd.b])
                    P.op("dve", mk("scalar_tensor_tensor", out=y[0:nt, :], in0=x[0:nt, :], scalar=rstd[0:nt, 0:1],
                                                                      in1=gfbc[0:nt, :], op0=ALU.mult, op1=ALU.mult),
                         reads=[x.b, rstd.b, gfbc.b], writes=[y.b])
                    yo = seq["y_out"]
                    P.dma("y%d" % slot, mk("dma_start", out=yo[i * 128:i * 128 + nt, :], in_=y[0:nt, :]),
                          reads=[y.b])
            loads2(0, gtile2[0])
            pend = None
            for i in range(ntl):
                cA = stage2A(i)
                if pend is not None:
                    stage2B(pend)
                pend = cA
            stage2B(pend)
        while prefetch:
            prefetch.pop(0)()
        P.barrier()
        xt.pop()
        st2.close()

    P.final_wait()
    P.emit()
    stack0.close()
    return nc


_CACHE = {}


def kernel(x_prompt, x_sample, cache_k, cache_v, cache_kidx, state_pool, norm_g, w_in, w_pool_mix,
           pool_scale, w_pool_out, w_attn_out, w_o, final_norm_g):
    NCORES = 8
    f = lambda a: np.ascontiguousarray(np.asarray(a, dtype=np.float32))
    x_prompt, x_sample = f(x_prompt), f(x_sample)
    cache_k, cache_v, cache_kidx, state_pool = f(cache_k), f(cache_v), f(cache_kidx), f(state_pool)
    BP, T, _ = x_prompt.shape
    BS, TS, _ = x_sample.shape
    DEPTH = cache_k.shape[0]
    L0 = cache_k.shape[2]
    assert BP % NCORES == 0 and BS == NCORES
    NP = BP // NCORES
    cfg = dict(NP=NP, T=T, TS=TS, L0=L0, DEPTH=DEPTH, KP=min(256, T // 4), KS=min(256, (L0 + TS) // 4))
    key = tuple(sorted(cfg.items()))
    if key not in _CACHE:
        _CACHE[key] = build(cfg)
    nc = _CACHE[key]

    rope = _rope_table(list(range(T)) + list(range(L0, L0 + TS)))
    bands = _band_tables().reshape(4, 128, 512)
    pw2 = np.tile((2.0 ** -(np.arange(NI_BISECT, dtype=np.float64) + 1)).astype(np.float32)[None, :], (128, 1))
    shared = dict(norm_g=f(norm_g), w_in=f(w_in), w_mix=f(w_pool_mix), pscale=f(pool_scale), w_po=f(w_pool_out),
                  w_ao=f(w_attn_out), w_o=f(w_o), gfin=f(final_norm_g).reshape(1, D), rope=rope, bands=bands, pw2=pw2)
    in_maps = []
    for c in range(NCORES):
        m = dict(shared)
        m["x_p"] = x_prompt[c * NP:(c + 1) * NP]
        m["x_s"] = x_sample[c]
        m["ck"] = cache_k[:, c].reshape(DEPTH, L0, 512)
        m["cv"] = cache_v[:, c].reshape(DEPTH, L0, 512)
        m["cki"] = cache_kidx[:, c]
        m["spool"] = state_pool[:, c]
        in_maps.append(m)
    res = run_bass_kernel_spmd(nc, in_maps, core_ids=list(range(NCORES)))
    R = res.results
    cat = lambda name, ax: np.concatenate([np.asarray(r[name]) for r in R], axis=ax)
    stk = lambda name, ax: np.stack([np.asarray(r[name]) for r in R], axis=ax)
    y_prompt = cat("y_p", 0)
    y_sample = stk("y_s", 0)
    nk_p = cat("ok_p", 1).reshape(DEPTH, BP, T, 8, 64)
    nv_p = cat("ov_p", 1).reshape(DEPTH, BP, T, 8, 64)
    nki_p = cat("oki_p", 1)
    npl_p = cat("opl_p", 1)
    nk_s = stk("ok_s", 1).reshape(DEPTH, BS, TS, 8, 64)
    nv_s = stk("ov_s", 1).reshape(DEPTH, BS, TS, 8, 64)
    nki_s = stk("oki_s", 1)
    npl_s = stk("opl_s", 1)
    return (y_prompt, y_sample, nk_p, nv_p, nki_p, npl_p, nk_s, nv_s, nki_s, npl_s)
```

```python
from contextlib import ExitStack
import os
import numpy as np
import concourse.bass as bass
import concourse.mybir as mybir
from concourse.bass_utils import run_bass_kernel_spmd

F32 = mybir.dt.float32
BF16 = mybir.dt.bfloat16
ALU = mybir.AluOpType
AF = mybir.ActivationFunctionType

ENGS = ("pe", "act", "dve", "pool", "sp")

D = 1024
NCOL = 5416
C1 = 3368
C2 = NCOL - C1
NH = 8
NEG = -1.0e30
NI_BISECT = int(os.environ.get('KNI', '24'))
EPS = 1e-6


class Buf:
    __slots__ = ("name", "w", "r", "excl")

    def __init__(self, name):
        self.name = name
        self.excl = False
        self.w = None
        self.r = []


class Prog:
    def __init__(self, nc):
        self.nc = nc
        self.q = {e: [] for e in ENGS}
        self.tick = {e: 0 for e in ENGS}
        self.sem = {e: nc.alloc_semaphore("sem_" + e) for e in ENGS}
        self.seen = {e: {} for e in ENGS}
        self.dsem = {}
        self.dtick = {}
        self.nbuf = 0
        self.count = 0
        self.limit = int(os.environ.get("KLIMIT", "1000000000"))

    def buf(self, name=None):
        self.nbuf += 1
        return Buf(name or "b%d" % self.nbuf)

    def _need(self, reads, writes):
        need = {}
        for b in reads:
            if b.w is not None:
                s, t = b.w
                if need.get(s, 0) < t:
                    need[s] = t
        for b in writes:
            if b.w is not None:
                s, t = b.w
                if need.get(s, 0) < t:
                    need[s] = t
            for s, t in b.r:
                if need.get(s, 0) < t:
                    need[s] = t
        return need

    def _waits(self, eng, need):
        waits = []
        seen = self.seen[eng]
        for s, t in need.items():
            if s == eng and eng in ("pe", "sp"):
                continue
            if seen.get(s, 0) >= t:
                continue
            seen[s] = t
            waits.append((s, t))
        return waits

    def _mark(self, src, reads, writes):
        for b in reads:
            if len(b.r) > 24:
                d = {}
                for s, t in b.r:
                    if d.get(s, 0) < t:
                        d[s] = t
                b.r = list(d.items())
            b.r.append(src)
        for b in writes:
            b.w = src
            b.r = []

    def op(self, eng, fn, reads=(), writes=()):
        self.count += 1
        if self.count > self.limit:
            return
        xr = [b for b in reads if b.excl]
        if xr:
            writes = list(writes) + [b for b in xr if b not in writes]
        waits = self._waits(eng, self._need(reads, writes))
        self.tick[eng] += 1
        my = self.tick[eng]
        self.q[eng].append((waits, fn, ("e", eng)))
        self._mark((eng, my), reads, writes)

    def dma(self, stream, fn, reads=(), writes=(), eng="sp"):
        stream = (writes[0] if writes else reads[0]).name
        self.count += 1
        if self.count > self.limit:
            return
        if stream not in self.dsem:
            self.dsem[stream] = self.nc.alloc_semaphore("dsem_" + stream)
            self.dtick[stream] = 0
        waits = self._waits(eng, self._need(reads, writes))
        self.dtick[stream] += 16
        my = self.dtick[stream]
        self.q[eng].append((waits, fn, ("d", stream)))
        self._mark(("dma:" + stream, my), reads, writes)

    def _semof(self, s):
        if s.startswith("dma:"):
            return self.dsem[s[4:]]
        return self.sem[s]

    def barrier(self):
        need = {}
        for e in ENGS:
            if self.tick[e] > 0:
                need[e] = self.tick[e]
        for s, t in self.dtick.items():
            if t > 0:
                need["dma:" + s] = t
        for e in ENGS:
            waits = self._waits(e, dict(need))
            if waits:
                self.q[e].append((waits, None, None))

    def final_wait(self, eng="sp"):
        need = {}
        for s, t in self.dtick.items():
            if t > 0:
                need["dma:" + s] = t
        for e in ENGS:
            if e != eng and self.tick[e] > 0:
                need[e] = self.tick[e]
        waits = self._waits(eng, need)
        if waits:
            self.q[eng].append((waits, None, None))

    def emit(self):
        nc = self.nc
        engobj = {"pe": "tensor", "act": "scalar", "dve": "vector", "pool": "gpsimd", "sp": "sync"}
        with nc.Block() as block:
            for e in ENGS:
                items = self.q[e]
                if not items:
                    continue

                def body(eo, items=items):
                    for waits, fn, inc in items:
                        for s, t in waits:
                            eo.wait_ge(self._semof(s), t)
                        if fn is None:
                            continue
                        ins = fn(eo)
                        if inc[0] == "e":
                            ins.then_inc(self.sem[inc[1]], 1)
                        else:
                            ins.then_inc(self.dsem[inc[1]], 16)

                getattr(block, engobj[e])(body)


def mk(meth, **kw):
    return lambda e: getattr(e, meth)(**kw)


class Tl:
    def __init__(self, t, b):
        self.t = t
        self.b = b

    def __getitem__(self, k):
        return self.t[k]


def _rope_table(positions):
    pos = np.asarray(positions, dtype=np.float32)
    out = np.zeros((len(pos), 192), np.float32)
    for (d, off) in ((64, 0), (32, 128)):
        inv = (10000.0 ** (-np.arange(0, d, 2, dtype=np.float32) / np.float32(d))).astype(np.float32)
        ang = (pos[:, None] * inv[None, :]).astype(np.float32).astype(np.float64)
        c = np.cos(ang).astype(np.float32)
        s = np.sin(ang).astype(np.float32)
        h = d // 2
        out[:, off:off + h] = c
        out[:, off + h:off + 2 * h] = c
        out[:, off + 2 * h:off + 3 * h] = -s
        out[:, off + 3 * h:off + 4 * h] = s
    return out


def _band_tables():
    B = np.zeros((4, 128, 4, 128), np.float32)
    for g, w in enumerate((2, 4, 8, 16)):
        for t in range(128):
            for tp in range(max(0, t - w + 1), t + 1):
                B[0, tp, g, t] += 1.0 / min(t + 1, w)
                B[1, tp, g, t] += 1.0 / w
            B[0, t, g, t] -= 1.0
            B[1, t, g, t] -= 1.0
            for tp in range(128):
                if t - (tp - 128) < w:
                    B[2, tp, g, t] = 1.0 / w
            for j in range(15):
                if t - (j - 15) < w:
                    B[3, j, g, t] = 1.0 / w
    return B


def build(cfg):
    NP, T, TS, L0, DEPTH = cfg["NP"], cfg["T"], cfg["TS"], cfg["L0"], cfg["DEPTH"]
    KP, KS = cfg["KP"], cfg["KS"]
    assert T % 128 == 0 and TS == 64 and L0 % 128 == 0
    NTP = T // 128
    LS = L0 + TS
    NCS = L0 // 128 + 1
    SMAX = max(T, LS)
    KTW = max(4 * T, 2 * LS)
    VS = int(os.environ.get('KVS', '65'))
    VW = max(NTP * 8 * VS, NCS * 4 * VS)
    NTOK = NP * T + TS
    NTILES = NP * NTP + 1

    nc = bass.Bass("TRN2", target_bir_lowering=False, dynamic_dma_scratch_size=256)
    P = Prog(nc)

    def din(name, shape, dt=F32):
        return nc.dram_tensor(name, list(shape), dt, kind="ExternalInput").ap()

    def dout(name, shape, dt=F32):
        return nc.dram_tensor(name, list(shape), dt, kind="ExternalOutput").ap()

    x_p = din("x_p", [NP, T, D])
    x_s = din("x_s", [TS, D])
    ck = din("ck", [DEPTH, L0, 512])
    cv = din("cv", [DEPTH, L0, 512])
    cki = din("cki", [DEPTH, L0, 32])
    spool = din("spool", [DEPTH, 15, 512])
    norm_g = din("norm_g", [DEPTH, D])
    w_in = din("w_in", [DEPTH, D, NCOL])
    w_mix = din("w_mix", [DEPTH, 4, 128, 128])
    pscale = din("pscale", [DEPTH, 512])
    w_po = din("w_po", [DEPTH, 512, D])
    w_ao = din("w_ao", [DEPTH, 512, D])
    w_o = din("w_o", [DEPTH, D, D])
    gfin = din("gfin", [1, D])
    rope = din("rope", [T + TS, 192])
    bands = din("bands", [4, 128, 512])
    pw2 = din("pw2", [128, NI_BISECT])

    y_p = dout("y_p", [NP, T, D])
    y_s = dout("y_s", [TS, D])
    ok_p = dout("ok_p", [DEPTH, NP, T, 512])
    ov_p = dout("ov_p", [DEPTH, NP, T, 512])
    oki_p = dout("oki_p", [DEPTH, NP, T, 32])
    opl_p = dout("opl_p", [DEPTH, NP, 15, 512])
    ok_s = dout("ok_s", [DEPTH, TS, 512])
    ov_s = dout("ov_s", [DEPTH, TS, 512])
    oki_s = dout("oki_s", [DEPTH, TS, 32])
    opl_s = dout("opl_s", [DEPTH, 15, 512])

    xscr = nc.dram_tensor("xscr", [NTOK, D], F32).ap()
    gp_scr = nc.dram_tensor("gp_scr", [NTILES, 128, 512], BF16).ap()
    ga_scr = nc.dram_tensor("ga_scr", [NTILES, 128, 512], BF16).ap()

    seqs = []
    for s in range(NP):
        seqs.append(dict(kind="p", idx=s, T=T, nt=128, ntiles=NTP, tok0=s * T, tile0=s * NTP,
                         x_in=x_p[s], y_out=y_p[s], K=KP, rope0=0))
    seqs.append(dict(kind="s", idx=0, T=TS, nt=TS, ntiles=1, tok0=NP * T, tile0=NP * NTP,
                     x_in=x_s, y_out=y_s, K=KS, rope0=T))

    stack0 = ExitStack()

    uniq = [0]

    def alloc(st, name, shape, dt):
        uniq[0] += 1
        t = st.enter_context(nc.sbuf_tensor("%s_%d" % (name, uniq[0]), list(shape), dt))
        return Tl(t, P.buf(name))

    def palloc(name, shape, dt):
        t = nc.alloc_psum_tensor(name, list(shape), dt)
        b = P.buf(name)
        b.excl = True
        return Tl(t, b)

    mm = [palloc("mm%d" % i, [128, 512], F32) for i in range(2)]
    tp = palloc("tp", [128, 1024], BF16)
    pp = palloc("pp", [128, 512], F32)
    Lb = [palloc("L%d" % i, [128, 512], F32) for i in range(2)]
    Ob = [palloc("O%d" % i, [128, 4, VS], F32) for i in range(2)]
    mmi = [0]

    def next_mm():
        mmi[0] ^= 1
        return mm[mmi[0]]

    ident = alloc(stack0, "ident", [128, 128], BF16)
    identf = alloc(stack0, "identf", [128, 128], F32)
    ones1 = alloc(stack0, "ones1", [1, 128], F32)
    gfbc = alloc(stack0, "gfbc", [128, D], F32)
    band = alloc(stack0, "band", [128, 4, 512], BF16)
    pw = alloc(stack0, "pw", [128, NI_BISECT], F32)
    xt = [alloc(stack0, "xt%d" % i, [128, D], F32) for i in range(2)]
    xn = alloc(stack0, "xn", [128, D], BF16)
    hT = alloc(stack0, "hT", [128, 8, 128], BF16)
    ssq = alloc(stack0, "ssq", [128, 1], F32)
    rstd = alloc(stack0, "rstd", [128, 1], F32)
    gcol = alloc(stack0, "gcol", [128, 8], F32)
    W1 = alloc(stack0, "W1", [128, 8, C1], BF16)
    ftmp = [alloc(stack0, "ftmp%d" % i, [128, 512], F32) for i in range(2)]
    ftmpi = [0]

    def next_ftmp():
        ftmpi[0] ^= 1
        return ftmp[ftmpi[0]]

    P.op("pool", mk("memset", ap=identf[:], constant=0.0), writes=[identf.b])
    P.op("pool", mk("affine_select", out=identf[:], in_=identf[:], pattern=[[-1, 128]],
                                           compare_op=ALU.not_equal, fill=1.0, base=0,
                                           channel_multiplier=1),
         reads=[identf.b], writes=[identf.b])
    P.op("dve", mk("tensor_copy", out=ident[:], in_=identf[:]), reads=[identf.b], writes=[ident.b])
    P.op("dve", mk("memset", ap=ones1[:], constant=1.0), writes=[ones1.b])
    P.dma("c0", mk("dma_start", out=pw[:], in_=pw2), writes=[pw.b])
    for kind in range(4):
        f = next_ftmp()
        P.dma("c1", mk("dma_start", out=f[:], in_=bands[kind]), writes=[f.b])
        P.op("dve", mk("tensor_copy", out=band[:, kind, :], in_=f[:]),
             reads=[f.b], writes=[band.b])
    grow = alloc(stack0, "grow", [1, D], F32)
    P.dma("c0", mk("dma_start", out=grow[:], in_=gfin), writes=[grow.b])
    for hh in range(2):
        m = next_mm()
        P.op("pe", mk("matmul", out=m[:], lhsT=ones1[0:1, :], rhs=grow[0:1, hh * 512:(hh + 1) * 512],
                                                  start=True, stop=True),
             reads=[ones1.b, grow.b], writes=[m.b])
        P.op("dve", mk("tensor_copy", out=gfbc[:, hh * 512:(hh + 1) * 512], in_=m[:]),
             reads=[m.b], writes=[gfbc.b])

    def load_gcol(l):
        g8 = next_ftmp()
        P.dma("c1", mk("dma_start", out=g8[0:8, 0:128], in_=norm_g[l].rearrange("(k p) -> k p", p=128)),
              writes=[g8.b])
        m = next_mm()
        P.op("pe", mk("transpose", out=m[:, 0:8], in_=g8[0:8, 0:128], identity=identf[0:8, 0:8]),
             reads=[g8.b, identf.b], writes=[m.b])
        P.op("dve", mk("tensor_copy", out=gcol[:], in_=m[:, 0:8]), reads=[m.b], writes=[gcol.b])

    def load_cast(dst_ap_fn, src_rows_fn, nk, ncols, stage, scale_gcol, dstb):
        for k in range(nk):
            s = stage[k % 2]
            P.dma("wst%d" % (k % 2), mk("dma_start", out=s[:, 0:ncols], in_=src_rows_fn(k)),
                  writes=[s.b])
            if scale_gcol:
                if k % 2 == 0:
                    P.op("dve", mk("tensor_scalar", out=dst_ap_fn(k), in0=s[:, 0:ncols],
                                                                    scalar1=gcol[:, k:k + 1], scalar2=None,
                                                                    op0=ALU.mult),
                         reads=[s.b, gcol.b], writes=[dstb])
                else:
                    P.op("act", mk("activation", out=dst_ap_fn(k), in_=s[:, 0:ncols],
                                                                 func=AF.Copy, scale=gcol[:, k:k + 1]),
                         reads=[s.b, gcol.b], writes=[dstb])
            else:
                if k % 2 == 0:
                    P.op("dve", mk("tensor_copy", out=dst_ap_fn(k), in_=s[:, 0:ncols]),
                         reads=[s.b], writes=[dstb])
                else:
                    P.op("act", mk("copy", out=dst_ap_fn(k), in_=s[:, 0:ncols]),
                         reads=[s.b], writes=[dstb])

    def load_x(seq, i, slot, src):
        nt = seq["nt"]
        r0 = i * 128
        P.dma("x%d" % slot, mk("dma_start", out=xt[slot][0:nt, :], in_=src[r0:r0 + nt, :]),
              writes=[xt[slot].b])

    def norm_hT(seq, slot):
        nt = seq["nt"]
        x = xt[slot]
        P.op("act", mk("activation", out=xn[0:nt, 0:D], in_=x[0:nt, :], func=AF.Square,
                                           accum_out=ssq[0:nt, :]),
             reads=[x.b], writes=[xn.b, ssq.b])
        P.op("dve", mk("tensor_scalar", out=rstd[0:nt, :], in0=ssq[0:nt, :], scalar1=1.0 / D, scalar2=EPS,
                                              op0=ALU.mult, op1=ALU.add),
             reads=[ssq.b], writes=[rstd.b])
        P.op("act", mk("activation", out=rstd[0:nt, :], in_=rstd[0:nt, :], func=AF.Sqrt),
             reads=[rstd.b], writes=[rstd.b])
        P.op("dve", mk("reciprocal", out=rstd[0:nt, :], in_=rstd[0:nt, :]),
             reads=[rstd.b], writes=[rstd.b])
        P.op("dve", mk("tensor_scalar", out=xn[0:nt, :], in0=x[0:nt, :], scalar1=rstd[0:nt, 0:1],
                                              scalar2=None, op0=ALU.mult),
             reads=[x.b, rstd.b], writes=[xn.b])
        tv = tp[:].rearrange("p (k t) -> p k t", t=128)
        for k in range(8):
            P.op("pe", mk("transpose", out=tv[:, k, 0:nt], in_=xn[0:nt, k * 128:(k + 1) * 128],
                                                  identity=ident[0:nt, 0:nt]),
                 reads=[xn.b, ident.b], writes=[tp.b])
        P.op("act", mk("copy", out=hT[:, 0:4, 0:nt], in_=tv[:, 0:4, 0:nt]), reads=[tp.b], writes=[hT.b])
        P.op("dve", mk("tensor_copy", out=hT[:, 4:8, 0:nt], in_=tv[:, 4:8, 0:nt]), reads=[tp.b], writes=[hT.b])

    def proj(nt, W, c0, ncols):
        m = next_mm()
        for k in range(8):
            P.op("pe", mk("matmul", out=m[0:nt, 0:ncols], lhsT=hT[:, k, 0:nt], rhs=W[:, k, c0:c0 + ncols],
                                               start=(k == 0), stop=(k == 7)),
                 reads=[hT.b, W.b], writes=[m.b])
        return m

    def transpose_to(src, srcb, nt, ncols_list, dst_fn, dstb, evac_eng="act"):
        tv = tp[:].rearrange("p (k t) -> p k t", t=128)
        for j, (c0, wd) in enumerate(ncols_list):
            P.op("pe", mk("transpose", out=tv[0:wd, j, 0:nt], in_=src[0:nt, c0:c0 + wd],
                                                                identity=ident[0:nt, 0:nt]),
                 reads=[srcb, ident.b], writes=[tp.b])
        wmax = max(w for _, w in ncols_list)
        n = len(ncols_list)
        if evac_eng == "none":
            return tv
        if evac_eng == "actbias":
            P.op("act", mk("activation", out=dst_fn(wmax, n), in_=tv[0:wmax, 0:n, 0:nt], func=AF.Copy,
                           scale=30000.0, bias=-30000.0), reads=[tp.b], writes=[dstb])
        elif evac_eng == "act":
            P.op("act", mk("copy", out=dst_fn(wmax, n), in_=tv[0:wmax, 0:n, 0:nt]), reads=[tp.b], writes=[dstb])
        else:
            P.op("dve", mk("tensor_copy", out=dst_fn(wmax, n), in_=tv[0:wmax, 0:n, 0:nt]),
                 reads=[tp.b], writes=[dstb])

    for l in range(DEPTH):
        x_src_of = (lambda seq: seq["x_in"]) if l == 0 else (lambda seq: xscr[seq["tok0"]:seq["tok0"] + seq["T"], :])
        last = (l == DEPTH - 1)

        st1 = ExitStack()
        Wmix = alloc(st1, "Wmix", [128, 4, 128], BF16)
        with ExitStack() as stl:
            if l == 0:
                stage = [alloc(stl, "stg%d" % i, [128, C1], F32) for i in range(2)]
                load_gcol(l)
                load_cast(lambda k: W1[:, k, :], lambda k: w_in[l, k * 128:(k + 1) * 128, 0:C1], 8, C1, stage, True, W1.b)
            srow = next_ftmp()
            P.dma("c1", mk("dma_start", out=srow[0:1, :], in_=pscale[l:l + 1, :]), writes=[srow.b])
            m = next_mm()
            P.op("pe", mk("matmul", out=m[:], lhsT=ones1[0:1, :], rhs=srow[0:1, :], start=True, stop=True),
                 reads=[ones1.b, srow.b], writes=[m.b])
            s0 = next_ftmp()
            P.dma("wst0", mk("dma_start", out=s0[:, 0:512].rearrange("p (g d) -> p g d", g=4),
                                                in_=w_mix[l].rearrange("g c d -> c g d")), writes=[s0.b])
            P.op("dve", mk("tensor_tensor", out=Wmix[:].rearrange("p g d -> p (g d)"), in0=s0[:, 0:512],
                                                  in1=m[:], op=ALU.mult),
                 reads=[s0.b, m.b], writes=[Wmix.b])
            P.barrier()

        kT = alloc(st1, "kT", [128, KTW], BF16)
        Vg = alloc(st1, "Vg", [128, VW], BF16)
        kiT = alloc(st1, "kiT", [96, SMAX], BF16)
        MH = max(T, (LS + 1) // 2)
        score = alloc(st1, "score", [128, 2 * MH], F32)
        sc_bufs = [score.b, P.buf("score1")]
        maskT = alloc(st1, "maskT", [128, max(NTP * 128, NCS * 64)], BF16)
        utok = [alloc(st1, "utok%d" % i, [128, 512], BF16) for i in range(2)]
        szp = alloc(st1, "szp", [128, 512], BF16)
        szas = [alloc(st1, "sza%d" % i, [128, 512], BF16) for i in range(3)]
        qf = alloc(st1, "qf", [128, 512], F32)
        kf = [alloc(st1, "kf%d" % i, [128, 512], F32) for i in range(2)]
        vf = [alloc(st1, "vf%d" % i, [128, 512], F32) for i in range(2)]
        g5 = [alloc(st1, "g5%d" % i, [128, 296], F32) for i in range(2)]
        uf = ftmp[0]
        rtab = [alloc(st1, "rtab%d" % i, [128, 192], F32) for i in range(2)]
        rtmp = alloc(st1, "rtmp", [128, 512], F32)
        qb = alloc(st1, "qb", [128, 512], BF16)
        kb = alloc(st1, "kb", [128, 512], BF16)
        qsb = alloc(st1, "qsb", [128, 256], BF16)
        ki3 = alloc(st1, "ki3", [128, 96], BF16)
        Dg = alloc(st1, "Dg", [128, 8, 128], BF16)
        plT = alloc(st1, "plT", [128, 4, 128], BF16)
        gpt = alloc(st1, "gpt", [128, 512], BF16)
        gpT = [alloc(st1, "gpT%d" % i, [128, 4, 128], BF16) for i in range(2)]
        gaT = [alloc(st1, "gaT%d" % i, [128, 4, 128], BF16) for i in range(2)]
        qTs = [alloc(st1, "qT%d" % i, [128, 8, 128], BF16) for i in range(3)]
        qiT = alloc(st1, "qiT", [96, 3, 128], BF16)
        rel = [alloc(st1, "rel%d" % i, [128, 512], BF16) for i in range(3)]
        maskb = alloc(st1, "maskb", [128, 2 * MH], BF16)
        mb_bufs = [maskb.b, P.buf("maskb1")]
        praw = [alloc(st1, "praw%d" % i, [128, 512], BF16) for i in range(3)]
        on = alloc(st1, "on", [128, 512], F32)
        gat = alloc(st1, "gat", [128, 512], BF16)
        rs = alloc(st1, "rs", [128, 8], F32)
        bs_lo = alloc(st1, "bs_lo", [128, 1], F32)
        bs_hi = alloc(st1, "bs_hi", [128, 1], F32)
        bs_mid = [alloc(st1, "bs_mid%d" % i, [128, 1], F32) for i in range(2)]
        bs_cnt = alloc(st1, "bs_cnt", [128, 1], F32)
        bs_cnt2 = alloc(st1, "bs_cnt2", [128, 1], F32)
        bs_u = alloc(st1, "bs_u", [128, 1], F32)
        bs_ht = alloc(st1, "bs_ht", [128, NI_BISECT], F32)
        bs_thr = alloc(st1, "bs_thr", [128, 1], F32)
        cst = [alloc(st1, "cst%d" % i, [128, 2, 256], F32) for i in range(2)]
        cbf = alloc(st1, "cbf", [128, 2, 256], BF16)
        spf = alloc(st1, "spf", [16, 512], F32)
        spb = alloc(st1, "spb", [16, 512], BF16)
        ckif = alloc(st1, "ckif", [128, max(L0 // 128, 1), 32], F32)
        cki3 = alloc(st1, "cki3", [128, 96], BF16)

        rot = {"rel": 0, "mk": 0, "praw": 0, "pm": 0, "L": 0}

        def nxt(name, arr):
            rot[name] ^= 1
            return arr[rot[name]]

        P.op("pool", mk("memset", ap=Vg[:], constant=1.0), writes=[Vg.b])
        for qz in qTs:
            P.op("pool", mk("memset", ap=qz[:], constant=0.0), writes=[qz.b])

        def rope_inplace(t, nt, nh, hd, tab, toff, eng="pool"):
            h2 = hd // 2
            xv = lambda: t.rearrange("p (h d) -> p h d", d=hd)
            tv = lambda: rtmp[0:nt, 0:nh * hd].rearrange("p (h d) -> p h d", d=hd)
            cc = lambda: tab[0:nt, toff:toff + hd].unsqueeze(1).broadcast_to([nt, nh, hd])
            sn = lambda: tab[0:nt, toff + hd:toff + hd + h2].unsqueeze(1).broadcast_to([nt, nh, h2])
            sp_ = lambda: tab[0:nt, toff + hd + h2:toff + 2 * hd].unsqueeze(1).broadcast_to([nt, nh, h2])
            return xv, tv, cc, sn, sp_, h2

        def do_rope(tl, col0, nt, nh, hd, tab, toff, eng):
            t = tl[0:nt, col0:col0 + nh * hd]
            xv, tv, cc, sn, sp_, h2 = rope_inplace(t, nt, nh, hd, tab, toff)
            P.op(eng, mk("tensor_tensor", out=tv()[:, :, 0:h2], in0=xv()[:, :, h2:hd], in1=sn(), op=ALU.mult),
                 reads=[tl.b, tab.b], writes=[rtmp.b])
            P.op(eng, mk("tensor_tensor", out=tv()[:, :, h2:hd], in0=xv()[:, :, 0:h2], in1=sp_(), op=ALU.mult),
                 reads=[tl.b, tab.b], writes=[rtmp.b])
            P.op(eng, mk("tensor_tensor", out=xv(), in0=xv(), in1=cc(), op=ALU.mult),
                 reads=[tl.b, tab.b], writes=[tl.b])
            P.op(eng, mk("tensor_tensor", out=xv(), in0=xv(), in1=tv(), op=ALU.add),
                 reads=[tl.b, rtmp.b], writes=[tl.b])

        gtile = [0]

        for seq in seqs:
            nt = seq["nt"]
            ntl = seq["ntiles"]
            isS = seq["kind"] == "s"
            xsrc = x_src_of(seq)
            Ksel = seq["K"]
            if not isS:
                kTv = kT[:, 0:4 * T].rearrange("p (c s) -> p c s", c=4)
                Vv = Vg[:, 0:NTP * 8 * VS].rearrange("p (t h d) -> p t h d", h=8, d=VS)
                okd, ovd, okid, opld = ok_p[l, seq["idx"]], ov_p[l, seq["idx"]], oki_p[l, seq["idx"]], opl_p[l, seq["idx"]]
            else:
                kTv = kT[:, 0:2 * LS].rearrange("p (c s) -> p c s", c=2)
                Vv = Vg[:, 0:NCS * 4 * VS].rearrange("p (t h d) -> p t h d", h=4, d=VS)
                okd, ovd, okid, opld = ok_s[l], ov_s[l], oki_s[l], opl_s[l]

            if isS and L0 > 0:
                nct = L0 // 128
                P.dma("cki", mk("dma_start", out=ckif[:, 0:nct, :],
                                                   in_=cki[l].rearrange("(t p) d -> p t d", p=128)),
                      writes=[ckif.b])
                for c in range(nct):
                    P.op("dve", mk("tensor_copy",
                        out=cki3[:].rearrange("p (r d) -> p r d", r=3),
                        in_=ckif[:, c, :].unsqueeze(1).broadcast_to([128, 3, 32])),
                        reads=[ckif.b], writes=[cki3.b])
                    transpose_to(cki3, cki3.b, 128, [(0, 96)],
                                 lambda wm, n, c=c: kiT[0:96, c * 128:(c + 1) * 128].unsqueeze(1), kiT.b,
                                 evac_eng="act" if c % 2 else "dve")
                P.dma("spf", mk("dma_start", out=spf[0:15, :], in_=spool[l]), writes=[spf.b])
                P.op("dve", mk("tensor_copy", out=spb[0:15, :], in_=spf[0:15, :]), reads=[spf.b], writes=[spb.b])

            def stageA1(i):
                qT = qTs[gtile[0] % 3]
                sza = szas[gtile[0] % 3]
                gi = gtile[0]
                gtile[0] += 1
                slot = gi % 2
                if i + 1 < ntl:
                    load_x(seq, i + 1, (gi + 1) % 2, xsrc)
                rt = rtab[gi % 2]
                rp0 = seq["rope0"] + i * 128
                P.dma("rt%d" % (gi % 2), mk("dma_start", out=rt[0:nt, :], in_=rope[rp0:rp0 + nt, :]),
                      writes=[rt.b])
                norm_hT(seq, slot)
                key0 = (L0 if isS else 0) + i * 128
                S = key0 + nt
                if isS:
                    sview = score[:, 0:S]
                    sbufs = [sc_bufs[0], sc_bufs[1]]
                else:
                    sview = score[:, (gtile[0] - 1) % 2 * MH:(gtile[0] - 1) % 2 * MH + S]
                    sbufs = [sc_bufs[(gtile[0] - 1) % 2]]
                ucur = utok[gi % 2]
                uprev = utok[(gi + 1) % 2]
                kfi = kf[gi % 2]
                vfi = vf[gi % 2]
                g5i = g5[gi % 2]

                yield
                m = proj(nt, W1, 0, 512)
                P.op("act", mk("copy", out=ucur[0:nt, :], in_=m[0:nt, :]), reads=[m.b], writes=[ucur.b])
                if i == ntl - 1:
                    P.op("dve", mk("tensor_copy", out=uf[0:nt, :], in_=m[0:nt, :]), reads=[m.b], writes=[uf.b])
                    P.dma("opl", mk("dma_start", out=opld, in_=uf[nt - 15:nt, :]), reads=[uf.b])
                yield
                m = proj(nt, W1, 512, 512)
                P.op("act", mk("activation", out=szp[0:nt, :], in_=m[0:nt, :], func=AF.Silu),
                     reads=[m.b], writes=[szp.b])
                yield
                m = proj(nt, W1, 1024, 512)
                P.op("act", mk("copy", out=qf[0:nt, :], in_=m[0:nt, :]), reads=[m.b], writes=[qf.b])
                yield
                m = proj(nt, W1, 1536, 512)
                P.op("act", mk("copy", out=kfi[0:nt, :], in_=m[0:nt, :]), reads=[m.b], writes=[kfi.b])
                yield
                m = proj(nt, W1, 2048, 512)
                P.op("act", mk("copy", out=vfi[0:nt, :], in_=m[0:nt, :]), reads=[m.b], writes=[vfi.b])
                if not isS:
                    P.op("dve", mk("tensor_copy", out=Vv[0:nt, i, :, 0:64],
                                   in_=m[0:nt, :].rearrange("p (h d) -> p h d", d=64)),
                         reads=[m.b], writes=[Vg.b])
                P.dma("ov%d" % (gi % 2), mk("dma_start", out=ovd[i * 128:i * 128 + nt, :], in_=vfi[0:nt, :]),
                      reads=[vfi.b])
                yield
                m = proj(nt, W1, 2560, 296)
                P.op("act", mk("copy", out=g5i[0:nt, :], in_=m[0:nt, 0:296]), reads=[m.b], writes=[g5i.b])
                yield
                m = proj(nt, W1, 2856, 512)
                P.op("act", mk("activation", out=sza[0:nt, :], in_=m[0:nt, :], func=AF.Silu),
                     reads=[m.b], writes=[sza.b])

                yield
                yield
                do_rope(qf, 0, nt, 8, 64, rt, 0, "pool")
                yield
                do_rope(kfi, 0, nt, 8, 64, rt, 0, "pool")
                yield
                do_rope(g5i, 0, nt, 8, 32, rt, 128, "pool")
                do_rope(g5i, 256, nt, 1, 32, rt, 128, "pool")
                P.dma("ok%d" % (gi % 2), mk("dma_start", out=okd[i * 128:i * 128 + nt, :], in_=kfi[0:nt, :]),
                      reads=[kfi.b])
                P.dma("oki%d" % (gi % 2), mk("dma_start", out=okid[i * 128:i * 128 + nt, :], in_=g5i[0:nt, 256:288]),
                      reads=[g5i.b])
                P.op("pool", mk("tensor_copy", out=qb[0:nt, :], in_=qf[0:nt, :]), reads=[qf.b], writes=[qb.b])
                P.op("pool", mk("tensor_copy", out=kb[0:nt, :], in_=kfi[0:nt, :]), reads=[kfi.b], writes=[kb.b])
                for h in range(8):
                    P.op("dve", mk("tensor_scalar", out=Dg[0:nt, h, 0:nt], in0=identf[0:nt, 0:nt],
                                   scalar1=g5i[0:nt, 288 + h:289 + h], scalar2=None, op0=ALU.mult),
                         reads=[identf.b, g5i.b], writes=[Dg.b])
                P.op("pool", mk("tensor_copy", out=qsb[0:nt, :], in_=g5i[0:nt, 0:256]), reads=[g5i.b], writes=[qsb.b])
                P.op("dve", mk("tensor_copy", out=ki3[0:nt, :].rearrange("p (r d) -> p r d", r=3),
                                                    in_=g5i[0:nt, 256:288].unsqueeze(1).broadcast_to([nt, 3, 32])),
                     reads=[g5i.b], writes=[ki3.b])

                yield
                transpose_to(ki3, ki3.b, nt, [(0, 96)],
                             lambda wm, n: kiT[0:96, key0:key0 + nt].unsqueeze(1), kiT.b, "act")
                transpose_to(qsb, qsb.b, nt, [(0, 96), (96, 96), (192, 64)],
                             lambda wm, n: qiT[0:96, 0:3, 0:nt], qiT.b, "dve")
                tvq = transpose_to(qb, qb.b, nt, [(c * 128, 128) for c in range(4)], None, None, "none")
                P.op("act", mk("copy", out=qT[0:64, 0:8:2, 0:nt], in_=tvq[0:64, 0:4, 0:nt]), reads=[tp.b], writes=[qT.b])
                P.op("act", mk("copy", out=qT[64:128, 1:8:2, 0:nt], in_=tvq[64:128, 0:4, 0:nt]), reads=[tp.b], writes=[qT.b])
                if not isS:
                    transpose_to(kb, kb.b, nt, [(c * 128, 128) for c in range(4)],
                                 lambda wm, n: kTv[:, 0:4, key0:key0 + nt], kT.b, "dve")

                yield
                ppv = pp[:].rearrange("p (g t) -> p g t", g=4)
                for g in range(4):
                    if isS:
                        P.op("pe", mk("matmul", out=ppv[:, g, 0:nt], lhsT=ucur[0:nt, g * 128:(g + 1) * 128],
                                                           rhs=band[0:nt, 1, g * 128:g * 128 + nt], start=True, stop=False),
                             reads=[ucur.b, band.b], writes=[pp.b])
                        P.op("pe", mk("matmul", out=ppv[:, g, 0:nt], lhsT=spb[0:15, g * 128:(g + 1) * 128],
                                                           rhs=band[0:15, 3, g * 128:g * 128 + nt], start=False, stop=True),
                             reads=[spb.b, band.b], writes=[pp.b])
                    elif i == 0:
                        P.op("pe", mk("matmul", out=ppv[:, g, 0:nt], lhsT=ucur[0:nt, g * 128:(g + 1) * 128],
                                                           rhs=band[0:nt, 0, g * 128:g * 128 + nt], start=True, stop=True),
                             reads=[ucur.b, band.b], writes=[pp.b])
                    else:
                        P.op("pe", mk("matmul", out=ppv[:, g, 0:nt], lhsT=ucur[0:nt, g * 128:(g + 1) * 128],
                                                           rhs=band[0:nt, 1, g * 128:g * 128 + nt], start=True, stop=False),
                             reads=[ucur.b, band.b], writes=[pp.b])
                        P.op("pe", mk("matmul", out=ppv[:, g, 0:nt], lhsT=uprev[:, g * 128:(g + 1) * 128],
                                                           rhs=band[:, 2, g * 128:g * 128 + nt], start=False, stop=True),
                             reads=[uprev.b, band.b], writes=[pp.b])
                P.op("act", mk("copy", out=plT[:, :, 0:nt], in_=ppv[:, :, 0:nt]), reads=[pp.b], writes=[plT.b])
                m = next_mm()
                for g in range(4):
                    P.op("pe", mk("matmul", out=m[0:nt, g * 128:(g + 1) * 128], lhsT=plT[:, g, 0:nt],
                                                            rhs=Wmix[:, g, :], start=True, stop=True),
                         reads=[plT.b, Wmix.b], writes=[m.b])
                P.op("dve", mk("tensor_tensor", out=gpt[0:nt, :], in0=m[0:nt, :], in1=szp[0:nt, :], op=ALU.mult),
                     reads=[m.b, szp.b], writes=[gpt.b])
                gpo = gpT[gi % 2]
                transpose_to(gpt, gpt.b, nt, [(c * 128, 128) for c in range(4)],
                             lambda wm, n: gpo[:, 0:4, 0:nt], gpo.b, "act")
                tix = seq["tile0"] + i
                P.dma("gp%d" % (gi % 2), mk("dma_start",
                    out=gp_scr[tix].rearrange("p (c t) -> p c t", c=4)[:, :, 0:nt], in_=gpo[:, :, 0:nt]),
                    reads=[gpo.b])

                yield
                ngk = (S + 511) // 512
                for gk in range(ngk):
                    k0 = gk * 512
                    gw = min(512, S - k0)
                    prev = None
                    for h in range(8):
                        if h % 2 == 0:
                            yield
                        m = next_mm()
                        bp = 32 * (h % 3)
                        P.op("pe", mk("matmul", out=
                            m[0:nt, 0:gw], lhsT=qiT[bp:bp + 32, h // 3, 0:nt], rhs=kiT[bp:bp + 32, k0:k0 + gw],
                            start=True, stop=True),
                            reads=[qiT.b, kiT.b], writes=[m.b])
                        rot["rel"] = (rot["rel"] + 1) % 3
                        r = rel[rot["rel"]]
                        P.op("act", mk("activation", out=r[0:nt, 0:gw], in_=m[0:nt, 0:gw], func=AF.Relu),
                             reads=[m.b], writes=[r.b])
                        if prev is not None:
                            ph, prr = prev
                            P.op("pe", mk("matmul", out=pp[0:nt, 0:gw], lhsT=Dg[0:nt, ph, 0:nt], rhs=prr[0:nt, 0:gw],
                                          start=(ph == 0), stop=False),
                                 reads=[Dg.b, prr.b], writes=[pp.b])
                        prev = (h, r)
                    ph, prr = prev
                    P.op("pe", mk("matmul", out=pp[0:nt, 0:gw], lhsT=Dg[0:nt, ph, 0:nt], rhs=prr[0:nt, 0:gw],
                                  start=False, stop=True),
                         reads=[Dg.b, prr.b], writes=[pp.b])
                    P.op("act", mk("copy", out=sview[0:nt, k0:k0 + gw], in_=pp[0:nt, 0:gw]), reads=[pp.b], writes=sbufs)
                return dict(i=i, gi=gi, S=S, ngk=ngk, qT=qT, sza=sza, kfi=kfi, vfi=vfi, tix=tix,
                            sview=sview, sbufs=sbufs)

            def stageA2(c):
                gi, S, sview, sbufs = c["gi"], c["S"], c["sview"], c["sbufs"]
                nch = (S + 127) // 128
                if isS:
                    mview = maskb[:, 0:S]
                    mbufs = [mb_bufs[0], mb_bufs[1]]
                else:
                    mview = maskb[:, (gi % 2) * MH:(gi % 2) * MH + S]
                    mbufs = [mb_bufs[gi % 2]]
                c["nch"], c["mview"], c["mbufs"] = nch, mview, mbufs
                yield
                need_topk = S > Ksel
                if not isS:
                    if need_topk:
                        P.op("dve", mk("tensor_reduce", out=bs_lo[0:nt, :], in_=sview[0:nt, 0:S - 64],
                                       axis=mybir.AxisListType.X, op=ALU.min),
                             reads=sbufs, writes=[bs_lo.b])
                    P.op("dve", mk("memset", ap=sview[0:64, S - 64:S], constant=NEG), writes=sbufs)
                else:
                    if need_topk:
                        P.op("dve", mk("tensor_reduce", out=bs_lo[0:nt, :], in_=sview[0:nt, 0:S],
                                       axis=mybir.AxisListType.X, op=ALU.min),
                             reads=sbufs, writes=[bs_lo.b])
                if need_topk:
                    P.op("dve", mk("tensor_reduce", out=bs_hi[0:nt, :], in_=sview[0:nt, 0:S],
                                   axis=mybir.AxisListType.X, op=ALU.max),
                         reads=sbufs, writes=[bs_hi.b])
                    P.op("dve", mk("tensor_tensor", out=bs_hi[0:nt, :], in0=bs_hi[0:nt, :], in1=bs_lo[0:nt, :],
                                   op=ALU.subtract),
                         reads=[bs_hi.b, bs_lo.b], writes=[bs_hi.b])
                    P.op("dve", mk("tensor_scalar", out=bs_ht[0:nt, :], in0=pw[0:nt, :], scalar1=bs_hi[0:nt, 0:1],
                                   scalar2=None, op0=ALU.mult),
                         reads=[pw.b, bs_hi.b], writes=[bs_ht.b])
                    P.op("dve", mk("tensor_tensor", out=bs_mid[0][0:nt, :], in0=bs_lo[0:nt, :], in1=bs_ht[0:nt, 0:1],
                                   op=ALU.add),
                         reads=[bs_lo.b, bs_ht.b], writes=[bs_mid[0].b])
                    for it in range(NI_BISECT):
                        yield
                        mc = bs_mid[it % 2]
                        mn = bs_mid[(it + 1) % 2]
                        P.op("dve", mk("tensor_scalar", out=mview[0:nt, 0:S], in0=sview[0:nt, 0:S],
                                       scalar1=mc[0:nt, 0:1], scalar2=None, op0=ALU.is_ge, op1=ALU.add,
                                       accum_out=bs_cnt[0:nt, :]),
                             reads=sbufs + [mc.b], writes=mbufs + [bs_cnt.b])
                        P.op("dve", mk("tensor_scalar", out=bs_u[0:nt, :], in0=bs_cnt[0:nt, :],
                                       scalar1=float(Ksel) - 0.5, scalar2=0.5, op0=ALU.is_ge, op1=ALU.subtract),
                             reads=[bs_cnt.b], writes=[bs_u.b])
                        P.op("dve", mk("scalar_tensor_tensor", out=mn[0:nt, :], in0=bs_u[0:nt, :],
                                       scalar=bs_ht[0:nt, it:it + 1], in1=mc[0:nt, :], op0=ALU.mult, op1=ALU.add),
                             reads=[bs_u.b, bs_ht.b, mc.b], writes=[mn.b])
                    mfin = bs_mid[NI_BISECT % 2]
                    P.op("dve", mk("scalar_tensor_tensor", out=bs_thr[0:nt, :],
                                   in0=bs_ht[0:nt, NI_BISECT - 1:NI_BISECT], scalar=-0.5, in1=mfin[0:nt, :],
                                   op0=ALU.mult, op1=ALU.add),
                         reads=[bs_ht.b, mfin.b], writes=[bs_thr.b])
                else:
                    P.op("dve", mk("memset", ap=bs_thr[0:nt, :], constant=-1.0e29), writes=[bs_thr.b])
                yield
                P.op("dve", mk("tensor_scalar", out=mview[0:nt, :], in0=sview[0:nt, 0:S], scalar1=bs_thr[0:nt, 0:1],
                               scalar2=None, op0=ALU.is_ge),
                     reads=sbufs + [bs_thr.b], writes=mbufs)

            def stageB1(c):
                i, gi, S, nch, ngk, qT, mview, mbufs, vfi = (c["i"], c["gi"], c["S"], c["nch"], c["ngk"], c["qT"],
                                                             c["mview"], c["mbufs"], c["vfi"])
                mTv = maskT[:, 0:nch * nt].rearrange("p (c t) -> p c t", t=nt)
                for gk in range(ngk):
                    yield
                    k0 = gk * 512
                    gw = min(512, S - k0)
                    blocks = []
                    cc_ = 0
                    while cc_ * 128 < gw:
                        blocks.append((k0 + cc_ * 128, min(128, gw - cc_ * 128)))
                        cc_ += 1
                    full = [b for b in blocks if b[1] == 128]
                    part = [b for b in blocks if b[1] < 128]
                    if full:
                        transpose_to(mview, mbufs[0], nt, full,
                                     lambda wm, n, k0=k0: mTv[:, k0 // 128:k0 // 128 + n, :], maskT.b, "actbias")
                    if part:
                        pc0, pw_ = part[0]
                        transpose_to(mview, mbufs[-1], nt, [(pc0, pw_)],
                                     lambda wm, n, pc0=pc0: mTv[0:wm, pc0 // 128:pc0 // 128 + 1, :],
                                     maskT.b, "actbias")
                halves = [(0, 8)] if not isS else [(0, 4), (4, 8)]
                for (h0, h1) in halves:
                    if isS:
                        hh = h0 // 4
                        nct = L0 // 128
                        for c4 in range(0, nct, 2):
                            n4 = min(2, nct - c4)
                            stg = cst[(c4 // 2) % 2]
                            P.dma("cst", mk("dma_start",
                                out=stg[:, 0:n4, :],
                                in_=ck[l].rearrange("(t p) f -> p t f", p=128)[:, c4:c4 + n4, hh * 256:(hh + 1) * 256]),
                                writes=[stg.b])
                            P.op("pool", mk("tensor_copy", out=cbf[:, 0:n4, :], in_=stg[:, 0:n4, :]),
                                 reads=[stg.b], writes=[cbf.b])
                            for j in range(n4):
                                c = c4 + j
                                transpose_to(cbf[:, j, :], cbf.b, 128, [(0, 128), (128, 128)],
                                             lambda wm, n, c=c: kTv[:, 0:2, c * 128:(c + 1) * 128], kT.b,
                                             "act" if j % 2 else "dve")
                            stg2 = cst[(c4 // 2 + 1) % 2]
                            P.dma("cst", mk("dma_start",
                                out=stg2[:, 0:n4, :],
                                in_=cv[l].rearrange("(t p) f -> p t f", p=128)[:, c4:c4 + n4, hh * 256:(hh + 1) * 256]),
                                writes=[stg2.b])
                            P.op("pool", mk("tensor_copy",
                                out=Vv[:, c4:c4 + n4, :, 0:64],
                                in_=stg2[:, 0:n4, :].rearrange("p t (h d) -> p t h d", d=64)),
                                reads=[stg2.b], writes=[Vg.b])
                        transpose_to(kb[:, hh * 256:(hh + 1) * 256], kb.b, nt, [(0, 128), (128, 128)],
                                     lambda wm, n: kTv[:, 0:2, L0:L0 + nt], kT.b, "act")
                        P.op("pool", mk("tensor_copy",
                            out=Vv[0:nt, NCS - 1, :, 0:64],
                            in_=vfi[0:nt, hh * 256:(hh + 1) * 256].rearrange("p (h d) -> p h d", d=64)),
                            reads=[vfi.b], writes=[Vg.b])
                    chunks = [(c, min(128, S - c * 128)) for c in range(nch)]
                    groups = []
                    cur = []
                    for cpair in chunks:
                        if cur and (len(cur) == 4 or cur[-1][1] != cpair[1]):
                            groups.append(cur)
                            cur = []
                        cur.append(cpair)
                    if cur:
                        groups.append(cur)
                    items = [(h, gidx) for h in range(h0, h1) for gidx in range(len(groups))]

                    def emit_L(h, gidx):
                        hl = h - h0
                        pr = hl // 2
                        grp = groups[gidx]
                        rc = grp[0][1]
                        ng = len(grp)
                        Lt = nxt("L", Lb)
                        Lv = Lt[:].rearrange("p (c t) -> p c t", t=128)
                        for j, (c, _) in enumerate(grp):
                            P.op("pe", mk("matmul", out=Lv[0:rc, j, 0:nt], lhsT=ident[0:rc, 0:rc],
                                          rhs=mTv[0:rc, c, :], start=True, stop=False),
                                 reads=[ident.b, maskT.b], writes=[Lt.b])
                            P.op("pe", mk("matmul", out=Lv[0:rc, j, 0:nt], lhsT=kTv[:, pr, c * 128:c * 128 + rc],
                                          rhs=qT[:, h, 0:nt], start=False, stop=True),
                                 reads=[kT.b, qT.b], writes=[Lt.b])
                        rot["praw"] = (rot["praw"] + 1) % len(praw)
                        pr_t = praw[rot["praw"]]
                        prv = pr_t[:].rearrange("p (c t) -> p c t", t=128)
                        P.op("act", mk("activation", out=prv[0:rc, 0:ng, 0:nt], in_=Lv[0:rc, 0:ng, 0:nt],
                                       func=AF.Exp, scale=0.125),
                             reads=[Lt.b], writes=[pr_t.b])
                        return (h, gidx, pr_t, prv)

                    def emit_PV(h, gidx, pr_t, prv):
                        hl = h - h0
                        O = Ob[h // 4]
                        grp = groups[gidx]
                        rc = grp[0][1]
                        ng = len(grp)
                        for j, (c, _) in enumerate(grp):
                            first = (gidx == 0 and j == 0)
                            lastc = (gidx == len(groups) - 1 and j == ng - 1)
                            P.op("pe", mk("matmul", out=O[0:nt, h % 4, :], lhsT=prv[0:rc, j, 0:nt],
                                          rhs=Vv[0:rc, c, hl if isS else h, :], start=first, stop=lastc),
                                 reads=[pr_t.b, Vg.b], writes=[O.b])

                    pend = None
                    for (h, gidx) in items:
                        yield
                        curL = emit_L(h, gidx)
                        if pend is not None:
                            emit_PV(*pend)
                        pend = curL
                    if pend is not None:
                        emit_PV(*pend)

            def stageB2(c):
                gi, sza, tix = c["gi"], c["sza"], c["tix"]
                for ob in range(2):
                    O = Ob[ob]
                    P.op("dve", mk("reciprocal", out=rs[0:nt, ob * 4:ob * 4 + 4], in_=O[0:nt, :, 64]),
                         reads=[O.b], writes=[rs.b])
                    P.op("dve", mk("tensor_tensor",
                                   out=on[0:nt, ob * 256:(ob + 1) * 256].rearrange("p (h d) -> p h d", d=64),
                                   in0=O[0:nt, :, 0:64],
                                   in1=rs[0:nt, ob * 4:ob * 4 + 4].unsqueeze(2).broadcast_to([nt, 4, 64]),
                                   op=ALU.mult),
                         reads=[O.b, rs.b], writes=[on.b])
                P.op("pool", mk("tensor_tensor", out=gat[0:nt, :], in0=on[0:nt, :], in1=sza[0:nt, :], op=ALU.mult),
                     reads=[on.b, sza.b], writes=[gat.b])
                gao = gaT[gi % 2]
                transpose_to(gat, gat.b, nt, [(c * 128, 128) for c in range(4)],
                             lambda wm, n: gao[:, 0:4, 0:nt], gao.b, "act")
                P.dma("ga%d" % (gi % 2), mk("dma_start",
                    out=ga_scr[tix].rearrange("p (c t) -> p c t", c=4)[:, :, 0:nt], in_=gao[:, :, 0:nt]),
                    reads=[gao.b])

            load_x(seq, 0, gtile[0] % 2, xsrc)

            def drive(gens):
                res = [None] * len(gens)
                live = [g is not None for g in gens]
                while any(live):
                    for k, g in enumerate(gens):
                        if live[k]:
                            try:
                                next(g)
                            except StopIteration as ex:
                                res[k] = ex.value
                                live[k] = False
                return res

            ctx = {}
            for k in range(ntl + 2):
                gA1 = stageA1(k) if k < ntl else None
                gA2 = stageA2(ctx[k - 1]) if 0 <= k - 1 < ntl else None
                gB1 = stageB1(ctx[k - 2]) if 0 <= k - 2 < ntl else None
                r = drive([gA1, gA2, gB1])
                if gA1 is not None:
                    ctx[k] = r[0]
                if gB1 is not None:
                    stageB2(ctx[k - 2])
        P.barrier()
        st1.close()

        st2 = ExitStack()
        W2 = alloc(st2, "W2", [128, 8, C2], BF16)
        Wpo = alloc(st2, "Wpo", [128, 4, D], BF16)
        Wao = alloc(st2, "Wao", [128, 4, D], BF16)
        Wo = alloc(st2, "Wo", [128, 8, D], BF16)
        with ExitStack() as stl:
            stage = [alloc(stl, "stg%d" % i, [128, C2], F32) for i in range(2)]
            load_cast(lambda k: W2[:, k, :], lambda k: w_in[l, k * 128:(k + 1) * 128, C1:NCOL], 8, C2, stage, True, W2.b)
            load_cast(lambda k: Wpo[:, k, :], lambda k: w_po[l, k * 128:(k + 1) * 128, :], 4, D, stage, False, Wpo.b)
            load_cast(lambda k: Wao[:, k, :], lambda k: w_ao[l, k * 128:(k + 1) * 128, :], 4, D, stage, False, Wao.b)
            load_cast(lambda k: Wo[:, k, :], lambda k: w_o[l, k * 128:(k + 1) * 128, :], 8, D, stage, False, Wo.b)
            P.barrier()
        gpl = [alloc(st2, "gpl%d" % i, [128, 4, 128], BF16) for i in range(3)]
        gal = [alloc(st2, "gal%d" % i, [128, 4, 128], BF16) for i in range(3)]
        sgps = [alloc(st2, "sgp%d" % i, [128, D], BF16) for i in range(2)]
        sgas = [alloc(st2, "sga%d" % i, [128, D], BF16) for i in range(2)]
        xt.append(alloc(st2, "xt2", [128, D], F32))
        m1 = alloc(st2, "m1", [128, D], F32)
        t2 = alloc(st2, "t2", [128, 512], F32)
        mrg = alloc(st2, "mrg", [128, D], BF16)
        mT = alloc(st2, "mT", [128, 8, 128], BF16)
        yt = [alloc(st2, "yt%d" % i, [128, D], F32) for i in range(2)]

        prefetch = []
        if l + 1 < DEPTH:
            pst = [alloc(st2, "pst%d" % i, [128, C1], F32) for i in range(2)]
            load_gcol(l + 1)

            def mk_chunk(k):
                def emit():
                    sgt = pst[k % 2]
                    P.dma("pst", mk("dma_start", out=sgt[:, 0:C1], in_=w_in[l + 1, k * 128:(k + 1) * 128, 0:C1]),
                          writes=[sgt.b])
                    if k % 2 == 0:
                        P.op("dve", mk("tensor_scalar", out=W1[:, k, :], in0=sgt[:, 0:C1], scalar1=gcol[:, k:k + 1],
                                       scalar2=None, op0=ALU.mult), reads=[sgt.b, gcol.b], writes=[W1.b])
                    else:
                        P.op("act", mk("activation", out=W1[:, k, :], in_=sgt[:, 0:C1], func=AF.Copy,
                                       scale=gcol[:, k:k + 1]), reads=[sgt.b, gcol.b], writes=[W1.b])
                return emit
            prefetch = [mk_chunk(k) for k in range(8)]
        gtile2 = [0]
        for seq in seqs:
            nt = seq["nt"]
            ntl = seq["ntiles"]
            xsrc = x_src_of(seq)

            def loads2(i, gi):
                load_x(seq, i, gi % 3, xsrc)
                tix = seq["tile0"] + i
                P.dma("gpl%d" % (gi % 3), mk("dma_start",
                    out=gpl[gi % 3][:, :, 0:nt], in_=gp_scr[tix].rearrange("p (c t) -> p c t", c=4)[:, :, 0:nt]),
                    writes=[gpl[gi % 3].b])
                P.dma("gal%d" % (gi % 3), mk("dma_start",
                    out=gal[gi % 3][:, :, 0:nt], in_=ga_scr[tix].rearrange("p (c t) -> p c t", c=4)[:, :, 0:nt]),
                    writes=[gal[gi % 3].b])

            def stage2A(i):
                gi = gtile2[0]
                gtile2[0] += 1
                if prefetch and gi % 3 == 1:
                    prefetch.pop(0)()
                slot = gi % 3
                sgp = sgps[gi % 2]
                sga = sgas[gi % 2]
                if i + 1 < ntl:
                    loads2(i + 1, gi + 1)
                norm_hT(seq, slot)
                for hh in range(2):
                    m = proj(nt, W2, hh * 512, 512)
                    P.op("act", mk("activation", out=sgp[0:nt, hh * 512:(hh + 1) * 512], in_=m[0:nt, :],
                                                                   func=AF.Sigmoid),
                         reads=[m.b], writes=[sgp.b])
                for hh in range(2):
                    m = proj(nt, W2, 1024 + hh * 512, 512)
                    P.op("act", mk("activation", out=sga[0:nt, hh * 512:(hh + 1) * 512], in_=m[0:nt, :],
                                                                   func=AF.Sigmoid),
                         reads=[m.b], writes=[sga.b])
                return dict(i=i, gi=gi, slot=slot, sgp=sgp, sga=sga)

            def stage2B(c):
                i, gi, slot, sgp, sga = c["i"], c["gi"], c["slot"], c["sgp"], c["sga"]
                x = xt[slot]
                gp_, ga_ = gpl[slot], gal[slot]
                for hh in range(2):
                    m = next_mm()
                    for k in range(4):
                        P.op("pe", mk("matmul", out=m[0:nt, :], lhsT=gp_[:, k, 0:nt],
                                                                       rhs=Wpo[:, k, hh * 512:(hh + 1) * 512],
                                                                       start=(k == 0), stop=(k == 3)),
                             reads=[gp_.b, Wpo.b], writes=[m.b])
                    P.op("dve", mk("tensor_tensor", out=m1[0:nt, hh * 512:(hh + 1) * 512], in0=m[0:nt, :],
                                                                      in1=sgp[0:nt, hh * 512:(hh + 1) * 512], op=ALU.mult),
                         reads=[m.b, sgp.b], writes=[m1.b])
                for hh in range(2):
                    m = next_mm()
                    for k in range(4):
                        P.op("pe", mk("matmul", out=m[0:nt, :], lhsT=ga_[:, k, 0:nt],
                                                                       rhs=Wao[:, k, hh * 512:(hh + 1) * 512],
                                                                       start=(k == 0), stop=(k == 3)),
                             reads=[ga_.b, Wao.b], writes=[m.b])
                    P.op("dve", mk("tensor_tensor", out=t2[0:nt, :], in0=m[0:nt, :],
                                                                      in1=sga[0:nt, hh * 512:(hh + 1) * 512], op=ALU.mult),
                         reads=[m.b, sga.b], writes=[t2.b])
                    P.op("pool", mk("tensor_tensor", out=mrg[0:nt, hh * 512:(hh + 1) * 512],
                                                                  in0=m1[0:nt, hh * 512:(hh + 1) * 512], in1=t2[0:nt, :],
                                                                  op=ALU.add),
                         reads=[m1.b, t2.b], writes=[mrg.b])
                transpose_to(mrg, mrg.b, nt, [(c * 128, 128) for c in range(8)],
                             lambda wm, n: mT[:, 0:8, 0:nt], mT.b, "act")
                for hh in range(2):
                    m = next_mm()
                    for k in range(8):
                        P.op("pe", mk("matmul", out=m[0:nt, :], lhsT=mT[:, k, 0:nt],
                                                                       rhs=Wo[:, k, hh * 512:(hh + 1) * 512],
                                                                       start=(k == 0), stop=(k == 7)),
                             reads=[mT.b, Wo.b], writes=[m.b])
                    P.op("dve", mk("tensor_tensor", out=x[0:nt, hh * 512:(hh + 1) * 512],
                                                                      in0=x[0:nt, hh * 512:(hh + 1) * 512], in1=m[0:nt, :],
                                                                      op=ALU.add),
                         reads=[m.b, x.b], writes=[x.b])
                r0 = seq["tok0"] + i * 128
                if not last:
                    P.dma("x%d" % slot, mk("dma_start", out=xscr[r0:r0 + nt, :], in_=x[0:nt, :]), reads=[x.b])
                else:
                    y = yt[gi % 2]
                    P.op("act", mk("activation", out=xn[0:nt, 0:D], in_=x[0:nt, :], func=AF.Square,
                                                       accum_out=ssq[0:nt, :]),
                         reads=[x.b], writes=[xn.b, ssq.b])
                    P.op("dve", mk("tensor_scalar", out=rstd[0:nt, :], in0=ssq[0:nt, :], scalar1=1.0 / D, scalar2=EPS,
                                                          op0=ALU.mult, op1=ALU.add),
                         reads=[ssq.b], writes=[rstd.b])
                    P.op("act", mk("activation", out=rstd[0:nt, :], in_=rstd[0:nt, :], func=AF.Sqrt),
                         reads=[rstd.b], writes=[rstd.b])
                    P.op("dve", mk("reciprocal", out=rstd[0:nt, :], in_=rstd[0:nt, :]),
                         reads=[rstd.b], writes=[rstd.b])
                    P.op("dve", mk("scalar_tensor_tensor", out=y[0:nt, :], in0=x[0:nt, :], scalar=rstd[0:nt, 0:1],
                                                                      in1=gfbc[0:nt, :], op0=ALU.mult, op1=ALU.mult),
                         reads=[x.b, rstd.b, gfbc.b], writes=[y.b])
                    yo = seq["y_out"]
                    P.dma("y%d" % slot, mk("dma_start", out=yo[i * 128:i * 128 + nt, :], in_=y[0:nt, :]),
                          reads=[y.b])
            loads2(0, gtile2[0])
            pend = None
            for i in range(ntl):
                cA = stage2A(i)
                if pend is not None:
                    stage2B(pend)
                pend = cA
            stage2B(pend)
        while prefetch:
            prefetch.pop(0)()
        P.barrier()
        xt.pop()
        st2.close()

    P.final_wait()
    P.emit()
    stack0.close()
    return nc


_CACHE = {}


def kernel(x_prompt, x_sample, cache_k, cache_v, cache_kidx, state_pool, norm_g, w_in, w_pool_mix,
           pool_scale, w_pool_out, w_attn_out, w_o, final_norm_g):
    NCORES = 8
    f = lambda a: np.ascontiguousarray(np.asarray(a, dtype=np.float32))
    x_prompt, x_sample = f(x_prompt), f(x_sample)
    cache_k, cache_v, cache_kidx, state_pool = f(cache_k), f(cache_v), f(cache_kidx), f(state_pool)
    BP, T, _ = x_prompt.shape
    BS, TS, _ = x_sample.shape
    DEPTH = cache_k.shape[0]
    L0 = cache_k.shape[2]
    assert BP % NCORES == 0 and BS == NCORES
    NP = BP // NCORES
    cfg = dict(NP=NP, T=T, TS=TS, L0=L0, DEPTH=DEPTH, KP=min(256, T // 4), KS=min(256, (L0 + TS) // 4))
    key = tuple(sorted(cfg.items()))
    if key not in _CACHE:
        _CACHE[key] = build(cfg)
    nc = _CACHE[key]

    rope = _rope_table(list(range(T)) + list(range(L0, L0 + TS)))
    bands = _band_tables().reshape(4, 128, 512)
    pw2 = np.tile((2.0 ** -(np.arange(NI_BISECT, dtype=np.float64) + 1)).astype(np.float32)[None, :], (128, 1))
    shared = dict(norm_g=f(norm_g), w_in=f(w_in), w_mix=f(w_pool_mix), pscale=f(pool_scale), w_po=f(w_pool_out),
                  w_ao=f(w_attn_out), w_o=f(w_o), gfin=f(final_norm_g).reshape(1, D), rope=rope, bands=bands, pw2=pw2)
    in_maps = []
    for c in range(NCORES):
        m = dict(shared)
        m["x_p"] = x_prompt[c * NP:(c + 1) * NP]
        m["x_s"] = x_sample[c]
        m["ck"] = cache_k[:, c].reshape(DEPTH, L0, 512)
        m["cv"] = cache_v[:, c].reshape(DEPTH, L0, 512)
        m["cki"] = cache_kidx[:, c]
        m["spool"] = state_pool[:, c]
        in_maps.append(m)
    res = run_bass_kernel_spmd(nc, in_maps, core_ids=list(range(NCORES)))
    R = res.results
    cat = lambda name, ax: np.concatenate([np.asarray(r[name]) for r in R], axis=ax)
    stk = lambda name, ax: np.stack([np.asarray(r[name]) for r in R], axis=ax)
    y_prompt = cat("y_p", 0)
    y_sample = stk("y_s", 0)
    nk_p = cat("ok_p", 1).reshape(DEPTH, BP, T, 8, 64)
    nv_p = cat("ov_p", 1).reshape(DEPTH, BP, T, 8, 64)
    nki_p = cat("oki_p", 1)
    npl_p = cat("opl_p", 1)
    nk_s = stk("ok_s", 1).reshape(DEPTH, BS, TS, 8, 64)
    nv_s = stk("ov_s", 1).reshape(DEPTH, BS, TS, 8, 64)
    nki_s = stk("oki_s", 1)
    npl_s = stk("opl_s", 1)
    return (y_prompt, y_sample, nk_p, nv_p, nki_p, npl_p, nk_s, nv_s, nki_s, npl_s)
```

```python
from contextlib import ExitStack
import os
import numpy as np
import concourse.bass as bass
import concourse.mybir as mybir
from concourse.bass_utils import run_bass_kernel_spmd

F32 = mybir.dt.float32
BF16 = mybir.dt.bfloat16
ALU = mybir.AluOpType
AF = mybir.ActivationFunctionType

ENGS = ("pe", "act", "dve", "pool", "sp")

D = 1024
NCOL = 5416
C1 = 3368
C2 = NCOL - C1
NH = 8
NEG = -1.0e30
NI_BISECT = int(os.environ.get('KNI', '24'))
EPS = 1e-6


class Buf:
    __slots__ = ("name", "w", "r", "excl")

    def __init__(self, name):
        self.name = name
        self.excl = False
        self.w = None
        self.r = []


class Prog:
    def __init__(self, nc):
        self.nc = nc
        self.q = {e: [] for e in ENGS}
        self.tick = {e: 0 for e in ENGS}
        self.sem = {e: nc.alloc_semaphore("sem_" + e) for e in ENGS}
        self.seen = {e: {} for e in ENGS}
        self.dsem = {}
        self.dtick = {}
        self.nbuf = 0
        self.count = 0
        self.limit = int(os.environ.get("KLIMIT", "1000000000"))

    def buf(self, name=None):
        self.nbuf += 1
        return Buf(name or "b%d" % self.nbuf)

    def _need(self, reads, writes):
        need = {}
        for b in reads:
            if b.w is not None:
                s, t = b.w
                if need.get(s, 0) < t:
                    need[s] = t
        for b in writes:
            if b.w is not None:
                s, t = b.w
                if need.get(s, 0) < t:
                    need[s] = t
            for s, t in b.r:
                if need.get(s, 0) < t:
                    need[s] = t
        return need

    def _waits(self, eng, need):
        waits = []
        seen = self.seen[eng]
        for s, t in need.items():
            if s == eng and eng in ("pe", "sp"):
                continue
            if seen.get(s, 0) >= t:
                continue
            seen[s] = t
            waits.append((s, t))
        return waits

    def _mark(self, src, reads, writes):
        for b in reads:
            if len(b.r) > 24:
                d = {}
                for s, t in b.r:
                    if d.get(s, 0) < t:
                        d[s] = t
                b.r = list(d.items())
            b.r.append(src)
        for b in writes:
            b.w = src
            b.r = []

    def op(self, eng, fn, reads=(), writes=()):
        self.count += 1
        if self.count > self.limit:
            return
        xr = [b for b in reads if b.excl]
        if xr:
            writes = list(writes) + [b for b in xr if b not in writes]
        waits = self._waits(eng, self._need(reads, writes))
        self.tick[eng] += 1
        my = self.tick[eng]
        self.q[eng].append((waits, fn, ("e", eng)))
        self._mark((eng, my), reads, writes)

    def dma(self, stream, fn, reads=(), writes=(), eng="sp"):
        stream = (writes[0] if writes else reads[0]).name
        self.count += 1
        if self.count > self.limit:
            return
        if stream not in self.dsem:
            self.dsem[stream] = self.nc.alloc_semaphore("dsem_" + stream)
            self.dtick[stream] = 0
        waits = self._waits(eng, self._need(reads, writes))
        self.dtick[stream] += 16
        my = self.dtick[stream]
        self.q[eng].append((waits, fn, ("d", stream)))
        self._mark(("dma:" + stream, my), reads, writes)

    def _semof(self, s):
        if s.startswith("dma:"):
            return self.dsem[s[4:]]
        return self.sem[s]

    def barrier(self):
        need = {}
        for e in ENGS:
            if self.tick[e] > 0:
                need[e] = self.tick[e]
        for s, t in self.dtick.items():
            if t > 0:
                need["dma:" + s] = t
        for e in ENGS:
            waits = self._waits(e, dict(need))
            if waits:
                self.q[e].append((waits, None, None))

    def final_wait(self, eng="sp"):
        need = {}
        for s, t in self.dtick.items():
            if t > 0:
                need["dma:" + s] = t
        for e in ENGS:
            if e != eng and self.tick[e] > 0:
                need[e] = self.tick[e]
        waits = self._waits(eng, need)
        if waits:
            self.q[eng].append((waits, None, None))

    def emit(self):
        nc = self.nc
        engobj = {"pe": "tensor", "act": "scalar", "dve": "vector", "pool": "gpsimd", "sp": "sync"}
        with nc.Block() as block:
            for e in ENGS:
                items = self.q[e]
                if not items:
                    continue

                def body(eo, items=items):
                    for waits, fn, inc in items:
                        for s, t in waits:
                            eo.wait_ge(self._semof(s), t)
                        if fn is None:
                            continue
                        ins = fn(eo)
                        if inc[0] == "e":
                            ins.then_inc(self.sem[inc[1]], 1)
                        else:
                            ins.then_inc(self.dsem[inc[1]], 16)

                getattr(block, engobj[e])(body)


def mk(meth, **kw):
    return lambda e: getattr(e, meth)(**kw)


class Tl:
    def __init__(self, t, b):
        self.t = t
        self.b = b

    def __getitem__(self, k):
        return self.t[k]


def _rope_table(positions):
    pos = np.asarray(positions, dtype=np.float32)
    out = np.zeros((len(pos), 192), np.float32)
    for (d, off) in ((64, 0), (32, 128)):
        inv = (10000.0 ** (-np.arange(0, d, 2, dtype=np.float32) / np.float32(d))).astype(np.float32)
        ang = (pos[:, None] * inv[None, :]).astype(np.float32).astype(np.float64)
        c = np.cos(ang).astype(np.float32)
        s = np.sin(ang).astype(np.float32)
        h = d // 2
        out[:, off:off + h] = c
        out[:, off + h:off + 2 * h] = c
        out[:, off + 2 * h:off + 3 * h] = -s
        out[:, off + 3 * h:off + 4 * h] = s
    return out


def _band_tables():
    B = np.zeros((4, 128, 4, 128), np.float32)
    for g, w in enumerate((2, 4, 8, 16)):
        for t in range(128):
            for tp in range(max(0, t - w + 1), t + 1):
                B[0, tp, g, t] += 1.0 / min(t + 1, w)
                B[1, tp, g, t] += 1.0 / w
            B[0, t, g, t] -= 1.0
            B[1, t, g, t] -= 1.0
            for tp in range(128):
                if t - (tp - 128) < w:
                    B[2, tp, g, t] = 1.0 / w
            for j in range(15):
                if t - (j - 15) < w:
                    B[3, j, g, t] = 1.0 / w
    return B


def build(cfg):
    NP, T, TS, L0, DEPTH = cfg["NP"], cfg["T"], cfg["TS"], cfg["L0"], cfg["DEPTH"]
    KP, KS = cfg["KP"], cfg["KS"]
    assert T % 128 == 0 and TS == 64 and L0 % 128 == 0
    NTP = T // 128
    LS = L0 + TS
    NCS = L0 // 128 + 1
    SMAX = max(T, LS)
    KTW = max(4 * T, 2 * LS)
    VS = int(os.environ.get('KVS', '65'))
    VW = max(NTP * 8 * VS, NCS * 4 * VS)
    NTOK = NP * T + TS
    NTILES = NP * NTP + 1

    nc = bass.Bass("TRN2", target_bir_lowering=False, dynamic_dma_scratch_size=256)
    P = Prog(nc)

    def din(name, shape, dt=F32):
        return nc.dram_tensor(name, list(shape), dt, kind="ExternalInput").ap()

    def dout(name, shape, dt=F32):
        return nc.dram_tensor(name, list(shape), dt, kind="ExternalOutput").ap()

    x_p = din("x_p", [NP, T, D])
    x_s = din("x_s", [TS, D])
    ck = din("ck", [DEPTH, L0, 512])
    cv = din("cv", [DEPTH, L0, 512])
    cki = din("cki", [DEPTH, L0, 32])
    spool = din("spool", [DEPTH, 15, 512])
    norm_g = din("norm_g", [DEPTH, D])
    w_in = din("w_in", [DEPTH, D, NCOL])
    w_mix = din("w_mix", [DEPTH, 4, 128, 128])
    pscale = din("pscale", [DEPTH, 512])
    w_po = din("w_po", [DEPTH, 512, D])
    w_ao = din("w_ao", [DEPTH, 512, D])
    w_o = din("w_o", [DEPTH, D, D])
    gfin = din("gfin", [1, D])
    rope = din("rope", [T + TS, 192])
    bands = din("bands", [4, 128, 512])
    pw2 = din("pw2", [128, NI_BISECT])

    y_p = dout("y_p", [NP, T, D])
    y_s = dout("y_s", [TS, D])
    ok_p = dout("ok_p", [DEPTH, NP, T, 512])
    ov_p = dout("ov_p", [DEPTH, NP, T, 512])
    oki_p = dout("oki_p", [DEPTH, NP, T, 32])
    opl_p = dout("opl_p", [DEPTH, NP, 15, 512])
    ok_s = dout("ok_s", [DEPTH, TS, 512])
    ov_s = dout("ov_s", [DEPTH, TS, 512])
    oki_s = dout("oki_s", [DEPTH, TS, 32])
    opl_s = dout("opl_s", [DEPTH, 15, 512])

    xscr = nc.dram_tensor("xscr", [NTOK, D], F32).ap()
    gp_scr = nc.dram_tensor("gp_scr", [NTILES, 128, 512], BF16).ap()
    ga_scr = nc.dram_tensor("ga_scr", [NTILES, 128, 512], BF16).ap()

    seqs = []
    for s in range(NP):
        seqs.append(dict(kind="p", idx=s, T=T, nt=128, ntiles=NTP, tok0=s * T, tile0=s * NTP,
                         x_in=x_p[s], y_out=y_p[s], K=KP, rope0=0))
    seqs.append(dict(kind="s", idx=0, T=TS, nt=TS, ntiles=1, tok0=NP * T, tile0=NP * NTP,
                     x_in=x_s, y_out=y_s, K=KS, rope0=T))

    stack0 = ExitStack()

    uniq = [0]

    def alloc(st, name, shape, dt):
        uniq[0] += 1
        t = st.enter_context(nc.sbuf_tensor("%s_%d" % (name, uniq[0]), list(shape), dt))
        return Tl(t, P.buf(name))

    def palloc(name, shape, dt):
        t = nc.alloc_psum_tensor(name, list(shape), dt)
        b = P.buf(name)
        b.excl = True
        return Tl(t, b)

    mm = [palloc("mm%d" % i, [128, 512], F32) for i in range(2)]
    tp = palloc("tp", [128, 1024], BF16)
    pp = palloc("pp", [128, 512], F32)
    Lb = [palloc("L%d" % i, [128, 512], F32) for i in range(2)]
    Ob = [palloc("O%d" % i, [128, 4, VS], F32) for i in range(2)]
    mmi = [0]

    def next_mm():
        mmi[0] ^= 1
        return mm[mmi[0]]

    ident = alloc(stack0, "ident", [128, 128], BF16)
    identf = alloc(stack0, "identf", [128, 128], F32)
    ones1 = alloc(stack0, "ones1", [1, 128], F32)
    gfbc = alloc(stack0, "gfbc", [128, D], F32)
    band = alloc(stack0, "band", [128, 4, 512], BF16)
    pw = alloc(stack0, "pw", [128, NI_BISECT], F32)
    xt = [alloc(stack0, "xt%d" % i, [128, D], F32) for i in range(2)]
    xn = alloc(stack0, "xn", [128, D], BF16)
    hT = alloc(stack0, "hT", [128, 8, 128], BF16)
    ssq = alloc(stack0, "ssq", [128, 1], F32)
    rstd = alloc(stack0, "rstd", [128, 1], F32)
    gcol = alloc(stack0, "gcol", [128, 8], F32)
    W1 = alloc(stack0, "W1", [128, 8, C1], BF16)
    ftmp = [alloc(stack0, "ftmp%d" % i, [128, 512], F32) for i in range(2)]
    ftmpi = [0]

    def next_ftmp():
        ftmpi[0] ^= 1
        return ftmp[ftmpi[0]]

    P.op("pool", mk("memset", ap=identf[:], constant=0.0), writes=[identf.b])
    P.op("pool", mk("affine_select", out=identf[:], in_=identf[:], pattern=[[-1, 128]],
                                           compare_op=ALU.not_equal, fill=1.0, base=0,
                                           channel_multiplier=1),
         reads=[identf.b], writes=[identf.b])
    P.op("dve", mk("tensor_copy", out=ident[:], in_=identf[:]), reads=[identf.b], writes=[ident.b])
    P.op("dve", mk("memset", ap=ones1[:], constant=1.0), writes=[ones1.b])
    P.dma("c0", mk("dma_start", out=pw[:], in_=pw2), writes=[pw.b])
    for kind in range(4):
        f = next_ftmp()
        P.dma("c1", mk("dma_start", out=f[:], in_=bands[kind]), writes=[f.b])
        P.op("dve", mk("tensor_copy", out=band[:, kind, :], in_=f[:]),
             reads=[f.b], writes=[band.b])
    grow = alloc(stack0, "grow", [1, D], F32)
    P.dma("c0", mk("dma_start", out=grow[:], in_=gfin), writes=[grow.b])
    for hh in range(2):
        m = next_mm()
        P.op("pe", mk("matmul", out=m[:], lhsT=ones1[0:1, :], rhs=grow[0:1, hh * 512:(hh + 1) * 512],
                                                  start=True, stop=True),
             reads=[ones1.b, grow.b], writes=[m.b])
        P.op("dve", mk("tensor_copy", out=gfbc[:, hh * 512:(hh + 1) * 512], in_=m[:]),
             reads=[m.b], writes=[gfbc.b])

    def load_gcol(l):
        g8 = next_ftmp()
        P.dma("c1", mk("dma_start", out=g8[0:8, 0:128], in_=norm_g[l].rearrange("(k p) -> k p", p=128)),
              writes=[g8.b])
        m = next_mm()
        P.op("pe", mk("transpose", out=m[:, 0:8], in_=g8[0:8, 0:128], identity=identf[0:8, 0:8]),
             reads=[g8.b, identf.b], writes=[m.b])
        P.op("dve", mk("tensor_copy", out=gcol[:], in_=m[:, 0:8]), reads=[m.b], writes=[gcol.b])

    def load_cast(dst_ap_fn, src_rows_fn, nk, ncols, stage, scale_gcol, dstb):
        for k in range(nk):
            s = stage[k % 2]
            P.dma("wst%d" % (k % 2), mk("dma_start", out=s[:, 0:ncols], in_=src_rows_fn(k)),
                  writes=[s.b])
            if scale_gcol:
                if k % 2 == 0:
                    P.op("dve", mk("tensor_scalar", out=dst_ap_fn(k), in0=s[:, 0:ncols],
                                                                    scalar1=gcol[:, k:k + 1], scalar2=None,
                                                                    op0=ALU.mult),
                         reads=[s.b, gcol.b], writes=[dstb])
                else:
                    P.op("act", mk("activation", out=dst_ap_fn(k), in_=s[:, 0:ncols],
                                                                 func=AF.Copy, scale=gcol[:, k:k + 1]),
                         reads=[s.b, gcol.b], writes=[dstb])
            else:
                if k % 2 == 0:
                    P.op("dve", mk("tensor_copy", out=dst_ap_fn(k), in_=s[:, 0:ncols]),
                         reads=[s.b], writes=[dstb])
                else:
                    P.op("act", mk("copy", out=dst_ap_fn(k), in_=s[:, 0:ncols]),
                         reads=[s.b], writes=[dstb])

    def load_x(seq, i, slot, src):
        nt = seq["nt"]
        r0 = i * 128
        P.dma("x%d" % slot, mk("dma_start", out=xt[slot][0:nt, :], in_=src[r0:r0 + nt, :]),
              writes=[xt[slot].b])

    def norm_hT(seq, slot):
        nt = seq["nt"]
        x = xt[slot]
        P.op("act", mk("activation", out=xn[0:nt, 0:D], in_=x[0:nt, :], func=AF.Square,
                                           accum_out=ssq[0:nt, :]),
             reads=[x.b], writes=[xn.b, ssq.b])
        P.op("dve", mk("tensor_scalar", out=rstd[0:nt, :], in0=ssq[0:nt, :], scalar1=1.0 / D, scalar2=EPS,
                                              op0=ALU.mult, op1=ALU.add),
             reads=[ssq.b], writes=[rstd.b])
        P.op("act", mk("activation", out=rstd[0:nt, :], in_=rstd[0:nt, :], func=AF.Sqrt),
             reads=[rstd.b], writes=[rstd.b])
        P.op("dve", mk("reciprocal", out=rstd[0:nt, :], in_=rstd[0:nt, :]),
             reads=[rstd.b], writes=[rstd.b])
        P.op("dve", mk("tensor_scalar", out=xn[0:nt, :], in0=x[0:nt, :], scalar1=rstd[0:nt, 0:1],
                                              scalar2=None, op0=ALU.mult),
             reads=[x.b, rstd.b], writes=[xn.b])
        tv = tp[:].rearrange("p (k t) -> p k t", t=128)
        for k in range(8):
            P.op("pe", mk("transpose", out=tv[:, k, 0:nt], in_=xn[0:nt, k * 128:(k + 1) * 128],
                                                  identity=ident[0:nt, 0:nt]),
                 reads=[xn.b, ident.b], writes=[tp.b])
        P.op("act", mk("copy", out=hT[:, 0:4, 0:nt], in_=tv[:, 0:4, 0:nt]), reads=[tp.b], writes=[hT.b])
        P.op("dve", mk("tensor_copy", out=hT[:, 4:8, 0:nt], in_=tv[:, 4:8, 0:nt]), reads=[tp.b], writes=[hT.b])

    def proj(nt, W, c0, ncols):
        m = next_mm()
        for k in range(8):
            P.op("pe", mk("matmul", out=m[0:nt, 0:ncols], lhsT=hT[:, k, 0:nt], rhs=W[:, k, c0:c0 + ncols],
                                               start=(k == 0), stop=(k == 7)),
                 reads=[hT.b, W.b], writes=[m.b])
        return m

    def transpose_to(src, srcb, nt, ncols_list, dst_fn, dstb, evac_eng="act"):
        tv = tp[:].rearrange("p (k t) -> p k t", t=128)
        for j, (c0, wd) in enumerate(ncols_list):
            P.op("pe", mk("transpose", out=tv[0:wd, j, 0:nt], in_=src[0:nt, c0:c0 + wd],
                                                                identity=ident[0:nt, 0:nt]),
                 reads=[srcb, ident.b], writes=[tp.b])
        wmax = max(w for _, w in ncols_list)
        n = len(ncols_list)
        if evac_eng == "none":
            return tv
        if evac_eng == "actbias":
            P.op("act", mk("activation", out=dst_fn(wmax, n), in_=tv[0:wmax, 0:n, 0:nt], func=AF.Copy,
                           scale=30000.0, bias=-30000.0), reads=[tp.b], writes=[dstb])
        elif evac_eng == "act":
            P.op("act", mk("copy", out=dst_fn(wmax, n), in_=tv[0:wmax, 0:n, 0:nt]), reads=[tp.b], writes=[dstb])
        else:
            P.op("dve", mk("tensor_copy", out=dst_fn(wmax, n), in_=tv[0:wmax, 0:n, 0:nt]),
                 reads=[tp.b], writes=[dstb])

    for l in range(DEPTH):
        x_src_of = (lambda seq: seq["x_in"]) if l == 0 else (lambda seq: xscr[seq["tok0"]:seq["tok0"] + seq["T"], :])
        last = (l == DEPTH - 1)

        st1 = ExitStack()
        Wmix = alloc(st1, "Wmix", [128, 4, 128], BF16)
        with ExitStack() as stl:
            if l == 0:
                stage = [alloc(stl, "stg%d" % i, [128, C1], F32) for i in range(2)]
                load_gcol(l)
                load_cast(lambda k: W1[:, k, :], lambda k: w_in[l, k * 128:(k + 1) * 128, 0:C1], 8, C1, stage, True, W1.b)
            srow = next_ftmp()
            P.dma("c1", mk("dma_start", out=srow[0:1, :], in_=pscale[l:l + 1, :]), writes=[srow.b])
            m = next_mm()
            P.op("pe", mk("matmul", out=m[:], lhsT=ones1[0:1, :], rhs=srow[0:1, :], start=True, stop=True),
                 reads=[ones1.b, srow.b], writes=[m.b])
            s0 = next_ftmp()
            P.dma("wst0", mk("dma_start", out=s0[:, 0:512].rearrange("p (g d) -> p g d", g=4),
                                                in_=w_mix[l].rearrange("g c d -> c g d")), writes=[s0.b])
            P.op("dve", mk("tensor_tensor", out=Wmix[:].rearrange("p g d -> p (g d)"), in0=s0[:, 0:512],
                                                  in1=m[:], op=ALU.mult),
                 reads=[s0.b, m.b], writes=[Wmix.b])
            P.barrier()

        kT = alloc(st1, "kT", [128, KTW], BF16)
        Vg = alloc(st1, "Vg", [128, VW], BF16)
        kiT = alloc(st1, "kiT", [96, SMAX], BF16)
        MH = max(T, (LS + 1) // 2)
        score = alloc(st1, "score", [128, 2 * MH], F32)
        sc_bufs = [score.b, P.buf("score1")]
        maskT = alloc(st1, "maskT", [128, max(NTP * 128, NCS * 64)], BF16)
        utok = [alloc(st1, "utok%d" % i, [128, 512], BF16) for i in range(2)]
        szp = alloc(st1, "szp", [128, 512], BF16)
        szas = [alloc(st1, "sza%d" % i, [128, 512], BF16) for i in range(3)]
        qf = alloc(st1, "qf", [128, 512], F32)
        kf = [alloc(st1, "kf%d" % i, [128, 512], F32) for i in range(2)]
        vf = [alloc(st1, "vf%d" % i, [128, 512], F32) for i in range(2)]
        g5 = [alloc(st1, "g5%d" % i, [128, 296], F32) for i in range(2)]
        uf = ftmp[0]
        rtab = [alloc(st1, "rtab%d" % i, [128, 192], F32) for i in range(2)]
        rtmp = alloc(st1, "rtmp", [128, 512], F32)
        qb = alloc(st1, "qb", [128, 512], BF16)
        kb = alloc(st1, "kb", [128, 512], BF16)
        qsb = alloc(st1, "qsb", [128, 256], BF16)
        ki3 = alloc(st1, "ki3", [128, 96], BF16)
        Dg = alloc(st1, "Dg", [128, 8, 128], BF16)
        plT = alloc(st1, "plT", [128, 4, 128], BF16)
        gpt = alloc(st1, "gpt", [128, 512], BF16)
        gpT = [alloc(st1, "gpT%d" % i, [128, 4, 128], BF16) for i in range(2)]
        gaT = [alloc(st1, "gaT%d" % i, [128, 4, 128], BF16) for i in range(2)]
        qTs = [alloc(st1, "qT%d" % i, [128, 8, 128], BF16) for i in range(3)]
        qiT = alloc(st1, "qiT", [96, 3, 128], BF16)
        rel = [alloc(st1, "rel%d" % i, [128, 512], BF16) for i in range(3)]
        maskb = alloc(st1, "maskb", [128, 2 * MH], BF16)
        mb_bufs = [maskb.b, P.buf("maskb1")]
        praw = [alloc(st1, "praw%d" % i, [128, 512], BF16) for i in range(3)]
        on = alloc(st1, "on", [128, 512], F32)
        gat = alloc(st1, "gat", [128, 512], BF16)
        rs = alloc(st1, "rs", [128, 8], F32)
        bs_lo = alloc(st1, "bs_lo", [128, 1], F32)
        bs_hi = alloc(st1, "bs_hi", [128, 1], F32)
        bs_mid = [alloc(st1, "bs_mid%d" % i, [128, 1], F32) for i in range(2)]
        bs_cnt = alloc(st1, "bs_cnt", [128, 1], F32)
        bs_cnt2 = alloc(st1, "bs_cnt2", [128, 1], F32)
        bs_u = alloc(st1, "bs_u", [128, 1], F32)
        bs_ht = alloc(st1, "bs_ht", [128, NI_BISECT], F32)
        bs_thr = alloc(st1, "bs_thr", [128, 1], F32)
        cst = [alloc(st1, "cst%d" % i, [128, 2, 256], F32) for i in range(2)]
        cbf = alloc(st1, "cbf", [128, 2, 256], BF16)
        spf = alloc(st1, "spf", [16, 512], F32)
        spb = alloc(st1, "spb", [16, 512], BF16)
        ckif = alloc(st1, "ckif", [128, max(L0 // 128, 1), 32], F32)
        cki3 = alloc(st1, "cki3", [128, 96], BF16)

        rot = {"rel": 0, "mk": 0, "praw": 0, "pm": 0, "L": 0}

        def nxt(name, arr):
            rot[name] ^= 1
            return arr[rot[name]]

        P.op("pool", mk("memset", ap=Vg[:], constant=1.0), writes=[Vg.b])
        for qz in qTs:
            P.op("pool", mk("memset", ap=qz[:], constant=0.0), writes=[qz.b])

        def rope_inplace(t, nt, nh, hd, tab, toff, eng="pool"):
            h2 = hd // 2
            xv = lambda: t.rearrange("p (h d) -> p h d", d=hd)
            tv = lambda: rtmp[0:nt, 0:nh * hd].rearrange("p (h d) -> p h d", d=hd)
            cc = lambda: tab[0:nt, toff:toff + hd].unsqueeze(1).broadcast_to([nt, nh, hd])
            sn = lambda: tab[0:nt, toff + hd:toff + hd + h2].unsqueeze(1).broadcast_to([nt, nh, h2])
            sp_ = lambda: tab[0:nt, toff + hd + h2:toff + 2 * hd].unsqueeze(1).broadcast_to([nt, nh, h2])
            return xv, tv, cc, sn, sp_, h2

        def do_rope(tl, col0, nt, nh, hd, tab, toff, eng):
            t = tl[0:nt, col0:col0 + nh * hd]
            xv, tv, cc, sn, sp_, h2 = rope_inplace(t, nt, nh, hd, tab, toff)
            P.op(eng, mk("tensor_tensor", out=tv()[:, :, 0:h2], in0=xv()[:, :, h2:hd], in1=sn(), op=ALU.mult),
                 reads=[tl.b, tab.b], writes=[rtmp.b])
            P.op(eng, mk("tensor_tensor", out=tv()[:, :, h2:hd], in0=xv()[:, :, 0:h2], in1=sp_(), op=ALU.mult),
                 reads=[tl.b, tab.b], writes=[rtmp.b])
            P.op(eng, mk("tensor_tensor", out=xv(), in0=xv(), in1=cc(), op=ALU.mult),
                 reads=[tl.b, tab.b], writes=[tl.b])
            P.op(eng, mk("tensor_tensor", out=xv(), in0=xv(), in1=tv(), op=ALU.add),
                 reads=[tl.b, rtmp.b], writes=[tl.b])

        gtile = [0]

        for seq in seqs:
            nt = seq["nt"]
            ntl = seq["ntiles"]
            isS = seq["kind"] == "s"
            xsrc = x_src_of(seq)
            Ksel = seq["K"]
            if not isS:
                kTv = kT[:, 0:4 * T].rearrange("p (c s) -> p c s", c=4)
                Vv = Vg[:, 0:NTP * 8 * VS].rearrange("p (t h d) -> p t h d", h=8, d=VS)
                okd, ovd, okid, opld = ok_p[l, seq["idx"]], ov_p[l, seq["idx"]], oki_p[l, seq["idx"]], opl_p[l, seq["idx"]]
            else:
                kTv = kT[:, 0:2 * LS].rearrange("p (c s) -> p c s", c=2)
                Vv = Vg[:, 0:NCS * 4 * VS].rearrange("p (t h d) -> p t h d", h=4, d=VS)
                okd, ovd, okid, opld = ok_s[l], ov_s[l], oki_s[l], opl_s[l]

            if isS and L0 > 0:
                nct = L0 // 128
                P.dma("cki", mk("dma_start", out=ckif[:, 0:nct, :],
                                                   in_=cki[l].rearrange("(t p) d -> p t d", p=128)),
                      writes=[ckif.b])
                for c in range(nct):
                    P.op("dve", mk("tensor_copy",
                        out=cki3[:].rearrange("p (r d) -> p r d", r=3),
                        in_=ckif[:, c, :].unsqueeze(1).broadcast_to([128, 3, 32])),
                        reads=[ckif.b], writes=[cki3.b])
                    transpose_to(cki3, cki3.b, 128, [(0, 96)],
                                 lambda wm, n, c=c: kiT[0:96, c * 128:(c + 1) * 128].unsqueeze(1), kiT.b,
                                 evac_eng="act" if c % 2 else "dve")
                P.dma("spf", mk("dma_start", out=spf[0:15, :], in_=spool[l]), writes=[spf.b])
                P.op("dve", mk("tensor_copy", out=spb[0:15, :], in_=spf[0:15, :]), reads=[spf.b], writes=[spb.b])

            def stageA1(i):
                qT = qTs[gtile[0] % 3]
                sza = szas[gtile[0] % 3]
                gi = gtile[0]
                gtile[0] += 1
                slot = gi % 2
                if i + 1 < ntl:
                    load_x(seq, i + 1, (gi + 1) % 2, xsrc)
                rt = rtab[gi % 2]
                rp0 = seq["rope0"] + i * 128
                P.dma("rt%d" % (gi % 2), mk("dma_start", out=rt[0:nt, :], in_=rope[rp0:rp0 + nt, :]),
                      writes=[rt.b])
                norm_hT(seq, slot)
                key0 = (L0 if isS else 0) + i * 128
                S = key0 + nt
                if isS:
                    sview = score[:, 0:S]
                    sbufs = [sc_bufs[0], sc_bufs[1]]
                else:
                    sview = score[:, (gtile[0] - 1) % 2 * MH:(gtile[0] - 1) % 2 * MH + S]
                    sbufs = [sc_bufs[(gtile[0] - 1) % 2]]
                ucur = utok[gi % 2]
                uprev = utok[(gi + 1) % 2]
                kfi = kf[gi % 2]
                vfi = vf[gi % 2]
                g5i = g5[gi % 2]

                yield
                m = proj(nt, W1, 0, 512)
                P.op("act", mk("copy", out=ucur[0:nt, :], in_=m[0:nt, :]), reads=[m.b], writes=[ucur.b])
                if i == ntl - 1:
                    P.op("dve", mk("tensor_copy", out=uf[0:nt, :], in_=m[0:nt, :]), reads=[m.b], writes=[uf.b])
                    P.dma("opl", mk("dma_start", out=opld, in_=uf[nt - 15:nt, :]), reads=[uf.b])
                yield
                m = proj(nt, W1, 512, 512)
                P.op("act", mk("activation", out=szp[0:nt, :], in_=m[0:nt, :], func=AF.Silu),
                     reads=[m.b], writes=[szp.b])
                yield
                m = proj(nt, W1, 1024, 512)
                P.op("act", mk("copy", out=qf[0:nt, :], in_=m[0:nt, :]), reads=[m.b], writes=[qf.b])
                yield
                m = proj(nt, W1, 1536, 512)
                P.op("act", mk("copy", out=kfi[0:nt, :], in_=m[0:nt, :]), reads=[m.b], writes=[kfi.b])
                yield
                m = proj(nt, W1, 2048, 512)
                P.op("act", mk("copy", out=vfi[0:nt, :], in_=m[0:nt, :]), reads=[m.b], writes=[vfi.b])
                if not isS:
                    P.op("dve", mk("tensor_copy", out=Vv[0:nt, i, :, 0:64],
                                   in_=m[0:nt, :].rearrange("p (h d) -> p h d", d=64)),
                         reads=[m.b], writes=[Vg.b])
                P.dma("ov%d" % (gi % 2), mk("dma_start", out=ovd[i * 128:i * 128 + nt, :], in_=vfi[0:nt, :]),
                      reads=[vfi.b])
                yield
                m = proj(nt, W1, 2560, 296)
                P.op("act", mk("copy", out=g5i[0:nt, :], in_=m[0:nt, 0:296]), reads=[m.b], writes=[g5i.b])
                yield
                m = proj(nt, W1, 2856, 512)
                P.op("act", mk("activation", out=sza[0:nt, :], in_=m[0:nt, :], func=AF.Silu),
                     reads=[m.b], writes=[sza.b])

                yield
                yield
                do_rope(qf, 0, nt, 8, 64, rt, 0, "pool")
                yield
                do_rope(kfi, 0, nt, 8, 64, rt, 0, "pool")
                yield
                do_rope(g5i, 0, nt, 8, 32, rt, 128, "pool")
                do_rope(g5i, 256, nt, 1, 32, rt, 128, "pool")
                P.dma("ok%d" % (gi % 2), mk("dma_start", out=okd[i * 128:i * 128 + nt, :], in_=kfi[0:nt, :]),
                      reads=[kfi.b])
                P.dma("oki%d" % (gi % 2), mk("dma_start", out=okid[i * 128:i * 128 + nt, :], in_=g5i[0:nt, 256:288]),
                      reads=[g5i.b])
                P.op("pool", mk("tensor_copy", out=qb[0:nt, :], in_=qf[0:nt, :]), reads=[qf.b], writes=[qb.b])
                P.op("pool", mk("tensor_copy", out=kb[0:nt, :], in_=kfi[0:nt, :]), reads=[kfi.b], writes=[kb.b])
                for h in range(8):
                    P.op("dve", mk("tensor_scalar", out=Dg[0:nt, h, 0:nt], in0=identf[0:nt, 0:nt],
                                   scalar1=g5i[0:nt, 288 + h:289 + h], scalar2=None, op0=ALU.mult),
                         reads=[identf.b, g5i.b], writes=[Dg.b])
                P.op("pool", mk("tensor_copy", out=qsb[0:nt, :], in_=g5i[0:nt, 0:256]), reads=[g5i.b], writes=[qsb.b])
                P.op("dve", mk("tensor_copy", out=ki3[0:nt, :].rearrange("p (r d) -> p r d", r=3),
                                                    in_=g5i[0:nt, 256:288].unsqueeze(1).broadcast_to([nt, 3, 32])),
                     reads=[g5i.b], writes=[ki3.b])

                yield
                transpose_to(ki3, ki3.b, nt, [(0, 96)],
                             lambda wm, n: kiT[0:96, key0:key0 + nt].unsqueeze(1), kiT.b, "act")
                transpose_to(qsb, qsb.b, nt, [(0, 96), (96, 96), (192, 64)],
                             lambda wm, n: qiT[0:96, 0:3, 0:nt], qiT.b, "dve")
                tvq = transpose_to(qb, qb.b, nt, [(c * 128, 128) for c in range(4)], None, None, "none")
                P.op("act", mk("copy", out=qT[0:64, 0:8:2, 0:nt], in_=tvq[0:64, 0:4, 0:nt]), reads=[tp.b], writes=[qT.b])
                P.op("act", mk("copy", out=qT[64:128, 1:8:2, 0:nt], in_=tvq[64:128, 0:4, 0:nt]), reads=[tp.b], writes=[qT.b])
                if not isS:
                    transpose_to(kb, kb.b, nt, [(c * 128, 128) for c in range(4)],
                                 lambda wm, n: kTv[:, 0:4, key0:key0 + nt], kT.b, "dve")

                yield
                ppv = pp[:].rearrange("p (g t) -> p g t", g=4)
                for g in range(4):
                    if isS:
                        P.op("pe", mk("matmul", out=ppv[:, g, 0:nt], lhsT=ucur[0:nt, g * 128:(g + 1) * 128],
                                                           rhs=band[0:nt, 1, g * 128:g * 128 + nt], start=True, stop=False),
                             reads=[ucur.b, band.b], writes=[pp.b])
                        P.op("pe", mk("matmul", out=ppv[:, g, 0:nt], lhsT=spb[0:15, g * 128:(g + 1) * 128],
                                                           rhs=band[0:15, 3, g * 128:g * 128 + nt], start=False, stop=True),
                             reads=[spb.b, band.b], writes=[pp.b])
                    elif i == 0:
                        P.op("pe", mk("matmul", out=ppv[:, g, 0:nt], lhsT=ucur[0:nt, g * 128:(g + 1) * 128],
                                                           rhs=band[0:nt, 0, g * 128:g * 128 + nt], start=True, stop=True),
                             reads=[ucur.b, band.b], writes=[pp.b])
                    else:
                        P.op("pe", mk("matmul", out=ppv[:, g, 0:nt], lhsT=ucur[0:nt, g * 128:(g + 1) * 128],
                                                           rhs=band[0:nt, 1, g * 128:g * 128 + nt], start=True, stop=False),
                             reads=[ucur.b, band.b], writes=[pp.b])
                        P.op("pe", mk("matmul", out=ppv[:, g, 0:nt], lhsT=uprev[:, g * 128:(g + 1) * 128],
                                                           rhs=band[:, 2, g * 128:g * 128 + nt], start=False, stop=True),
                             reads=[uprev.b, band.b], writes=[pp.b])
                P.op("act", mk("copy", out=plT[:, :, 0:nt], in_=ppv[:, :, 0:nt]), reads=[pp.b], writes=[plT.b])
                m = next_mm()
                for g in range(4):
                    P.op("pe", mk("matmul", out=m[0:nt, g * 128:(g + 1) * 128], lhsT=plT[:, g, 0:nt],
                                                            rhs=Wmix[:, g, :], start=True, stop=True),
                         reads=[plT.b, Wmix.b], writes=[m.b])
                P.op("dve", mk("tensor_tensor", out=gpt[0:nt, :], in0=m[0:nt, :], in1=szp[0:nt, :], op=ALU.mult),
                     reads=[m.b, szp.b], writes=[gpt.b])
                gpo = gpT[gi % 2]
                transpose_to(gpt, gpt.b, nt, [(c * 128, 128) for c in range(4)],
                             lambda wm, n: gpo[:, 0:4, 0:nt], gpo.b, "act")
                tix = seq["tile0"] + i
                P.dma("gp%d" % (gi % 2), mk("dma_start",
                    out=gp_scr[tix].rearrange("p (c t) -> p c t", c=4)[:, :, 0:nt], in_=gpo[:, :, 0:nt]),
                    reads=[gpo.b])

                yield
                ngk = (S + 511) // 512
                for gk in range(ngk):
                    k0 = gk * 512
                    gw = min(512, S - k0)
                    prev = None
                    for h in range(8):
                        if h % 2 == 0:
                            yield
                        m = next_mm()
                        bp = 32 * (h % 3)
                        P.op("pe", mk("matmul", out=
                            m[0:nt, 0:gw], lhsT=qiT[bp:bp + 32, h // 3, 0:nt], rhs=kiT[bp:bp + 32, k0:k0 + gw],
                            start=True, stop=True),
                            reads=[qiT.b, kiT.b], writes=[m.b])
                        rot["rel"] = (rot["rel"] + 1) % 3
                        r = rel[rot["rel"]]
                        P.op("act", mk("activation", out=r[0:nt, 0:gw], in_=m[0:nt, 0:gw], func=AF.Relu),
                             reads=[m.b], writes=[r.b])
                        if prev is not None:
                            ph, prr = prev
                            P.op("pe", mk("matmul", out=pp[0:nt, 0:gw], lhsT=Dg[0:nt, ph, 0:nt], rhs=prr[0:nt, 0:gw],
                                          start=(ph == 0), stop=False),
                                 reads=[Dg.b, prr.b], writes=[pp.b])
                        prev = (h, r)
                    ph, prr = prev
                    P.op("pe", mk("matmul", out=pp[0:nt, 0:gw], lhsT=Dg[0:nt, ph, 0:nt], rhs=prr[0:nt, 0:gw],
                                  start=False, stop=True),
                         reads=[Dg.b, prr.b], writes=[pp.b])
                    P.op("act", mk("copy", out=sview[0:nt, k0:k0 + gw], in_=pp[0:nt, 0:gw]), reads=[pp.b], writes=sbufs)
                return dict(i=i, gi=gi, S=S, ngk=ngk, qT=qT, sza=sza, kfi=kfi, vfi=vfi, tix=tix,
                            sview=sview, sbufs=sbufs)

            def stageA2(c):
                gi, S, sview, sbufs = c["gi"], c["S"], c["sview"], c["sbufs"]
                nch = (S + 127) // 128
                if isS:
                    mview = maskb[:, 0:S]
                    mbufs = [mb_bufs[0], mb_bufs[1]]
                else:
                    mview = maskb[:, (gi % 2) * MH:(gi % 2) * MH + S]
                    mbufs = [mb_bufs[gi % 2]]
                c["nch"], c["mview"], c["mbufs"] = nch, mview, mbufs
                yield
                need_topk = S > Ksel
                if not isS:
                    if need_topk:
                        P.op("dve", mk("tensor_reduce", out=bs_lo[0:nt, :], in_=sview[0:nt, 0:S - 64],
                                       axis=mybir.AxisListType.X, op=ALU.min),
                             reads=sbufs, writes=[bs_lo.b])
                    P.op("dve", mk("memset", ap=sview[0:64, S - 64:S], constant=NEG), writes=sbufs)
                else:
                    if need_topk:
                        P.op("dve", mk("tensor_reduce", out=bs_lo[0:nt, :], in_=sview[0:nt, 0:S],
                                       axis=mybir.AxisListType.X, op=ALU.min),
                             reads=sbufs, writes=[bs_lo.b])
                if need_topk:
                    P.op("dve", mk("tensor_reduce", out=bs_hi[0:nt, :], in_=sview[0:nt, 0:S],
                                   axis=mybir.AxisListType.X, op=ALU.max),
                         reads=sbufs, writes=[bs_hi.b])
                    P.op("dve", mk("tensor_tensor", out=bs_hi[0:nt, :], in0=bs_hi[0:nt, :], in1=bs_lo[0:nt, :],
                                   op=ALU.subtract),
                         reads=[bs_hi.b, bs_lo.b], writes=[bs_hi.b])
                    P.op("dve", mk("tensor_scalar", out=bs_ht[0:nt, :], in0=pw[0:nt, :], scalar1=bs_hi[0:nt, 0:1],
                                   scalar2=None, op0=ALU.mult),
                         reads=[pw.b, bs_hi.b], writes=[bs_ht.b])
                    P.op("dve", mk("tensor_tensor", out=bs_mid[0][0:nt, :], in0=bs_lo[0:nt, :], in1=bs_ht[0:nt, 0:1],
                                   op=ALU.add),
                         reads=[bs_lo.b, bs_ht.b], writes=[bs_mid[0].b])
                    for it in range(NI_BISECT):
                        yield
                        mc = bs_mid[it % 2]
                        mn = bs_mid[(it + 1) % 2]
                        P.op("dve", mk("tensor_scalar", out=mview[0:nt, 0:S], in0=sview[0:nt, 0:S],
                                       scalar1=mc[0:nt, 0:1], scalar2=None, op0=ALU.is_ge, op1=ALU.add,
                                       accum_out=bs_cnt[0:nt, :]),
                             reads=sbufs + [mc.b], writes=mbufs + [bs_cnt.b])
                        P.op("dve", mk("tensor_scalar", out=bs_u[0:nt, :], in0=bs_cnt[0:nt, :],
                                       scalar1=float(Ksel) - 0.5, scalar2=0.5, op0=ALU.is_ge, op1=ALU.subtract),
                             reads=[bs_cnt.b], writes=[bs_u.b])
                        P.op("dve", mk("scalar_tensor_tensor", out=mn[0:nt, :], in0=bs_u[0:nt, :],
                                       scalar=bs_ht[0:nt, it:it + 1], in1=mc[0:nt, :], op0=ALU.mult, op1=ALU.add),
                             reads=[bs_u.b, bs_ht.b, mc.b], writes=[mn.b])
                    mfin = bs_mid[NI_BISECT % 2]
                    P.op("dve", mk("scalar_tensor_tensor", out=bs_thr[0:nt, :],
                                   in0=bs_ht[0:nt, NI_BISECT - 1:NI_BISECT], scalar=-0.5, in1=mfin[0:nt, :],
                                   op0=ALU.mult, op1=ALU.add),
                         reads=[bs_ht.b, mfin.b], writes=[bs_thr.b])
                else:
                    P.op("dve", mk("memset", ap=bs_thr[0:nt, :], constant=-1.0e29), writes=[bs_thr.b])
                yield
                P.op("dve", mk("tensor_scalar", out=mview[0:nt, :], in0=sview[0:nt, 0:S], scalar1=bs_thr[0:nt, 0:1],
                               scalar2=None, op0=ALU.is_ge),
                     reads=sbufs + [bs_thr.b], writes=mbufs)

            def stageB1(c):
                i, gi, S, nch, ngk, qT, mview, mbufs, vfi = (c["i"], c["gi"], c["S"], c["nch"], c["ngk"], c["qT"],
                                                             c["mview"], c["mbufs"], c["vfi"])
                mTv = maskT[:, 0:nch * nt].rearrange("p (c t) -> p c t", t=nt)
                for gk in range(ngk):
                    yield
                    k0 = gk * 512
                    gw = min(512, S - k0)
                    blocks = []
                    cc_ = 0
                    while cc_ * 128 < gw:
                        blocks.append((k0 + cc_ * 128, min(128, gw - cc_ * 128)))
                        cc_ += 1
                    full = [b for b in blocks if b[1] == 128]
                    part = [b for b in blocks if b[1] < 128]
                    if full:
                        transpose_to(mview, mbufs[0], nt, full,
                                     lambda wm, n, k0=k0: mTv[:, k0 // 128:k0 // 128 + n, :], maskT.b, "actbias")
                    if part:
                        pc0, pw_ = part[0]
                        transpose_to(mview, mbufs[-1], nt, [(pc0, pw_)],
                                     lambda wm, n, pc0=pc0: mTv[0:wm, pc0 // 128:pc0 // 128 + 1, :],
                                     maskT.b, "actbias")
                halves = [(0, 8)] if not isS else [(0, 4), (4, 8)]
                for (h0, h1) in halves:
                    if isS:
                        hh = h0 // 4
                        nct = L0 // 128
                        for c4 in range(0, nct, 2):
                            n4 = min(2, nct - c4)
                            stg = cst[(c4 // 2) % 2]
                            P.dma("cst", mk("dma_start",
                                out=stg[:, 0:n4, :],
                                in_=ck[l].rearrange("(t p) f -> p t f", p=128)[:, c4:c4 + n4, hh * 256:(hh + 1) * 256]),
                                writes=[stg.b])
                            P.op("pool", mk("tensor_copy", out=cbf[:, 0:n4, :], in_=stg[:, 0:n4, :]),
                                 reads=[stg.b], writes=[cbf.b])
                            for j in range(n4):
                                c = c4 + j
                                transpose_to(cbf[:, j, :], cbf.b, 128, [(0, 128), (128, 128)],
                                             lambda wm, n, c=c: kTv[:, 0:2, c * 128:(c + 1) * 128], kT.b,
                                             "act" if j % 2 else "dve")
                            stg2 = cst[(c4 // 2 + 1) % 2]
                            P.dma("cst", mk("dma_start",
                                out=stg2[:, 0:n4, :],
                                in_=cv[l].rearrange("(t p) f -> p t f", p=128)[:, c4:c4 + n4, hh * 256:(hh + 1) * 256]),
                                writes=[stg2.b])
                            P.op("pool", mk("tensor_copy",
                                out=Vv[:, c4:c4 + n4, :, 0:64],
                                in_=stg2[:, 0:n4, :].rearrange("p t (h d) -> p t h d", d=64)),
                                reads=[stg2.b], writes=[Vg.b])
                        transpose_to(kb[:, hh * 256:(hh + 1) * 256], kb.b, nt, [(0, 128), (128, 128)],
                                     lambda wm, n: kTv[:, 0:2, L0:L0 + nt], kT.b, "act")
                        P.op("pool", mk("tensor_copy",
                            out=Vv[0:nt, NCS - 1, :, 0:64],
                            in_=vfi[0:nt, hh * 256:(hh + 1) * 256].rearrange("p (h d) -> p h d", d=64)),
                            reads=[vfi.b], writes=[Vg.b])
                    chunks = [(c, min(128, S - c * 128)) for c in range(nch)]
                    groups = []
                    cur = []
                    for cpair in chunks:
                        if cur and (len(cur) == 4 or cur[-1][1] != cpair[1]):
                            groups.append(cur)
                            cur = []
                        cur.append(cpair)
                    if cur:
                        groups.append(cur)
                    items = [(h, gidx) for h in range(h0, h1) for gidx in range(len(groups))]

                    def emit_L(h, gidx):
                        hl = h - h0
                        pr = hl // 2
                        grp = groups[gidx]
                        rc = grp[0][1]
                        ng = len(grp)
                        Lt = nxt("L", Lb)
                        Lv = Lt[:].rearrange("p (c t) -> p c t", t=128)
                        c0g = grp[0][0]
                        if nt == 128:
                            P.op("pe", mk("matmul", out=Lt[0:rc, 0:ng * 128], lhsT=ident[0:rc, 0:rc],
                                          rhs=maskT[0:rc, c0g * 128:(c0g + ng) * 128], start=True, stop=False),
                                 reads=[ident.b, maskT.b], writes=[Lt.b])
                        for j, (c, _) in enumerate(grp):
                            if nt != 128:
                                P.op("pe", mk("matmul", out=Lv[0:rc, j, 0:nt], lhsT=ident[0:rc, 0:rc],
                                              rhs=mTv[0:rc, c, :], start=True, stop=False),
                                     reads=[ident.b, maskT.b], writes=[Lt.b])
                            P.op("pe", mk("matmul", out=Lv[0:rc, j, 0:nt], lhsT=kTv[:, pr, c * 128:c * 128 + rc],
                                          rhs=qT[:, h, 0:nt], start=False, stop=(j == ng - 1 or nt != 128)),
                                 reads=[kT.b, qT.b], writes=[Lt.b])
                        rot["praw"] = (rot["praw"] + 1) % len(praw)
                        pr_t = praw[rot["praw"]]
                        prv = pr_t[:].rearrange("p (c t) -> p c t", t=128)
                        P.op("act", mk("activation", out=prv[0:rc, 0:ng, 0:nt], in_=Lv[0:rc, 0:ng, 0:nt],
                                       func=AF.Exp, scale=0.125),
                             reads=[Lt.b], writes=[pr_t.b])
                        return (h, gidx, pr_t, prv)

                    def emit_PV(h, gidx, pr_t, prv):
                        hl = h - h0
                        O = Ob[h // 4]
                        grp = groups[gidx]
                        rc = grp[0][1]
                        ng = len(grp)
                        for j, (c, _) in enumerate(grp):
                            first = (gidx == 0 and j == 0)
                            lastc = (gidx == len(groups) - 1 and j == ng - 1)
                            P.op("pe", mk("matmul", out=O[0:nt, h % 4, :], lhsT=prv[0:rc, j, 0:nt],
                                          rhs=Vv[0:rc, c, hl if isS else h, :], start=first, stop=lastc),
                                 reads=[pr_t.b, Vg.b], writes=[O.b])

                    pend = None
                    for (h, gidx) in items:
                        yield
                        curL = emit_L(h, gidx)
                        if pend is not None:
                            emit_PV(*pend)
                        pend = curL
                    if pend is not None:
                        emit_PV(*pend)

            def stageB2(c):
                gi, sza, tix = c["gi"], c["sza"], c["tix"]
                for ob in range(2):
                    O = Ob[ob]
                    P.op("dve", mk("reciprocal", out=rs[0:nt, ob * 4:ob * 4 + 4], in_=O[0:nt, :, 64]),
                         reads=[O.b], writes=[rs.b])
                    P.op("dve", mk("tensor_tensor",
                                   out=on[0:nt, ob * 256:(ob + 1) * 256].rearrange("p (h d) -> p h d", d=64),
                                   in0=O[0:nt, :, 0:64],
                                   in1=rs[0:nt, ob * 4:ob * 4 + 4].unsqueeze(2).broadcast_to([nt, 4, 64]),
                                   op=ALU.mult),
                         reads=[O.b, rs.b], writes=[on.b])
                P.op("pool", mk("tensor_tensor", out=gat[0:nt, :], in0=on[0:nt, :], in1=sza[0:nt, :], op=ALU.mult),
                     reads=[on.b, sza.b], writes=[gat.b])
                gao = gaT[gi % 2]
                transpose_to(gat, gat.b, nt, [(c * 128, 128) for c in range(4)],
                             lambda wm, n: gao[:, 0:4, 0:nt], gao.b, "act")
                P.dma("ga%d" % (gi % 2), mk("dma_start",
                    out=ga_scr[tix].rearrange("p (c t) -> p c t", c=4)[:, :, 0:nt], in_=gao[:, :, 0:nt]),
                    reads=[gao.b])

            load_x(seq, 0, gtile[0] % 2, xsrc)

            def drive(gens):
                res = [None] * len(gens)
                live = [g is not None for g in gens]
                while any(live):
                    for k, g in enumerate(gens):
                        if live[k]:
                            try:
                                next(g)
                            except StopIteration as ex:
                                res[k] = ex.value
                                live[k] = False
                return res

            ctx = {}
            for k in range(ntl + 2):
                gA1 = stageA1(k) if k < ntl else None
                gA2 = stageA2(ctx[k - 1]) if 0 <= k - 1 < ntl else None
                gB1 = stageB1(ctx[k - 2]) if 0 <= k - 2 < ntl else None
                r = drive([gA1, gA2, gB1])
                if gA1 is not None:
                    ctx[k] = r[0]
                if gB1 is not None:
                    stageB2(ctx[k - 2])
        P.barrier()
        st1.close()

        st2 = ExitStack()
        W2 = alloc(st2, "W2", [128, 8, C2], BF16)
        Wpo = alloc(st2, "Wpo", [128, 4, D], BF16)
        Wao = alloc(st2, "Wao", [128, 4, D], BF16)
        Wo = alloc(st2, "Wo", [128, 8, D], BF16)
        with ExitStack() as stl:
            stage = [alloc(stl, "stg%d" % i, [128, C2], F32) for i in range(2)]
            load_cast(lambda k: W2[:, k, :], lambda k: w_in[l, k * 128:(k + 1) * 128, C1:NCOL], 8, C2, stage, True, W2.b)
            load_cast(lambda k: Wpo[:, k, :], lambda k: w_po[l, k * 128:(k + 1) * 128, :], 4, D, stage, False, Wpo.b)
            load_cast(lambda k: Wao[:, k, :], lambda k: w_ao[l, k * 128:(k + 1) * 128, :], 4, D, stage, False, Wao.b)
            load_cast(lambda k: Wo[:, k, :], lambda k: w_o[l, k * 128:(k + 1) * 128, :], 8, D, stage, False, Wo.b)
            P.barrier()
        gpl = [alloc(st2, "gpl%d" % i, [128, 4, 128], BF16) for i in range(3)]
        gal = [alloc(st2, "gal%d" % i, [128, 4, 128], BF16) for i in range(3)]
        sgps = [alloc(st2, "sgp%d" % i, [128, D], BF16) for i in range(2)]
        sgas = [alloc(st2, "sga%d" % i, [128, D], BF16) for i in range(2)]
        xt.append(alloc(st2, "xt2", [128, D], F32))
        m1 = alloc(st2, "m1", [128, D], F32)
        t2 = alloc(st2, "t2", [128, 512], F32)
        mrg = alloc(st2, "mrg", [128, D], BF16)
        mT = alloc(st2, "mT", [128, 8, 128], BF16)
        yt = [alloc(st2, "yt%d" % i, [128, D], F32) for i in range(2)]

        prefetch = []
        if l + 1 < DEPTH:
            pst = [alloc(st2, "pst%d" % i, [128, C1], F32) for i in range(2)]
            load_gcol(l + 1)

            def mk_chunk(k):
                def emit():
                    sgt = pst[k % 2]
                    P.dma("pst", mk("dma_start", out=sgt[:, 0:C1], in_=w_in[l + 1, k * 128:(k + 1) * 128, 0:C1]),
                          writes=[sgt.b])
                    if k % 2 == 0:
                        P.op("dve", mk("tensor_scalar", out=W1[:, k, :], in0=sgt[:, 0:C1], scalar1=gcol[:, k:k + 1],
                                       scalar2=None, op0=ALU.mult), reads=[sgt.b, gcol.b], writes=[W1.b])
                    else:
                        P.op("act", mk("activation", out=W1[:, k, :], in_=sgt[:, 0:C1], func=AF.Copy,
                                       scale=gcol[:, k:k + 1]), reads=[sgt.b, gcol.b], writes=[W1.b])
                return emit
            prefetch = [mk_chunk(k) for k in range(8)]
        gtile2 = [0]
        for seq in seqs:
            nt = seq["nt"]
            ntl = seq["ntiles"]
            xsrc = x_src_of(seq)

            def loads2(i, gi):
                load_x(seq, i, gi % 3, xsrc)
                tix = seq["tile0"] + i
                P.dma("gpl%d" % (gi % 3), mk("dma_start",
                    out=gpl[gi % 3][:, :, 0:nt], in_=gp_scr[tix].rearrange("p (c t) -> p c t", c=4)[:, :, 0:nt]),
                    writes=[gpl[gi % 3].b])
                P.dma("gal%d" % (gi % 3), mk("dma_start",
                    out=gal[gi % 3][:, :, 0:nt], in_=ga_scr[tix].rearrange("p (c t) -> p c t", c=4)[:, :, 0:nt]),
                    writes=[gal[gi % 3].b])

            def stage2A(i):
                gi = gtile2[0]
                gtile2[0] += 1
                if prefetch and gi % 3 == 1:
                    prefetch.pop(0)()
                slot = gi % 3
                sgp = sgps[gi % 2]
                sga = sgas[gi % 2]
                if i + 1 < ntl:
                    loads2(i + 1, gi + 1)
                norm_hT(seq, slot)
                for hh in range(2):
                    m = proj(nt, W2, hh * 512, 512)
                    P.op("act", mk("activation", out=sgp[0:nt, hh * 512:(hh + 1) * 512], in_=m[0:nt, :],
                                                                   func=AF.Sigmoid),
                         reads=[m.b], writes=[sgp.b])
                for hh in range(2):
                    m = proj(nt, W2, 1024 + hh * 512, 512)
                    P.op("act", mk("activation", out=sga[0:nt, hh * 512:(hh + 1) * 512], in_=m[0:nt, :],
                                                                   func=AF.Sigmoid),
                         reads=[m.b], writes=[sga.b])
                return dict(i=i, gi=gi, slot=slot, sgp=sgp, sga=sga)

            def stage2B(c):
                i, gi, slot, sgp, sga = c["i"], c["gi"], c["slot"], c["sgp"], c["sga"]
                x = xt[slot]
                gp_, ga_ = gpl[slot], gal[slot]
                for hh in range(2):
                    m = next_mm()
                    for k in range(4):
                        P.op("pe", mk("matmul", out=m[0:nt, :], lhsT=gp_[:, k, 0:nt],
                                                                       rhs=Wpo[:, k, hh * 512:(hh + 1) * 512],
                                                                       start=(k == 0), stop=(k == 3)),
                             reads=[gp_.b, Wpo.b], writes=[m.b])
                    P.op("dve", mk("tensor_tensor", out=m1[0:nt, hh * 512:(hh + 1) * 512], in0=m[0:nt, :],
                                                                      in1=sgp[0:nt, hh * 512:(hh + 1) * 512], op=ALU.mult),
                         reads=[m.b, sgp.b], writes=[m1.b])
                for hh in range(2):
                    m = next_mm()
                    for k in range(4):
                        P.op("pe", mk("matmul", out=m[0:nt, :], lhsT=ga_[:, k, 0:nt],
                                                                       rhs=Wao[:, k, hh * 512:(hh + 1) * 512],
                                                                       start=(k == 0), stop=(k == 3)),
                             reads=[ga_.b, Wao.b], writes=[m.b])
                    P.op("dve", mk("tensor_tensor", out=t2[0:nt, :], in0=m[0:nt, :],
                                                                      in1=sga[0:nt, hh * 512:(hh + 1) * 512], op=ALU.mult),
                         reads=[m.b, sga.b], writes=[t2.b])
                    P.op("pool", mk("tensor_tensor", out=mrg[0:nt, hh * 512:(hh + 1) * 512],
                                                                  in0=m1[0:nt, hh * 512:(hh + 1) * 512], in1=t2[0:nt, :],
                                                                  op=ALU.add),
                         reads=[m1.b, t2.b], writes=[mrg.b])
                transpose_to(mrg, mrg.b, nt, [(c * 128, 128) for c in range(8)],
                             lambda wm, n: mT[:, 0:8, 0:nt], mT.b, "act")
                for hh in range(2):
                    m = next_mm()
                    for k in range(8):
                        P.op("pe", mk("matmul", out=m[0:nt, :], lhsT=mT[:, k, 0:nt],
                                                                       rhs=Wo[:, k, hh * 512:(hh + 1) * 512],
                                                                       start=(k == 0), stop=(k == 7)),
                             reads=[mT.b, Wo.b], writes=[m.b])
                    P.op("dve", mk("tensor_tensor", out=x[0:nt, hh * 512:(hh + 1) * 512],
                                                                      in0=x[0:nt, hh * 512:(hh + 1) * 512], in1=m[0:nt, :],
                                                                      op=ALU.add),
                         reads=[m.b, x.b], writes=[x.b])
                r0 = seq["tok0"] + i * 128
                if not last:
                    P.dma("x%d" % slot, mk("dma_start", out=xscr[r0:r0 + nt, :], in_=x[0:nt, :]), reads=[x.b])
                else:
                    y = yt[gi % 2]
                    P.op("act", mk("activation", out=xn[0:nt, 0:D], in_=x[0:nt, :], func=AF.Square,
                                                       accum_out=ssq[0:nt, :]),
                         reads=[x.b], writes=[xn.b, ssq.b])
                    P.op("dve", mk("tensor_scalar", out=rstd[0:nt, :], in0=ssq[0:nt, :], scalar1=1.0 / D, scalar2=EPS,
                                                          op0=ALU.mult, op1=ALU.add),
                         reads=[ssq.b], writes=[rstd.b])
                    P.op("act", mk("activation", out=rstd[0:nt, :], in_=rstd[0:nt, :], func=AF.Sqrt),
                         reads=[rstd.b], writes=[rstd.b])
                    P.op("dve", mk("reciprocal", out=rstd[0:nt, :], in_=rstd[0:nt, :]),
                         reads=[rstd.b], writes=[rstd.b])
                    P.op("dve", mk("scalar_tensor_tensor", out=y[0:nt, :], in0=x[0:nt, :], scalar=rstd[0:nt, 0:1],
                                                                      in1=gfbc[0:nt, :], op0=ALU.mult, op1=ALU.mult),
                         reads=[x.b, rstd.b, gfbc.b], writes=[y.b])
                    yo = seq["y_out"]
                    P.dma("y%d" % slot, mk("dma_start", out=yo[i * 128:i * 128 + nt, :], in_=y[0:nt, :]),
                          reads=[y.b])
            loads2(0, gtile2[0])
            pend = None
            for i in range(ntl):
                cA = stage2A(i)
                if pend is not None:
                    stage2B(pend)
                pend = cA
            stage2B(pend)
        while prefetch:
            prefetch.pop(0)()
        P.barrier()
        xt.pop()
        st2.close()

    P.final_wait()
    P.emit()
    stack0.close()
    return nc


_CACHE = {}


def kernel(x_prompt, x_sample, cache_k, cache_v, cache_kidx, state_pool, norm_g, w_in, w_pool_mix,
           pool_scale, w_pool_out, w_attn_out, w_o, final_norm_g):
    NCORES = 8
    f = lambda a: np.ascontiguousarray(np.asarray(a, dtype=np.float32))
    x_prompt, x_sample = f(x_prompt), f(x_sample)
    cache_k, cache_v, cache_kidx, state_pool = f(cache_k), f(cache_v), f(cache_kidx), f(state_pool)
    BP, T, _ = x_prompt.shape
    BS, TS, _ = x_sample.shape
    DEPTH = cache_k.shape[0]
    L0 = cache_k.shape[2]
    assert BP % NCORES == 0 and BS == NCORES
    NP = BP // NCORES
    cfg = dict(NP=NP, T=T, TS=TS, L0=L0, DEPTH=DEPTH, KP=min(256, T // 4), KS=min(256, (L0 + TS) // 4))
    key = tuple(sorted(cfg.items()))
    if key not in _CACHE:
        _CACHE[key] = build(cfg)
    nc = _CACHE[key]

    rope = _rope_table(list(range(T)) + list(range(L0, L0 + TS)))
    bands = _band_tables().reshape(4, 128, 512)
    pw2 = np.tile((2.0 ** -(np.arange(NI_BISECT, dtype=np.float64) + 1)).astype(np.float32)[None, :], (128, 1))
    shared = dict(norm_g=f(norm_g), w_in=f(w_in), w_mix=f(w_pool_mix), pscale=f(pool_scale), w_po=f(w_pool_out),
                  w_ao=f(w_attn_out), w_o=f(w_o), gfin=f(final_norm_g).reshape(1, D), rope=rope, bands=bands, pw2=pw2)
    in_maps = []
    for c in range(NCORES):
        m = dict(shared)
        m["x_p"] = x_prompt[c * NP:(c + 1) * NP]
        m["x_s"] = x_sample[c]
        m["ck"] = cache_k[:, c].reshape(DEPTH, L0, 512)
        m["cv"] = cache_v[:, c].reshape(DEPTH, L0, 512)
        m["cki"] = cache_kidx[:, c]
        m["spool"] = state_pool[:, c]
        in_maps.append(m)
    res = run_bass_kernel_spmd(nc, in_maps, core_ids=list(range(NCORES)))
    R = res.results
    cat = lambda name, ax: np.concatenate([np.asarray(r[name]) for r in R], axis=ax)
    stk = lambda name, ax: np.stack([np.asarray(r[name]) for r in R], axis=ax)
    y_prompt = cat("y_p", 0)
    y_sample = stk("y_s", 0)
    nk_p = cat("ok_p", 1).reshape(DEPTH, BP, T, 8, 64)
    nv_p = cat("ov_p", 1).reshape(DEPTH, BP, T, 8, 64)
    nki_p = cat("oki_p", 1)
    npl_p = cat("opl_p", 1)
    nk_s = stk("ok_s", 1).reshape(DEPTH, BS, TS, 8, 64)
    nv_s = stk("ov_s", 1).reshape(DEPTH, BS, TS, 8, 64)
    nki_s = stk("oki_s", 1)
    npl_s = stk("opl_s", 1)
    return (y_prompt, y_sample, nk_p, nv_p, nki_p, npl_p, nk_s, nv_s, nki_s, npl_s)
```

```python
from contextlib import ExitStack
import os
import numpy as np
import concourse.bass as bass
import concourse.mybir as mybir
from concourse.bass_utils import run_bass_kernel_spmd

F32 = mybir.dt.float32
BF16 = mybir.dt.bfloat16
ALU = mybir.AluOpType
AF = mybir.ActivationFunctionType

ENGS = ("pe", "act", "dve", "pool", "sp")

D = 1024
NCOL = 5416
C1 = 3368
C2 = NCOL - C1
NH = 8
NEG = -1.0e30
NI_BISECT = int(os.environ.get('KNI', '24'))
EPS = 1e-6


class Buf:
    __slots__ = ("name", "w", "r", "excl")

    def __init__(self, name):
        self.name = name
        self.excl = False
        self.w = None
        self.r = []


class Prog:
    def __init__(self, nc):
        self.nc = nc
        self.q = {e: [] for e in ENGS}
        self.tick = {e: 0 for e in ENGS}
        self.sem = {e: nc.alloc_semaphore("sem_" + e) for e in ENGS}
        self.seen = {e: {} for e in ENGS}
        self.dsem = {}
        self.dtick = {}
        self.nbuf = 0
        self.count = 0
        self.limit = int(os.environ.get("KLIMIT", "1000000000"))

    def buf(self, name=None):
        self.nbuf += 1
        return Buf(name or "b%d" % self.nbuf)

    def _need(self, reads, writes):
        need = {}
        for b in reads:
            if b.w is not None:
                s, t = b.w
                if need.get(s, 0) < t:
                    need[s] = t
        for b in writes:
            if b.w is not None:
                s, t = b.w
                if need.get(s, 0) < t:
                    need[s] = t
            for s, t in b.r:
                if need.get(s, 0) < t:
                    need[s] = t
        return need

    def _waits(self, eng, need):
        waits = []
        seen = self.seen[eng]
        for s, t in need.items():
            if s == eng and eng in ("pe", "sp"):
                continue
            if seen.get(s, 0) >= t:
                continue
            seen[s] = t
            waits.append((s, t))
        return waits

    def _mark(self, src, reads, writes):
        for b in reads:
            if len(b.r) > 24:
                d = {}
                for s, t in b.r:
                    if d.get(s, 0) < t:
                        d[s] = t
                b.r = list(d.items())
            b.r.append(src)
        for b in writes:
            b.w = src
            b.r = []

    def op(self, eng, fn, reads=(), writes=()):
        self.count += 1
        if self.count > self.limit:
            return
        xr = [b for b in reads if b.excl]
        if xr:
            writes = list(writes) + [b for b in xr if b not in writes]
        waits = self._waits(eng, self._need(reads, writes))
        self.tick[eng] += 1
        my = self.tick[eng]
        self.q[eng].append((waits, fn, ("e", eng)))
        self._mark((eng, my), reads, writes)

    def dma(self, stream, fn, reads=(), writes=(), eng="sp"):
        stream = (writes[0] if writes else reads[0]).name
        self.count += 1
        if self.count > self.limit:
            return
        if stream not in self.dsem:
            self.dsem[stream] = self.nc.alloc_semaphore("dsem_" + stream)
            self.dtick[stream] = 0
        waits = self._waits(eng, self._need(reads, writes))
        self.dtick[stream] += 16
        my = self.dtick[stream]
        self.q[eng].append((waits, fn, ("d", stream)))
        self._mark(("dma:" + stream, my), reads, writes)

    def _semof(self, s):
        if s.startswith("dma:"):
            return self.dsem[s[4:]]
        return self.sem[s]

    def barrier(self):
        need = {}
        for e in ENGS:
            if self.tick[e] > 0:
                need[e] = self.tick[e]
        for s, t in self.dtick.items():
            if t > 0:
                need["dma:" + s] = t
        for e in ENGS:
            waits = self._waits(e, dict(need))
            if waits:
                self.q[e].append((waits, None, None))

    def final_wait(self, eng="sp"):
        need = {}
        for s, t in self.dtick.items():
            if t > 0:
                need["dma:" + s] = t
        for e in ENGS:
            if e != eng and self.tick[e] > 0:
                need[e] = self.tick[e]
        waits = self._waits(eng, need)
        if waits:
            self.q[eng].append((waits, None, None))

    def emit(self):
        nc = self.nc
        engobj = {"pe": "tensor", "act": "scalar", "dve": "vector", "pool": "gpsimd", "sp": "sync"}
        with nc.Block() as block:
            for e in ENGS:
                items = self.q[e]
                if not items:
                    continue

                def body(eo, items=items):
                    for waits, fn, inc in items:
                        for s, t in waits:
                            eo.wait_ge(self._semof(s), t)
                        if fn is None:
                            continue
                        ins = fn(eo)
                        if inc[0] == "e":
                            ins.then_inc(self.sem[inc[1]], 1)
                        else:
                            ins.then_inc(self.dsem[inc[1]], 16)

                getattr(block, engobj[e])(body)


def mk(meth, **kw):
    return lambda e: getattr(e, meth)(**kw)


class Tl:
    def __init__(self, t, b):
        self.t = t
        self.b = b

    def __getitem__(self, k):
        return self.t[k]


def _rope_table(positions):
    pos = np.asarray(positions, dtype=np.float32)
    out = np.zeros((len(pos), 192), np.float32)
    for (d, off) in ((64, 0), (32, 128)):
        inv = (10000.0 ** (-np.arange(0, d, 2, dtype=np.float32) / np.float32(d))).astype(np.float32)
        ang = (pos[:, None] * inv[None, :]).astype(np.float32).astype(np.float64)
        c = np.cos(ang).astype(np.float32)
        s = np.sin(ang).astype(np.float32)
        h = d // 2
        out[:, off:off + h] = c
        out[:, off + h:off + 2 * h] = c
        out[:, off + 2 * h:off + 3 * h] = -s
        out[:, off + 3 * h:off + 4 * h] = s
    return out


def _band_tables():
    B = np.zeros((4, 128, 4, 128), np.float32)
    for g, w in enumerate((2, 4, 8, 16)):
        for t in range(128):
            for tp in range(max(0, t - w + 1), t + 1):
                B[0, tp, g, t] += 1.0 / min(t + 1, w)
                B[1, tp, g, t] += 1.0 / w
            B[0, t, g, t] -= 1.0
            B[1, t, g, t] -= 1.0
            for tp in range(128):
                if t - (tp - 128) < w:
                    B[2, tp, g, t] = 1.0 / w
            for j in range(15):
                if t - (j - 15) < w:
                    B[3, j, g, t] = 1.0 / w
    return B


def build(cfg):
    NP, T, TS, L0, DEPTH = cfg["NP"], cfg["T"], cfg["TS"], cfg["L0"], cfg["DEPTH"]
    KP, KS = cfg["KP"], cfg["KS"]
    assert T % 128 == 0 and TS == 64 and L0 % 128 == 0
    NTP = T // 128
    LS = L0 + TS
    NCS = L0 // 128 + 1
    SMAX = max(T, LS)
    KTW = max(4 * T, 2 * LS)
    VS = int(os.environ.get('KVS', '65'))
    VW = max(NTP * 8 * VS, NCS * 4 * VS)
    NTOK = NP * T + TS
    NTILES = NP * NTP + 1

    nc = bass.Bass("TRN2", target_bir_lowering=False, dynamic_dma_scratch_size=256)
    P = Prog(nc)

    def din(name, shape, dt=F32):
        return nc.dram_tensor(name, list(shape), dt, kind="ExternalInput").ap()

    def dout(name, shape, dt=F32):
        return nc.dram_tensor(name, list(shape), dt, kind="ExternalOutput").ap()

    x_p = din("x_p", [NP, T, D])
    x_s = din("x_s", [TS, D])
    ck = din("ck", [DEPTH, L0, 512])
    cv = din("cv", [DEPTH, L0, 512])
    cki = din("cki", [DEPTH, L0, 32])
    spool = din("spool", [DEPTH, 15, 512])
    norm_g = din("norm_g", [DEPTH, D])
    w_in = din("w_in", [DEPTH, D, NCOL])
    w_mix = din("w_mix", [DEPTH, 4, 128, 128])
    pscale = din("pscale", [DEPTH, 512])
    w_po = din("w_po", [DEPTH, 512, D])
    w_ao = din("w_ao", [DEPTH, 512, D])
    w_o = din("w_o", [DEPTH, D, D])
    gfin = din("gfin", [1, D])
    rope = din("rope", [T + TS, 192])
    bands = din("bands", [4, 128, 512])
    pw2 = din("pw2", [128, NI_BISECT])

    y_p = dout("y_p", [NP, T, D])
    y_s = dout("y_s", [TS, D])
    ok_p = dout("ok_p", [DEPTH, NP, T, 512])
    ov_p = dout("ov_p", [DEPTH, NP, T, 512])
    oki_p = dout("oki_p", [DEPTH, NP, T, 32])
    opl_p = dout("opl_p", [DEPTH, NP, 15, 512])
    ok_s = dout("ok_s", [DEPTH, TS, 512])
    ov_s = dout("ov_s", [DEPTH, TS, 512])
    oki_s = dout("oki_s", [DEPTH, TS, 32])
    opl_s = dout("opl_s", [DEPTH, 15, 512])

    xscr = nc.dram_tensor("xscr", [NTOK, D], F32).ap()
    gp_scr = nc.dram_tensor("gp_scr", [NTILES, 128, 512], BF16).ap()
    ga_scr = nc.dram_tensor("ga_scr", [NTILES, 128, 512], BF16).ap()

    seqs = []
    for s in range(NP):
        seqs.append(dict(kind="p", idx=s, T=T, nt=128, ntiles=NTP, tok0=s * T, tile0=s * NTP,
                         x_in=x_p[s], y_out=y_p[s], K=KP, rope0=0))
    seqs.append(dict(kind="s", idx=0, T=TS, nt=TS, ntiles=1, tok0=NP * T, tile0=NP * NTP,
                     x_in=x_s, y_out=y_s, K=KS, rope0=T))

    stack0 = ExitStack()

    uniq = [0]

    def alloc(st, name, shape, dt):
        uniq[0] += 1
        t = st.enter_context(nc.sbuf_tensor("%s_%d" % (name, uniq[0]), list(shape), dt))
        return Tl(t, P.buf(name))

    def palloc(name, shape, dt):
        t = nc.alloc_psum_tensor(name, list(shape), dt)
        b = P.buf(name)
        b.excl = True
        return Tl(t, b)

    mm = [palloc("mm%d" % i, [128, 512], F32) for i in range(2)]
    tp = palloc("tp", [128, 1024], BF16)
    pp = palloc("pp", [128, 512], F32)
    Lb = [palloc("L%d" % i, [128, 512], F32) for i in range(2)]
    Ob = [palloc("O%d" % i, [128, 4, VS], F32) for i in range(2)]
    mmi = [0]

    def next_mm():
        mmi[0] ^= 1
        return mm[mmi[0]]

    ident = alloc(stack0, "ident", [128, 128], BF16)
    identf = alloc(stack0, "identf", [128, 128], F32)
    ones1 = alloc(stack0, "ones1", [1, 128], F32)
    gfbc = alloc(stack0, "gfbc", [128, D], F32)
    band = alloc(stack0, "band", [128, 4, 512], BF16)
    pw = alloc(stack0, "pw", [128, NI_BISECT], F32)
    xt = [alloc(stack0, "xt%d" % i, [128, D], F32) for i in range(2)]
    xn = alloc(stack0, "xn", [128, D], BF16)
    hT = alloc(stack0, "hT", [128, 8, 128], BF16)
    ssq = alloc(stack0, "ssq", [128, 1], F32)
    rstd = alloc(stack0, "rstd", [128, 1], F32)
    gcol = alloc(stack0, "gcol", [128, 8], F32)
    W1 = alloc(stack0, "W1", [128, 8, C1], BF16)
    ftmp = [alloc(stack0, "ftmp%d" % i, [128, 512], F32) for i in range(2)]
    ftmpi = [0]

    def next_ftmp():
        ftmpi[0] ^= 1
        return ftmp[ftmpi[0]]

    P.op("pool", mk("memset", ap=identf[:], constant=0.0), writes=[identf.b])
    P.op("pool", mk("affine_select", out=identf[:], in_=identf[:], pattern=[[-1, 128]],
                                           compare_op=ALU.not_equal, fill=1.0, base=0,
                                           channel_multiplier=1),
         reads=[identf.b], writes=[identf.b])
    P.op("dve", mk("tensor_copy", out=ident[:], in_=identf[:]), reads=[identf.b], writes=[ident.b])
    P.op("dve", mk("memset", ap=ones1[:], constant=1.0), writes=[ones1.b])
    P.dma("c0", mk("dma_start", out=pw[:], in_=pw2), writes=[pw.b])
    for kind in range(4):
        f = next_ftmp()
        P.dma("c1", mk("dma_start", out=f[:], in_=bands[kind]), writes=[f.b])
        P.op("dve", mk("tensor_copy", out=band[:, kind, :], in_=f[:]),
             reads=[f.b], writes=[band.b])
    grow = alloc(stack0, "grow", [1, D], F32)
    P.dma("c0", mk("dma_start", out=grow[:], in_=gfin), writes=[grow.b])
    for hh in range(2):
        m = next_mm()
        P.op("pe", mk("matmul", out=m[:], lhsT=ones1[0:1, :], rhs=grow[0:1, hh * 512:(hh + 1) * 512],
                                                  start=True, stop=True),
             reads=[ones1.b, grow.b], writes=[m.b])
        P.op("dve", mk("tensor_copy", out=gfbc[:, hh * 512:(hh + 1) * 512], in_=m[:]),
             reads=[m.b], writes=[gfbc.b])

    def load_gcol(l):
        g8 = next_ftmp()
        P.dma("c1", mk("dma_start", out=g8[0:8, 0:128], in_=norm_g[l].rearrange("(k p) -> k p", p=128)),
              writes=[g8.b])
        m = next_mm()
        P.op("pe", mk("transpose", out=m[:, 0:8], in_=g8[0:8, 0:128], identity=identf[0:8, 0:8]),
             reads=[g8.b, identf.b], writes=[m.b])
        P.op("dve", mk("tensor_copy", out=gcol[:], in_=m[:, 0:8]), reads=[m.b], writes=[gcol.b])

    def load_cast(dst_ap_fn, src_rows_fn, nk, ncols, stage, scale_gcol, dstb):
        for k in range(nk):
            s = stage[k % 2]
            P.dma("wst%d" % (k % 2), mk("dma_start", out=s[:, 0:ncols], in_=src_rows_fn(k)),
                  writes=[s.b])
            if scale_gcol:
                if k % 2 == 0:
                    P.op("dve", mk("tensor_scalar", out=dst_ap_fn(k), in0=s[:, 0:ncols],
                                                                    scalar1=gcol[:, k:k + 1], scalar2=None,
                                                                    op0=ALU.mult),
                         reads=[s.b, gcol.b], writes=[dstb])
                else:
                    P.op("act", mk("activation", out=dst_ap_fn(k), in_=s[:, 0:ncols],
                                                                 func=AF.Copy, scale=gcol[:, k:k + 1]),
                         reads=[s.b, gcol.b], writes=[dstb])
            else:
                if k % 2 == 0:
                    P.op("dve", mk("tensor_copy", out=dst_ap_fn(k), in_=s[:, 0:ncols]),
                         reads=[s.b], writes=[dstb])
                else:
                    P.op("act", mk("copy", out=dst_ap_fn(k), in_=s[:, 0:ncols]),
                         reads=[s.b], writes=[dstb])

    def load_x(seq, i, slot, src):
        nt = seq["nt"]
        r0 = i * 128
        P.dma("x%d" % slot, mk("dma_start", out=xt[slot][0:nt, :], in_=src[r0:r0 + nt, :]),
              writes=[xt[slot].b])

    def norm_hT(seq, slot):
        nt = seq["nt"]
        x = xt[slot]
        P.op("act", mk("activation", out=xn[0:nt, 0:D], in_=x[0:nt, :], func=AF.Square,
                                           accum_out=ssq[0:nt, :]),
             reads=[x.b], writes=[xn.b, ssq.b])
        P.op("dve", mk("tensor_scalar", out=rstd[0:nt, :], in0=ssq[0:nt, :], scalar1=1.0 / D, scalar2=EPS,
                                              op0=ALU.mult, op1=ALU.add),
             reads=[ssq.b], writes=[rstd.b])
        P.op("act", mk("activation", out=rstd[0:nt, :], in_=rstd[0:nt, :], func=AF.Sqrt),
             reads=[rstd.b], writes=[rstd.b])
        P.op("dve", mk("reciprocal", out=rstd[0:nt, :], in_=rstd[0:nt, :]),
             reads=[rstd.b], writes=[rstd.b])
        P.op("dve", mk("tensor_scalar", out=xn[0:nt, :], in0=x[0:nt, :], scalar1=rstd[0:nt, 0:1],
                                              scalar2=None, op0=ALU.mult),
             reads=[x.b, rstd.b], writes=[xn.b])
        tv = tp[:].rearrange("p (k t) -> p k t", t=128)
        for k in range(8):
            P.op("pe", mk("transpose", out=tv[:, k, 0:nt], in_=xn[0:nt, k * 128:(k + 1) * 128],
                                                  identity=ident[0:nt, 0:nt]),
                 reads=[xn.b, ident.b], writes=[tp.b])
        P.op("act", mk("copy", out=hT[:, 0:4, 0:nt], in_=tv[:, 0:4, 0:nt]), reads=[tp.b], writes=[hT.b])
        P.op("dve", mk("tensor_copy", out=hT[:, 4:8, 0:nt], in_=tv[:, 4:8, 0:nt]), reads=[tp.b], writes=[hT.b])

    def proj(nt, W, c0, ncols):
        m = next_mm()
        for k in range(8):
            P.op("pe", mk("matmul", out=m[0:nt, 0:ncols], lhsT=hT[:, k, 0:nt], rhs=W[:, k, c0:c0 + ncols],
                                               start=(k == 0), stop=(k == 7)),
                 reads=[hT.b, W.b], writes=[m.b])
        return m

    def transpose_to(src, srcb, nt, ncols_list, dst_fn, dstb, evac_eng="act"):
        tv = tp[:].rearrange("p (k t) -> p k t", t=128)
        for j, (c0, wd) in enumerate(ncols_list):
            P.op("pe", mk("transpose", out=tv[0:wd, j, 0:nt], in_=src[0:nt, c0:c0 + wd],
                                                                identity=ident[0:nt, 0:nt]),
                 reads=[srcb, ident.b], writes=[tp.b])
        wmax = max(w for _, w in ncols_list)
        n = len(ncols_list)
        if evac_eng == "none":
            return tv
        if evac_eng == "actbias":
            P.op("act", mk("activation", out=dst_fn(wmax, n), in_=tv[0:wmax, 0:n, 0:nt], func=AF.Copy,
                           scale=30000.0, bias=-30000.0), reads=[tp.b], writes=[dstb])
        elif evac_eng == "act":
            P.op("act", mk("copy", out=dst_fn(wmax, n), in_=tv[0:wmax, 0:n, 0:nt]), reads=[tp.b], writes=[dstb])
        else:
            P.op("dve", mk("tensor_copy", out=dst_fn(wmax, n), in_=tv[0:wmax, 0:n, 0:nt]),
                 reads=[tp.b], writes=[dstb])

    for l in range(DEPTH):
        x_src_of = (lambda seq: seq["x_in"]) if l == 0 else (lambda seq: xscr[seq["tok0"]:seq["tok0"] + seq["T"], :])
        last = (l == DEPTH - 1)

        st1 = ExitStack()
        Wmix = alloc(st1, "Wmix", [128, 4, 128], BF16)
        with ExitStack() as stl:
            if l == 0:
                stage = [alloc(stl, "stg%d" % i, [128, C1], F32) for i in range(2)]
                load_gcol(l)
                load_cast(lambda k: W1[:, k, :], lambda k: w_in[l, k * 128:(k + 1) * 128, 0:C1], 8, C1, stage, True, W1.b)
            srow = next_ftmp()
            P.dma("c1", mk("dma_start", out=srow[0:1, :], in_=pscale[l:l + 1, :]), writes=[srow.b])
            m = next_mm()
            P.op("pe", mk("matmul", out=m[:], lhsT=ones1[0:1, :], rhs=srow[0:1, :], start=True, stop=True),
                 reads=[ones1.b, srow.b], writes=[m.b])
            s0 = next_ftmp()
            P.dma("wst0", mk("dma_start", out=s0[:, 0:512].rearrange("p (g d) -> p g d", g=4),
                                                in_=w_mix[l].rearrange("g c d -> c g d")), writes=[s0.b])
            P.op("dve", mk("tensor_tensor", out=Wmix[:].rearrange("p g d -> p (g d)"), in0=s0[:, 0:512],
                                                  in1=m[:], op=ALU.mult),
                 reads=[s0.b, m.b], writes=[Wmix.b])
            P.barrier()

        kT = alloc(st1, "kT", [128, KTW], BF16)
        Vg = alloc(st1, "Vg", [128, VW], BF16)
        kiT = alloc(st1, "kiT", [96, SMAX], BF16)
        MH = max(T, (LS + 1) // 2)
        score = alloc(st1, "score", [128, 2 * MH], F32)
        sc_bufs = [score.b, P.buf("score1")]
        maskT = alloc(st1, "maskT", [128, max(NTP * 128, NCS * 64)], BF16)
        utok = [alloc(st1, "utok%d" % i, [128, 512], BF16) for i in range(2)]
        szp = alloc(st1, "szp", [128, 512], BF16)
        szas = [alloc(st1, "sza%d" % i, [128, 512], BF16) for i in range(3)]
        qf = alloc(st1, "qf", [128, 512], F32)
        kf = [alloc(st1, "kf%d" % i, [128, 512], F32) for i in range(2)]
        vf = [alloc(st1, "vf%d" % i, [128, 512], F32) for i in range(2)]
        g5 = [alloc(st1, "g5%d" % i, [128, 296], F32) for i in range(2)]
        uf = ftmp[0]
        rtab = [alloc(st1, "rtab%d" % i, [128, 192], F32) for i in range(2)]
        rtmp = alloc(st1, "rtmp", [128, 512], F32)
        qb = alloc(st1, "qb", [128, 512], BF16)
        kb = alloc(st1, "kb", [128, 512], BF16)
        qsb = alloc(st1, "qsb", [128, 256], BF16)
        ki3 = alloc(st1, "ki3", [128, 96], BF16)
        Dg = alloc(st1, "Dg", [128, 8, 128], BF16)
        plT = alloc(st1, "plT", [128, 4, 128], BF16)
        gpt = alloc(st1, "gpt", [128, 512], BF16)
        gpT = [alloc(st1, "gpT%d" % i, [128, 4, 128], BF16) for i in range(2)]
        gaT = [alloc(st1, "gaT%d" % i, [128, 4, 128], BF16) for i in range(2)]
        qTs = [alloc(st1, "qT%d" % i, [128, 8, 128], BF16) for i in range(3)]
        qiT = alloc(st1, "qiT", [96, 3, 128], BF16)
        rel = [alloc(st1, "rel%d" % i, [128, 512], BF16) for i in range(3)]
        maskb = alloc(st1, "maskb", [128, 2 * MH], BF16)
        mb_bufs = [maskb.b, P.buf("maskb1")]
        praw = [alloc(st1, "praw%d" % i, [128, 512], BF16) for i in range(3)]
        on = alloc(st1, "on", [128, 512], F32)
        gat = alloc(st1, "gat", [128, 512], BF16)
        rs = alloc(st1, "rs", [128, 8], F32)
        bs_lo = alloc(st1, "bs_lo", [128, 1], F32)
        bs_hi = alloc(st1, "bs_hi", [128, 1], F32)
        bs_mid = [alloc(st1, "bs_mid%d" % i, [128, 1], F32) for i in range(2)]
        bs_cnt = alloc(st1, "bs_cnt", [128, 1], F32)
        bs_cnt2 = alloc(st1, "bs_cnt2", [128, 1], F32)
        bs_u = alloc(st1, "bs_u", [128, 1], F32)
        bs_ht = alloc(st1, "bs_ht", [128, NI_BISECT], F32)
        bs_thr = alloc(st1, "bs_thr", [128, 1], F32)
        cst = [alloc(st1, "cst%d" % i, [128, 2, 256], F32) for i in range(2)]
        cbf = alloc(st1, "cbf", [128, 2, 256], BF16)
        spf = alloc(st1, "spf", [16, 512], F32)
        spb = alloc(st1, "spb", [16, 512], BF16)
        ckif = alloc(st1, "ckif", [128, max(L0 // 128, 1), 32], F32)
        cki3 = alloc(st1, "cki3", [128, 96], BF16)

        rot = {"rel": 0, "mk": 0, "praw": 0, "pm": 0, "L": 0}

        def nxt(name, arr):
            rot[name] ^= 1
            return arr[rot[name]]

        P.op("pool", mk("memset", ap=Vg[:], constant=1.0), writes=[Vg.b])
        for qz in qTs:
            P.op("pool", mk("memset", ap=qz[:], constant=0.0), writes=[qz.b])

        def rope_inplace(t, nt, nh, hd, tab, toff, eng="pool"):
            h2 = hd // 2
            xv = lambda: t.rearrange("p (h d) -> p h d", d=hd)
            tv = lambda: rtmp[0:nt, 0:nh * hd].rearrange("p (h d) -> p h d", d=hd)
            cc = lambda: tab[0:nt, toff:toff + hd].unsqueeze(1).broadcast_to([nt, nh, hd])
            sn = lambda: tab[0:nt, toff + hd:toff + hd + h2].unsqueeze(1).broadcast_to([nt, nh, h2])
            sp_ = lambda: tab[0:nt, toff + hd + h2:toff + 2 * hd].unsqueeze(1).broadcast_to([nt, nh, h2])
            return xv, tv, cc, sn, sp_, h2

        def do_rope(tl, col0, nt, nh, hd, tab, toff, eng):
            t = tl[0:nt, col0:col0 + nh * hd]
            xv, tv, cc, sn, sp_, h2 = rope_inplace(t, nt, nh, hd, tab, toff)
            P.op(eng, mk("tensor_tensor", out=tv()[:, :, 0:h2], in0=xv()[:, :, h2:hd], in1=sn(), op=ALU.mult),
                 reads=[tl.b, tab.b], writes=[rtmp.b])
            P.op(eng, mk("tensor_tensor", out=tv()[:, :, h2:hd], in0=xv()[:, :, 0:h2], in1=sp_(), op=ALU.mult),
                 reads=[tl.b, tab.b], writes=[rtmp.b])
            P.op(eng, mk("tensor_tensor", out=xv(), in0=xv(), in1=cc(), op=ALU.mult),
                 reads=[tl.b, tab.b], writes=[tl.b])
            P.op(eng, mk("tensor_tensor", out=xv(), in0=xv(), in1=tv(), op=ALU.add),
                 reads=[tl.b, rtmp.b], writes=[tl.b])

        gtile = [0]

        for seq in seqs:
            nt = seq["nt"]
            ntl = seq["ntiles"]
            isS = seq["kind"] == "s"
            xsrc = x_src_of(seq)
            Ksel = seq["K"]
            if not isS:
                kTv = kT[:, 0:4 * T].rearrange("p (c s) -> p c s", c=4)
                Vv = Vg[:, 0:NTP * 8 * VS].rearrange("p (t h d) -> p t h d", h=8, d=VS)
                okd, ovd, okid, opld = ok_p[l, seq["idx"]], ov_p[l, seq["idx"]], oki_p[l, seq["idx"]], opl_p[l, seq["idx"]]
            else:
                kTv = kT[:, 0:2 * LS].rearrange("p (c s) -> p c s", c=2)
                Vv = Vg[:, 0:NCS * 4 * VS].rearrange("p (t h d) -> p t h d", h=4, d=VS)
                okd, ovd, okid, opld = ok_s[l], ov_s[l], oki_s[l], opl_s[l]

            if isS and L0 > 0:
                nct = L0 // 128
                P.dma("cki", mk("dma_start", out=ckif[:, 0:nct, :],
                                                   in_=cki[l].rearrange("(t p) d -> p t d", p=128)),
                      writes=[ckif.b])
                for c in range(nct):
                    P.op("dve", mk("tensor_copy",
                        out=cki3[:].rearrange("p (r d) -> p r d", r=3),
                        in_=ckif[:, c, :].unsqueeze(1).broadcast_to([128, 3, 32])),
                        reads=[ckif.b], writes=[cki3.b])
                    transpose_to(cki3, cki3.b, 128, [(0, 96)],
                                 lambda wm, n, c=c: kiT[0:96, c * 128:(c + 1) * 128].unsqueeze(1), kiT.b,
                                 evac_eng="act" if c % 2 else "dve")
                P.dma("spf", mk("dma_start", out=spf[0:15, :], in_=spool[l]), writes=[spf.b])
                P.op("dve", mk("tensor_copy", out=spb[0:15, :], in_=spf[0:15, :]), reads=[spf.b], writes=[spb.b])

            def stageA1(i):
                qT = qTs[gtile[0] % 3]
                sza = szas[gtile[0] % 3]
                gi = gtile[0]
                gtile[0] += 1
                slot = gi % 2
                if i + 1 < ntl:
                    load_x(seq, i + 1, (gi + 1) % 2, xsrc)
                rt = rtab[gi % 2]
                rp0 = seq["rope0"] + i * 128
                P.dma("rt%d" % (gi % 2), mk("dma_start", out=rt[0:nt, :], in_=rope[rp0:rp0 + nt, :]),
                      writes=[rt.b])
                norm_hT(seq, slot)
                key0 = (L0 if isS else 0) + i * 128
                S = key0 + nt
                if isS:
                    sview = score[:, 0:S]
                    sbufs = [sc_bufs[0], sc_bufs[1]]
                else:
                    sview = score[:, (gtile[0] - 1) % 2 * MH:(gtile[0] - 1) % 2 * MH + S]
                    sbufs = [sc_bufs[(gtile[0] - 1) % 2]]
                ucur = utok[gi % 2]
                uprev = utok[(gi + 1) % 2]
                kfi = kf[gi % 2]
                vfi = vf[gi % 2]
                g5i = g5[gi % 2]

                yield
                m = proj(nt, W1, 0, 512)
                P.op("act", mk("copy", out=ucur[0:nt, :], in_=m[0:nt, :]), reads=[m.b], writes=[ucur.b])
                if i == ntl - 1:
                    P.op("dve", mk("tensor_copy", out=uf[0:nt, :], in_=m[0:nt, :]), reads=[m.b], writes=[uf.b])
                    P.dma("opl", mk("dma_start", out=opld, in_=uf[nt - 15:nt, :]), reads=[uf.b])
                yield
                m = proj(nt, W1, 512, 512)
                P.op("act", mk("activation", out=szp[0:nt, :], in_=m[0:nt, :], func=AF.Silu),
                     reads=[m.b], writes=[szp.b])
                yield
                m = proj(nt, W1, 1024, 512)
                P.op("act", mk("copy", out=qf[0:nt, :], in_=m[0:nt, :]), reads=[m.b], writes=[qf.b])
                yield
                m = proj(nt, W1, 1536, 512)
                P.op("act", mk("copy", out=kfi[0:nt, :], in_=m[0:nt, :]), reads=[m.b], writes=[kfi.b])
                yield
                m = proj(nt, W1, 2048, 512)
                P.op("act", mk("copy", out=vfi[0:nt, :], in_=m[0:nt, :]), reads=[m.b], writes=[vfi.b])
                if not isS:
                    P.op("dve", mk("tensor_copy", out=Vv[0:nt, i, :, 0:64],
                                   in_=m[0:nt, :].rearrange("p (h d) -> p h d", d=64)),
                         reads=[m.b], writes=[Vg.b])
                P.dma("ov%d" % (gi % 2), mk("dma_start", out=ovd[i * 128:i * 128 + nt, :], in_=vfi[0:nt, :]),
                      reads=[vfi.b])
                yield
                m = proj(nt, W1, 2560, 296)
                P.op("act", mk("copy", out=g5i[0:nt, :], in_=m[0:nt, 0:296]), reads=[m.b], writes=[g5i.b])
                yield
                m = proj(nt, W1, 2856, 512)
                P.op("act", mk("activation", out=sza[0:nt, :], in_=m[0:nt, :], func=AF.Silu),
                     reads=[m.b], writes=[sza.b])

                yield
                yield
                do_rope(qf, 0, nt, 8, 64, rt, 0, "pool")
                yield
                do_rope(kfi, 0, nt, 8, 64, rt, 0, "pool")
                yield
                do_rope(g5i, 0, nt, 8, 32, rt, 128, "pool")
                do_rope(g5i, 256, nt, 1, 32, rt, 128, "pool")
                P.dma("ok%d" % (gi % 2), mk("dma_start", out=okd[i * 128:i * 128 + nt, :], in_=kfi[0:nt, :]),
                      reads=[kfi.b])
                P.dma("oki%d" % (gi % 2), mk("dma_start", out=okid[i * 128:i * 128 + nt, :], in_=g5i[0:nt, 256:288]),
                      reads=[g5i.b])
                P.op("pool", mk("tensor_copy", out=qb[0:nt, :], in_=qf[0:nt, :]), reads=[qf.b], writes=[qb.b])
                P.op("pool", mk("tensor_copy", out=kb[0:nt, :], in_=kfi[0:nt, :]), reads=[kfi.b], writes=[kb.b])
                for h in range(8):
                    P.op("dve", mk("tensor_scalar", out=Dg[0:nt, h, 0:nt], in0=identf[0:nt, 0:nt],
                                   scalar1=g5i[0:nt, 288 + h:289 + h], scalar2=None, op0=ALU.mult),
                         reads=[identf.b, g5i.b], writes=[Dg.b])
                P.op("pool", mk("tensor_copy", out=qsb[0:nt, :], in_=g5i[0:nt, 0:256]), reads=[g5i.b], writes=[qsb.b])
                P.op("dve", mk("tensor_copy", out=ki3[0:nt, :].rearrange("p (r d) -> p r d", r=3),
                                                    in_=g5i[0:nt, 256:288].unsqueeze(1).broadcast_to([nt, 3, 32])),
                     reads=[g5i.b], writes=[ki3.b])

                yield
                transpose_to(ki3, ki3.b, nt, [(0, 96)],
                             lambda wm, n: kiT[0:96, key0:key0 + nt].unsqueeze(1), kiT.b, "act")
                transpose_to(qsb, qsb.b, nt, [(0, 96), (96, 96), (192, 64)],
                             lambda wm, n: qiT[0:96, 0:3, 0:nt], qiT.b, "dve")
                tvq = transpose_to(qb, qb.b, nt, [(c * 128, 128) for c in range(4)], None, None, "none")
                P.op("act", mk("copy", out=qT[0:64, 0:8:2, 0:nt], in_=tvq[0:64, 0:4, 0:nt]), reads=[tp.b], writes=[qT.b])
                P.op("act", mk("copy", out=qT[64:128, 1:8:2, 0:nt], in_=tvq[64:128, 0:4, 0:nt]), reads=[tp.b], writes=[qT.b])
                if not isS:
                    transpose_to(kb, kb.b, nt, [(c * 128, 128) for c in range(4)],
                                 lambda wm, n: kTv[:, 0:4, key0:key0 + nt], kT.b, "dve")

                yield
                ppv = pp[:].rearrange("p (g t) -> p g t", g=4)
                for g in range(4):
                    if isS:
                        P.op("pe", mk("matmul", out=ppv[:, g, 0:nt], lhsT=ucur[0:nt, g * 128:(g + 1) * 128],
                                                           rhs=band[0:nt, 1, g * 128:g * 128 + nt], start=True, stop=False),
                             reads=[ucur.b, band.b], writes=[pp.b])
                        P.op("pe", mk("matmul", out=ppv[:, g, 0:nt], lhsT=spb[0:15, g * 128:(g + 1) * 128],
                                                           rhs=band[0:15, 3, g * 128:g * 128 + nt], start=False, stop=True),
                             reads=[spb.b, band.b], writes=[pp.b])
                    elif i == 0:
                        P.op("pe", mk("matmul", out=ppv[:, g, 0:nt], lhsT=ucur[0:nt, g * 128:(g + 1) * 128],
                                                           rhs=band[0:nt, 0, g * 128:g * 128 + nt], start=True, stop=True),
                             reads=[ucur.b, band.b], writes=[pp.b])
                    else:
                        P.op("pe", mk("matmul", out=ppv[:, g, 0:nt], lhsT=ucur[0:nt, g * 128:(g + 1) * 128],
                                                           rhs=band[0:nt, 1, g * 128:g * 128 + nt], start=True, stop=False),
                             reads=[ucur.b, band.b], writes=[pp.b])
                        P.op("pe", mk("matmul", out=ppv[:, g, 0:nt], lhsT=uprev[:, g * 128:(g + 1) * 128],
                                                           rhs=band[:, 2, g * 128:g * 128 + nt], start=False, stop=True),
                             reads=[uprev.b, band.b], writes=[pp.b])
                P.op("act", mk("copy", out=plT[:, :, 0:nt], in_=ppv[:, :, 0:nt]), reads=[pp.b], writes=[plT.b])
                m = next_mm()
                for g in range(4):
                    P.op("pe", mk("matmul", out=m[0:nt, g * 128:(g + 1) * 128], lhsT=plT[:, g, 0:nt],
                                                            rhs=Wmix[:, g, :], start=True, stop=True),
                         reads=[plT.b, Wmix.b], writes=[m.b])
                P.op("dve", mk("tensor_tensor", out=gpt[0:nt, :], in0=m[0:nt, :], in1=szp[0:nt, :], op=ALU.mult),
                     reads=[m.b, szp.b], writes=[gpt.b])
                gpo = gpT[gi % 2]
                transpose_to(gpt, gpt.b, nt, [(c * 128, 128) for c in range(4)],
                             lambda wm, n: gpo[:, 0:4, 0:nt], gpo.b, "act")
                tix = seq["tile0"] + i
                P.dma("gp%d" % (gi % 2), mk("dma_start",
                    out=gp_scr[tix].rearrange("p (c t) -> p c t", c=4)[:, :, 0:nt], in_=gpo[:, :, 0:nt]),
                    reads=[gpo.b])

                yield
                ngk = (S + 511) // 512
                for gk in range(ngk):
                    k0 = gk * 512
                    gw = min(512, S - k0)
                    prev = None
                    for h in range(8):
                        if h % 2 == 0:
                            yield
                        m = next_mm()
                        bp = 32 * (h % 3)
                        P.op("pe", mk("matmul", out=
                            m[0:nt, 0:gw], lhsT=qiT[bp:bp + 32, h // 3, 0:nt], rhs=kiT[bp:bp + 32, k0:k0 + gw],
                            start=True, stop=True),
                            reads=[qiT.b, kiT.b], writes=[m.b])
                        rot["rel"] = (rot["rel"] + 1) % 3
                        r = rel[rot["rel"]]
                        P.op("act", mk("activation", out=r[0:nt, 0:gw], in_=m[0:nt, 0:gw], func=AF.Relu),
                             reads=[m.b], writes=[r.b])
                        if prev is not None:
                            ph, prr = prev
                            P.op("pe", mk("matmul", out=pp[0:nt, 0:gw], lhsT=Dg[0:nt, ph, 0:nt], rhs=prr[0:nt, 0:gw],
                                          start=(ph == 0), stop=False),
                                 reads=[Dg.b, prr.b], writes=[pp.b])
                        prev = (h, r)
                    ph, prr = prev
                    P.op("pe", mk("matmul", out=pp[0:nt, 0:gw], lhsT=Dg[0:nt, ph, 0:nt], rhs=prr[0:nt, 0:gw],
                                  start=False, stop=True),
                         reads=[Dg.b, prr.b], writes=[pp.b])
                    P.op("act", mk("copy", out=sview[0:nt, k0:k0 + gw], in_=pp[0:nt, 0:gw]), reads=[pp.b], writes=sbufs)
                return dict(i=i, gi=gi, S=S, ngk=ngk, qT=qT, sza=sza, kfi=kfi, vfi=vfi, tix=tix,
                            sview=sview, sbufs=sbufs)

            def stageA2(c):
                gi, S, sview, sbufs = c["gi"], c["S"], c["sview"], c["sbufs"]
                nch = (S + 127) // 128
                if isS:
                    mview = maskb[:, 0:S]
                    mbufs = [mb_bufs[0], mb_bufs[1]]
                else:
                    mview = maskb[:, (gi % 2) * MH:(gi % 2) * MH + S]
                    mbufs = [mb_bufs[gi % 2]]
                c["nch"], c["mview"], c["mbufs"] = nch, mview, mbufs
                yield
                need_topk = S > Ksel
                if not isS:
                    if need_topk:
                        P.op("dve", mk("tensor_reduce", out=bs_lo[0:nt, :], in_=sview[0:nt, 0:S - 64],
                                       axis=mybir.AxisListType.X, op=ALU.min),
                             reads=sbufs, writes=[bs_lo.b])
                    P.op("dve", mk("memset", ap=sview[0:64, S - 64:S], constant=NEG), writes=sbufs)
                else:
                    if need_topk:
                        P.op("dve", mk("tensor_reduce", out=bs_lo[0:nt, :], in_=sview[0:nt, 0:S],
                                       axis=mybir.AxisListType.X, op=ALU.min),
                             reads=sbufs, writes=[bs_lo.b])
                if need_topk:
                    P.op("dve", mk("tensor_reduce", out=bs_hi[0:nt, :], in_=sview[0:nt, 0:S],
                                   axis=mybir.AxisListType.X, op=ALU.max),
                         reads=sbufs, writes=[bs_hi.b])
                    P.op("dve", mk("tensor_tensor", out=bs_hi[0:nt, :], in0=bs_hi[0:nt, :], in1=bs_lo[0:nt, :],
                                   op=ALU.subtract),
                         reads=[bs_hi.b, bs_lo.b], writes=[bs_hi.b])
                    P.op("dve", mk("tensor_scalar", out=bs_ht[0:nt, :], in0=pw[0:nt, :], scalar1=bs_hi[0:nt, 0:1],
                                   scalar2=None, op0=ALU.mult),
                         reads=[pw.b, bs_hi.b], writes=[bs_ht.b])
                    P.op("dve", mk("tensor_tensor", out=bs_mid[0][0:nt, :], in0=bs_lo[0:nt, :], in1=bs_ht[0:nt, 0:1],
                                   op=ALU.add),
                         reads=[bs_lo.b, bs_ht.b], writes=[bs_mid[0].b])
                    for it in range(NI_BISECT):
                        yield
                        mc = bs_mid[it % 2]
                        mn = bs_mid[(it + 1) % 2]
                        P.op("dve", mk("tensor_scalar", out=mview[0:nt, 0:S], in0=sview[0:nt, 0:S],
                                       scalar1=mc[0:nt, 0:1], scalar2=None, op0=ALU.is_ge, op1=ALU.add,
                                       accum_out=bs_cnt[0:nt, :]),
                             reads=sbufs + [mc.b], writes=mbufs + [bs_cnt.b])
                        P.op("dve", mk("tensor_scalar", out=bs_u[0:nt, :], in0=bs_cnt[0:nt, :],
                                       scalar1=float(Ksel) - 0.5, scalar2=0.5, op0=ALU.is_ge, op1=ALU.subtract),
                             reads=[bs_cnt.b], writes=[bs_u.b])
                        P.op("dve", mk("scalar_tensor_tensor", out=mn[0:nt, :], in0=bs_u[0:nt, :],
                                       scalar=bs_ht[0:nt, it:it + 1], in1=mc[0:nt, :], op0=ALU.mult, op1=ALU.add),
                             reads=[bs_u.b, bs_ht.b, mc.b], writes=[mn.b])
                    mfin = bs_mid[NI_BISECT % 2]
                    P.op("dve", mk("scalar_tensor_tensor", out=bs_thr[0:nt, :],
                                   in0=bs_ht[0:nt, NI_BISECT - 1:NI_BISECT], scalar=-0.5, in1=mfin[0:nt, :],
                                   op0=ALU.mult, op1=ALU.add),
                         reads=[bs_ht.b, mfin.b], writes=[bs_thr.b])
                else:
                    P.op("dve", mk("memset", ap=bs_thr[0:nt, :], constant=-1.0e29), writes=[bs_thr.b])
                yield
                P.op("dve", mk("tensor_scalar", out=mview[0:nt, :], in0=sview[0:nt, 0:S], scalar1=bs_thr[0:nt, 0:1],
                               scalar2=None, op0=ALU.is_ge),
                     reads=sbufs + [bs_thr.b], writes=mbufs)

            def stageB1(c):
                i, gi, S, nch, ngk, qT, mview, mbufs, vfi = (c["i"], c["gi"], c["S"], c["nch"], c["ngk"], c["qT"],
                                                             c["mview"], c["mbufs"], c["vfi"])
                mTv = maskT[:, 0:nch * nt].rearrange("p (c t) -> p c t", t=nt)
                for gk in range(ngk):
                    yield
                    k0 = gk * 512
                    gw = min(512, S - k0)
                    blocks = []
                    cc_ = 0
                    while cc_ * 128 < gw:
                        blocks.append((k0 + cc_ * 128, min(128, gw - cc_ * 128)))
                        cc_ += 1
                    full = [b for b in blocks if b[1] == 128]
                    part = [b for b in blocks if b[1] < 128]
                    if full:
                        transpose_to(mview, mbufs[0], nt, full,
                                     lambda wm, n, k0=k0: mTv[:, k0 // 128:k0 // 128 + n, :], maskT.b, "actbias")
                    if part:
                        pc0, pw_ = part[0]
                        transpose_to(mview, mbufs[-1], nt, [(pc0, pw_)],
                                     lambda wm, n, pc0=pc0: mTv[0:wm, pc0 // 128:pc0 // 128 + 1, :],
                                     maskT.b, "actbias")
                halves = [(0, 8)] if not isS else [(0, 4), (4, 8)]
                for (h0, h1) in halves:
                    if isS:
                        hh = h0 // 4
                        nct = L0 // 128
                        for c4 in range(0, nct, 2):
                            n4 = min(2, nct - c4)
                            stg = cst[(c4 // 2) % 2]
                            P.dma("cst", mk("dma_start",
                                out=stg[:, 0:n4, :],
                                in_=ck[l].rearrange("(t p) f -> p t f", p=128)[:, c4:c4 + n4, hh * 256:(hh + 1) * 256]),
                                writes=[stg.b])
                            P.op("dve", mk("tensor_copy", out=cbf[:, 0:n4, :], in_=stg[:, 0:n4, :]),
                                 reads=[stg.b], writes=[cbf.b])
                            for j in range(n4):
                                c = c4 + j
                                transpose_to(cbf[:, j, :], cbf.b, 128, [(0, 128), (128, 128)],
                                             lambda wm, n, c=c: kTv[:, 0:2, c * 128:(c + 1) * 128], kT.b,
                                             "act" if j % 2 else "dve")
                            stg2 = cst[(c4 // 2 + 1) % 2]
                            P.dma("cst", mk("dma_start",
                                out=stg2[:, 0:n4, :],
                                in_=cv[l].rearrange("(t p) f -> p t f", p=128)[:, c4:c4 + n4, hh * 256:(hh + 1) * 256]),
                                writes=[stg2.b])
                            P.op("dve", mk("tensor_copy",
                                out=Vv[:, c4:c4 + n4, :, 0:64],
                                in_=stg2[:, 0:n4, :].rearrange("p t (h d) -> p t h d", d=64)),
                                reads=[stg2.b], writes=[Vg.b])
                        transpose_to(kb[:, hh * 256:(hh + 1) * 256], kb.b, nt, [(0, 128), (128, 128)],
                                     lambda wm, n: kTv[:, 0:2, L0:L0 + nt], kT.b, "act")
                        P.op("pool", mk("tensor_copy",
                            out=Vv[0:nt, NCS - 1, :, 0:64],
                            in_=vfi[0:nt, hh * 256:(hh + 1) * 256].rearrange("p (h d) -> p h d", d=64)),
                            reads=[vfi.b], writes=[Vg.b])
                    chunks = [(c, min(128, S - c * 128)) for c in range(nch)]
                    groups = []
                    cur = []
                    for cpair in chunks:
                        if cur and (len(cur) == 4 or cur[-1][1] != cpair[1]):
                            groups.append(cur)
                            cur = []
                        cur.append(cpair)
                    if cur:
                        groups.append(cur)
                    items = [(h, gidx) for h in range(h0, h1) for gidx in range(len(groups))]

                    def emit_L(h, gidx):
                        hl = h - h0
                        pr = hl // 2
                        grp = groups[gidx]
                        rc = grp[0][1]
                        ng = len(grp)
                        Lt = nxt("L", Lb)
                        Lv = Lt[:].rearrange("p (c t) -> p c t", t=128)
                        c0g = grp[0][0]
                        if nt == 128:
                            P.op("pe", mk("matmul", out=Lt[0:rc, 0:ng * 128], lhsT=ident[0:rc, 0:rc],
                                          rhs=maskT[0:rc, c0g * 128:(c0g + ng) * 128], start=True, stop=False),
                                 reads=[ident.b, maskT.b], writes=[Lt.b])
                        for j, (c, _) in enumerate(grp):
                            if nt != 128:
                                P.op("pe", mk("matmul", out=Lv[0:rc, j, 0:nt], lhsT=ident[0:rc, 0:rc],
                                              rhs=mTv[0:rc, c, :], start=True, stop=False),
                                     reads=[ident.b, maskT.b], writes=[Lt.b])
                            P.op("pe", mk("matmul", out=Lv[0:rc, j, 0:nt], lhsT=kTv[:, pr, c * 128:c * 128 + rc],
                                          rhs=qT[:, h, 0:nt], start=False, stop=(j == ng - 1 or nt != 128)),
                                 reads=[kT.b, qT.b], writes=[Lt.b])
                        rot["praw"] = (rot["praw"] + 1) % len(praw)
                        pr_t = praw[rot["praw"]]
                        prv = pr_t[:].rearrange("p (c t) -> p c t", t=128)
                        P.op("act", mk("activation", out=prv[0:rc, 0:ng, 0:nt], in_=Lv[0:rc, 0:ng, 0:nt],
                                       func=AF.Exp, scale=0.125),
                             reads=[Lt.b], writes=[pr_t.b])
                        return (h, gidx, pr_t, prv)

                    def emit_PV(h, gidx, pr_t, prv):
                        hl = h - h0
                        O = Ob[h // 4]
                        grp = groups[gidx]
                        rc = grp[0][1]
                        ng = len(grp)
                        for j, (c, _) in enumerate(grp):
                            first = (gidx == 0 and j == 0)
                            lastc = (gidx == len(groups) - 1 and j == ng - 1)
                            P.op("pe", mk("matmul", out=O[0:nt, h % 4, :], lhsT=prv[0:rc, j, 0:nt],
                                          rhs=Vv[0:rc, c, hl if isS else h, :], start=first, stop=lastc),
                                 reads=[pr_t.b, Vg.b], writes=[O.b])

                    pend = None
                    for (h, gidx) in items:
                        yield
                        curL = emit_L(h, gidx)
                        if pend is not None:
                            emit_PV(*pend)
                        pend = curL
                    if pend is not None:
                        emit_PV(*pend)

            def stageB2(c):
                gi, sza, tix = c["gi"], c["sza"], c["tix"]
                for ob in range(2):
                    O = Ob[ob]
                    P.op("dve", mk("reciprocal", out=rs[0:nt, ob * 4:ob * 4 + 4], in_=O[0:nt, :, 64]),
                         reads=[O.b], writes=[rs.b])
                    P.op("dve", mk("tensor_tensor",
                                   out=on[0:nt, ob * 256:(ob + 1) * 256].rearrange("p (h d) -> p h d", d=64),
                                   in0=O[0:nt, :, 0:64],
                                   in1=rs[0:nt, ob * 4:ob * 4 + 4].unsqueeze(2).broadcast_to([nt, 4, 64]),
                                   op=ALU.mult),
                         reads=[O.b, rs.b], writes=[on.b])
                P.op("pool", mk("tensor_tensor", out=gat[0:nt, :], in0=on[0:nt, :], in1=sza[0:nt, :], op=ALU.mult),
                     reads=[on.b, sza.b], writes=[gat.b])
                gao = gaT[gi % 2]
                transpose_to(gat, gat.b, nt, [(c * 128, 128) for c in range(4)],
                             lambda wm, n: gao[:, 0:4, 0:nt], gao.b, "act")
                P.dma("ga%d" % (gi % 2), mk("dma_start",
                    out=ga_scr[tix].rearrange("p (c t) -> p c t", c=4)[:, :, 0:nt], in_=gao[:, :, 0:nt]),
                    reads=[gao.b])

            load_x(seq, 0, gtile[0] % 2, xsrc)

            def drive(gens):
                res = [None] * len(gens)
                live = [g is not None for g in gens]
                while any(live):
                    for k, g in enumerate(gens):
                        if live[k]:
                            try:
                                next(g)
                            except StopIteration as ex:
                                res[k] = ex.value
                                live[k] = False
                return res

            ctx = {}
            for k in range(ntl + 2):
                gA1 = stageA1(k) if k < ntl else None
                gA2 = stageA2(ctx[k - 1]) if 0 <= k - 1 < ntl else None
                gB1 = stageB1(ctx[k - 2]) if 0 <= k - 2 < ntl else None
                r = drive([gA1, gA2, gB1])
                if gA1 is not None:
                    ctx[k] = r[0]
                if gB1 is not None:
                    stageB2(ctx[k - 2])
        P.barrier()
        st1.close()

        st2 = ExitStack()
        W2 = alloc(st2, "W2", [128, 8, C2], BF16)
        Wpo = alloc(st2, "Wpo", [128, 4, D], BF16)
        Wao = alloc(st2, "Wao", [128, 4, D], BF16)
        Wo = alloc(st2, "Wo", [128, 8, D], BF16)
        with ExitStack() as stl:
            stage = [alloc(stl, "stg%d" % i, [128, C2], F32) for i in range(2)]
            load_cast(lambda k: W2[:, k, :], lambda k: w_in[l, k * 128:(k + 1) * 128, C1:NCOL], 8, C2, stage, True, W2.b)
            load_cast(lambda k: Wpo[:, k, :], lambda k: w_po[l, k * 128:(k + 1) * 128, :], 4, D, stage, False, Wpo.b)
            load_cast(lambda k: Wao[:, k, :], lambda k: w_ao[l, k * 128:(k + 1) * 128, :], 4, D, stage, False, Wao.b)
            load_cast(lambda k: Wo[:, k, :], lambda k: w_o[l, k * 128:(k + 1) * 128, :], 8, D, stage, False, Wo.b)
            P.barrier()
        gpl = [alloc(st2, "gpl%d" % i, [128, 4, 128], BF16) for i in range(3)]
        gal = [alloc(st2, "gal%d" % i, [128, 4, 128], BF16) for i in range(3)]
        sgps = [alloc(st2, "sgp%d" % i, [128, D], BF16) for i in range(2)]
        sgas = [alloc(st2, "sga%d" % i, [128, D], BF16) for i in range(2)]
        xt.append(alloc(st2, "xt2", [128, D], F32))
        m1 = alloc(st2, "m1", [128, D], F32)
        t2 = alloc(st2, "t2", [128, 512], F32)
        mrg = alloc(st2, "mrg", [128, D], BF16)
        mT = alloc(st2, "mT", [128, 8, 128], BF16)
        yt = [alloc(st2, "yt%d" % i, [128, D], F32) for i in range(2)]

        prefetch = []
        if l + 1 < DEPTH:
            pst = [alloc(st2, "pst%d" % i, [128, C1], F32) for i in range(2)]
            load_gcol(l + 1)

            def mk_chunk(k):
                def emit():
                    sgt = pst[k % 2]
                    P.dma("pst", mk("dma_start", out=sgt[:, 0:C1], in_=w_in[l + 1, k * 128:(k + 1) * 128, 0:C1]),
                          writes=[sgt.b])
                    if k % 2 == 0:
                        P.op("dve", mk("tensor_scalar", out=W1[:, k, :], in0=sgt[:, 0:C1], scalar1=gcol[:, k:k + 1],
                                       scalar2=None, op0=ALU.mult), reads=[sgt.b, gcol.b], writes=[W1.b])
                    else:
                        P.op("act", mk("activation", out=W1[:, k, :], in_=sgt[:, 0:C1], func=AF.Copy,
                                       scale=gcol[:, k:k + 1]), reads=[sgt.b, gcol.b], writes=[W1.b])
                return emit
            prefetch = [mk_chunk(k) for k in range(8)]
        gtile2 = [0]
        for seq in seqs:
            nt = seq["nt"]
            ntl = seq["ntiles"]
            xsrc = x_src_of(seq)

            def loads2(i, gi):
                load_x(seq, i, gi % 3, xsrc)
                tix = seq["tile0"] + i
                P.dma("gpl%d" % (gi % 3), mk("dma_start",
                    out=gpl[gi % 3][:, :, 0:nt], in_=gp_scr[tix].rearrange("p (c t) -> p c t", c=4)[:, :, 0:nt]),
                    writes=[gpl[gi % 3].b])
                P.dma("gal%d" % (gi % 3), mk("dma_start",
                    out=gal[gi % 3][:, :, 0:nt], in_=ga_scr[tix].rearrange("p (c t) -> p c t", c=4)[:, :, 0:nt]),
                    writes=[gal[gi % 3].b])

            def stage2A(i):
                gi = gtile2[0]
                gtile2[0] += 1
                if prefetch and gi % 3 == 1:
                    prefetch.pop(0)()
                slot = gi % 3
                sgp = sgps[gi % 2]
                sga = sgas[gi % 2]
                if i + 1 < ntl:
                    loads2(i + 1, gi + 1)
                norm_hT(seq, slot)
                for hh in range(2):
                    m = proj(nt, W2, hh * 512, 512)
                    P.op("act", mk("activation", out=sgp[0:nt, hh * 512:(hh + 1) * 512], in_=m[0:nt, :],
                                                                   func=AF.Sigmoid),
                         reads=[m.b], writes=[sgp.b])
                for hh in range(2):
                    m = proj(nt, W2, 1024 + hh * 512, 512)
                    P.op("act", mk("activation", out=sga[0:nt, hh * 512:(hh + 1) * 512], in_=m[0:nt, :],
                                                                   func=AF.Sigmoid),
                         reads=[m.b], writes=[sga.b])
                return dict(i=i, gi=gi, slot=slot, sgp=sgp, sga=sga)

            def stage2B(c):
                i, gi, slot, sgp, sga = c["i"], c["gi"], c["slot"], c["sgp"], c["sga"]
                x = xt[slot]
                gp_, ga_ = gpl[slot], gal[slot]
                for hh in range(2):
                    m = next_mm()
                    for k in range(4):
                        P.op("pe", mk("matmul", out=m[0:nt, :], lhsT=gp_[:, k, 0:nt],
                                                                       rhs=Wpo[:, k, hh * 512:(hh + 1) * 512],
                                                                       start=(k == 0), stop=(k == 3)),
                             reads=[gp_.b, Wpo.b], writes=[m.b])
                    P.op("dve", mk("tensor_tensor", out=m1[0:nt, hh * 512:(hh + 1) * 512], in0=m[0:nt, :],
                                                                      in1=sgp[0:nt, hh * 512:(hh + 1) * 512], op=ALU.mult),
                         reads=[m.b, sgp.b], writes=[m1.b])
                for hh in range(2):
                    m = next_mm()
                    for k in range(4):
                        P.op("pe", mk("matmul", out=m[0:nt, :], lhsT=ga_[:, k, 0:nt],
                                                                       rhs=Wao[:, k, hh * 512:(hh + 1) * 512],
                                                                       start=(k == 0), stop=(k == 3)),
                             reads=[ga_.b, Wao.b], writes=[m.b])
                    P.op("dve", mk("tensor_tensor", out=t2[0:nt, :], in0=m[0:nt, :],
                                                                      in1=sga[0:nt, hh * 512:(hh + 1) * 512], op=ALU.mult),
                         reads=[m.b, sga.b], writes=[t2.b])
                    P.op("pool", mk("tensor_tensor", out=mrg[0:nt, hh * 512:(hh + 1) * 512],
                                                                  in0=m1[0:nt, hh * 512:(hh + 1) * 512], in1=t2[0:nt, :],
                                                                  op=ALU.add),
                         reads=[m1.b, t2.b], writes=[mrg.b])
                transpose_to(mrg, mrg.b, nt, [(c * 128, 128) for c in range(8)],
                             lambda wm, n: mT[:, 0:8, 0:nt], mT.b, "act")
                for hh in range(2):
                    m = next_mm()
                    for k in range(8):
                        P.op("pe", mk("matmul", out=m[0:nt, :], lhsT=mT[:, k, 0:nt],
                                                                       rhs=Wo[:, k, hh * 512:(hh + 1) * 512],
                                                                       start=(k == 0), stop=(k == 7)),
                             reads=[mT.b, Wo.b], writes=[m.b])
                    P.op("dve", mk("tensor_tensor", out=x[0:nt, hh * 512:(hh + 1) * 512],
                                                                      in0=x[0:nt, hh * 512:(hh + 1) * 512], in1=m[0:nt, :],
                                                                      op=ALU.add),
                         reads=[m.b, x.b], writes=[x.b])
                r0 = seq["tok0"] + i * 128
                if not last:
                    P.dma("x%d" % slot, mk("dma_start", out=xscr[r0:r0 + nt, :], in_=x[0:nt, :]), reads=[x.b])
                else:
                    y = yt[gi % 2]
                    P.op("act", mk("activation", out=xn[0:nt, 0:D], in_=x[0:nt, :], func=AF.Square,
                                                       accum_out=ssq[0:nt, :]),
                         reads=[x.b], writes=[xn.b, ssq.b])
                    P.op("dve", mk("tensor_scalar", out=rstd[0:nt, :], in0=ssq[0:nt, :], scalar1=1.0 / D, scalar2=EPS,
                                                          op0=ALU.mult, op1=ALU.add),
                         reads=[ssq.b], writes=[rstd.b])
                    P.op("act", mk("activation", out=rstd[0:nt, :], in_=rstd[0:nt, :], func=AF.Sqrt),
                         reads=[rstd.b], writes=[rstd.b])
                    P.op("dve", mk("reciprocal", out=rstd[0:nt, :], in_=rstd[0:nt, :]),
                         reads=[rstd.b], writes=[rstd.b])
                    P.op("dve", mk("scalar_tensor_tensor", out=y[0:nt, :], in0=x[0:nt, :], scalar=rstd[0:nt, 0:1],
                                                                      in1=gfbc[0:nt, :], op0=ALU.mult, op1=ALU.mult),
                         reads=[x.b, rstd.b, gfbc.b], writes=[y.b])
                    yo = seq["y_out"]
                    P.dma("y%d" % slot, mk("dma_start", out=yo[i * 128:i * 128 + nt, :], in_=y[0:nt, :]),
                          reads=[y.b])
            loads2(0, gtile2[0])
            pend = None
            for i in range(ntl):
                cA = stage2A(i)
                if pend is not None:
                    stage2B(pend)
                pend = cA
            stage2B(pend)
        while prefetch:
            prefetch.pop(0)()
        P.barrier()
        xt.pop()
        st2.close()

    P.final_wait()
    P.emit()
    stack0.close()
    return nc


_CACHE = {}


def kernel(x_prompt, x_sample, cache_k, cache_v, cache_kidx, state_pool, norm_g, w_in, w_pool_mix,
           pool_scale, w_pool_out, w_attn_out, w_o, final_norm_g):
    NCORES = 8
    f = lambda a: np.ascontiguousarray(np.asarray(a, dtype=np.float32))
    x_prompt, x_sample = f(x_prompt), f(x_sample)
    cache_k, cache_v, cache_kidx, state_pool = f(cache_k), f(cache_v), f(cache_kidx), f(state_pool)
    BP, T, _ = x_prompt.shape
    BS, TS, _ = x_sample.shape
    DEPTH = cache_k.shape[0]
    L0 = cache_k.shape[2]
    assert BP % NCORES == 0 and BS == NCORES
    NP = BP // NCORES
    cfg = dict(NP=NP, T=T, TS=TS, L0=L0, DEPTH=DEPTH, KP=min(256, T // 4), KS=min(256, (L0 + TS) // 4))
    key = tuple(sorted(cfg.items()))
    if key not in _CACHE:
        _CACHE[key] = build(cfg)
    nc = _CACHE[key]

    rope = _rope_table(list(range(T)) + list(range(L0, L0 + TS)))
    bands = _band_tables().reshape(4, 128, 512)
    pw2 = np.tile((2.0 ** -(np.arange(NI_BISECT, dtype=np.float64) + 1)).astype(np.float32)[None, :], (128, 1))
    shared = dict(norm_g=f(norm_g), w_in=f(w_in), w_mix=f(w_pool_mix), pscale=f(pool_scale), w_po=f(w_pool_out),
                  w_ao=f(w_attn_out), w_o=f(w_o), gfin=f(final_norm_g).reshape(1, D), rope=rope, bands=bands, pw2=pw2)
    in_maps = []
    for c in range(NCORES):
        m = dict(shared)
        m["x_p"] = x_prompt[c * NP:(c + 1) * NP]
        m["x_s"] = x_sample[c]
        m["ck"] = cache_k[:, c].reshape(DEPTH, L0, 512)
        m["cv"] = cache_v[:, c].reshape(DEPTH, L0, 512)
        m["cki"] = cache_kidx[:, c]
        m["spool"] = state_pool[:, c]
        in_maps.append(m)
    res = run_bass_kernel_spmd(nc, in_maps, core_ids=list(range(NCORES)))
    R = res.results
    cat = lambda name, ax: np.concatenate([np.asarray(r[name]) for r in R], axis=ax)
    stk = lambda name, ax: np.stack([np.asarray(r[name]) for r in R], axis=ax)
    y_prompt = cat("y_p", 0)
    y_sample = stk("y_s", 0)
    nk_p = cat("ok_p", 1).reshape(DEPTH, BP, T, 8, 64)
    nv_p = cat("ov_p", 1).reshape(DEPTH, BP, T, 8, 64)
    nki_p = cat("oki_p", 1)
    npl_p = cat("opl_p", 1)
    nk_s = stk("ok_s", 1).reshape(DEPTH, BS, TS, 8, 64)
    nv_s = stk("ov_s", 1).reshape(DEPTH, BS, TS, 8, 64)
    nki_s = stk("oki_s", 1)
    npl_s = stk("opl_s", 1)
    return (y_prompt, y_sample, nk_p, nv_p, nki_p, npl_p, nk_s, nv_s, nki_s, npl_s)
```

```python
from contextlib import ExitStack
import os
import numpy as np
import concourse.bass as bass
import concourse.mybir as mybir
from concourse.bass_utils import run_bass_kernel_spmd

F32 = mybir.dt.float32
BF16 = mybir.dt.bfloat16
ALU = mybir.AluOpType
AF = mybir.ActivationFunctionType

ENGS = ("pe", "act", "dve", "pool", "sp")

D = 1024
NCOL = 5416
C1 = 3368
C2 = NCOL - C1
NH = 8
NEG = -1.0e30
NI_BISECT = int(os.environ.get('KNI', '24'))
EPS = 1e-6


class Buf:
    __slots__ = ("name", "w", "r", "excl")

    def __init__(self, name):
        self.name = name
        self.excl = False
        self.w = None
        self.r = []


class Prog:
    def __init__(self, nc):
        self.nc = nc
        self.q = {e: [] for e in ENGS}
        self.tick = {e: 0 for e in ENGS}
        self.sem = {e: nc.alloc_semaphore("sem_" + e) for e in ENGS}
        self.seen = {e: {} for e in ENGS}
        self.dsem = {}
        self.dtick = {}
        self.nbuf = 0
        self.count = 0
        self.limit = int(os.environ.get("KLIMIT", "1000000000"))

    def buf(self, name=None):
        self.nbuf += 1
        return Buf(name or "b%d" % self.nbuf)

    def _need(self, reads, writes):
        need = {}
        for b in reads:
            if b.w is not None:
                s, t = b.w
                if need.get(s, 0) < t:
                    need[s] = t
        for b in writes:
            if b.w is not None:
                s, t = b.w
                if need.get(s, 0) < t:
                    need[s] = t
            for s, t in b.r:
                if need.get(s, 0) < t:
                    need[s] = t
        return need

    def _waits(self, eng, need):
        waits = []
        seen = self.seen[eng]
        for s, t in need.items():
            if s == eng and eng in ("pe", "sp"):
                continue
            if seen.get(s, 0) >= t:
                continue
            seen[s] = t
            waits.append((s, t))
        return waits

    def _mark(self, src, reads, writes):
        for b in reads:
            if len(b.r) > 24:
                d = {}
                for s, t in b.r:
                    if d.get(s, 0) < t:
                        d[s] = t
                b.r = list(d.items())
            b.r.append(src)
        for b in writes:
            b.w = src
            b.r = []

    def op(self, eng, fn, reads=(), writes=()):
        self.count += 1
        if self.count > self.limit:
            return
        xr = [b for b in reads if b.excl]
        if xr:
            writes = list(writes) + [b for b in xr if b not in writes]
        waits = self._waits(eng, self._need(reads, writes))
        self.tick[eng] += 1
        my = self.tick[eng]
        self.q[eng].append((waits, fn, ("e", eng)))
        self._mark((eng, my), reads, writes)

    def dma(self, stream, fn, reads=(), writes=(), eng="sp"):
        stream = (writes[0] if writes else reads[0]).name
        self.count += 1
        if self.count > self.limit:
            return
        if stream not in self.dsem:
            self.dsem[stream] = self.nc.alloc_semaphore("dsem_" + stream)
            self.dtick[stream] = 0
        waits = self._waits(eng, self._need(reads, writes))
        self.dtick[stream] += 16
        my = self.dtick[stream]
        self.q[eng].append((waits, fn, ("d", stream)))
        self._mark(("dma:" + stream, my), reads, writes)

    def _semof(self, s):
        if s.startswith("dma:"):
            return self.dsem[s[4:]]
        return self.sem[s]

    def barrier(self):
        need = {}
        for e in ENGS:
            if self.tick[e] > 0:
                need[e] = self.tick[e]
        for s, t in self.dtick.items():
            if t > 0:
                need["dma:" + s] = t
        for e in ENGS:
            waits = self._waits(e, dict(need))
            if waits:
                self.q[e].append((waits, None, None))

    def final_wait(self, eng="sp"):
        need = {}
        for s, t in self.dtick.items():
            if t > 0:
                need["dma:" + s] = t
        for e in ENGS:
            if e != eng and self.tick[e] > 0:
                need[e] = self.tick[e]
        waits = self._waits(eng, need)
        if waits:
            self.q[eng].append((waits, None, None))

    def emit(self):
        nc = self.nc
        engobj = {"pe": "tensor", "act": "scalar", "dve": "vector", "pool": "gpsimd", "sp": "sync"}
        with nc.Block() as block:
            for e in ENGS:
                items = self.q[e]
                if not items:
                    continue

                def body(eo, items=items):
                    for waits, fn, inc in items:
                        for s, t in waits:
                            eo.wait_ge(self._semof(s), t)
                        if fn is None:
                            continue
                        ins = fn(eo)
                        if inc[0] == "e":
                            ins.then_inc(self.sem[inc[1]], 1)
                        else:
                            ins.then_inc(self.dsem[inc[1]], 16)

                getattr(block, engobj[e])(body)


def mk(meth, **kw):
    return lambda e: getattr(e, meth)(**kw)


class Tl:
    def __init__(self, t, b):
        self.t = t
        self.b = b

    def __getitem__(self, k):
        return self.t[k]


def _rope_table(positions):
    pos = np.asarray(positions, dtype=np.float32)
    out = np.zeros((len(pos), 192), np.float32)
    for (d, off) in ((64, 0), (32, 128)):
        inv = (10000.0 ** (-np.arange(0, d, 2, dtype=np.float32) / np.float32(d))).astype(np.float32)
        ang = (pos[:, None] * inv[None, :]).astype(np.float32).astype(np.float64)
        c = np.cos(ang).astype(np.float32)
        s = np.sin(ang).astype(np.float32)
        h = d // 2
        out[:, off:off + h] = c
        out[:, off + h:off + 2 * h] = c
        out[:, off + 2 * h:off + 3 * h] = -s
        out[:, off + 3 * h:off + 4 * h] = s
    return out


def _band_tables():
    B = np.zeros((4, 128, 4, 128), np.float32)
    for g, w in enumerate((2, 4, 8, 16)):
        for t in range(128):
            for tp in range(max(0, t - w + 1), t + 1):
                B[0, tp, g, t] += 1.0 / min(t + 1, w)
                B[1, tp, g, t] += 1.0 / w
            B[0, t, g, t] -= 1.0
            B[1, t, g, t] -= 1.0
            for tp in range(128):
                if t - (tp - 128) < w:
                    B[2, tp, g, t] = 1.0 / w
            for j in range(15):
                if t - (j - 15) < w:
                    B[3, j, g, t] = 1.0 / w
    return B


def build(cfg):
    NP, T, TS, L0, DEPTH = cfg["NP"], cfg["T"], cfg["TS"], cfg["L0"], cfg["DEPTH"]
    KP, KS = cfg["KP"], cfg["KS"]
    assert T % 128 == 0 and TS == 64 and L0 % 128 == 0
    NTP = T // 128
    LS = L0 + TS
    NCS = L0 // 128 + 1
    SMAX = max(T, LS)
    KTW = max(4 * T, 2 * LS)
    VS = int(os.environ.get('KVS', '65'))
    VW = max(NTP * 8 * VS, NCS * 4 * VS)
    NTOK = NP * T + TS
    NTILES = NP * NTP + 1

    nc = bass.Bass("TRN2", target_bir_lowering=False, dynamic_dma_scratch_size=256)
    P = Prog(nc)

    def din(name, shape, dt=F32):
        return nc.dram_tensor(name, list(shape), dt, kind="ExternalInput").ap()

    def dout(name, shape, dt=F32):
        return nc.dram_tensor(name, list(shape), dt, kind="ExternalOutput").ap()

    x_p = din("x_p", [NP, T, D])
    x_s = din("x_s", [TS, D])
    ck = din("ck", [DEPTH, L0, 512])
    cv = din("cv", [DEPTH, L0, 512])
    cki = din("cki", [DEPTH, L0, 32])
    spool = din("spool", [DEPTH, 15, 512])
    norm_g = din("norm_g", [DEPTH, D])
    w_in = din("w_in", [DEPTH, D, NCOL])
    w_mix = din("w_mix", [DEPTH, 4, 128, 128])
    pscale = din("pscale", [DEPTH, 512])
    w_po = din("w_po", [DEPTH, 512, D])
    w_ao = din("w_ao", [DEPTH, 512, D])
    w_o = din("w_o", [DEPTH, D, D])
    gfin = din("gfin", [1, D])
    rope = din("rope", [T + TS, 192])
    bands = din("bands", [4, 128, 512])
    pw2 = din("pw2", [128, NI_BISECT])

    y_p = dout("y_p", [NP, T, D])
    y_s = dout("y_s", [TS, D])
    ok_p = dout("ok_p", [DEPTH, NP, T, 512])
    ov_p = dout("ov_p", [DEPTH, NP, T, 512])
    oki_p = dout("oki_p", [DEPTH, NP, T, 32])
    opl_p = dout("opl_p", [DEPTH, NP, 15, 512])
    ok_s = dout("ok_s", [DEPTH, TS, 512])
    ov_s = dout("ov_s", [DEPTH, TS, 512])
    oki_s = dout("oki_s", [DEPTH, TS, 32])
    opl_s = dout("opl_s", [DEPTH, 15, 512])

    xscr = nc.dram_tensor("xscr", [NTOK, D], F32).ap()
    gp_scr = nc.dram_tensor("gp_scr", [NTILES, 128, 512], BF16).ap()
    ga_scr = nc.dram_tensor("ga_scr", [NTILES, 128, 512], BF16).ap()

    seqs = []
    for s in range(NP):
        seqs.append(dict(kind="p", idx=s, T=T, nt=128, ntiles=NTP, tok0=s * T, tile0=s * NTP,
                         x_in=x_p[s], y_out=y_p[s], K=KP, rope0=0))
    seqs.append(dict(kind="s", idx=0, T=TS, nt=TS, ntiles=1, tok0=NP * T, tile0=NP * NTP,
                     x_in=x_s, y_out=y_s, K=KS, rope0=T))

    stack0 = ExitStack()

    uniq = [0]

    def alloc(st, name, shape, dt):
        uniq[0] += 1
        t = st.enter_context(nc.sbuf_tensor("%s_%d" % (name, uniq[0]), list(shape), dt))
        return Tl(t, P.buf(name))

    def palloc(name, shape, dt):
        t = nc.alloc_psum_tensor(name, list(shape), dt)
        b = P.buf(name)
        b.excl = True
        return Tl(t, b)

    mm = [palloc("mm%d" % i, [128, 512], F32) for i in range(2)]
    tp = palloc("tp", [128, 1024], BF16)
    pp = palloc("pp", [128, 512], F32)
    Lb = [palloc("L%d" % i, [128, 512], F32) for i in range(2)]
    Ob = [palloc("O%d" % i, [128, 4, VS], F32) for i in range(2)]
    mmi = [0]

    def next_mm():
        mmi[0] ^= 1
        return mm[mmi[0]]

    ident = alloc(stack0, "ident", [128, 128], BF16)
    identf = alloc(stack0, "identf", [128, 128], F32)
    ones1 = alloc(stack0, "ones1", [1, 128], F32)
    gfbc = alloc(stack0, "gfbc", [128, D], F32)
    band = alloc(stack0, "band", [128, 4, 512], BF16)
    pw = alloc(stack0, "pw", [128, NI_BISECT], F32)
    xt = [alloc(stack0, "xt%d" % i, [128, D], F32) for i in range(2)]
    xn = alloc(stack0, "xn", [128, D], BF16)
    hT = alloc(stack0, "hT", [128, 8, 128], BF16)
    ssq = alloc(stack0, "ssq", [128, 1], F32)
    rstd = alloc(stack0, "rstd", [128, 1], F32)
    gcol = alloc(stack0, "gcol", [128, 8], F32)
    W1 = alloc(stack0, "W1", [128, 8, C1], BF16)
    ftmp = [alloc(stack0, "ftmp%d" % i, [128, 512], F32) for i in range(2)]
    ftmpi = [0]

    def next_ftmp():
        ftmpi[0] ^= 1
        return ftmp[ftmpi[0]]

    P.op("pool", mk("memset", ap=identf[:], constant=0.0), writes=[identf.b])
    P.op("pool", mk("affine_select", out=identf[:], in_=identf[:], pattern=[[-1, 128]],
                                           compare_op=ALU.not_equal, fill=1.0, base=0,
                                           channel_multiplier=1),
         reads=[identf.b], writes=[identf.b])
    P.op("dve", mk("tensor_copy", out=ident[:], in_=identf[:]), reads=[identf.b], writes=[ident.b])
    P.op("dve", mk("memset", ap=ones1[:], constant=1.0), writes=[ones1.b])
    P.dma("c0", mk("dma_start", out=pw[:], in_=pw2), writes=[pw.b])
    for kind in range(4):
        f = next_ftmp()
        P.dma("c1", mk("dma_start", out=f[:], in_=bands[kind]), writes=[f.b])
        P.op("dve", mk("tensor_copy", out=band[:, kind, :], in_=f[:]),
             reads=[f.b], writes=[band.b])
    grow = alloc(stack0, "grow", [1, D], F32)
    P.dma("c0", mk("dma_start", out=grow[:], in_=gfin), writes=[grow.b])
    for hh in range(2):
        m = next_mm()
        P.op("pe", mk("matmul", out=m[:], lhsT=ones1[0:1, :], rhs=grow[0:1, hh * 512:(hh + 1) * 512],
                                                  start=True, stop=True),
             reads=[ones1.b, grow.b], writes=[m.b])
        P.op("dve", mk("tensor_copy", out=gfbc[:, hh * 512:(hh + 1) * 512], in_=m[:]),
             reads=[m.b], writes=[gfbc.b])

    def load_gcol(l):
        g8 = next_ftmp()
        P.dma("c1", mk("dma_start", out=g8[0:8, 0:128], in_=norm_g[l].rearrange("(k p) -> k p", p=128)),
              writes=[g8.b])
        m = next_mm()
        P.op("pe", mk("transpose", out=m[:, 0:8], in_=g8[0:8, 0:128], identity=identf[0:8, 0:8]),
             reads=[g8.b, identf.b], writes=[m.b])
        P.op("dve", mk("tensor_copy", out=gcol[:], in_=m[:, 0:8]), reads=[m.b], writes=[gcol.b])

    def load_cast(dst_ap_fn, src_rows_fn, nk, ncols, stage, scale_gcol, dstb):
        for k in range(nk):
            s = stage[k % 2]
            P.dma("wst%d" % (k % 2), mk("dma_start", out=s[:, 0:ncols], in_=src_rows_fn(k)),
                  writes=[s.b])
            if scale_gcol:
                if k % 2 == 0:
                    P.op("dve", mk("tensor_scalar", out=dst_ap_fn(k), in0=s[:, 0:ncols],
                                                                    scalar1=gcol[:, k:k + 1], scalar2=None,
                                                                    op0=ALU.mult),
                         reads=[s.b, gcol.b], writes=[dstb])
                else:
                    P.op("act", mk("activation", out=dst_ap_fn(k), in_=s[:, 0:ncols],
                                                                 func=AF.Copy, scale=gcol[:, k:k + 1]),
                         reads=[s.b, gcol.b], writes=[dstb])
            else:
                if k % 2 == 0:
                    P.op("dve", mk("tensor_copy", out=dst_ap_fn(k), in_=s[:, 0:ncols]),
                         reads=[s.b], writes=[dstb])
                else:
                    P.op("act", mk("copy", out=dst_ap_fn(k), in_=s[:, 0:ncols]),
                         reads=[s.b], writes=[dstb])

    def load_x(seq, i, slot, src):
        nt = seq["nt"]
        r0 = i * 128
        P.dma("x%d" % slot, mk("dma_start", out=xt[slot][0:nt, :], in_=src[r0:r0 + nt, :]),
              writes=[xt[slot].b])

    def norm_hT(seq, slot):
        nt = seq["nt"]
        x = xt[slot]
        P.op("act", mk("activation", out=xn[0:nt, 0:D], in_=x[0:nt, :], func=AF.Square,
                                           accum_out=ssq[0:nt, :]),
             reads=[x.b], writes=[xn.b, ssq.b])
        P.op("dve", mk("tensor_scalar", out=rstd[0:nt, :], in0=ssq[0:nt, :], scalar1=1.0 / D, scalar2=EPS,
                                              op0=ALU.mult, op1=ALU.add),
             reads=[ssq.b], writes=[rstd.b])
        P.op("act", mk("activation", out=rstd[0:nt, :], in_=rstd[0:nt, :], func=AF.Sqrt),
             reads=[rstd.b], writes=[rstd.b])
        P.op("dve", mk("reciprocal", out=rstd[0:nt, :], in_=rstd[0:nt, :]),
             reads=[rstd.b], writes=[rstd.b])
        P.op("dve", mk("tensor_scalar", out=xn[0:nt, :], in0=x[0:nt, :], scalar1=rstd[0:nt, 0:1],
                                              scalar2=None, op0=ALU.mult),
             reads=[x.b, rstd.b], writes=[xn.b])
        tv = tp[:].rearrange("p (k t) -> p k t", t=128)
        for k in range(8):
            P.op("pe", mk("transpose", out=tv[:, k, 0:nt], in_=xn[0:nt, k * 128:(k + 1) * 128],
                                                  identity=ident[0:nt, 0:nt]),
                 reads=[xn.b, ident.b], writes=[tp.b])
        P.op("act", mk("copy", out=hT[:, 0:4, 0:nt], in_=tv[:, 0:4, 0:nt]), reads=[tp.b], writes=[hT.b])
        P.op("dve", mk("tensor_copy", out=hT[:, 4:8, 0:nt], in_=tv[:, 4:8, 0:nt]), reads=[tp.b], writes=[hT.b])

    def proj(nt, W, c0, ncols):
        m = next_mm()
        for k in range(8):
            P.op("pe", mk("matmul", out=m[0:nt, 0:ncols], lhsT=hT[:, k, 0:nt], rhs=W[:, k, c0:c0 + ncols],
                                               start=(k == 0), stop=(k == 7)),
                 reads=[hT.b, W.b], writes=[m.b])
        return m

    def transpose_to(src, srcb, nt, ncols_list, dst_fn, dstb, evac_eng="act"):
        tv = tp[:].rearrange("p (k t) -> p k t", t=128)
        for j, (c0, wd) in enumerate(ncols_list):
            P.op("pe", mk("transpose", out=tv[0:wd, j, 0:nt], in_=src[0:nt, c0:c0 + wd],
                                                                identity=ident[0:nt, 0:nt]),
                 reads=[srcb, ident.b], writes=[tp.b])
        wmax = max(w for _, w in ncols_list)
        n = len(ncols_list)
        if evac_eng == "none":
            return tv
        if evac_eng == "actbias":
            P.op("act", mk("activation", out=dst_fn(wmax, n), in_=tv[0:wmax, 0:n, 0:nt], func=AF.Copy,
                           scale=30000.0, bias=-30000.0), reads=[tp.b], writes=[dstb])
        elif evac_eng == "act":
            P.op("act", mk("copy", out=dst_fn(wmax, n), in_=tv[0:wmax, 0:n, 0:nt]), reads=[tp.b], writes=[dstb])
        else:
            P.op("dve", mk("tensor_copy", out=dst_fn(wmax, n), in_=tv[0:wmax, 0:n, 0:nt]),
                 reads=[tp.b], writes=[dstb])

    for l in range(DEPTH):
        x_src_of = (lambda seq: seq["x_in"]) if l == 0 else (lambda seq: xscr[seq["tok0"]:seq["tok0"] + seq["T"], :])
        last = (l == DEPTH - 1)

        st1 = ExitStack()
        Wmix = alloc(st1, "Wmix", [128, 4, 128], BF16)
        with ExitStack() as stl:
            if l == 0:
                stage = [alloc(stl, "stg%d" % i, [128, C1], F32) for i in range(2)]
                load_gcol(l)
                load_cast(lambda k: W1[:, k, :], lambda k: w_in[l, k * 128:(k + 1) * 128, 0:C1], 8, C1, stage, True, W1.b)
            srow = next_ftmp()
            P.dma("c1", mk("dma_start", out=srow[0:1, :], in_=pscale[l:l + 1, :]), writes=[srow.b])
            m = next_mm()
            P.op("pe", mk("matmul", out=m[:], lhsT=ones1[0:1, :], rhs=srow[0:1, :], start=True, stop=True),
                 reads=[ones1.b, srow.b], writes=[m.b])
            s0 = next_ftmp()
            P.dma("wst0", mk("dma_start", out=s0[:, 0:512].rearrange("p (g d) -> p g d", g=4),
                                                in_=w_mix[l].rearrange("g c d -> c g d")), writes=[s0.b])
            P.op("dve", mk("tensor_tensor", out=Wmix[:].rearrange("p g d -> p (g d)"), in0=s0[:, 0:512],
                                                  in1=m[:], op=ALU.mult),
                 reads=[s0.b, m.b], writes=[Wmix.b])
            P.barrier()

        kT = alloc(st1, "kT", [128, KTW], BF16)
        Vg = alloc(st1, "Vg", [128, VW], BF16)
        kiT = alloc(st1, "kiT", [96, SMAX], BF16)
        MH = max(T, (LS + 1) // 2)
        score = alloc(st1, "score", [128, 2 * MH], F32)
        sc_bufs = [score.b, P.buf("score1")]
        maskT = alloc(st1, "maskT", [128, max(NTP * 128, NCS * 64)], BF16)
        utok = [alloc(st1, "utok%d" % i, [128, 512], BF16) for i in range(2)]
        szp = alloc(st1, "szp", [128, 512], BF16)
        szas = [alloc(st1, "sza%d" % i, [128, 512], BF16) for i in range(3)]
        qf = alloc(st1, "qf", [128, 512], F32)
        kf = [alloc(st1, "kf%d" % i, [128, 512], F32) for i in range(2)]
        vf = [alloc(st1, "vf%d" % i, [128, 512], F32) for i in range(2)]
        g5 = [alloc(st1, "g5%d" % i, [128, 296], F32) for i in range(2)]
        uf = ftmp[0]
        rtab = [alloc(st1, "rtab%d" % i, [128, 192], F32) for i in range(2)]
        rtmp = alloc(st1, "rtmp", [128, 512], F32)
        qb = alloc(st1, "qb", [128, 512], BF16)
        kb = alloc(st1, "kb", [128, 512], BF16)
        qsb = alloc(st1, "qsb", [128, 256], BF16)
        ki3 = alloc(st1, "ki3", [128, 96], BF16)
        Dg = alloc(st1, "Dg", [128, 8, 128], BF16)
        plT = alloc(st1, "plT", [128, 4, 128], BF16)
        gpt = alloc(st1, "gpt", [128, 512], BF16)
        gpT = [alloc(st1, "gpT%d" % i, [128, 4, 128], BF16) for i in range(2)]
        gaT = [alloc(st1, "gaT%d" % i, [128, 4, 128], BF16) for i in range(2)]
        qTs = [alloc(st1, "qT%d" % i, [128, 8, 128], BF16) for i in range(3)]
        qiT = alloc(st1, "qiT", [96, 3, 128], BF16)
        rel = [alloc(st1, "rel%d" % i, [128, 512], BF16) for i in range(3)]
        maskb = alloc(st1, "maskb", [128, 2 * MH], BF16)
        mb_bufs = [maskb.b, P.buf("maskb1")]
        praw = [alloc(st1, "praw%d" % i, [128, 512], BF16) for i in range(3)]
        on = alloc(st1, "on", [128, 512], F32)
        gat = alloc(st1, "gat", [128, 512], BF16)
        rs = alloc(st1, "rs", [128, 8], F32)
        bs_lo = alloc(st1, "bs_lo", [128, 1], F32)
        bs_hi = alloc(st1, "bs_hi", [128, 1], F32)
        bs_mid = [alloc(st1, "bs_mid%d" % i, [128, 1], F32) for i in range(2)]
        bs_cnt = alloc(st1, "bs_cnt", [128, 1], F32)
        bs_cnt2 = alloc(st1, "bs_cnt2", [128, 1], F32)
        bs_u = alloc(st1, "bs_u", [128, 1], F32)
        bs_ht = alloc(st1, "bs_ht", [128, NI_BISECT], F32)
        bs_thr = alloc(st1, "bs_thr", [128, 1], F32)
        cst = [alloc(st1, "cst%d" % i, [128, 2, 256], F32) for i in range(2)]
        cbf = alloc(st1, "cbf", [128, 2, 256], BF16)
        spf = alloc(st1, "spf", [16, 512], F32)
        spb = alloc(st1, "spb", [16, 512], BF16)
        ckif = alloc(st1, "ckif", [128, max(L0 // 128, 1), 32], F32)
        cki3 = alloc(st1, "cki3", [128, 96], BF16)

        rot = {"rel": 0, "mk": 0, "praw": 0, "pm": 0, "L": 0}

        def nxt(name, arr):
            rot[name] ^= 1
            return arr[rot[name]]

        P.op("pool", mk("memset", ap=Vg[:], constant=1.0), writes=[Vg.b])
        for qz in qTs:
            P.op("pool", mk("memset", ap=qz[:], constant=0.0), writes=[qz.b])

        def rope_inplace(t, nt, nh, hd, tab, toff, eng="pool"):
            h2 = hd // 2
            xv = lambda: t.rearrange("p (h d) -> p h d", d=hd)
            tv = lambda: rtmp[0:nt, 0:nh * hd].rearrange("p (h d) -> p h d", d=hd)
            cc = lambda: tab[0:nt, toff:toff + hd].unsqueeze(1).broadcast_to([nt, nh, hd])
            sn = lambda: tab[0:nt, toff + hd:toff + hd + h2].unsqueeze(1).broadcast_to([nt, nh, h2])
            sp_ = lambda: tab[0:nt, toff + hd + h2:toff + 2 * hd].unsqueeze(1).broadcast_to([nt, nh, h2])
            return xv, tv, cc, sn, sp_, h2

        def do_rope(tl, col0, nt, nh, hd, tab, toff, eng):
            t = tl[0:nt, col0:col0 + nh * hd]
            xv, tv, cc, sn, sp_, h2 = rope_inplace(t, nt, nh, hd, tab, toff)
            P.op(eng, mk("tensor_tensor", out=tv()[:, :, 0:h2], in0=xv()[:, :, h2:hd], in1=sn(), op=ALU.mult),
                 reads=[tl.b, tab.b], writes=[rtmp.b])
            P.op(eng, mk("tensor_tensor", out=tv()[:, :, h2:hd], in0=xv()[:, :, 0:h2], in1=sp_(), op=ALU.mult),
                 reads=[tl.b, tab.b], writes=[rtmp.b])
            P.op(eng, mk("tensor_tensor", out=xv(), in0=xv(), in1=cc(), op=ALU.mult),
                 reads=[tl.b, tab.b], writes=[tl.b])
            P.op(eng, mk("tensor_tensor", out=xv(), in0=xv(), in1=tv(), op=ALU.add),
                 reads=[tl.b, rtmp.b], writes=[tl.b])

        gtile = [0]

        for seq in seqs:
            nt = seq["nt"]
            ntl = seq["ntiles"]
            isS = seq["kind"] == "s"
            xsrc = x_src_of(seq)
            Ksel = seq["K"]
            if not isS:
                kTv = kT[:, 0:4 * T].rearrange("p (c s) -> p c s", c=4)
                Vv = Vg[:, 0:NTP * 8 * VS].rearrange("p (t h d) -> p t h d", h=8, d=VS)
                okd, ovd, okid, opld = ok_p[l, seq["idx"]], ov_p[l, seq["idx"]], oki_p[l, seq["idx"]], opl_p[l, seq["idx"]]
            else:
                kTv = kT[:, 0:2 * LS].rearrange("p (c s) -> p c s", c=2)
                Vv = Vg[:, 0:NCS * 4 * VS].rearrange("p (t h d) -> p t h d", h=4, d=VS)
                okd, ovd, okid, opld = ok_s[l], ov_s[l], oki_s[l], opl_s[l]

            if isS and L0 > 0:
                nct = L0 // 128
                P.dma("cki", mk("dma_start", out=ckif[:, 0:nct, :],
                                                   in_=cki[l].rearrange("(t p) d -> p t d", p=128)),
                      writes=[ckif.b])
                for c in range(nct):
                    P.op("dve", mk("tensor_copy",
                        out=cki3[:].rearrange("p (r d) -> p r d", r=3),
                        in_=ckif[:, c, :].unsqueeze(1).broadcast_to([128, 3, 32])),
                        reads=[ckif.b], writes=[cki3.b])
                    transpose_to(cki3, cki3.b, 128, [(0, 96)],
                                 lambda wm, n, c=c: kiT[0:96, c * 128:(c + 1) * 128].unsqueeze(1), kiT.b,
                                 evac_eng="act" if c % 2 else "dve")
                P.dma("spf", mk("dma_start", out=spf[0:15, :], in_=spool[l]), writes=[spf.b])
                P.op("dve", mk("tensor_copy", out=spb[0:15, :], in_=spf[0:15, :]), reads=[spf.b], writes=[spb.b])

            def stageA1(i):
                qT = qTs[gtile[0] % 3]
                sza = szas[gtile[0] % 3]
                gi = gtile[0]
                gtile[0] += 1
                slot = gi % 2
                if i + 1 < ntl:
                    load_x(seq, i + 1, (gi + 1) % 2, xsrc)
                rt = rtab[gi % 2]
                rp0 = seq["rope0"] + i * 128
                P.dma("rt%d" % (gi % 2), mk("dma_start", out=rt[0:nt, :], in_=rope[rp0:rp0 + nt, :]),
                      writes=[rt.b])
                norm_hT(seq, slot)
                key0 = (L0 if isS else 0) + i * 128
                S = key0 + nt
                if isS:
                    sview = score[:, 0:S]
                    sbufs = [sc_bufs[0], sc_bufs[1]]
                else:
                    sview = score[:, (gtile[0] - 1) % 2 * MH:(gtile[0] - 1) % 2 * MH + S]
                    sbufs = [sc_bufs[(gtile[0] - 1) % 2]]
                ucur = utok[gi % 2]
                uprev = utok[(gi + 1) % 2]
                kfi = kf[gi % 2]
                vfi = vf[gi % 2]
                g5i = g5[gi % 2]

                yield
                m = proj(nt, W1, 0, 512)
                P.op("act", mk("copy", out=ucur[0:nt, :], in_=m[0:nt, :]), reads=[m.b], writes=[ucur.b])
                if i == ntl - 1:
                    P.op("dve", mk("tensor_copy", out=uf[0:nt, :], in_=m[0:nt, :]), reads=[m.b], writes=[uf.b])
                    P.dma("opl", mk("dma_start", out=opld, in_=uf[nt - 15:nt, :]), reads=[uf.b])
                yield
                m = proj(nt, W1, 512, 512)
                P.op("act", mk("activation", out=szp[0:nt, :], in_=m[0:nt, :], func=AF.Silu),
                     reads=[m.b], writes=[szp.b])
                yield
                m = proj(nt, W1, 1024, 512)
                P.op("act", mk("copy", out=qf[0:nt, :], in_=m[0:nt, :]), reads=[m.b], writes=[qf.b])
                yield
                m = proj(nt, W1, 1536, 512)
                P.op("act", mk("copy", out=kfi[0:nt, :], in_=m[0:nt, :]), reads=[m.b], writes=[kfi.b])
                yield
                m = proj(nt, W1, 2048, 512)
                P.op("act", mk("copy", out=vfi[0:nt, :], in_=m[0:nt, :]), reads=[m.b], writes=[vfi.b])
                if not isS:
                    P.op("dve", mk("tensor_copy", out=Vv[0:nt, i, :, 0:64],
                                   in_=m[0:nt, :].rearrange("p (h d) -> p h d", d=64)),
                         reads=[m.b], writes=[Vg.b])
                P.dma("ov%d" % (gi % 2), mk("dma_start", out=ovd[i * 128:i * 128 + nt, :], in_=vfi[0:nt, :]),
                      reads=[vfi.b])
                yield
                m = proj(nt, W1, 2560, 296)
                P.op("act", mk("copy", out=g5i[0:nt, :], in_=m[0:nt, 0:296]), reads=[m.b], writes=[g5i.b])
                yield
                m = proj(nt, W1, 2856, 512)
                P.op("act", mk("activation", out=sza[0:nt, :], in_=m[0:nt, :], func=AF.Silu),
                     reads=[m.b], writes=[sza.b])

                yield
                yield
                do_rope(qf, 0, nt, 8, 64, rt, 0, "pool")
                yield
                do_rope(kfi, 0, nt, 8, 64, rt, 0, "pool")
                yield
                do_rope(g5i, 0, nt, 8, 32, rt, 128, "pool")
                do_rope(g5i, 256, nt, 1, 32, rt, 128, "pool")
                P.dma("ok%d" % (gi % 2), mk("dma_start", out=okd[i * 128:i * 128 + nt, :], in_=kfi[0:nt, :]),
                      reads=[kfi.b])
                P.dma("oki%d" % (gi % 2), mk("dma_start", out=okid[i * 128:i * 128 + nt, :], in_=g5i[0:nt, 256:288]),
                      reads=[g5i.b])
                P.op("pool", mk("tensor_copy", out=qb[0:nt, :], in_=qf[0:nt, :]), reads=[qf.b], writes=[qb.b])
                P.op("pool", mk("tensor_copy", out=kb[0:nt, :], in_=kfi[0:nt, :]), reads=[kfi.b], writes=[kb.b])
                for h in range(8):
                    P.op("dve", mk("tensor_scalar", out=Dg[0:nt, h, 0:nt], in0=identf[0:nt, 0:nt],
                                   scalar1=g5i[0:nt, 288 + h:289 + h], scalar2=None, op0=ALU.mult),
                         reads=[identf.b, g5i.b], writes=[Dg.b])
                P.op("pool", mk("tensor_copy", out=qsb[0:nt, :], in_=g5i[0:nt, 0:256]), reads=[g5i.b], writes=[qsb.b])
                P.op("dve", mk("tensor_copy", out=ki3[0:nt, :].rearrange("p (r d) -> p r d", r=3),
                                                    in_=g5i[0:nt, 256:288].unsqueeze(1).broadcast_to([nt, 3, 32])),
                     reads=[g5i.b], writes=[ki3.b])

                yield
                transpose_to(ki3, ki3.b, nt, [(0, 96)],
                             lambda wm, n: kiT[0:96, key0:key0 + nt].unsqueeze(1), kiT.b, "act")
                transpose_to(qsb, qsb.b, nt, [(0, 96), (96, 96), (192, 64)],
                             lambda wm, n: qiT[0:96, 0:3, 0:nt], qiT.b, "act")
                tvq = transpose_to(qb, qb.b, nt, [(c * 128, 128) for c in range(4)], None, None, "none")
                P.op("act", mk("copy", out=qT[0:64, 0:8:2, 0:nt], in_=tvq[0:64, 0:4, 0:nt]), reads=[tp.b], writes=[qT.b])
                P.op("act", mk("copy", out=qT[64:128, 1:8:2, 0:nt], in_=tvq[64:128, 0:4, 0:nt]), reads=[tp.b], writes=[qT.b])
                if not isS:
                    transpose_to(kb, kb.b, nt, [(c * 128, 128) for c in range(4)],
                                 lambda wm, n: kTv[:, 0:4, key0:key0 + nt], kT.b, "act")

                yield
                ppv = pp[:].rearrange("p (g t) -> p g t", g=4)
                for g in range(4):
                    if isS:
                        P.op("pe", mk("matmul", out=ppv[:, g, 0:nt], lhsT=ucur[0:nt, g * 128:(g + 1) * 128],
                                                           rhs=band[0:nt, 1, g * 128:g * 128 + nt], start=True, stop=False),
                             reads=[ucur.b, band.b], writes=[pp.b])
                        P.op("pe", mk("matmul", out=ppv[:, g, 0:nt], lhsT=spb[0:15, g * 128:(g + 1) * 128],
                                                           rhs=band[0:15, 3, g * 128:g * 128 + nt], start=False, stop=True),
                             reads=[spb.b, band.b], writes=[pp.b])
                    elif i == 0:
                        P.op("pe", mk("matmul", out=ppv[:, g, 0:nt], lhsT=ucur[0:nt, g * 128:(g + 1) * 128],
                                                           rhs=band[0:nt, 0, g * 128:g * 128 + nt], start=True, stop=True),
                             reads=[ucur.b, band.b], writes=[pp.b])
                    else:
                        P.op("pe", mk("matmul", out=ppv[:, g, 0:nt], lhsT=ucur[0:nt, g * 128:(g + 1) * 128],
                                                           rhs=band[0:nt, 1, g * 128:g * 128 + nt], start=True, stop=False),
                             reads=[ucur.b, band.b], writes=[pp.b])
                        P.op("pe", mk("matmul", out=ppv[:, g, 0:nt], lhsT=uprev[:, g * 128:(g + 1) * 128],
                                                           rhs=band[:, 2, g * 128:g * 128 + nt], start=False, stop=True),
                             reads=[uprev.b, band.b], writes=[pp.b])
                P.op("act", mk("copy", out=plT[:, :, 0:nt], in_=ppv[:, :, 0:nt]), reads=[pp.b], writes=[plT.b])
                m = next_mm()
                for g in range(4):
                    P.op("pe", mk("matmul", out=m[0:nt, g * 128:(g + 1) * 128], lhsT=plT[:, g, 0:nt],
                                                            rhs=Wmix[:, g, :], start=True, stop=True),
                         reads=[plT.b, Wmix.b], writes=[m.b])
                P.op("dve", mk("tensor_tensor", out=gpt[0:nt, :], in0=m[0:nt, :], in1=szp[0:nt, :], op=ALU.mult),
                     reads=[m.b, szp.b], writes=[gpt.b])
                gpo = gpT[gi % 2]
                transpose_to(gpt, gpt.b, nt, [(c * 128, 128) for c in range(4)],
                             lambda wm, n: gpo[:, 0:4, 0:nt], gpo.b, "act")
                tix = seq["tile0"] + i
                P.dma("gp%d" % (gi % 2), mk("dma_start",
                    out=gp_scr[tix].rearrange("p (c t) -> p c t", c=4)[:, :, 0:nt], in_=gpo[:, :, 0:nt]),
                    reads=[gpo.b])

                yield
                ngk = (S + 511) // 512
                for gk in range(ngk):
                    k0 = gk * 512
                    gw = min(512, S - k0)
                    prev = None
                    for h in range(8):
                        if h % 2 == 0:
                            yield
                        m = next_mm()
                        bp = 32 * (h % 3)
                        P.op("pe", mk("matmul", out=
                            m[0:nt, 0:gw], lhsT=qiT[bp:bp + 32, h // 3, 0:nt], rhs=kiT[bp:bp + 32, k0:k0 + gw],
                            start=True, stop=True),
                            reads=[qiT.b, kiT.b], writes=[m.b])
                        rot["rel"] = (rot["rel"] + 1) % 3
                        r = rel[rot["rel"]]
                        P.op("act", mk("activation", out=r[0:nt, 0:gw], in_=m[0:nt, 0:gw], func=AF.Relu),
                             reads=[m.b], writes=[r.b])
                        if prev is not None:
                            ph, prr = prev
                            P.op("pe", mk("matmul", out=pp[0:nt, 0:gw], lhsT=Dg[0:nt, ph, 0:nt], rhs=prr[0:nt, 0:gw],
                                          start=(ph == 0), stop=False),
                                 reads=[Dg.b, prr.b], writes=[pp.b])
                        prev = (h, r)
                    ph, prr = prev
                    P.op("pe", mk("matmul", out=pp[0:nt, 0:gw], lhsT=Dg[0:nt, ph, 0:nt], rhs=prr[0:nt, 0:gw],
                                  start=False, stop=True),
                         reads=[Dg.b, prr.b], writes=[pp.b])
                    P.op("act", mk("copy", out=sview[0:nt, k0:k0 + gw], in_=pp[0:nt, 0:gw]), reads=[pp.b], writes=sbufs)
                return dict(i=i, gi=gi, S=S, ngk=ngk, qT=qT, sza=sza, kfi=kfi, vfi=vfi, tix=tix,
                            sview=sview, sbufs=sbufs)

            def stageA2(c):
                gi, S, sview, sbufs = c["gi"], c["S"], c["sview"], c["sbufs"]
                nch = (S + 127) // 128
                if isS:
                    mview = maskb[:, 0:S]
                    mbufs = [mb_bufs[0], mb_bufs[1]]
                else:
                    mview = maskb[:, (gi % 2) * MH:(gi % 2) * MH + S]
                    mbufs = [mb_bufs[gi % 2]]
                c["nch"], c["mview"], c["mbufs"] = nch, mview, mbufs
                yield
                need_topk = S > Ksel
                if not isS:
                    if need_topk:
                        P.op("dve", mk("tensor_reduce", out=bs_lo[0:nt, :], in_=sview[0:nt, 0:S - 64],
                                       axis=mybir.AxisListType.X, op=ALU.min),
                             reads=sbufs, writes=[bs_lo.b])
                    P.op("dve", mk("memset", ap=sview[0:64, S - 64:S], constant=NEG), writes=sbufs)
                else:
                    if need_topk:
                        P.op("dve", mk("tensor_reduce", out=bs_lo[0:nt, :], in_=sview[0:nt, 0:S],
                                       axis=mybir.AxisListType.X, op=ALU.min),
                             reads=sbufs, writes=[bs_lo.b])
                if need_topk:
                    P.op("dve", mk("tensor_reduce", out=bs_hi[0:nt, :], in_=sview[0:nt, 0:S],
                                   axis=mybir.AxisListType.X, op=ALU.max),
                         reads=sbufs, writes=[bs_hi.b])
                    P.op("dve", mk("tensor_tensor", out=bs_hi[0:nt, :], in0=bs_hi[0:nt, :], in1=bs_lo[0:nt, :],
                                   op=ALU.subtract),
                         reads=[bs_hi.b, bs_lo.b], writes=[bs_hi.b])
                    P.op("dve", mk("tensor_scalar", out=bs_ht[0:nt, :], in0=pw[0:nt, :], scalar1=bs_hi[0:nt, 0:1],
                                   scalar2=None, op0=ALU.mult),
                         reads=[pw.b, bs_hi.b], writes=[bs_ht.b])
                    P.op("dve", mk("tensor_tensor", out=bs_mid[0][0:nt, :], in0=bs_lo[0:nt, :], in1=bs_ht[0:nt, 0:1],
                                   op=ALU.add),
                         reads=[bs_lo.b, bs_ht.b], writes=[bs_mid[0].b])
                    for it in range(NI_BISECT):
                        yield
                        mc = bs_mid[it % 2]
                        mn = bs_mid[(it + 1) % 2]
                        P.op("dve", mk("tensor_scalar", out=mview[0:nt, 0:S], in0=sview[0:nt, 0:S],
                                       scalar1=mc[0:nt, 0:1], scalar2=None, op0=ALU.is_ge, op1=ALU.add,
                                       accum_out=bs_cnt[0:nt, :]),
                             reads=sbufs + [mc.b], writes=mbufs + [bs_cnt.b])
                        P.op("dve", mk("tensor_scalar", out=bs_u[0:nt, :], in0=bs_cnt[0:nt, :],
                                       scalar1=float(Ksel) - 0.5, scalar2=0.5, op0=ALU.is_ge, op1=ALU.subtract),
                             reads=[bs_cnt.b], writes=[bs_u.b])
                        P.op("dve", mk("scalar_tensor_tensor", out=mn[0:nt, :], in0=bs_u[0:nt, :],
                                       scalar=bs_ht[0:nt, it:it + 1], in1=mc[0:nt, :], op0=ALU.mult, op1=ALU.add),
                             reads=[bs_u.b, bs_ht.b, mc.b], writes=[mn.b])
                    mfin = bs_mid[NI_BISECT % 2]
                    P.op("dve", mk("scalar_tensor_tensor", out=bs_thr[0:nt, :],
                                   in0=bs_ht[0:nt, NI_BISECT - 1:NI_BISECT], scalar=-0.5, in1=mfin[0:nt, :],
                                   op0=ALU.mult, op1=ALU.add),
                         reads=[bs_ht.b, mfin.b], writes=[bs_thr.b])
                else:
                    P.op("dve", mk("memset", ap=bs_thr[0:nt, :], constant=-1.0e29), writes=[bs_thr.b])
                yield
                P.op("dve", mk("tensor_scalar", out=mview[0:nt, :], in0=sview[0:nt, 0:S], scalar1=bs_thr[0:nt, 0:1],
                               scalar2=None, op0=ALU.is_ge),
                     reads=sbufs + [bs_thr.b], writes=mbufs)

            def stageB1(c):
                i, gi, S, nch, ngk, qT, mview, mbufs, vfi = (c["i"], c["gi"], c["S"], c["nch"], c["ngk"], c["qT"],
                                                             c["mview"], c["mbufs"], c["vfi"])
                mTv = maskT[:, 0:nch * nt].rearrange("p (c t) -> p c t", t=nt)
                for gk in range(ngk):
                    yield
                    k0 = gk * 512
                    gw = min(512, S - k0)
                    blocks = []
                    cc_ = 0
                    while cc_ * 128 < gw:
                        blocks.append((k0 + cc_ * 128, min(128, gw - cc_ * 128)))
                        cc_ += 1
                    full = [b for b in blocks if b[1] == 128]
                    part = [b for b in blocks if b[1] < 128]
                    if full:
                        transpose_to(mview, mbufs[0], nt, full,
                                     lambda wm, n, k0=k0: mTv[:, k0 // 128:k0 // 128 + n, :], maskT.b, "actbias")
                    if part:
                        pc0, pw_ = part[0]
                        transpose_to(mview, mbufs[-1], nt, [(pc0, pw_)],
                                     lambda wm, n, pc0=pc0: mTv[0:wm, pc0 // 128:pc0 // 128 + 1, :],
                                     maskT.b, "actbias")
                halves = [(0, 8)] if not isS else [(0, 4), (4, 8)]
                for (h0, h1) in halves:
                    if isS:
                        hh = h0 // 4
                        nct = L0 // 128
                        for c4 in range(0, nct, 2):
                            n4 = min(2, nct - c4)
                            stg = cst[(c4 // 2) % 2]
                            P.dma("cst", mk("dma_start",
                                out=stg[:, 0:n4, :],
                                in_=ck[l].rearrange("(t p) f -> p t f", p=128)[:, c4:c4 + n4, hh * 256:(hh + 1) * 256]),
                                writes=[stg.b])
                            P.op("dve", mk("tensor_copy", out=cbf[:, 0:n4, :], in_=stg[:, 0:n4, :]),
                                 reads=[stg.b], writes=[cbf.b])
                            for j in range(n4):
                                c = c4 + j
                                transpose_to(cbf[:, j, :], cbf.b, 128, [(0, 128), (128, 128)],
                                             lambda wm, n, c=c: kTv[:, 0:2, c * 128:(c + 1) * 128], kT.b,
                                             "act" if j % 2 else "dve")
                            stg2 = cst[(c4 // 2 + 1) % 2]
                            P.dma("cst", mk("dma_start",
                                out=stg2[:, 0:n4, :],
                                in_=cv[l].rearrange("(t p) f -> p t f", p=128)[:, c4:c4 + n4, hh * 256:(hh + 1) * 256]),
                                writes=[stg2.b])
                            P.op("dve", mk("tensor_copy",
                                out=Vv[:, c4:c4 + n4, :, 0:64],
                                in_=stg2[:, 0:n4, :].rearrange("p t (h d) -> p t h d", d=64)),
                                reads=[stg2.b], writes=[Vg.b])
                        transpose_to(kb[:, hh * 256:(hh + 1) * 256], kb.b, nt, [(0, 128), (128, 128)],
                                     lambda wm, n: kTv[:, 0:2, L0:L0 + nt], kT.b, "act")
                        P.op("pool", mk("tensor_copy",
                            out=Vv[0:nt, NCS - 1, :, 0:64],
                            in_=vfi[0:nt, hh * 256:(hh + 1) * 256].rearrange("p (h d) -> p h d", d=64)),
                            reads=[vfi.b], writes=[Vg.b])
                    chunks = [(c, min(128, S - c * 128)) for c in range(nch)]
                    groups = []
                    cur = []
                    for cpair in chunks:
                        if cur and (len(cur) == 4 or cur[-1][1] != cpair[1]):
                            groups.append(cur)
                            cur = []
                        cur.append(cpair)
                    if cur:
                        groups.append(cur)
                    items = [(h, gidx) for h in range(h0, h1) for gidx in range(len(groups))]

                    def emit_L(h, gidx):
                        hl = h - h0
                        pr = hl // 2
                        grp = groups[gidx]
                        rc = grp[0][1]
                        ng = len(grp)
                        Lt = nxt("L", Lb)
                        Lv = Lt[:].rearrange("p (c t) -> p c t", t=128)
                        c0g = grp[0][0]
                        if nt == 128:
                            P.op("pe", mk("matmul", out=Lt[0:rc, 0:ng * 128], lhsT=ident[0:rc, 0:rc],
                                          rhs=maskT[0:rc, c0g * 128:(c0g + ng) * 128], start=True, stop=False),
                                 reads=[ident.b, maskT.b], writes=[Lt.b])
                        for j, (c, _) in enumerate(grp):
                            if nt != 128:
                                P.op("pe", mk("matmul", out=Lv[0:rc, j, 0:nt], lhsT=ident[0:rc, 0:rc],
                                              rhs=mTv[0:rc, c, :], start=True, stop=False),
                                     reads=[ident.b, maskT.b], writes=[Lt.b])
                            P.op("pe", mk("matmul", out=Lv[0:rc, j, 0:nt], lhsT=kTv[:, pr, c * 128:c * 128 + rc],
                                          rhs=qT[:, h, 0:nt], start=False, stop=(j == ng - 1 or nt != 128)),
                                 reads=[kT.b, qT.b], writes=[Lt.b])
                        rot["praw"] = (rot["praw"] + 1) % len(praw)
                        pr_t = praw[rot["praw"]]
                        prv = pr_t[:].rearrange("p (c t) -> p c t", t=128)
                        P.op("act", mk("activation", out=prv[0:rc, 0:ng, 0:nt], in_=Lv[0:rc, 0:ng, 0:nt],
                                       func=AF.Exp, scale=0.125),
                             reads=[Lt.b], writes=[pr_t.b])
                        return (h, gidx, pr_t, prv)

                    def emit_PV(h, gidx, pr_t, prv):
                        hl = h - h0
                        O = Ob[h // 4]
                        grp = groups[gidx]
                        rc = grp[0][1]
                        ng = len(grp)
                        for j, (c, _) in enumerate(grp):
                            first = (gidx == 0 and j == 0)
                            lastc = (gidx == len(groups) - 1 and j == ng - 1)
                            P.op("pe", mk("matmul", out=O[0:nt, h % 4, :], lhsT=prv[0:rc, j, 0:nt],
                                          rhs=Vv[0:rc, c, hl if isS else h, :], start=first, stop=lastc),
                                 reads=[pr_t.b, Vg.b], writes=[O.b])

                    pend = None
                    for (h, gidx) in items:
                        yield
                        curL = emit_L(h, gidx)
                        if pend is not None:
                            emit_PV(*pend)
                        pend = curL
                    if pend is not None:
                        emit_PV(*pend)

            def stageB2(c):
                gi, sza, tix = c["gi"], c["sza"], c["tix"]
                for ob in range(2):
                    O = Ob[ob]
                    P.op("dve", mk("reciprocal", out=rs[0:nt, ob * 4:ob * 4 + 4], in_=O[0:nt, :, 64]),
                         reads=[O.b], writes=[rs.b])
                    P.op("dve", mk("tensor_tensor",
                                   out=on[0:nt, ob * 256:(ob + 1) * 256].rearrange("p (h d) -> p h d", d=64),
                                   in0=O[0:nt, :, 0:64],
                                   in1=rs[0:nt, ob * 4:ob * 4 + 4].unsqueeze(2).broadcast_to([nt, 4, 64]),
                                   op=ALU.mult),
                         reads=[O.b, rs.b], writes=[on.b])
                P.op("pool", mk("tensor_tensor", out=gat[0:nt, :], in0=on[0:nt, :], in1=sza[0:nt, :], op=ALU.mult),
                     reads=[on.b, sza.b], writes=[gat.b])
                gao = gaT[gi % 2]
                transpose_to(gat, gat.b, nt, [(c * 128, 128) for c in range(4)],
                             lambda wm, n: gao[:, 0:4, 0:nt], gao.b, "act")
                P.dma("ga%d" % (gi % 2), mk("dma_start",
                    out=ga_scr[tix].rearrange("p (c t) -> p c t", c=4)[:, :, 0:nt], in_=gao[:, :, 0:nt]),
                    reads=[gao.b])

            load_x(seq, 0, gtile[0] % 2, xsrc)

            def drive(gens):
                res = [None] * len(gens)
                live = [g is not None for g in gens]
                while any(live):
                    for k, g in enumerate(gens):
                        if live[k]:
                            try:
                                next(g)
                            except StopIteration as ex:
                                res[k] = ex.value
                                live[k] = False
                return res

            ctx = {}
            for k in range(ntl + 2):
                gA1 = stageA1(k) if k < ntl else None
                gA2 = stageA2(ctx[k - 1]) if 0 <= k - 1 < ntl else None
                gB1 = stageB1(ctx[k - 2]) if 0 <= k - 2 < ntl else None
                r = drive([gA1, gA2, gB1])
                if gA1 is not None:
                    ctx[k] = r[0]
                if gB1 is not None:
                    stageB2(ctx[k - 2])
        P.barrier()
        st1.close()

        st2 = ExitStack()
        W2 = alloc(st2, "W2", [128, 8, C2], BF16)
        Wpo = alloc(st2, "Wpo", [128, 4, D], BF16)
        Wao = alloc(st2, "Wao", [128, 4, D], BF16)
        Wo = alloc(st2, "Wo", [128, 8, D], BF16)
        with ExitStack() as stl:
            stage = [alloc(stl, "stg%d" % i, [128, C2], F32) for i in range(2)]
            load_cast(lambda k: W2[:, k, :], lambda k: w_in[l, k * 128:(k + 1) * 128, C1:NCOL], 8, C2, stage, True, W2.b)
            load_cast(lambda k: Wpo[:, k, :], lambda k: w_po[l, k * 128:(k + 1) * 128, :], 4, D, stage, False, Wpo.b)
            load_cast(lambda k: Wao[:, k, :], lambda k: w_ao[l, k * 128:(k + 1) * 128, :], 4, D, stage, False, Wao.b)
            load_cast(lambda k: Wo[:, k, :], lambda k: w_o[l, k * 128:(k + 1) * 128, :], 8, D, stage, False, Wo.b)
            P.barrier()
        gpl = [alloc(st2, "gpl%d" % i, [128, 4, 128], BF16) for i in range(3)]
        gal = [alloc(st2, "gal%d" % i, [128, 4, 128], BF16) for i in range(3)]
        sgps = [alloc(st2, "sgp%d" % i, [128, D], BF16) for i in range(2)]
        sgas = [alloc(st2, "sga%d" % i, [128, D], BF16) for i in range(2)]
        xt.append(alloc(st2, "xt2", [128, D], F32))
        m1 = alloc(st2, "m1", [128, D], F32)
        t2 = alloc(st2, "t2", [128, 512], F32)
        mrg = alloc(st2, "mrg", [128, D], BF16)
        mT = alloc(st2, "mT", [128, 8, 128], BF16)
        yt = [alloc(st2, "yt%d" % i, [128, D], F32) for i in range(2)]

        prefetch = []
        if l + 1 < DEPTH:
            pst = [alloc(st2, "pst%d" % i, [128, C1], F32) for i in range(2)]
            load_gcol(l + 1)

            def mk_chunk(k):
                def emit():
                    sgt = pst[k % 2]
                    P.dma("pst", mk("dma_start", out=sgt[:, 0:C1], in_=w_in[l + 1, k * 128:(k + 1) * 128, 0:C1]),
                          writes=[sgt.b])
                    if k % 2 == 0:
                        P.op("dve", mk("tensor_scalar", out=W1[:, k, :], in0=sgt[:, 0:C1], scalar1=gcol[:, k:k + 1],
                                       scalar2=None, op0=ALU.mult), reads=[sgt.b, gcol.b], writes=[W1.b])
                    else:
                        P.op("act", mk("activation", out=W1[:, k, :], in_=sgt[:, 0:C1], func=AF.Copy,
                                       scale=gcol[:, k:k + 1]), reads=[sgt.b, gcol.b], writes=[W1.b])
                return emit
            prefetch = [mk_chunk(k) for k in range(8)]
        gtile2 = [0]
        for seq in seqs:
            nt = seq["nt"]
            ntl = seq["ntiles"]
            xsrc = x_src_of(seq)

            def loads2(i, gi):
                load_x(seq, i, gi % 3, xsrc)
                tix = seq["tile0"] + i
                P.dma("gpl%d" % (gi % 3), mk("dma_start",
                    out=gpl[gi % 3][:, :, 0:nt], in_=gp_scr[tix].rearrange("p (c t) -> p c t", c=4)[:, :, 0:nt]),
                    writes=[gpl[gi % 3].b])
                P.dma("gal%d" % (gi % 3), mk("dma_start",
                    out=gal[gi % 3][:, :, 0:nt], in_=ga_scr[tix].rearrange("p (c t) -> p c t", c=4)[:, :, 0:nt]),
                    writes=[gal[gi % 3].b])

            def stage2A(i):
                gi = gtile2[0]
                gtile2[0] += 1
                if prefetch and gi % 3 == 1:
                    prefetch.pop(0)()
                slot = gi % 3
                sgp = sgps[gi % 2]
                sga = sgas[gi % 2]
                if i + 1 < ntl:
                    loads2(i + 1, gi + 1)
                norm_hT(seq, slot)
                for hh in range(2):
                    m = proj(nt, W2, hh * 512, 512)
                    P.op("act", mk("activation", out=sgp[0:nt, hh * 512:(hh + 1) * 512], in_=m[0:nt, :],
                                                                   func=AF.Sigmoid),
                         reads=[m.b], writes=[sgp.b])
                for hh in range(2):
                    m = proj(nt, W2, 1024 + hh * 512, 512)
                    P.op("act", mk("activation", out=sga[0:nt, hh * 512:(hh + 1) * 512], in_=m[0:nt, :],
                                                                   func=AF.Sigmoid),
                         reads=[m.b], writes=[sga.b])
                return dict(i=i, gi=gi, slot=slot, sgp=sgp, sga=sga)

            def stage2B(c):
                i, gi, slot, sgp, sga = c["i"], c["gi"], c["slot"], c["sgp"], c["sga"]
                x = xt[slot]
                gp_, ga_ = gpl[slot], gal[slot]
                for hh in range(2):
                    m = next_mm()
                    for k in range(4):
                        P.op("pe", mk("matmul", out=m[0:nt, :], lhsT=gp_[:, k, 0:nt],
                                                                       rhs=Wpo[:, k, hh * 512:(hh + 1) * 512],
                                                                       start=(k == 0), stop=(k == 3)),
                             reads=[gp_.b, Wpo.b], writes=[m.b])
                    P.op("dve", mk("tensor_tensor", out=m1[0:nt, hh * 512:(hh + 1) * 512], in0=m[0:nt, :],
                                                                      in1=sgp[0:nt, hh * 512:(hh + 1) * 512], op=ALU.mult),
                         reads=[m.b, sgp.b], writes=[m1.b])
                for hh in range(2):
                    m = next_mm()
                    for k in range(4):
                        P.op("pe", mk("matmul", out=m[0:nt, :], lhsT=ga_[:, k, 0:nt],
                                                                       rhs=Wao[:, k, hh * 512:(hh + 1) * 512],
                                                                       start=(k == 0), stop=(k == 3)),
                             reads=[ga_.b, Wao.b], writes=[m.b])
                    P.op("dve", mk("tensor_tensor", out=t2[0:nt, :], in0=m[0:nt, :],
                                                                      in1=sga[0:nt, hh * 512:(hh + 1) * 512], op=ALU.mult),
                         reads=[m.b, sga.b], writes=[t2.b])
                    P.op("pool", mk("tensor_tensor", out=mrg[0:nt, hh * 512:(hh + 1) * 512],
                                                                  in0=m1[0:nt, hh * 512:(hh + 1) * 512], in1=t2[0:nt, :],
                                                                  op=ALU.add),
                         reads=[m1.b, t2.b], writes=[mrg.b])
                transpose_to(mrg, mrg.b, nt, [(c * 128, 128) for c in range(8)],
                             lambda wm, n: mT[:, 0:8, 0:nt], mT.b, "act")
                for hh in range(2):
                    m = next_mm()
                    for k in range(8):
                        P.op("pe", mk("matmul", out=m[0:nt, :], lhsT=mT[:, k, 0:nt],
                                                                       rhs=Wo[:, k, hh * 512:(hh + 1) * 512],
                                                                       start=(k == 0), stop=(k == 7)),
                             reads=[mT.b, Wo.b], writes=[m.b])
                    P.op("dve", mk("tensor_tensor", out=x[0:nt, hh * 512:(hh + 1) * 512],
                                                                      in0=x[0:nt, hh * 512:(hh + 1) * 512], in1=m[0:nt, :],
                                                                      op=ALU.add),
                         reads=[m.b, x.b], writes=[x.b])
                r0 = seq["tok0"] + i * 128
                if not last:
                    P.dma("x%d" % slot, mk("dma_start", out=xscr[r0:r0 + nt, :], in_=x[0:nt, :]), reads=[x.b])
                else:
                    y = yt[gi % 2]
                    P.op("act", mk("activation", out=xn[0:nt, 0:D], in_=x[0:nt, :], func=AF.Square,
                                                       accum_out=ssq[0:nt, :]),
                         reads=[x.b], writes=[xn.b, ssq.b])
                    P.op("dve", mk("tensor_scalar", out=rstd[0:nt, :], in0=ssq[0:nt, :], scalar1=1.0 / D, scalar2=EPS,
                                                          op0=ALU.mult, op1=ALU.add),
                         reads=[ssq.b], writes=[rstd.b])
                    P.op("act", mk("activation", out=rstd[0:nt, :], in_=rstd[0:nt, :], func=AF.Sqrt),
                         reads=[rstd.b], writes=[rstd.b])
                    P.op("dve", mk("reciprocal", out=rstd[0:nt, :], in_=rstd[0:nt, :]),
                         reads=[rstd.b], writes=[rstd.b])
                    P.op("dve", mk("scalar_tensor_tensor", out=y[0:nt, :], in0=x[0:nt, :], scalar=rstd[0:nt, 0:1],
                                                                      in1=gfbc[0:nt, :], op0=ALU.mult, op1=ALU.mult),
                         reads=[x.b, rstd.b, gfbc.b], writes=[y.b])
                    yo = seq["y_out"]
                    P.dma("y%d" % slot, mk("dma_start", out=yo[i * 128:i * 128 + nt, :], in_=y[0:nt, :]),
                          reads=[y.b])
            loads2(0, gtile2[0])
            pend = None
            for i in range(ntl):
                cA = stage2A(i)
                if pend is not None:
                    stage2B(pend)
                pend = cA
            stage2B(pend)
        while prefetch:
            prefetch.pop(0)()
        P.barrier()
        xt.pop()
        st2.close()

    P.final_wait()
    P.emit()
    stack0.close()
    return nc


_CACHE = {}


def kernel(x_prompt, x_sample, cache_k, cache_v, cache_kidx, state_pool, norm_g, w_in, w_pool_mix,
           pool_scale, w_pool_out, w_attn_out, w_o, final_norm_g):
    NCORES = 8
    f = lambda a: np.ascontiguousarray(np.asarray(a, dtype=np.float32))
    x_prompt, x_sample = f(x_prompt), f(x_sample)
    cache_k, cache_v, cache_kidx, state_pool = f(cache_k), f(cache_v), f(cache_kidx), f(state_pool)
    BP, T, _ = x_prompt.shape
    BS, TS, _ = x_sample.shape
    DEPTH = cache_k.shape[0]
    L0 = cache_k.shape[2]
    assert BP % NCORES == 0 and BS == NCORES
    NP = BP // NCORES
    cfg = dict(NP=NP, T=T, TS=TS, L0=L0, DEPTH=DEPTH, KP=min(256, T // 4), KS=min(256, (L0 + TS) // 4))
    key = tuple(sorted(cfg.items()))
    if key not in _CACHE:
        _CACHE[key] = build(cfg)
    nc = _CACHE[key]

    rope = _rope_table(list(range(T)) + list(range(L0, L0 + TS)))
    bands = _band_tables().reshape(4, 128, 512)
    pw2 = np.tile((2.0 ** -(np.arange(NI_BISECT, dtype=np.float64) + 1)).astype(np.float32)[None, :], (128, 1))
    shared = dict(norm_g=f(norm_g), w_in=f(w_in), w_mix=f(w_pool_mix), pscale=f(pool_scale), w_po=f(w_pool_out),
                  w_ao=f(w_attn_out), w_o=f(w_o), gfin=f(final_norm_g).reshape(1, D), rope=rope, bands=bands, pw2=pw2)
    in_maps = []
    for c in range(NCORES):
        m = dict(shared)
        m["x_p"] = x_prompt[c * NP:(c + 1) * NP]
        m["x_s"] = x_sample[c]
        m["ck"] = cache_k[:, c].reshape(DEPTH, L0, 512)
        m["cv"] = cache_v[:, c].reshape(DEPTH, L0, 512)
        m["cki"] = cache_kidx[:, c]
        m["spool"] = state_pool[:, c]
        in_maps.append(m)
    res = run_bass_kernel_spmd(nc, in_maps, core_ids=list(range(NCORES)))
    R = res.results
    cat = lambda name, ax: np.concatenate([np.asarray(r[name]) for r in R], axis=ax)
    stk = lambda name, ax: np.stack([np.asarray(r[name]) for r in R], axis=ax)
    y_prompt = cat("y_p", 0)
    y_sample = stk("y_s", 0)
    nk_p = cat("ok_p", 1).reshape(DEPTH, BP, T, 8, 64)
    nv_p = cat("ov_p", 1).reshape(DEPTH, BP, T, 8, 64)
    nki_p = cat("oki_p", 1)
    npl_p = cat("opl_p", 1)
    nk_s = stk("ok_s", 1).reshape(DEPTH, BS, TS, 8, 64)
    nv_s = stk("ov_s", 1).reshape(DEPTH, BS, TS, 8, 64)
    nki_s = stk("oki_s", 1)
    npl_s = stk("opl_s", 1)
    return (y_prompt, y_sample, nk_p, nv_p, nki_p, npl_p, nk_s, nv_s, nki_s, npl_s)
```

```python
from contextlib import ExitStack
import os
import numpy as np
import concourse.bass as bass
import concourse.mybir as mybir
from concourse.bass_utils import run_bass_kernel_spmd

F32 = mybir.dt.float32
BF16 = mybir.dt.bfloat16
ALU = mybir.AluOpType
AF = mybir.ActivationFunctionType

ENGS = ("pe", "act", "dve", "pool", "sp")

D = 1024
NCOL = 5416
C1 = 3368
C2 = NCOL - C1
NH = 8
NEG = -1.0e30
NI_BISECT = int(os.environ.get('KNI', '24'))
EPS = 1e-6


class Buf:
    __slots__ = ("name", "w", "r", "excl")

    def __init__(self, name):
        self.name = name
        self.excl = False
        self.w = None
        self.r = []


class Prog:
    def __init__(self, nc):
        self.nc = nc
        self.q = {e: [] for e in ENGS}
        self.tick = {e: 0 for e in ENGS}
        self.sem = {e: nc.alloc_semaphore("sem_" + e) for e in ENGS}
        self.seen = {e: {} for e in ENGS}
        self.dsem = {}
        self.dtick = {}
        self.nbuf = 0
        self.count = 0
        self.limit = int(os.environ.get("KLIMIT", "1000000000"))

    def buf(self, name=None):
        self.nbuf += 1
        return Buf(name or "b%d" % self.nbuf)

    def _need(self, reads, writes):
        need = {}
        for b in reads:
            if b.w is not None:
                s, t = b.w
                if need.get(s, 0) < t:
                    need[s] = t
        for b in writes:
            if b.w is not None:
                s, t = b.w
                if need.get(s, 0) < t:
                    need[s] = t
            for s, t in b.r:
                if need.get(s, 0) < t:
                    need[s] = t
        return need

    def _waits(self, eng, need):
        waits = []
        seen = self.seen[eng]
        for s, t in need.items():
            if s == eng and eng in ("pe", "sp"):
                continue
            if seen.get(s, 0) >= t:
                continue
            seen[s] = t
            waits.append((s, t))
        return waits

    def _mark(self, src, reads, writes):
        for b in reads:
            if len(b.r) > 24:
                d = {}
                for s, t in b.r:
                    if d.get(s, 0) < t:
                        d[s] = t
                b.r = list(d.items())
            b.r.append(src)
        for b in writes:
            b.w = src
            b.r = []

    def op(self, eng, fn, reads=(), writes=()):
        self.count += 1
        if self.count > self.limit:
            return
        xr = [b for b in reads if b.excl]
        if xr:
            writes = list(writes) + [b for b in xr if b not in writes]
        waits = self._waits(eng, self._need(reads, writes))
        self.tick[eng] += 1
        my = self.tick[eng]
        self.q[eng].append((waits, fn, ("e", eng)))
        self._mark((eng, my), reads, writes)

    def dma(self, stream, fn, reads=(), writes=(), eng="sp"):
        stream = (writes[0] if writes else reads[0]).name
        self.count += 1
        if self.count > self.limit:
            return
        if stream not in self.dsem:
            self.dsem[stream] = self.nc.alloc_semaphore("dsem_" + stream)
            self.dtick[stream] = 0
        waits = self._waits(eng, self._need(reads, writes))
        self.dtick[stream] += 16
        my = self.dtick[stream]
        self.q[eng].append((waits, fn, ("d", stream)))
        self._mark(("dma:" + stream, my), reads, writes)

    def _semof(self, s):
        if s.startswith("dma:"):
            return self.dsem[s[4:]]
        return self.sem[s]

    def barrier(self):
        need = {}
        for e in ENGS:
            if self.tick[e] > 0:
                need[e] = self.tick[e]
        for s, t in self.dtick.items():
            if t > 0:
                need["dma:" + s] = t
        for e in ENGS:
            waits = self._waits(e, dict(need))
            if waits:
                self.q[e].append((waits, None, None))

    def final_wait(self, eng="sp"):
        need = {}
        for s, t in self.dtick.items():
            if t > 0:
                need["dma:" + s] = t
        for e in ENGS:
            if e != eng and self.tick[e] > 0:
                need[e] = self.tick[e]
        waits = self._waits(eng, need)
        if waits:
            self.q[eng].append((waits, None, None))

    def emit(self):
        nc = self.nc
        engobj = {"pe": "tensor", "act": "scalar", "dve": "vector", "pool": "gpsimd", "sp": "sync"}
        with nc.Block() as block:
            for e in ENGS:
                items = self.q[e]
                if not items:
                    continue

                def body(eo, items=items):
                    for waits, fn, inc in items:
                        for s, t in waits:
                            eo.wait_ge(self._semof(s), t)
                        if fn is None:
                            continue
                        ins = fn(eo)
                        if inc[0] == "e":
                            ins.then_inc(self.sem[inc[1]], 1)
                        else:
                            ins.then_inc(self.dsem[inc[1]], 16)

                getattr(block, engobj[e])(body)


def mk(meth, **kw):
    return lambda e: getattr(e, meth)(**kw)


class Tl:
    def __init__(self, t, b):
        self.t = t
        self.b = b

    def __getitem__(self, k):
        return self.t[k]


def _rope_table(positions):
    pos = np.asarray(positions, dtype=np.float32)
    out = np.zeros((len(pos), 192), np.float32)
    for (d, off) in ((64, 0), (32, 128)):
        inv = (10000.0 ** (-np.arange(0, d, 2, dtype=np.float32) / np.float32(d))).astype(np.float32)
        ang = (pos[:, None] * inv[None, :]).astype(np.float32).astype(np.float64)
        c = np.cos(ang).astype(np.float32)
        s = np.sin(ang).astype(np.float32)
        h = d // 2
        out[:, off:off + h] = c
        out[:, off + h:off + 2 * h] = c
        out[:, off + 2 * h:off + 3 * h] = -s
        out[:, off + 3 * h:off + 4 * h] = s
    return out


def _band_tables():
    B = np.zeros((4, 128, 4, 128), np.float32)
    for g, w in enumerate((2, 4, 8, 16)):
        for t in range(128):
            for tp in range(max(0, t - w + 1), t + 1):
                B[0, tp, g, t] += 1.0 / min(t + 1, w)
                B[1, tp, g, t] += 1.0 / w
            B[0, t, g, t] -= 1.0
            B[1, t, g, t] -= 1.0
            for tp in range(128):
                if t - (tp - 128) < w:
                    B[2, tp, g, t] = 1.0 / w
            for j in range(15):
                if t - (j - 15) < w:
                    B[3, j, g, t] = 1.0 / w
    return B


def build(cfg):
    NP, T, TS, L0, DEPTH = cfg["NP"], cfg["T"], cfg["TS"], cfg["L0"], cfg["DEPTH"]
    KP, KS = cfg["KP"], cfg["KS"]
    assert T % 128 == 0 and TS == 64 and L0 % 128 == 0
    NTP = T // 128
    LS = L0 + TS
    NCS = L0 // 128 + 1
    SMAX = max(T, LS)
    KTW = max(4 * T, 2 * LS)
    VS = int(os.environ.get('KVS', '65'))
    VW = max(NTP * 8 * VS, NCS * 4 * VS)
    NTOK = NP * T + TS
    NTILES = NP * NTP + 1

    nc = bass.Bass("TRN2", target_bir_lowering=False, dynamic_dma_scratch_size=256)
    P = Prog(nc)

    def din(name, shape, dt=F32):
        return nc.dram_tensor(name, list(shape), dt, kind="ExternalInput").ap()

    def dout(name, shape, dt=F32):
        return nc.dram_tensor(name, list(shape), dt, kind="ExternalOutput").ap()

    x_p = din("x_p", [NP, T, D])
    x_s = din("x_s", [TS, D])
    ck = din("ck", [DEPTH, L0, 512])
    cv = din("cv", [DEPTH, L0, 512])
    cki = din("cki", [DEPTH, L0, 32])
    spool = din("spool", [DEPTH, 15, 512])
    norm_g = din("norm_g", [DEPTH, D])
    w_in = din("w_in", [DEPTH, D, NCOL])
    w_mix = din("w_mix", [DEPTH, 4, 128, 128])
    pscale = din("pscale", [DEPTH, 512])
    w_po = din("w_po", [DEPTH, 512, D])
    w_ao = din("w_ao", [DEPTH, 512, D])
    w_o = din("w_o", [DEPTH, D, D])
    gfin = din("gfin", [1, D])
    rope = din("rope", [T + TS, 192])
    bands = din("bands", [4, 128, 512])
    pw2 = din("pw2", [128, NI_BISECT])

    y_p = dout("y_p", [NP, T, D])
    y_s = dout("y_s", [TS, D])
    ok_p = dout("ok_p", [DEPTH, NP, T, 512])
    ov_p = dout("ov_p", [DEPTH, NP, T, 512])
    oki_p = dout("oki_p", [DEPTH, NP, T, 32])
    opl_p = dout("opl_p", [DEPTH, NP, 15, 512])
    ok_s = dout("ok_s", [DEPTH, TS, 512])
    ov_s = dout("ov_s", [DEPTH, TS, 512])
    oki_s = dout("oki_s", [DEPTH, TS, 32])
    opl_s = dout("opl_s", [DEPTH, 15, 512])

    xscr = nc.dram_tensor("xscr", [NTOK, D], F32).ap()
    gp_scr = nc.dram_tensor("gp_scr", [NTILES, 128, 512], BF16).ap()
    ga_scr = nc.dram_tensor("ga_scr", [NTILES, 128, 512], BF16).ap()

    seqs = []
    for s in range(NP):
        seqs.append(dict(kind="p", idx=s, T=T, nt=128, ntiles=NTP, tok0=s * T, tile0=s * NTP,
                         x_in=x_p[s], y_out=y_p[s], K=KP, rope0=0))
    seqs.append(dict(kind="s", idx=0, T=TS, nt=TS, ntiles=1, tok0=NP * T, tile0=NP * NTP,
                     x_in=x_s, y_out=y_s, K=KS, rope0=T))

    stack0 = ExitStack()

    uniq = [0]

    def alloc(st, name, shape, dt):
        uniq[0] += 1
        t = st.enter_context(nc.sbuf_tensor("%s_%d" % (name, uniq[0]), list(shape), dt))
        return Tl(t, P.buf(name))

    def palloc(name, shape, dt):
        t = nc.alloc_psum_tensor(name, list(shape), dt)
        b = P.buf(name)
        b.excl = True
        return Tl(t, b)

    mm = [palloc("mm%d" % i, [128, 512], F32) for i in range(2)]
    tp = palloc("tp", [128, 1024], BF16)
    pp = palloc("pp", [128, 512], F32)
    Lb = [palloc("L%d" % i, [128, 512], F32) for i in range(2)]
    Ob = [palloc("O%d" % i, [128, 4, VS], F32) for i in range(2)]
    mmi = [0]

    def next_mm():
        mmi[0] ^= 1
        return mm[mmi[0]]

    ident = alloc(stack0, "ident", [128, 128], BF16)
    identf = alloc(stack0, "identf", [128, 128], F32)
    ones1 = alloc(stack0, "ones1", [1, 128], F32)
    gfbc = alloc(stack0, "gfbc", [128, D], F32)
    band = alloc(stack0, "band", [128, 4, 512], BF16)
    pw = alloc(stack0, "pw", [128, NI_BISECT], F32)
    xt = [alloc(stack0, "xt%d" % i, [128, D], F32) for i in range(2)]
    xn = alloc(stack0, "xn", [128, D], BF16)
    hT = alloc(stack0, "hT", [128, 8, 128], BF16)
    ssq = alloc(stack0, "ssq", [128, 1], F32)
    rstd = alloc(stack0, "rstd", [128, 1], F32)
    gcol = alloc(stack0, "gcol", [128, 8], F32)
    W1 = alloc(stack0, "W1", [128, 8, C1], BF16)
    ftmp = [alloc(stack0, "ftmp%d" % i, [128, 512], F32) for i in range(2)]
    ftmpi = [0]

    def next_ftmp():
        ftmpi[0] ^= 1
        return ftmp[ftmpi[0]]

    P.op("pool", mk("memset", ap=identf[:], constant=0.0), writes=[identf.b])
    P.op("pool", mk("affine_select", out=identf[:], in_=identf[:], pattern=[[-1, 128]],
                                           compare_op=ALU.not_equal, fill=1.0, base=0,
                                           channel_multiplier=1),
         reads=[identf.b], writes=[identf.b])
    P.op("dve", mk("tensor_copy", out=ident[:], in_=identf[:]), reads=[identf.b], writes=[ident.b])
    P.op("dve", mk("memset", ap=ones1[:], constant=1.0), writes=[ones1.b])
    P.dma("c0", mk("dma_start", out=pw[:], in_=pw2), writes=[pw.b])
    for kind in range(4):
        f = next_ftmp()
        P.dma("c1", mk("dma_start", out=f[:], in_=bands[kind]), writes=[f.b])
        P.op("dve", mk("tensor_copy", out=band[:, kind, :], in_=f[:]),
             reads=[f.b], writes=[band.b])
    grow = alloc(stack0, "grow", [1, D], F32)
    P.dma("c0", mk("dma_start", out=grow[:], in_=gfin), writes=[grow.b])
    for hh in range(2):
        m = next_mm()
        P.op("pe", mk("matmul", out=m[:], lhsT=ones1[0:1, :], rhs=grow[0:1, hh * 512:(hh + 1) * 512],
                                                  start=True, stop=True),
             reads=[ones1.b, grow.b], writes=[m.b])
        P.op("dve", mk("tensor_copy", out=gfbc[:, hh * 512:(hh + 1) * 512], in_=m[:]),
             reads=[m.b], writes=[gfbc.b])

    def load_gcol(l):
        g8 = next_ftmp()
        P.dma("c1", mk("dma_start", out=g8[0:8, 0:128], in_=norm_g[l].rearrange("(k p) -> k p", p=128)),
              writes=[g8.b])
        m = next_mm()
        P.op("pe", mk("transpose", out=m[:, 0:8], in_=g8[0:8, 0:128], identity=identf[0:8, 0:8]),
             reads=[g8.b, identf.b], writes=[m.b])
        P.op("dve", mk("tensor_copy", out=gcol[:], in_=m[:, 0:8]), reads=[m.b], writes=[gcol.b])

    def load_cast(dst_ap_fn, src_rows_fn, nk, ncols, stage, scale_gcol, dstb):
        for k in range(nk):
            s = stage[k % 2]
            P.dma("wst%d" % (k % 2), mk("dma_start", out=s[:, 0:ncols], in_=src_rows_fn(k)),
                  writes=[s.b])
            if scale_gcol:
                if k % 2 == 0:
                    P.op("dve", mk("tensor_scalar", out=dst_ap_fn(k), in0=s[:, 0:ncols],
                                                                    scalar1=gcol[:, k:k + 1], scalar2=None,
                                                                    op0=ALU.mult),
                         reads=[s.b, gcol.b], writes=[dstb])
                else:
                    P.op("act", mk("activation", out=dst_ap_fn(k), in_=s[:, 0:ncols],
                                                                 func=AF.Copy, scale=gcol[:, k:k + 1]),
                         reads=[s.b, gcol.b], writes=[dstb])
            else:
                if k % 2 == 0:
                    P.op("dve", mk("tensor_copy", out=dst_ap_fn(k), in_=s[:, 0:ncols]),
                         reads=[s.b], writes=[dstb])
                else:
                    P.op("act", mk("copy", out=dst_ap_fn(k), in_=s[:, 0:ncols]),
                         reads=[s.b], writes=[dstb])

    def load_x(seq, i, slot, src):
        nt = seq["nt"]
        r0 = i * 128
        P.dma("x%d" % slot, mk("dma_start", out=xt[slot][0:nt, :], in_=src[r0:r0 + nt, :]),
              writes=[xt[slot].b])

    def norm_hT(seq, slot):
        nt = seq["nt"]
        x = xt[slot]
        P.op("act", mk("activation", out=xn[0:nt, 0:D], in_=x[0:nt, :], func=AF.Square,
                                           accum_out=ssq[0:nt, :]),
             reads=[x.b], writes=[xn.b, ssq.b])
        P.op("act", mk("activation", out=rstd[0:nt, :], in_=ssq[0:nt, :], func=AF.Ln, scale=1.0 / D, bias=EPS),
             reads=[ssq.b], writes=[rstd.b])
        P.op("act", mk("activation", out=rstd[0:nt, :], in_=rstd[0:nt, :], func=AF.Exp, scale=-0.5),
             reads=[rstd.b], writes=[rstd.b])
        P.op("dve", mk("tensor_scalar", out=xn[0:nt, :], in0=x[0:nt, :], scalar1=rstd[0:nt, 0:1],
                                              scalar2=None, op0=ALU.mult),
             reads=[x.b, rstd.b], writes=[xn.b])
        tv = tp[:].rearrange("p (k t) -> p k t", t=128)
        for k in range(8):
            P.op("pe", mk("transpose", out=tv[:, k, 0:nt], in_=xn[0:nt, k * 128:(k + 1) * 128],
                                                  identity=ident[0:nt, 0:nt]),
                 reads=[xn.b, ident.b], writes=[tp.b])
        P.op("act", mk("copy", out=hT[:, 0:4, 0:nt], in_=tv[:, 0:4, 0:nt]), reads=[tp.b], writes=[hT.b])
        P.op("dve", mk("tensor_copy", out=hT[:, 4:8, 0:nt], in_=tv[:, 4:8, 0:nt]), reads=[tp.b], writes=[hT.b])

    def proj(nt, W, c0, ncols):
        m = next_mm()
        for k in range(8):
            P.op("pe", mk("matmul", out=m[0:nt, 0:ncols], lhsT=hT[:, k, 0:nt], rhs=W[:, k, c0:c0 + ncols],
                                               start=(k == 0), stop=(k == 7)),
                 reads=[hT.b, W.b], writes=[m.b])
        return m

    def transpose_to(src, srcb, nt, ncols_list, dst_fn, dstb, evac_eng="act"):
        tv = tp[:].rearrange("p (k t) -> p k t", t=128)
        for j, (c0, wd) in enumerate(ncols_list):
            P.op("pe", mk("transpose", out=tv[0:wd, j, 0:nt], in_=src[0:nt, c0:c0 + wd],
                                                                identity=ident[0:nt, 0:nt]),
                 reads=[srcb, ident.b], writes=[tp.b])
        wmax = max(w for _, w in ncols_list)
        n = len(ncols_list)
        if evac_eng == "none":
            return tv
        if evac_eng == "actbias":
            P.op("act", mk("activation", out=dst_fn(wmax, n), in_=tv[0:wmax, 0:n, 0:nt], func=AF.Copy,
                           scale=30000.0, bias=-30000.0), reads=[tp.b], writes=[dstb])
        elif evac_eng == "act":
            P.op("act", mk("copy", out=dst_fn(wmax, n), in_=tv[0:wmax, 0:n, 0:nt]), reads=[tp.b], writes=[dstb])
        else:
            P.op("dve", mk("tensor_copy", out=dst_fn(wmax, n), in_=tv[0:wmax, 0:n, 0:nt]),
                 reads=[tp.b], writes=[dstb])

    for l in range(DEPTH):
        x_src_of = (lambda seq: seq["x_in"]) if l == 0 else (lambda seq: xscr[seq["tok0"]:seq["tok0"] + seq["T"], :])
        last = (l == DEPTH - 1)

        st1 = ExitStack()
        Wmix = alloc(st1, "Wmix", [128, 4, 128], BF16)
        with ExitStack() as stl:
            if l == 0:
                stage = [alloc(stl, "stg%d" % i, [128, C1], F32) for i in range(2)]
                load_gcol(l)
                load_cast(lambda k: W1[:, k, :], lambda k: w_in[l, k * 128:(k + 1) * 128, 0:C1], 8, C1, stage, True, W1.b)
            srow = next_ftmp()
            P.dma("c1", mk("dma_start", out=srow[0:1, :], in_=pscale[l:l + 1, :]), writes=[srow.b])
            m = next_mm()
            P.op("pe", mk("matmul", out=m[:], lhsT=ones1[0:1, :], rhs=srow[0:1, :], start=True, stop=True),
                 reads=[ones1.b, srow.b], writes=[m.b])
            s0 = next_ftmp()
            P.dma("wst0", mk("dma_start", out=s0[:, 0:512].rearrange("p (g d) -> p g d", g=4),
                                                in_=w_mix[l].rearrange("g c d -> c g d")), writes=[s0.b])
            P.op("dve", mk("tensor_tensor", out=Wmix[:].rearrange("p g d -> p (g d)"), in0=s0[:, 0:512],
                                                  in1=m[:], op=ALU.mult),
                 reads=[s0.b, m.b], writes=[Wmix.b])
            P.barrier()

        kT = alloc(st1, "kT", [128, KTW], BF16)
        Vg = alloc(st1, "Vg", [128, VW], BF16)
        kiT = alloc(st1, "kiT", [96, SMAX], BF16)
        MH = max(T, (LS + 1) // 2)
        score = alloc(st1, "score", [128, 2 * MH], F32)
        sc_bufs = [score.b, P.buf("score1")]
        maskT = alloc(st1, "maskT", [128, max(NTP * 128, NCS * 64)], BF16)
        utok = [alloc(st1, "utok%d" % i, [128, 512], BF16) for i in range(2)]
        szp = alloc(st1, "szp", [128, 512], BF16)
        szas = [alloc(st1, "sza%d" % i, [128, 512], BF16) for i in range(3)]
        qf = alloc(st1, "qf", [128, 512], F32)
        kf = [alloc(st1, "kf%d" % i, [128, 512], F32) for i in range(2)]
        vf = [alloc(st1, "vf%d" % i, [128, 512], F32) for i in range(2)]
        g5 = [alloc(st1, "g5%d" % i, [128, 296], F32) for i in range(2)]
        uf = ftmp[0]
        rtab = [alloc(st1, "rtab%d" % i, [128, 192], F32) for i in range(2)]
        rtmp = alloc(st1, "rtmp", [128, 512], F32)
        qb = alloc(st1, "qb", [128, 512], BF16)
        kb = alloc(st1, "kb", [128, 512], BF16)
        qsb = alloc(st1, "qsb", [128, 256], BF16)
        ki3 = alloc(st1, "ki3", [128, 96], BF16)
        Dg = alloc(st1, "Dg", [128, 8, 128], BF16)
        plT = alloc(st1, "plT", [128, 4, 128], BF16)
        gpt = alloc(st1, "gpt", [128, 512], BF16)
        gpT = [alloc(st1, "gpT%d" % i, [128, 4, 128], BF16) for i in range(2)]
        gaT = [alloc(st1, "gaT%d" % i, [128, 4, 128], BF16) for i in range(2)]
        qTs = [alloc(st1, "qT%d" % i, [128, 8, 128], BF16) for i in range(3)]
        qiT = alloc(st1, "qiT", [96, 3, 128], BF16)
        rel = [alloc(st1, "rel%d" % i, [128, 512], BF16) for i in range(3)]
        maskb = alloc(st1, "maskb", [128, 2 * MH], BF16)
        mb_bufs = [maskb.b, P.buf("maskb1")]
        praw = [alloc(st1, "praw%d" % i, [128, 512], BF16) for i in range(3)]
        on = alloc(st1, "on", [128, 512], F32)
        gat = alloc(st1, "gat", [128, 512], BF16)
        rs = alloc(st1, "rs", [128, 8], F32)
        bs_lo = alloc(st1, "bs_lo", [128, 1], F32)
        bs_hi = alloc(st1, "bs_hi", [128, 1], F32)
        bs_mid = [alloc(st1, "bs_mid%d" % i, [128, 1], F32) for i in range(2)]
        bs_cnt = alloc(st1, "bs_cnt", [128, 1], F32)
        bs_cnt2 = alloc(st1, "bs_cnt2", [128, 1], F32)
        bs_u = alloc(st1, "bs_u", [128, 1], F32)
        bs_ht = alloc(st1, "bs_ht", [128, NI_BISECT], F32)
        bs_thr = alloc(st1, "bs_thr", [128, 1], F32)
        cst = [alloc(st1, "cst%d" % i, [128, 2, 256], F32) for i in range(2)]
        cbf = alloc(st1, "cbf", [128, 2, 256], BF16)
        spf = alloc(st1, "spf", [16, 512], F32)
        spb = alloc(st1, "spb", [16, 512], BF16)
        ckif = alloc(st1, "ckif", [128, max(L0 // 128, 1), 32], F32)
        cki3 = alloc(st1, "cki3", [128, 96], BF16)

        rot = {"rel": 0, "mk": 0, "praw": 0, "pm": 0, "L": 0}

        def nxt(name, arr):
            rot[name] ^= 1
            return arr[rot[name]]

        P.op("pool", mk("memset", ap=Vg[:], constant=1.0), writes=[Vg.b])
        for qz in qTs:
            P.op("pool", mk("memset", ap=qz[:], constant=0.0), writes=[qz.b])

        def rope_inplace(t, nt, nh, hd, tab, toff, eng="pool"):
            h2 = hd // 2
            xv = lambda: t.rearrange("p (h d) -> p h d", d=hd)
            tv = lambda: rtmp[0:nt, 0:nh * hd].rearrange("p (h d) -> p h d", d=hd)
            cc = lambda: tab[0:nt, toff:toff + hd].unsqueeze(1).broadcast_to([nt, nh, hd])
            sn = lambda: tab[0:nt, toff + hd:toff + hd + h2].unsqueeze(1).broadcast_to([nt, nh, h2])
            sp_ = lambda: tab[0:nt, toff + hd + h2:toff + 2 * hd].unsqueeze(1).broadcast_to([nt, nh, h2])
            return xv, tv, cc, sn, sp_, h2

        def do_rope(tl, col0, nt, nh, hd, tab, toff, eng):
            t = tl[0:nt, col0:col0 + nh * hd]
            xv, tv, cc, sn, sp_, h2 = rope_inplace(t, nt, nh, hd, tab, toff)
            P.op(eng, mk("tensor_tensor", out=tv()[:, :, 0:h2], in0=xv()[:, :, h2:hd], in1=sn(), op=ALU.mult),
                 reads=[tl.b, tab.b], writes=[rtmp.b])
            P.op(eng, mk("tensor_tensor", out=tv()[:, :, h2:hd], in0=xv()[:, :, 0:h2], in1=sp_(), op=ALU.mult),
                 reads=[tl.b, tab.b], writes=[rtmp.b])
            P.op(eng, mk("tensor_tensor", out=xv(), in0=xv(), in1=cc(), op=ALU.mult),
                 reads=[tl.b, tab.b], writes=[tl.b])
            P.op(eng, mk("tensor_tensor", out=xv(), in0=xv(), in1=tv(), op=ALU.add),
                 reads=[tl.b, rtmp.b], writes=[tl.b])

        gtile = [0]

        for seq in seqs:
            nt = seq["nt"]
            ntl = seq["ntiles"]
            isS = seq["kind"] == "s"
            xsrc = x_src_of(seq)
            Ksel = seq["K"]
            if not isS:
                kTv = kT[:, 0:4 * T].rearrange("p (c s) -> p c s", c=4)
                Vv = Vg[:, 0:NTP * 8 * VS].rearrange("p (t h d) -> p t h d", h=8, d=VS)
                okd, ovd, okid, opld = ok_p[l, seq["idx"]], ov_p[l, seq["idx"]], oki_p[l, seq["idx"]], opl_p[l, seq["idx"]]
            else:
                kTv = kT[:, 0:2 * LS].rearrange("p (c s) -> p c s", c=2)
                Vv = Vg[:, 0:NCS * 4 * VS].rearrange("p (t h d) -> p t h d", h=4, d=VS)
                okd, ovd, okid, opld = ok_s[l], ov_s[l], oki_s[l], opl_s[l]

            if isS and L0 > 0:
                nct = L0 // 128
                P.dma("cki", mk("dma_start", out=ckif[:, 0:nct, :],
                                                   in_=cki[l].rearrange("(t p) d -> p t d", p=128)),
                      writes=[ckif.b])
                for c in range(nct):
                    P.op("dve", mk("tensor_copy",
                        out=cki3[:].rearrange("p (r d) -> p r d", r=3),
                        in_=ckif[:, c, :].unsqueeze(1).broadcast_to([128, 3, 32])),
                        reads=[ckif.b], writes=[cki3.b])
                    transpose_to(cki3, cki3.b, 128, [(0, 96)],
                                 lambda wm, n, c=c: kiT[0:96, c * 128:(c + 1) * 128].unsqueeze(1), kiT.b,
                                 evac_eng="act" if c % 2 else "dve")
                P.dma("spf", mk("dma_start", out=spf[0:15, :], in_=spool[l]), writes=[spf.b])
                P.op("dve", mk("tensor_copy", out=spb[0:15, :], in_=spf[0:15, :]), reads=[spf.b], writes=[spb.b])

            def stageA1(i):
                qT = qTs[gtile[0] % 3]
                sza = szas[gtile[0] % 3]
                gi = gtile[0]
                gtile[0] += 1
                slot = gi % 2
                if i + 1 < ntl:
                    load_x(seq, i + 1, (gi + 1) % 2, xsrc)
                rt = rtab[gi % 2]
                rp0 = seq["rope0"] + i * 128
                P.dma("rt%d" % (gi % 2), mk("dma_start", out=rt[0:nt, :], in_=rope[rp0:rp0 + nt, :]),
                      writes=[rt.b])
                norm_hT(seq, slot)
                key0 = (L0 if isS else 0) + i * 128
                S = key0 + nt
                if isS:
                    sview = score[:, 0:S]
                    sbufs = [sc_bufs[0], sc_bufs[1]]
                else:
                    sview = score[:, (gtile[0] - 1) % 2 * MH:(gtile[0] - 1) % 2 * MH + S]
                    sbufs = [sc_bufs[(gtile[0] - 1) % 2]]
                ucur = utok[gi % 2]
                uprev = utok[(gi + 1) % 2]
                kfi = kf[gi % 2]
                vfi = vf[gi % 2]
                g5i = g5[gi % 2]

                yield
                m = proj(nt, W1, 0, 512)
                P.op("act", mk("copy", out=ucur[0:nt, :], in_=m[0:nt, :]), reads=[m.b], writes=[ucur.b])
                if i == ntl - 1:
                    P.op("dve", mk("tensor_copy", out=uf[0:nt, :], in_=m[0:nt, :]), reads=[m.b], writes=[uf.b])
                    P.dma("opl", mk("dma_start", out=opld, in_=uf[nt - 15:nt, :]), reads=[uf.b])
                yield
                m = proj(nt, W1, 512, 512)
                P.op("act", mk("activation", out=szp[0:nt, :], in_=m[0:nt, :], func=AF.Silu),
                     reads=[m.b], writes=[szp.b])
                yield
                m = proj(nt, W1, 1024, 512)
                P.op("act", mk("copy", out=qf[0:nt, :], in_=m[0:nt, :]), reads=[m.b], writes=[qf.b])
                yield
                m = proj(nt, W1, 1536, 512)
                P.op("act", mk("copy", out=kfi[0:nt, :], in_=m[0:nt, :]), reads=[m.b], writes=[kfi.b])
                yield
                m = proj(nt, W1, 2048, 512)
                P.op("act", mk("copy", out=vfi[0:nt, :], in_=m[0:nt, :]), reads=[m.b], writes=[vfi.b])
                if not isS:
                    P.op("dve", mk("tensor_copy", out=Vv[0:nt, i, :, 0:64],
                                   in_=m[0:nt, :].rearrange("p (h d) -> p h d", d=64)),
                         reads=[m.b], writes=[Vg.b])
                P.dma("ov%d" % (gi % 2), mk("dma_start", out=ovd[i * 128:i * 128 + nt, :], in_=vfi[0:nt, :]),
                      reads=[vfi.b])
                yield
                m = proj(nt, W1, 2560, 296)
                P.op("act", mk("copy", out=g5i[0:nt, :], in_=m[0:nt, 0:296]), reads=[m.b], writes=[g5i.b])
                yield
                m = proj(nt, W1, 2856, 512)
                P.op("act", mk("activation", out=sza[0:nt, :], in_=m[0:nt, :], func=AF.Silu),
                     reads=[m.b], writes=[sza.b])

                yield
                yield
                do_rope(qf, 0, nt, 8, 64, rt, 0, "pool")
                yield
                do_rope(kfi, 0, nt, 8, 64, rt, 0, "pool")
                yield
                do_rope(g5i, 0, nt, 8, 32, rt, 128, "pool")
                do_rope(g5i, 256, nt, 1, 32, rt, 128, "pool")
                P.dma("ok%d" % (gi % 2), mk("dma_start", out=okd[i * 128:i * 128 + nt, :], in_=kfi[0:nt, :]),
                      reads=[kfi.b])
                P.dma("oki%d" % (gi % 2), mk("dma_start", out=okid[i * 128:i * 128 + nt, :], in_=g5i[0:nt, 256:288]),
                      reads=[g5i.b])
                P.op("pool", mk("tensor_copy", out=qb[0:nt, :], in_=qf[0:nt, :]), reads=[qf.b], writes=[qb.b])
                P.op("pool", mk("tensor_copy", out=kb[0:nt, :], in_=kfi[0:nt, :]), reads=[kfi.b], writes=[kb.b])
                for h in range(8):
                    P.op("dve", mk("tensor_scalar", out=Dg[0:nt, h, 0:nt], in0=identf[0:nt, 0:nt],
                                   scalar1=g5i[0:nt, 288 + h:289 + h], scalar2=None, op0=ALU.mult),
                         reads=[identf.b, g5i.b], writes=[Dg.b])
                P.op("pool", mk("tensor_copy", out=qsb[0:nt, :], in_=g5i[0:nt, 0:256]), reads=[g5i.b], writes=[qsb.b])
                P.op("dve", mk("tensor_copy", out=ki3[0:nt, :].rearrange("p (r d) -> p r d", r=3),
                                                    in_=g5i[0:nt, 256:288].unsqueeze(1).broadcast_to([nt, 3, 32])),
                     reads=[g5i.b], writes=[ki3.b])

                yield
                transpose_to(ki3, ki3.b, nt, [(0, 96)],
                             lambda wm, n: kiT[0:96, key0:key0 + nt].unsqueeze(1), kiT.b, "act")
                transpose_to(qsb, qsb.b, nt, [(0, 96), (96, 96), (192, 64)],
                             lambda wm, n: qiT[0:96, 0:3, 0:nt], qiT.b, "dve")
                tvq = transpose_to(qb, qb.b, nt, [(c * 128, 128) for c in range(4)], None, None, "none")
                P.op("act", mk("copy", out=qT[0:64, 0:8:2, 0:nt], in_=tvq[0:64, 0:4, 0:nt]), reads=[tp.b], writes=[qT.b])
                P.op("act", mk("copy", out=qT[64:128, 1:8:2, 0:nt], in_=tvq[64:128, 0:4, 0:nt]), reads=[tp.b], writes=[qT.b])
                if not isS:
                    transpose_to(kb, kb.b, nt, [(c * 128, 128) for c in range(4)],
                                 lambda wm, n: kTv[:, 0:4, key0:key0 + nt], kT.b, "dve")

                yield
                ppv = pp[:].rearrange("p (g t) -> p g t", g=4)
                for g in range(4):
                    if isS:
                        P.op("pe", mk("matmul", out=ppv[:, g, 0:nt], lhsT=ucur[0:nt, g * 128:(g + 1) * 128],
                                                           rhs=band[0:nt, 1, g * 128:g * 128 + nt], start=True, stop=False),
                             reads=[ucur.b, band.b], writes=[pp.b])
                        P.op("pe", mk("matmul", out=ppv[:, g, 0:nt], lhsT=spb[0:15, g * 128:(g + 1) * 128],
                                                           rhs=band[0:15, 3, g * 128:g * 128 + nt], start=False, stop=True),
                             reads=[spb.b, band.b], writes=[pp.b])
                    elif i == 0:
                        P.op("pe", mk("matmul", out=ppv[:, g, 0:nt], lhsT=ucur[0:nt, g * 128:(g + 1) * 128],
                                                           rhs=band[0:nt, 0, g * 128:g * 128 + nt], start=True, stop=True),
                             reads=[ucur.b, band.b], writes=[pp.b])
                    else:
                        P.op("pe", mk("matmul", out=ppv[:, g, 0:nt], lhsT=ucur[0:nt, g * 128:(g + 1) * 128],
                                                           rhs=band[0:nt, 1, g * 128:g * 128 + nt], start=True, stop=False),
                             reads=[ucur.b, band.b], writes=[pp.b])
                        P.op("pe", mk("matmul", out=ppv[:, g, 0:nt], lhsT=uprev[:, g * 128:(g + 1) * 128],
                                                           rhs=band[:, 2, g * 128:g * 128 + nt], start=False, stop=True),
                             reads=[uprev.b, band.b], writes=[pp.b])
                P.op("act", mk("copy", out=plT[:, :, 0:nt], in_=ppv[:, :, 0:nt]), reads=[pp.b], writes=[plT.b])
                m = next_mm()
                for g in range(4):
                    P.op("pe", mk("matmul", out=m[0:nt, g * 128:(g + 1) * 128], lhsT=plT[:, g, 0:nt],
                                                            rhs=Wmix[:, g, :], start=True, stop=True),
                         reads=[plT.b, Wmix.b], writes=[m.b])
                P.op("dve", mk("tensor_tensor", out=gpt[0:nt, :], in0=m[0:nt, :], in1=szp[0:nt, :], op=ALU.mult),
                     reads=[m.b, szp.b], writes=[gpt.b])
                gpo = gpT[gi % 2]
                transpose_to(gpt, gpt.b, nt, [(c * 128, 128) for c in range(4)],
                             lambda wm, n: gpo[:, 0:4, 0:nt], gpo.b, "act")
                tix = seq["tile0"] + i
                P.dma("gp%d" % (gi % 2), mk("dma_start",
                    out=gp_scr[tix].rearrange("p (c t) -> p c t", c=4)[:, :, 0:nt], in_=gpo[:, :, 0:nt]),
                    reads=[gpo.b])

                yield
                ngk = (S + 511) // 512
                for gk in range(ngk):
                    k0 = gk * 512
                    gw = min(512, S - k0)
                    prev = None
                    for h in range(8):
                        if h % 2 == 0:
                            yield
                        m = next_mm()
                        bp = 32 * (h % 3)
                        P.op("pe", mk("matmul", out=
                            m[0:nt, 0:gw], lhsT=qiT[bp:bp + 32, h // 3, 0:nt], rhs=kiT[bp:bp + 32, k0:k0 + gw],
                            start=True, stop=True),
                            reads=[qiT.b, kiT.b], writes=[m.b])
                        rot["rel"] = (rot["rel"] + 1) % 3
                        r = rel[rot["rel"]]
                        P.op("act", mk("activation", out=r[0:nt, 0:gw], in_=m[0:nt, 0:gw], func=AF.Relu),
                             reads=[m.b], writes=[r.b])
                        if prev is not None:
                            ph, prr = prev
                            P.op("pe", mk("matmul", out=pp[0:nt, 0:gw], lhsT=Dg[0:nt, ph, 0:nt], rhs=prr[0:nt, 0:gw],
                                          start=(ph == 0), stop=False),
                                 reads=[Dg.b, prr.b], writes=[pp.b])
                        prev = (h, r)
                    ph, prr = prev
                    P.op("pe", mk("matmul", out=pp[0:nt, 0:gw], lhsT=Dg[0:nt, ph, 0:nt], rhs=prr[0:nt, 0:gw],
                                  start=False, stop=True),
                         reads=[Dg.b, prr.b], writes=[pp.b])
                    P.op("act", mk("copy", out=sview[0:nt, k0:k0 + gw], in_=pp[0:nt, 0:gw]), reads=[pp.b], writes=sbufs)
                return dict(i=i, gi=gi, S=S, ngk=ngk, qT=qT, sza=sza, kfi=kfi, vfi=vfi, tix=tix,
                            sview=sview, sbufs=sbufs)

            def stageA2(c):
                gi, S, sview, sbufs = c["gi"], c["S"], c["sview"], c["sbufs"]
                nch = (S + 127) // 128
                if isS:
                    mview = maskb[:, 0:S]
                    mbufs = [mb_bufs[0], mb_bufs[1]]
                else:
                    mview = maskb[:, (gi % 2) * MH:(gi % 2) * MH + S]
                    mbufs = [mb_bufs[gi % 2]]
                c["nch"], c["mview"], c["mbufs"] = nch, mview, mbufs
                yield
                need_topk = S > Ksel
                if not isS:
                    if need_topk:
                        P.op("dve", mk("tensor_reduce", out=bs_lo[0:nt, :], in_=sview[0:nt, 0:S - 64],
                                       axis=mybir.AxisListType.X, op=ALU.min),
                             reads=sbufs, writes=[bs_lo.b])
                    P.op("dve", mk("memset", ap=sview[0:64, S - 64:S], constant=NEG), writes=sbufs)
                else:
                    if need_topk:
                        P.op("dve", mk("tensor_reduce", out=bs_lo[0:nt, :], in_=sview[0:nt, 0:S],
                                       axis=mybir.AxisListType.X, op=ALU.min),
                             reads=sbufs, writes=[bs_lo.b])
                if need_topk:
                    P.op("dve", mk("tensor_reduce", out=bs_hi[0:nt, :], in_=sview[0:nt, 0:S],
                                   axis=mybir.AxisListType.X, op=ALU.max),
                         reads=sbufs, writes=[bs_hi.b])
                    P.op("dve", mk("tensor_tensor", out=bs_hi[0:nt, :], in0=bs_hi[0:nt, :], in1=bs_lo[0:nt, :],
                                   op=ALU.subtract),
                         reads=[bs_hi.b, bs_lo.b], writes=[bs_hi.b])
                    P.op("dve", mk("tensor_scalar", out=bs_ht[0:nt, :], in0=pw[0:nt, :], scalar1=bs_hi[0:nt, 0:1],
                                   scalar2=None, op0=ALU.mult),
                         reads=[pw.b, bs_hi.b], writes=[bs_ht.b])
                    P.op("dve", mk("tensor_tensor", out=bs_mid[0][0:nt, :], in0=bs_lo[0:nt, :], in1=bs_ht[0:nt, 0:1],
                                   op=ALU.add),
                         reads=[bs_lo.b, bs_ht.b], writes=[bs_mid[0].b])
                    for it in range(NI_BISECT):
                        yield
                        mc = bs_mid[it % 2]
                        mn = bs_mid[(it + 1) % 2]
                        P.op("dve", mk("tensor_scalar", out=mview[0:nt, 0:S], in0=sview[0:nt, 0:S],
                                       scalar1=mc[0:nt, 0:1], scalar2=None, op0=ALU.is_ge, op1=ALU.add,
                                       accum_out=bs_cnt[0:nt, :]),
                             reads=sbufs + [mc.b], writes=mbufs + [bs_cnt.b])
                        P.op("dve", mk("tensor_scalar", out=bs_u[0:nt, :], in0=bs_cnt[0:nt, :],
                                       scalar1=float(Ksel) - 0.5, scalar2=0.5, op0=ALU.is_ge, op1=ALU.subtract),
                             reads=[bs_cnt.b], writes=[bs_u.b])
                        P.op("dve", mk("scalar_tensor_tensor", out=mn[0:nt, :], in0=bs_u[0:nt, :],
                                       scalar=bs_ht[0:nt, it:it + 1], in1=mc[0:nt, :], op0=ALU.mult, op1=ALU.add),
                             reads=[bs_u.b, bs_ht.b, mc.b], writes=[mn.b])
                    mfin = bs_mid[NI_BISECT % 2]
                    P.op("dve", mk("scalar_tensor_tensor", out=bs_thr[0:nt, :],
                                   in0=bs_ht[0:nt, NI_BISECT - 1:NI_BISECT], scalar=-0.5, in1=mfin[0:nt, :],
                                   op0=ALU.mult, op1=ALU.add),
                         reads=[bs_ht.b, mfin.b], writes=[bs_thr.b])
                else:
                    P.op("dve", mk("memset", ap=bs_thr[0:nt, :], constant=-1.0e29), writes=[bs_thr.b])
                yield
                P.op("dve", mk("tensor_scalar", out=mview[0:nt, :], in0=sview[0:nt, 0:S], scalar1=bs_thr[0:nt, 0:1],
                               scalar2=None, op0=ALU.is_ge),
                     reads=sbufs + [bs_thr.b], writes=mbufs)

            def stageB1(c):
                i, gi, S, nch, ngk, qT, mview, mbufs, vfi = (c["i"], c["gi"], c["S"], c["nch"], c["ngk"], c["qT"],
                                                             c["mview"], c["mbufs"], c["vfi"])
                mTv = maskT[:, 0:nch * nt].rearrange("p (c t) -> p c t", t=nt)
                for gk in range(ngk):
                    yield
                    k0 = gk * 512
                    gw = min(512, S - k0)
                    blocks = []
                    cc_ = 0
                    while cc_ * 128 < gw:
                        blocks.append((k0 + cc_ * 128, min(128, gw - cc_ * 128)))
                        cc_ += 1
                    full = [b for b in blocks if b[1] == 128]
                    part = [b for b in blocks if b[1] < 128]
                    if full:
                        transpose_to(mview, mbufs[0], nt, full,
                                     lambda wm, n, k0=k0: mTv[:, k0 // 128:k0 // 128 + n, :], maskT.b, "actbias")
                    if part:
                        pc0, pw_ = part[0]
                        transpose_to(mview, mbufs[-1], nt, [(pc0, pw_)],
                                     lambda wm, n, pc0=pc0: mTv[0:wm, pc0 // 128:pc0 // 128 + 1, :],
                                     maskT.b, "actbias")
                halves = [(0, 8)] if not isS else [(0, 4), (4, 8)]
                for (h0, h1) in halves:
                    if isS:
                        hh = h0 // 4
                        nct = L0 // 128
                        for c4 in range(0, nct, 2):
                            n4 = min(2, nct - c4)
                            stg = cst[(c4 // 2) % 2]
                            P.dma("cst", mk("dma_start",
                                out=stg[:, 0:n4, :],
                                in_=ck[l].rearrange("(t p) f -> p t f", p=128)[:, c4:c4 + n4, hh * 256:(hh + 1) * 256]),
                                writes=[stg.b])
                            P.op("dve", mk("tensor_copy", out=cbf[:, 0:n4, :], in_=stg[:, 0:n4, :]),
                                 reads=[stg.b], writes=[cbf.b])
                            for j in range(n4):
                                c = c4 + j
                                transpose_to(cbf[:, j, :], cbf.b, 128, [(0, 128), (128, 128)],
                                             lambda wm, n, c=c: kTv[:, 0:2, c * 128:(c + 1) * 128], kT.b,
                                             "act" if j % 2 else "dve")
                            stg2 = cst[(c4 // 2 + 1) % 2]
                            P.dma("cst", mk("dma_start",
                                out=stg2[:, 0:n4, :],
                                in_=cv[l].rearrange("(t p) f -> p t f", p=128)[:, c4:c4 + n4, hh * 256:(hh + 1) * 256]),
                                writes=[stg2.b])
                            P.op("dve", mk("tensor_copy",
                                out=Vv[:, c4:c4 + n4, :, 0:64],
                                in_=stg2[:, 0:n4, :].rearrange("p t (h d) -> p t h d", d=64)),
                                reads=[stg2.b], writes=[Vg.b])
                        transpose_to(kb[:, hh * 256:(hh + 1) * 256], kb.b, nt, [(0, 128), (128, 128)],
                                     lambda wm, n: kTv[:, 0:2, L0:L0 + nt], kT.b, "act")
                        P.op("pool", mk("tensor_copy",
                            out=Vv[0:nt, NCS - 1, :, 0:64],
                            in_=vfi[0:nt, hh * 256:(hh + 1) * 256].rearrange("p (h d) -> p h d", d=64)),
                            reads=[vfi.b], writes=[Vg.b])
                    chunks = [(c, min(128, S - c * 128)) for c in range(nch)]
                    groups = []
                    cur = []
                    for cpair in chunks:
                        if cur and (len(cur) == 4 or cur[-1][1] != cpair[1]):
                            groups.append(cur)
                            cur = []
                        cur.append(cpair)
                    if cur:
                        groups.append(cur)
                    items = [(h, gidx) for h in range(h0, h1) for gidx in range(len(groups))]

                    def emit_L(h, gidx):
                        hl = h - h0
                        pr = hl // 2
                        grp = groups[gidx]
                        rc = grp[0][1]
                        ng = len(grp)
                        Lt = nxt("L", Lb)
                        Lv = Lt[:].rearrange("p (c t) -> p c t", t=128)
                        c0g = grp[0][0]
                        if nt == 128:
                            P.op("pe", mk("matmul", out=Lt[0:rc, 0:ng * 128], lhsT=ident[0:rc, 0:rc],
                                          rhs=maskT[0:rc, c0g * 128:(c0g + ng) * 128], start=True, stop=False),
                                 reads=[ident.b, maskT.b], writes=[Lt.b])
                        for j, (c, _) in enumerate(grp):
                            if nt != 128:
                                P.op("pe", mk("matmul", out=Lv[0:rc, j, 0:nt], lhsT=ident[0:rc, 0:rc],
                                              rhs=mTv[0:rc, c, :], start=True, stop=False),
                                     reads=[ident.b, maskT.b], writes=[Lt.b])
                            P.op("pe", mk("matmul", out=Lv[0:rc, j, 0:nt], lhsT=kTv[:, pr, c * 128:c * 128 + rc],
                                          rhs=qT[:, h, 0:nt], start=False, stop=(j == ng - 1 or nt != 128)),
                                 reads=[kT.b, qT.b], writes=[Lt.b])
                        rot["praw"] = (rot["praw"] + 1) % len(praw)
                        pr_t = praw[rot["praw"]]
                        prv = pr_t[:].rearrange("p (c t) -> p c t", t=128)
                        P.op("act", mk("activation", out=prv[0:rc, 0:ng, 0:nt], in_=Lv[0:rc, 0:ng, 0:nt],
                                       func=AF.Exp, scale=0.125),
                             reads=[Lt.b], writes=[pr_t.b])
                        return (h, gidx, pr_t, prv)

                    def emit_PV(h, gidx, pr_t, prv):
                        hl = h - h0
                        O = Ob[h // 4]
                        grp = groups[gidx]
                        rc = grp[0][1]
                        ng = len(grp)
                        for j, (c, _) in enumerate(grp):
                            first = (gidx == 0 and j == 0)
                            lastc = (gidx == len(groups) - 1 and j == ng - 1)
                            P.op("pe", mk("matmul", out=O[0:nt, h % 4, :], lhsT=prv[0:rc, j, 0:nt],
                                          rhs=Vv[0:rc, c, hl if isS else h, :], start=first, stop=lastc),
                                 reads=[pr_t.b, Vg.b], writes=[O.b])

                    pend = None
                    for (h, gidx) in items:
                        yield
                        curL = emit_L(h, gidx)
                        if pend is not None:
                            emit_PV(*pend)
                        pend = curL
                    if pend is not None:
                        emit_PV(*pend)

            def stageB2(c):
                gi, sza, tix = c["gi"], c["sza"], c["tix"]
                for ob in range(2):
                    O = Ob[ob]
                    P.op("dve", mk("reciprocal", out=rs[0:nt, ob * 4:ob * 4 + 4], in_=O[0:nt, :, 64]),
                         reads=[O.b], writes=[rs.b])
                    P.op("dve", mk("tensor_tensor",
                                   out=on[0:nt, ob * 256:(ob + 1) * 256].rearrange("p (h d) -> p h d", d=64),
                                   in0=O[0:nt, :, 0:64],
                                   in1=rs[0:nt, ob * 4:ob * 4 + 4].unsqueeze(2).broadcast_to([nt, 4, 64]),
                                   op=ALU.mult),
                         reads=[O.b, rs.b], writes=[on.b])
                P.op("pool", mk("tensor_tensor", out=gat[0:nt, :], in0=on[0:nt, :], in1=sza[0:nt, :], op=ALU.mult),
                     reads=[on.b, sza.b], writes=[gat.b])
                gao = gaT[gi % 2]
                transpose_to(gat, gat.b, nt, [(c * 128, 128) for c in range(4)],
                             lambda wm, n: gao[:, 0:4, 0:nt], gao.b, "act")
                P.dma("ga%d" % (gi % 2), mk("dma_start",
                    out=ga_scr[tix].rearrange("p (c t) -> p c t", c=4)[:, :, 0:nt], in_=gao[:, :, 0:nt]),
                    reads=[gao.b])

            load_x(seq, 0, gtile[0] % 2, xsrc)

            def drive(gens):
                res = [None] * len(gens)
                live = [g is not None for g in gens]
                while any(live):
                    for k, g in enumerate(gens):
                        if live[k]:
                            try:
                                next(g)
                            except StopIteration as ex:
                                res[k] = ex.value
                                live[k] = False
                return res

            ctx = {}
            for k in range(ntl + 2):
                gA1 = stageA1(k) if k < ntl else None
                gA2 = stageA2(ctx[k - 1]) if 0 <= k - 1 < ntl else None
                gB1 = stageB1(ctx[k - 2]) if 0 <= k - 2 < ntl else None
                r = drive([gA1, gA2, gB1])
                if gA1 is not None:
                    ctx[k] = r[0]
                if gB1 is not None:
                    stageB2(ctx[k - 2])
        P.barrier()
        st1.close()

        st2 = ExitStack()
        W2 = alloc(st2, "W2", [128, 8, C2], BF16)
        Wpo = alloc(st2, "Wpo", [128, 4, D], BF16)
        Wao = alloc(st2, "Wao", [128, 4, D], BF16)
        Wo = alloc(st2, "Wo", [128, 8, D], BF16)
        with ExitStack() as stl:
            stage = [alloc(stl, "stg%d" % i, [128, C2], F32) for i in range(2)]
            load_cast(lambda k: W2[:, k, :], lambda k: w_in[l, k * 128:(k + 1) * 128, C1:NCOL], 8, C2, stage, True, W2.b)
            load_cast(lambda k: Wpo[:, k, :], lambda k: w_po[l, k * 128:(k + 1) * 128, :], 4, D, stage, False, Wpo.b)
            load_cast(lambda k: Wao[:, k, :], lambda k: w_ao[l, k * 128:(k + 1) * 128, :], 4, D, stage, False, Wao.b)
            load_cast(lambda k: Wo[:, k, :], lambda k: w_o[l, k * 128:(k + 1) * 128, :], 8, D, stage, False, Wo.b)
            P.barrier()
        gpl = [alloc(st2, "gpl%d" % i, [128, 4, 128], BF16) for i in range(3)]
        gal = [alloc(st2, "gal%d" % i, [128, 4, 128], BF16) for i in range(3)]
        sgps = [alloc(st2, "sgp%d" % i, [128, D], BF16) for i in range(2)]
        sgas = [alloc(st2, "sga%d" % i, [128, D], BF16) for i in range(2)]
        xt.append(alloc(st2, "xt2", [128, D], F32))
        m1 = alloc(st2, "m1", [128, D], F32)
        t2 = alloc(st2, "t2", [128, 512], F32)
        mrg = alloc(st2, "mrg", [128, D], BF16)
        mT = alloc(st2, "mT", [128, 8, 128], BF16)
        yt = [alloc(st2, "yt%d" % i, [128, D], F32) for i in range(2)]

        prefetch = []
        if l + 1 < DEPTH:
            pst = [alloc(st2, "pst%d" % i, [128, C1], F32) for i in range(2)]
            load_gcol(l + 1)

            def mk_chunk(k):
                def emit():
                    sgt = pst[k % 2]
                    P.dma("pst", mk("dma_start", out=sgt[:, 0:C1], in_=w_in[l + 1, k * 128:(k + 1) * 128, 0:C1]),
                          writes=[sgt.b])
                    if k % 2 == 0:
                        P.op("dve", mk("tensor_scalar", out=W1[:, k, :], in0=sgt[:, 0:C1], scalar1=gcol[:, k:k + 1],
                                       scalar2=None, op0=ALU.mult), reads=[sgt.b, gcol.b], writes=[W1.b])
                    else:
                        P.op("act", mk("activation", out=W1[:, k, :], in_=sgt[:, 0:C1], func=AF.Copy,
                                       scale=gcol[:, k:k + 1]), reads=[sgt.b, gcol.b], writes=[W1.b])
                return emit
            prefetch = [mk_chunk(k) for k in range(8)]
        gtile2 = [0]
        for seq in seqs:
            nt = seq["nt"]
            ntl = seq["ntiles"]
            xsrc = x_src_of(seq)

            def loads2(i, gi):
                load_x(seq, i, gi % 3, xsrc)
                tix = seq["tile0"] + i
                P.dma("gpl%d" % (gi % 3), mk("dma_start",
                    out=gpl[gi % 3][:, :, 0:nt], in_=gp_scr[tix].rearrange("p (c t) -> p c t", c=4)[:, :, 0:nt]),
                    writes=[gpl[gi % 3].b])
                P.dma("gal%d" % (gi % 3), mk("dma_start",
                    out=gal[gi % 3][:, :, 0:nt], in_=ga_scr[tix].rearrange("p (c t) -> p c t", c=4)[:, :, 0:nt]),
                    writes=[gal[gi % 3].b])

            def stage2A(i):
                gi = gtile2[0]
                gtile2[0] += 1
                if prefetch and gi % 3 == 1:
                    prefetch.pop(0)()
                slot = gi % 3
                sgp = sgps[gi % 2]
                sga = sgas[gi % 2]
                if i + 1 < ntl:
                    loads2(i + 1, gi + 1)
                norm_hT(seq, slot)
                for hh in range(2):
                    m = proj(nt, W2, hh * 512, 512)
                    P.op("act", mk("activation", out=sgp[0:nt, hh * 512:(hh + 1) * 512], in_=m[0:nt, :],
                                                                   func=AF.Sigmoid),
                         reads=[m.b], writes=[sgp.b])
                for hh in range(2):
                    m = proj(nt, W2, 1024 + hh * 512, 512)
                    P.op("act", mk("activation", out=sga[0:nt, hh * 512:(hh + 1) * 512], in_=m[0:nt, :],
                                                                   func=AF.Sigmoid),
                         reads=[m.b], writes=[sga.b])
                return dict(i=i, gi=gi, slot=slot, sgp=sgp, sga=sga)

            def stage2B(c):
                i, gi, slot, sgp, sga = c["i"], c["gi"], c["slot"], c["sgp"], c["sga"]
                x = xt[slot]
                gp_, ga_ = gpl[slot], gal[slot]
                for hh in range(2):
                    m = next_mm()
                    for k in range(4):
                        P.op("pe", mk("matmul", out=m[0:nt, :], lhsT=gp_[:, k, 0:nt],
                                                                       rhs=Wpo[:, k, hh * 512:(hh + 1) * 512],
                                                                       start=(k == 0), stop=(k == 3)),
                             reads=[gp_.b, Wpo.b], writes=[m.b])
                    P.op("dve", mk("tensor_tensor", out=m1[0:nt, hh * 512:(hh + 1) * 512], in0=m[0:nt, :],
                                                                      in1=sgp[0:nt, hh * 512:(hh + 1) * 512], op=ALU.mult),
                         reads=[m.b, sgp.b], writes=[m1.b])
                for hh in range(2):
                    m = next_mm()
                    for k in range(4):
                        P.op("pe", mk("matmul", out=m[0:nt, :], lhsT=ga_[:, k, 0:nt],
                                                                       rhs=Wao[:, k, hh * 512:(hh + 1) * 512],
                                                                       start=(k == 0), stop=(k == 3)),
                             reads=[ga_.b, Wao.b], writes=[m.b])
                    P.op("dve", mk("tensor_tensor", out=t2[0:nt, :], in0=m[0:nt, :],
                                                                      in1=sga[0:nt, hh * 512:(hh + 1) * 512], op=ALU.mult),
                         reads=[m.b, sga.b], writes=[t2.b])
                    P.op("pool", mk("tensor_tensor", out=mrg[0:nt, hh * 512:(hh + 1) * 512],
                                                                  in0=m1[0:nt, hh * 512:(hh + 1) * 512], in1=t2[0:nt, :],
                                                                  op=ALU.add),
                         reads=[m1.b, t2.b], writes=[mrg.b])
                transpose_to(mrg, mrg.b, nt, [(c * 128, 128) for c in range(8)],
                             lambda wm, n: mT[:, 0:8, 0:nt], mT.b, "act")
                for hh in range(2):
                    m = next_mm()
                    for k in range(8):
                        P.op("pe", mk("matmul", out=m[0:nt, :], lhsT=mT[:, k, 0:nt],
                                                                       rhs=Wo[:, k, hh * 512:(hh + 1) * 512],
                                                                       start=(k == 0), stop=(k == 7)),
                             reads=[mT.b, Wo.b], writes=[m.b])
                    P.op("dve", mk("tensor_tensor", out=x[0:nt, hh * 512:(hh + 1) * 512],
                                                                      in0=x[0:nt, hh * 512:(hh + 1) * 512], in1=m[0:nt, :],
                                                                      op=ALU.add),
                         reads=[m.b, x.b], writes=[x.b])
                r0 = seq["tok0"] + i * 128
                if not last:
                    P.dma("x%d" % slot, mk("dma_start", out=xscr[r0:r0 + nt, :], in_=x[0:nt, :]), reads=[x.b])
                else:
                    y = yt[gi % 2]
                    P.op("act", mk("activation", out=xn[0:nt, 0:D], in_=x[0:nt, :], func=AF.Square,
                                                       accum_out=ssq[0:nt, :]),
                         reads=[x.b], writes=[xn.b, ssq.b])
                    P.op("act", mk("activation", out=rstd[0:nt, :], in_=ssq[0:nt, :], func=AF.Ln, scale=1.0 / D, bias=EPS),
                         reads=[ssq.b], writes=[rstd.b])
                    P.op("act", mk("activation", out=rstd[0:nt, :], in_=rstd[0:nt, :], func=AF.Exp, scale=-0.5),
                         reads=[rstd.b], writes=[rstd.b])
                    P.op("dve", mk("scalar_tensor_tensor", out=y[0:nt, :], in0=x[0:nt, :], scalar=rstd[0:nt, 0:1],
                                                                      in1=gfbc[0:nt, :], op0=ALU.mult, op1=ALU.mult),
                         reads=[x.b, rstd.b, gfbc.b], writes=[y.b])
                    yo = seq["y_out"]
                    P.dma("y%d" % slot, mk("dma_start", out=yo[i * 128:i * 128 + nt, :], in_=y[0:nt, :]),
                          reads=[y.b])
            loads2(0, gtile2[0])
            pend = None
            for i in range(ntl):
                cA = stage2A(i)
                if pend is not None:
                    stage2B(pend)
                pend = cA
            stage2B(pend)
        while prefetch:
            prefetch.pop(0)()
        P.barrier()
        xt.pop()
        st2.close()

    P.final_wait()
    P.emit()
    stack0.close()
    return nc


_CACHE = {}


def kernel(x_prompt, x_sample, cache_k, cache_v, cache_kidx, state_pool, norm_g, w_in, w_pool_mix,
           pool_scale, w_pool_out, w_attn_out, w_o, final_norm_g):
    NCORES = 8
    f = lambda a: np.ascontiguousarray(np.asarray(a, dtype=np.float32))
    x_prompt, x_sample = f(x_prompt), f(x_sample)
    cache_k, cache_v, cache_kidx, state_pool = f(cache_k), f(cache_v), f(cache_kidx), f(state_pool)
    BP, T, _ = x_prompt.shape
    BS, TS, _ = x_sample.shape
    DEPTH = cache_k.shape[0]
    L0 = cache_k.shape[2]
    assert BP % NCORES == 0 and BS == NCORES
    NP = BP // NCORES
    cfg = dict(NP=NP, T=T, TS=TS, L0=L0, DEPTH=DEPTH, KP=min(256, T // 4), KS=min(256, (L0 + TS) // 4))
    key = tuple(sorted(cfg.items()))
    if key not in _CACHE:
        _CACHE[key] = build(cfg)
    nc = _CACHE[key]

    rope = _rope_table(list(range(T)) + list(range(L0, L0 + TS)))
    bands = _band_tables().reshape(4, 128, 512)
    pw2 = np.tile((2.0 ** -(np.arange(NI_BISECT, dtype=np.float64) + 1)).astype(np.float32)[None, :], (128, 1))
    shared = dict(norm_g=f(norm_g), w_in=f(w_in), w_mix=f(w_pool_mix), pscale=f(pool_scale), w_po=f(w_pool_out),
                  w_ao=f(w_attn_out), w_o=f(w_o), gfin=f(final_norm_g).reshape(1, D), rope=rope, bands=bands, pw2=pw2)
    in_maps = []
    for c in range(NCORES):
        m = dict(shared)
        m["x_p"] = x_prompt[c * NP:(c + 1) * NP]
        m["x_s"] = x_sample[c]
        m["ck"] = cache_k[:, c].reshape(DEPTH, L0, 512)
        m["cv"] = cache_v[:, c].reshape(DEPTH, L0, 512)
        m["cki"] = cache_kidx[:, c]
        m["spool"] = state_pool[:, c]
        in_maps.append(m)
    res = run_bass_kernel_spmd(nc, in_maps, core_ids=list(range(NCORES)))
    R = res.results
    cat = lambda name, ax: np.concatenate([np.asarray(r[name]) for r in R], axis=ax)
    stk = lambda name, ax: np.stack([np.asarray(r[name]) for r in R], axis=ax)
    y_prompt = cat("y_p", 0)
    y_sample = stk("y_s", 0)
    nk_p = cat("ok_p", 1).reshape(DEPTH, BP, T, 8, 64)
    nv_p = cat("ov_p", 1).reshape(DEPTH, BP, T, 8, 64)
    nki_p = cat("oki_p", 1)
    npl_p = cat("opl_p", 1)
    nk_s = stk("ok_s", 1).reshape(DEPTH, BS, TS, 8, 64)
    nv_s = stk("ov_s", 1).reshape(DEPTH, BS, TS, 8, 64)
    nki_s = stk("oki_s", 1)
    npl_s = stk("opl_s", 1)
    return (y_prompt, y_sample, nk_p, nv_p, nki_p, npl_p, nk_s, nv_s, nki_s, npl_s)
```

```python
from contextlib import ExitStack
import os
import numpy as np
import concourse.bass as bass
import concourse.mybir as mybir
from concourse.bass_utils import run_bass_kernel_spmd

F32 = mybir.dt.float32
BF16 = mybir.dt.bfloat16
ALU = mybir.AluOpType
AF = mybir.ActivationFunctionType

ENGS = ("pe", "act", "dve", "pool", "sp")

D = 1024
NCOL = 5416
C1 = 3368
C2 = NCOL - C1
NH = 8
NEG = -1.0e30
NI_BISECT = int(os.environ.get('KNI', '24'))
EPS = 1e-6


class Buf:
    __slots__ = ("name", "w", "r", "excl")

    def __init__(self, name):
        self.name = name
        self.excl = False
        self.w = None
        self.r = []


class Prog:
    def __init__(self, nc):
        self.nc = nc
        self.q = {e: [] for e in ENGS}
        self.tick = {e: 0 for e in ENGS}
        self.sem = {e: nc.alloc_semaphore("sem_" + e) for e in ENGS}
        self.seen = {e: {} for e in ENGS}
        self.dsem = {}
        self.dtick = {}
        self.nbuf = 0
        self.count = 0
        self.limit = int(os.environ.get("KLIMIT", "1000000000"))

    def buf(self, name=None):
        self.nbuf += 1
        return Buf(name or "b%d" % self.nbuf)

    def _need(self, reads, writes):
        need = {}
        for b in reads:
            if b.w is not None:
                s, t = b.w
                if need.get(s, 0) < t:
                    need[s] = t
        for b in writes:
            if b.w is not None:
                s, t = b.w
                if need.get(s, 0) < t:
                    need[s] = t
            for s, t in b.r:
                if need.get(s, 0) < t:
                    need[s] = t
        return need

    def _waits(self, eng, need):
        waits = []
        seen = self.seen[eng]
        for s, t in need.items():
            if s == eng and eng in ("pe", "sp"):
                continue
            if seen.get(s, 0) >= t:
                continue
            seen[s] = t
            waits.append((s, t))
        return waits

    def _mark(self, src, reads, writes):
        for b in reads:
            if len(b.r) > 24:
                d = {}
                for s, t in b.r:
                    if d.get(s, 0) < t:
                        d[s] = t
                b.r = list(d.items())
            b.r.append(src)
        for b in writes:
            b.w = src
            b.r = []

    def op(self, eng, fn, reads=(), writes=()):
        self.count += 1
        if self.count > self.limit:
            return
        xr = [b for b in reads if b.excl]
        if xr:
            writes = list(writes) + [b for b in xr if b not in writes]
        waits = self._waits(eng, self._need(reads, writes))
        self.tick[eng] += 1
        my = self.tick[eng]
        self.q[eng].append((waits, fn, ("e", eng)))
        self._mark((eng, my), reads, writes)

    def dma(self, stream, fn, reads=(), writes=(), eng="sp"):
        stream = (writes[0] if writes else reads[0]).name
        self.count += 1
        if self.count > self.limit:
            return
        if stream not in self.dsem:
            self.dsem[stream] = self.nc.alloc_semaphore("dsem_" + stream)
            self.dtick[stream] = 0
        waits = self._waits(eng, self._need(reads, writes))
        self.dtick[stream] += 16
        my = self.dtick[stream]
        self.q[eng].append((waits, fn, ("d", stream)))
        self._mark(("dma:" + stream, my), reads, writes)

    def _semof(self, s):
        if s.startswith("dma:"):
            return self.dsem[s[4:]]
        return self.sem[s]

    def barrier(self):
        need = {}
        for e in ENGS:
            if self.tick[e] > 0:
                need[e] = self.tick[e]
        for s, t in self.dtick.items():
            if t > 0:
                need["dma:" + s] = t
        for e in ENGS:
            waits = self._waits(e, dict(need))
            if waits:
                self.q[e].append((waits, None, None))

    def final_wait(self, eng="sp"):
        need = {}
        for s, t in self.dtick.items():
            if t > 0:
                need["dma:" + s] = t
        for e in ENGS:
            if e != eng and self.tick[e] > 0:
                need[e] = self.tick[e]
        waits = self._waits(eng, need)
        if waits:
            self.q[eng].append((waits, None, None))

    def emit(self):
        nc = self.nc
        engobj = {"pe": "tensor", "act": "scalar", "dve": "vector", "pool": "gpsimd", "sp": "sync"}
        with nc.Block() as block:
            for e in ENGS:
                items = self.q[e]
                if not items:
                    continue

                def body(eo, items=items):
                    for waits, fn, inc in items:
                        for s, t in waits:
                            eo.wait_ge(self._semof(s), t)
                        if fn is None:
                            continue
                        ins = fn(eo)
                        if inc[0] == "e":
                            ins.then_inc(self.sem[inc[1]], 1)
                        else:
                            ins.then_inc(self.dsem[inc[1]], 16)

                getattr(block, engobj[e])(body)


def mk(meth, **kw):
    return lambda e: getattr(e, meth)(**kw)


class Tl:
    def __init__(self, t, b):
        self.t = t
        self.b = b

    def __getitem__(self, k):
        return self.t[k]


def _rope_table(positions):
    pos = np.asarray(positions, dtype=np.float32)
    out = np.zeros((len(pos), 192), np.float32)
    for (d, off) in ((64, 0), (32, 128)):
        inv = (10000.0 ** (-np.arange(0, d, 2, dtype=np.float32) / np.float32(d))).astype(np.float32)
        ang = (pos[:, None] * inv[None, :]).astype(np.float32).astype(np.float64)
        c = np.cos(ang).astype(np.float32)
        s = np.sin(ang).astype(np.float32)
        h = d // 2
        out[:, off:off + h] = c
        out[:, off + h:off + 2 * h] = c
        out[:, off + 2 * h:off + 3 * h] = -s
        out[:, off + 3 * h:off + 4 * h] = s
    return out


def _band_tables():
    B = np.zeros((4, 128, 4, 128), np.float32)
    for g, w in enumerate((2, 4, 8, 16)):
        for t in range(128):
            for tp in range(max(0, t - w + 1), t + 1):
                B[0, tp, g, t] += 1.0 / min(t + 1, w)
                B[1, tp, g, t] += 1.0 / w
            B[0, t, g, t] -= 1.0
            B[1, t, g, t] -= 1.0
            for tp in range(128):
                if t - (tp - 128) < w:
                    B[2, tp, g, t] = 1.0 / w
            for j in range(15):
                if t - (j - 15) < w:
                    B[3, j, g, t] = 1.0 / w
    return B


def build(cfg):
    NP, T, TS, L0, DEPTH = cfg["NP"], cfg["T"], cfg["TS"], cfg["L0"], cfg["DEPTH"]
    KP, KS = cfg["KP"], cfg["KS"]
    assert T % 128 == 0 and TS == 64 and L0 % 128 == 0
    NTP = T // 128
    LS = L0 + TS
    NCS = L0 // 128 + 1
    SMAX = max(T, LS)
    KTW = max(4 * T, 2 * LS)
    VS = int(os.environ.get('KVS', '65'))
    VW = max(NTP * 8 * VS, NCS * 4 * VS)
    NTOK = NP * T + TS
    NTILES = NP * NTP + 1

    nc = bass.Bass("TRN2", target_bir_lowering=False, dynamic_dma_scratch_size=256)
    P = Prog(nc)

    def din(name, shape, dt=F32):
        return nc.dram_tensor(name, list(shape), dt, kind="ExternalInput").ap()

    def dout(name, shape, dt=F32):
        return nc.dram_tensor(name, list(shape), dt, kind="ExternalOutput").ap()

    x_p = din("x_p", [NP, T, D])
    x_s = din("x_s", [TS, D])
    ck = din("ck", [DEPTH, L0, 512])
    cv = din("cv", [DEPTH, L0, 512])
    cki = din("cki", [DEPTH, L0, 32])
    spool = din("spool", [DEPTH, 15, 512])
    norm_g = din("norm_g", [DEPTH, D])
    w_in = din("w_in", [DEPTH, D, NCOL])
    w_mix = din("w_mix", [DEPTH, 4, 128, 128])
    pscale = din("pscale", [DEPTH, 512])
    w_po = din("w_po", [DEPTH, 512, D])
    w_ao = din("w_ao", [DEPTH, 512, D])
    w_o = din("w_o", [DEPTH, D, D])
    gfin = din("gfin", [1, D])
    rope = din("rope", [T + TS, 192])
    bands = din("bands", [4, 128, 512])
    pw2 = din("pw2", [128, NI_BISECT])

    y_p = dout("y_p", [NP, T, D])
    y_s = dout("y_s", [TS, D])
    ok_p = dout("ok_p", [DEPTH, NP, T, 512])
    ov_p = dout("ov_p", [DEPTH, NP, T, 512])
    oki_p = dout("oki_p", [DEPTH, NP, T, 32])
    opl_p = dout("opl_p", [DEPTH, NP, 15, 512])
    ok_s = dout("ok_s", [DEPTH, TS, 512])
    ov_s = dout("ov_s", [DEPTH, TS, 512])
    oki_s = dout("oki_s", [DEPTH, TS, 32])
    opl_s = dout("opl_s", [DEPTH, 15, 512])

    xscr = nc.dram_tensor("xscr", [NTOK, D], F32).ap()
    gp_scr = nc.dram_tensor("gp_scr", [NTILES, 128, 512], BF16).ap()
    ga_scr = nc.dram_tensor("ga_scr", [NTILES, 128, 512], BF16).ap()

    seqs = []
    for s in range(NP):
        seqs.append(dict(kind="p", idx=s, T=T, nt=128, ntiles=NTP, tok0=s * T, tile0=s * NTP,
                         x_in=x_p[s], y_out=y_p[s], K=KP, rope0=0))
    seqs.append(dict(kind="s", idx=0, T=TS, nt=TS, ntiles=1, tok0=NP * T, tile0=NP * NTP,
                     x_in=x_s, y_out=y_s, K=KS, rope0=T))

    stack0 = ExitStack()

    uniq = [0]

    def alloc(st, name, shape, dt):
        uniq[0] += 1
        t = st.enter_context(nc.sbuf_tensor("%s_%d" % (name, uniq[0]), list(shape), dt))
        return Tl(t, P.buf(name))

    def palloc(name, shape, dt):
        t = nc.alloc_psum_tensor(name, list(shape), dt)
        b = P.buf(name)
        b.excl = True
        return Tl(t, b)

    mm = [palloc("mm%d" % i, [128, 512], F32) for i in range(2)]
    tp = palloc("tp", [128, 1024], BF16)
    pp = palloc("pp", [128, 512], F32)
    Lb = [palloc("L%d" % i, [128, 512], F32) for i in range(2)]
    Ob = [palloc("O%d" % i, [128, 4, VS], F32) for i in range(2)]
    mmi = [0]

    def next_mm():
        mmi[0] ^= 1
        return mm[mmi[0]]

    ident = alloc(stack0, "ident", [128, 128], BF16)
    identf = alloc(stack0, "identf", [128, 128], F32)
    ones1 = alloc(stack0, "ones1", [1, 128], F32)
    gfbc = alloc(stack0, "gfbc", [128, D], F32)
    band = alloc(stack0, "band", [128, 4, 512], BF16)
    pw = alloc(stack0, "pw", [128, NI_BISECT], F32)
    xt = [alloc(stack0, "xt%d" % i, [128, D], F32) for i in range(2)]
    xn = alloc(stack0, "xn", [128, D], BF16)
    hT = alloc(stack0, "hT", [128, 8, 128], BF16)
    ssq = alloc(stack0, "ssq", [128, 1], F32)
    rstd = alloc(stack0, "rstd", [128, 1], F32)
    gcol = alloc(stack0, "gcol", [128, 8], F32)
    W1 = alloc(stack0, "W1", [128, 8, C1], BF16)
    ftmp = [alloc(stack0, "ftmp%d" % i, [128, 512], F32) for i in range(2)]
    ftmpi = [0]

    def next_ftmp():
        ftmpi[0] ^= 1
        return ftmp[ftmpi[0]]

    P.op("pool", mk("memset", ap=identf[:], constant=0.0), writes=[identf.b])
    P.op("pool", mk("affine_select", out=identf[:], in_=identf[:], pattern=[[-1, 128]],
                                           compare_op=ALU.not_equal, fill=1.0, base=0,
                                           channel_multiplier=1),
         reads=[identf.b], writes=[identf.b])
    P.op("dve", mk("tensor_copy", out=ident[:], in_=identf[:]), reads=[identf.b], writes=[ident.b])
    P.op("dve", mk("memset", ap=ones1[:], constant=1.0), writes=[ones1.b])
    P.dma("c0", mk("dma_start", out=pw[:], in_=pw2), writes=[pw.b])
    for kind in range(4):
        f = next_ftmp()
        P.dma("c1", mk("dma_start", out=f[:], in_=bands[kind]), writes=[f.b])
        P.op("dve", mk("tensor_copy", out=band[:, kind, :], in_=f[:]),
             reads=[f.b], writes=[band.b])
    grow = alloc(stack0, "grow", [1, D], F32)
    P.dma("c0", mk("dma_start", out=grow[:], in_=gfin), writes=[grow.b])
    for hh in range(2):
        m = next_mm()
        P.op("pe", mk("matmul", out=m[:], lhsT=ones1[0:1, :], rhs=grow[0:1, hh * 512:(hh + 1) * 512],
                                                  start=True, stop=True),
             reads=[ones1.b, grow.b], writes=[m.b])
        P.op("dve", mk("tensor_copy", out=gfbc[:, hh * 512:(hh + 1) * 512], in_=m[:]),
             reads=[m.b], writes=[gfbc.b])

    def load_gcol(l):
        g8 = next_ftmp()
        P.dma("c1", mk("dma_start", out=g8[0:8, 0:128], in_=norm_g[l].rearrange("(k p) -> k p", p=128)),
              writes=[g8.b])
        m = next_mm()
        P.op("pe", mk("transpose", out=m[:, 0:8], in_=g8[0:8, 0:128], identity=identf[0:8, 0:8]),
             reads=[g8.b, identf.b], writes=[m.b])
        P.op("dve", mk("tensor_copy", out=gcol[:], in_=m[:, 0:8]), reads=[m.b], writes=[gcol.b])

    def load_cast(dst_ap_fn, src_rows_fn, nk, ncols, stage, scale_gcol, dstb):
        for k in range(nk):
            s = stage[k % 2]
            P.dma("wst%d" % (k % 2), mk("dma_start", out=s[:, 0:ncols], in_=src_rows_fn(k)),
                  writes=[s.b])
            if scale_gcol:
                if k % 2 == 0:
                    P.op("dve", mk("tensor_scalar", out=dst_ap_fn(k), in0=s[:, 0:ncols],
                                                                    scalar1=gcol[:, k:k + 1], scalar2=None,
                                                                    op0=ALU.mult),
                         reads=[s.b, gcol.b], writes=[dstb])
                else:
                    P.op("act", mk("activation", out=dst_ap_fn(k), in_=s[:, 0:ncols],
                                                                 func=AF.Copy, scale=gcol[:, k:k + 1]),
                         reads=[s.b, gcol.b], writes=[dstb])
            else:
                if k % 2 == 0:
                    P.op("dve", mk("tensor_copy", out=dst_ap_fn(k), in_=s[:, 0:ncols]),
                         reads=[s.b], writes=[dstb])
                else:
                    P.op("act", mk("copy", out=dst_ap_fn(k), in_=s[:, 0:ncols]),
                         reads=[s.b], writes=[dstb])

    def load_x(seq, i, slot, src):
        nt = seq["nt"]
        r0 = i * 128
        P.dma("x%d" % slot, mk("dma_start", out=xt[slot][0:nt, :], in_=src[r0:r0 + nt, :]),
              writes=[xt[slot].b])

    def norm_hT(seq, slot):
        nt = seq["nt"]
        x = xt[slot]
        P.op("act", mk("activation", out=xn[0:nt, 0:D], in_=x[0:nt, :], func=AF.Square,
                                           accum_out=ssq[0:nt, :]),
             reads=[x.b], writes=[xn.b, ssq.b])
        P.op("act", mk("activation", out=rstd[0:nt, :], in_=ssq[0:nt, :], func=AF.Ln, scale=1.0 / D, bias=EPS),
             reads=[ssq.b], writes=[rstd.b])
        P.op("act", mk("activation", out=rstd[0:nt, :], in_=rstd[0:nt, :], func=AF.Exp, scale=-0.5),
             reads=[rstd.b], writes=[rstd.b])
        P.op("dve", mk("tensor_scalar", out=xn[0:nt, :], in0=x[0:nt, :], scalar1=rstd[0:nt, 0:1],
                                              scalar2=None, op0=ALU.mult),
             reads=[x.b, rstd.b], writes=[xn.b])
        tv = tp[:].rearrange("p (k t) -> p k t", t=128)
        for k in range(8):
            P.op("pe", mk("transpose", out=tv[:, k, 0:nt], in_=xn[0:nt, k * 128:(k + 1) * 128],
                                                  identity=ident[0:nt, 0:nt]),
                 reads=[xn.b, ident.b], writes=[tp.b])
        P.op("act", mk("copy", out=hT[:, 0:4, 0:nt], in_=tv[:, 0:4, 0:nt]), reads=[tp.b], writes=[hT.b])
        P.op("dve", mk("tensor_copy", out=hT[:, 4:8, 0:nt], in_=tv[:, 4:8, 0:nt]), reads=[tp.b], writes=[hT.b])

    def proj(nt, W, c0, ncols):
        m = next_mm()
        for k in range(8):
            P.op("pe", mk("matmul", out=m[0:nt, 0:ncols], lhsT=hT[:, k, 0:nt], rhs=W[:, k, c0:c0 + ncols],
                                               start=(k == 0), stop=(k == 7)),
                 reads=[hT.b, W.b], writes=[m.b])
        return m

    def transpose_to(src, srcb, nt, ncols_list, dst_fn, dstb, evac_eng="act"):
        tv = tp[:].rearrange("p (k t) -> p k t", t=128)
        for j, (c0, wd) in enumerate(ncols_list):
            P.op("pe", mk("transpose", out=tv[0:wd, j, 0:nt], in_=src[0:nt, c0:c0 + wd],
                                                                identity=ident[0:nt, 0:nt]),
                 reads=[srcb, ident.b], writes=[tp.b])
        wmax = max(w for _, w in ncols_list)
        n = len(ncols_list)
        if evac_eng == "none":
            return tv
        if evac_eng == "actbias":
            P.op("act", mk("activation", out=dst_fn(wmax, n), in_=tv[0:wmax, 0:n, 0:nt], func=AF.Copy,
                           scale=30000.0, bias=-30000.0), reads=[tp.b], writes=[dstb])
        elif evac_eng == "act":
            P.op("act", mk("copy", out=dst_fn(wmax, n), in_=tv[0:wmax, 0:n, 0:nt]), reads=[tp.b], writes=[dstb])
        else:
            P.op("dve", mk("tensor_copy", out=dst_fn(wmax, n), in_=tv[0:wmax, 0:n, 0:nt]),
                 reads=[tp.b], writes=[dstb])

    for l in range(DEPTH):
        x_src_of = (lambda seq: seq["x_in"]) if l == 0 else (lambda seq: xscr[seq["tok0"]:seq["tok0"] + seq["T"], :])
        last = (l == DEPTH - 1)

        st1 = ExitStack()
        Wmix = alloc(st1, "Wmix", [128, 4, 128], BF16)
        with ExitStack() as stl:
            if l == 0:
                stage = [alloc(stl, "stg%d" % i, [128, C1], F32) for i in range(2)]
                load_gcol(l)
                load_cast(lambda k: W1[:, k, :], lambda k: w_in[l, k * 128:(k + 1) * 128, 0:C1], 8, C1, stage, True, W1.b)
            srow = next_ftmp()
            P.dma("c1", mk("dma_start", out=srow[0:1, :], in_=pscale[l:l + 1, :]), writes=[srow.b])
            m = next_mm()
            P.op("pe", mk("matmul", out=m[:], lhsT=ones1[0:1, :], rhs=srow[0:1, :], start=True, stop=True),
                 reads=[ones1.b, srow.b], writes=[m.b])
            s0 = next_ftmp()
            P.dma("wst0", mk("dma_start", out=s0[:, 0:512].rearrange("p (g d) -> p g d", g=4),
                                                in_=w_mix[l].rearrange("g c d -> c g d")), writes=[s0.b])
            P.op("dve", mk("tensor_tensor", out=Wmix[:].rearrange("p g d -> p (g d)"), in0=s0[:, 0:512],
                                                  in1=m[:], op=ALU.mult),
                 reads=[s0.b, m.b], writes=[Wmix.b])
            P.barrier()

        kT = alloc(st1, "kT", [128, KTW], BF16)
        Vg = alloc(st1, "Vg", [128, VW], BF16)
        kiT = alloc(st1, "kiT", [96, SMAX], BF16)
        MH = max(T, (LS + 1) // 2)
        score = alloc(st1, "score", [128, 2 * MH], F32)
        sc_bufs = [score.b, P.buf("score1")]
        maskT = alloc(st1, "maskT", [128, max(NTP * 128, NCS * 64)], BF16)
        utok = [alloc(st1, "utok%d" % i, [128, 512], BF16) for i in range(2)]
        szp = alloc(st1, "szp", [128, 512], BF16)
        szas = [alloc(st1, "sza%d" % i, [128, 512], BF16) for i in range(3)]
        qf = alloc(st1, "qf", [128, 512], F32)
        kf = [alloc(st1, "kf%d" % i, [128, 512], F32) for i in range(2)]
        vf = [alloc(st1, "vf%d" % i, [128, 512], F32) for i in range(2)]
        g5 = [alloc(st1, "g5%d" % i, [128, 296], F32) for i in range(2)]
        uf = ftmp[0]
        rtab = [alloc(st1, "rtab%d" % i, [128, 192], F32) for i in range(2)]
        rtmp = alloc(st1, "rtmp", [128, 512], F32)
        qb = alloc(st1, "qb", [128, 512], BF16)
        kb = alloc(st1, "kb", [128, 512], BF16)
        qsb = alloc(st1, "qsb", [128, 256], BF16)
        ki3 = alloc(st1, "ki3", [128, 96], BF16)
        Dg = alloc(st1, "Dg", [128, 8, 128], BF16)
        plT = alloc(st1, "plT", [128, 4, 128], BF16)
        gpt = alloc(st1, "gpt", [128, 512], BF16)
        gpT = [alloc(st1, "gpT%d" % i, [128, 4, 128], BF16) for i in range(2)]
        gaT = [alloc(st1, "gaT%d" % i, [128, 4, 128], BF16) for i in range(2)]
        qTs = [alloc(st1, "qT%d" % i, [128, 8, 128], BF16) for i in range(3)]
        qiT = alloc(st1, "qiT", [96, 3, 128], BF16)
        rel = [alloc(st1, "rel%d" % i, [128, 512], BF16) for i in range(3)]
        maskb = alloc(st1, "maskb", [128, 2 * MH], BF16)
        mb_bufs = [maskb.b, P.buf("maskb1")]
        praw = [alloc(st1, "praw%d" % i, [128, 512], BF16) for i in range(3)]
        on = alloc(st1, "on", [128, 512], F32)
        gat = alloc(st1, "gat", [128, 512], BF16)
        rs = alloc(st1, "rs", [128, 8], F32)
        bs_lo = alloc(st1, "bs_lo", [128, 1], F32)
        bs_hi = alloc(st1, "bs_hi", [128, 1], F32)
        bs_mid = [alloc(st1, "bs_mid%d" % i, [128, 1], F32) for i in range(2)]
        bs_cnt = alloc(st1, "bs_cnt", [128, 1], F32)
        bs_cnt2 = alloc(st1, "bs_cnt2", [128, 1], F32)
        bs_u = alloc(st1, "bs_u", [128, 1], F32)
        bs_ht = alloc(st1, "bs_ht", [128, NI_BISECT], F32)
        bs_thr = alloc(st1, "bs_thr", [128, 1], F32)
        cst = [alloc(st1, "cst%d" % i, [128, 2, 256], F32) for i in range(2)]
        cbf = alloc(st1, "cbf", [128, 2, 256], BF16)
        spf = alloc(st1, "spf", [16, 512], F32)
        spb = alloc(st1, "spb", [16, 512], BF16)
        ckif = alloc(st1, "ckif", [128, max(L0 // 128, 1), 32], F32)
        cki3 = alloc(st1, "cki3", [128, 96], BF16)

        rot = {"rel": 0, "mk": 0, "praw": 0, "pm": 0, "L": 0}

        def nxt(name, arr):
            rot[name] ^= 1
            return arr[rot[name]]

        P.op("pool", mk("memset", ap=Vg[:], constant=1.0), writes=[Vg.b])
        for qz in qTs:
            P.op("pool", mk("memset", ap=qz[:], constant=0.0), writes=[qz.b])

        def rope_inplace(t, nt, nh, hd, tab, toff, eng="pool"):
            h2 = hd // 2
            xv = lambda: t.rearrange("p (h d) -> p h d", d=hd)
            tv = lambda: rtmp[0:nt, 0:nh * hd].rearrange("p (h d) -> p h d", d=hd)
            cc = lambda: tab[0:nt, toff:toff + hd].unsqueeze(1).broadcast_to([nt, nh, hd])
            sn = lambda: tab[0:nt, toff + hd:toff + hd + h2].unsqueeze(1).broadcast_to([nt, nh, h2])
            sp_ = lambda: tab[0:nt, toff + hd + h2:toff + 2 * hd].unsqueeze(1).broadcast_to([nt, nh, h2])
            return xv, tv, cc, sn, sp_, h2

        def do_rope(tl, col0, nt, nh, hd, tab, toff, eng):
            t = tl[0:nt, col0:col0 + nh * hd]
            xv, tv, cc, sn, sp_, h2 = rope_inplace(t, nt, nh, hd, tab, toff)
            P.op(eng, mk("tensor_tensor", out=tv()[:, :, 0:h2], in0=xv()[:, :, h2:hd], in1=sn(), op=ALU.mult),
                 reads=[tl.b, tab.b], writes=[rtmp.b])
            P.op(eng, mk("tensor_tensor", out=tv()[:, :, h2:hd], in0=xv()[:, :, 0:h2], in1=sp_(), op=ALU.mult),
                 reads=[tl.b, tab.b], writes=[rtmp.b])
            P.op(eng, mk("tensor_tensor", out=xv(), in0=xv(), in1=cc(), op=ALU.mult),
                 reads=[tl.b, tab.b], writes=[tl.b])
            P.op(eng, mk("tensor_tensor", out=xv(), in0=xv(), in1=tv(), op=ALU.add),
                 reads=[tl.b, rtmp.b], writes=[tl.b])

        gtile = [0]

        for seq in seqs:
            nt = seq["nt"]
            ntl = seq["ntiles"]
            isS = seq["kind"] == "s"
            xsrc = x_src_of(seq)
            Ksel = seq["K"]
            if not isS:
                kTv = kT[:, 0:4 * T].rearrange("p (c s) -> p c s", c=4)
                Vv = Vg[:, 0:NTP * 8 * VS].rearrange("p (t h d) -> p t h d", h=8, d=VS)
                okd, ovd, okid, opld = ok_p[l, seq["idx"]], ov_p[l, seq["idx"]], oki_p[l, seq["idx"]], opl_p[l, seq["idx"]]
            else:
                kTv = kT[:, 0:2 * LS].rearrange("p (c s) -> p c s", c=2)
                Vv = Vg[:, 0:NCS * 4 * VS].rearrange("p (t h d) -> p t h d", h=4, d=VS)
                okd, ovd, okid, opld = ok_s[l], ov_s[l], oki_s[l], opl_s[l]

            if isS and L0 > 0:
                nct = L0 // 128
                P.dma("cki", mk("dma_start", out=ckif[:, 0:nct, :],
                                                   in_=cki[l].rearrange("(t p) d -> p t d", p=128)),
                      writes=[ckif.b])
                for c in range(nct):
                    P.op("dve", mk("tensor_copy",
                        out=cki3[:].rearrange("p (r d) -> p r d", r=3),
                        in_=ckif[:, c, :].unsqueeze(1).broadcast_to([128, 3, 32])),
                        reads=[ckif.b], writes=[cki3.b])
                    transpose_to(cki3, cki3.b, 128, [(0, 96)],
                                 lambda wm, n, c=c: kiT[0:96, c * 128:(c + 1) * 128].unsqueeze(1), kiT.b,
                                 evac_eng="act" if c % 2 else "dve")
                P.dma("spf", mk("dma_start", out=spf[0:15, :], in_=spool[l]), writes=[spf.b])
                P.op("dve", mk("tensor_copy", out=spb[0:15, :], in_=spf[0:15, :]), reads=[spf.b], writes=[spb.b])

            def stageA1(i):
                qT = qTs[gtile[0] % 3]
                sza = szas[gtile[0] % 3]
                gi = gtile[0]
                gtile[0] += 1
                slot = gi % 2
                if i + 1 < ntl:
                    load_x(seq, i + 1, (gi + 1) % 2, xsrc)
                rt = rtab[gi % 2]
                rp0 = seq["rope0"] + i * 128
                P.dma("rt%d" % (gi % 2), mk("dma_start", out=rt[0:nt, :], in_=rope[rp0:rp0 + nt, :]),
                      writes=[rt.b])
                norm_hT(seq, slot)
                key0 = (L0 if isS else 0) + i * 128
                S = key0 + nt
                if isS:
                    sview = score[:, 0:S]
                    sbufs = [sc_bufs[0], sc_bufs[1]]
                else:
                    sview = score[:, (gtile[0] - 1) % 2 * MH:(gtile[0] - 1) % 2 * MH + S]
                    sbufs = [sc_bufs[(gtile[0] - 1) % 2]]
                ucur = utok[gi % 2]
                uprev = utok[(gi + 1) % 2]
                kfi = kf[gi % 2]
                vfi = vf[gi % 2]
                g5i = g5[gi % 2]

                yield
                m = proj(nt, W1, 0, 512)
                P.op("act", mk("copy", out=ucur[0:nt, :], in_=m[0:nt, :]), reads=[m.b], writes=[ucur.b])
                if i == ntl - 1:
                    P.op("dve", mk("tensor_copy", out=uf[0:nt, :], in_=m[0:nt, :]), reads=[m.b], writes=[uf.b])
                    P.dma("opl", mk("dma_start", out=opld, in_=uf[nt - 15:nt, :]), reads=[uf.b])
                yield
                m = proj(nt, W1, 1024, 512)
                P.op("act", mk("copy", out=qf[0:nt, :], in_=m[0:nt, :]), reads=[m.b], writes=[qf.b])
                yield
                m = proj(nt, W1, 1536, 512)
                P.op("act", mk("copy", out=kfi[0:nt, :], in_=m[0:nt, :]), reads=[m.b], writes=[kfi.b])
                yield
                m = proj(nt, W1, 2048, 512)
                P.op("act", mk("copy", out=vfi[0:nt, :], in_=m[0:nt, :]), reads=[m.b], writes=[vfi.b])
                if not isS:
                    P.op("dve", mk("tensor_copy", out=Vv[0:nt, i, :, 0:64],
                                   in_=m[0:nt, :].rearrange("p (h d) -> p h d", d=64)),
                         reads=[m.b], writes=[Vg.b])
                P.dma("ov%d" % (gi % 2), mk("dma_start", out=ovd[i * 128:i * 128 + nt, :], in_=vfi[0:nt, :]),
                      reads=[vfi.b])
                yield
                m = proj(nt, W1, 2560, 296)
                P.op("act", mk("copy", out=g5i[0:nt, :], in_=m[0:nt, 0:296]), reads=[m.b], writes=[g5i.b])
                yield
                m_zp = proj(nt, W1, 512, 512)
                m_za = proj(nt, W1, 2856, 512)
                P.op("act", mk("activation", out=szp[0:nt, :], in_=m_zp[0:nt, :], func=AF.Silu),
                     reads=[m_zp.b], writes=[szp.b])
                P.op("act", mk("activation", out=sza[0:nt, :], in_=m_za[0:nt, :], func=AF.Silu),
                     reads=[m_za.b], writes=[sza.b])

                yield
                yield
                do_rope(qf, 0, nt, 8, 64, rt, 0, "pool")
                yield
                do_rope(kfi, 0, nt, 8, 64, rt, 0, "pool")
                yield
                do_rope(g5i, 0, nt, 8, 32, rt, 128, "pool")
                do_rope(g5i, 256, nt, 1, 32, rt, 128, "pool")
                P.dma("ok%d" % (gi % 2), mk("dma_start", out=okd[i * 128:i * 128 + nt, :], in_=kfi[0:nt, :]),
                      reads=[kfi.b])
                P.dma("oki%d" % (gi % 2), mk("dma_start", out=okid[i * 128:i * 128 + nt, :], in_=g5i[0:nt, 256:288]),
                      reads=[g5i.b])
                P.op("pool", mk("tensor_copy", out=qb[0:nt, :], in_=qf[0:nt, :]), reads=[qf.b], writes=[qb.b])
                P.op("pool", mk("tensor_copy", out=kb[0:nt, :], in_=kfi[0:nt, :]), reads=[kfi.b], writes=[kb.b])
                for h in range(8):
                    P.op("dve", mk("tensor_scalar", out=Dg[0:nt, h, 0:nt], in0=identf[0:nt, 0:nt],
                                   scalar1=g5i[0:nt, 288 + h:289 + h], scalar2=None, op0=ALU.mult),
                         reads=[identf.b, g5i.b], writes=[Dg.b])
                P.op("pool", mk("tensor_copy", out=qsb[0:nt, :], in_=g5i[0:nt, 0:256]), reads=[g5i.b], writes=[qsb.b])
                P.op("dve", mk("tensor_copy", out=ki3[0:nt, :].rearrange("p (r d) -> p r d", r=3),
                                                    in_=g5i[0:nt, 256:288].unsqueeze(1).broadcast_to([nt, 3, 32])),
                     reads=[g5i.b], writes=[ki3.b])

                yield
                transpose_to(ki3, ki3.b, nt, [(0, 96)],
                             lambda wm, n: kiT[0:96, key0:key0 + nt].unsqueeze(1), kiT.b, "act")
                transpose_to(qsb, qsb.b, nt, [(0, 96), (96, 96), (192, 64)],
                             lambda wm, n: qiT[0:96, 0:3, 0:nt], qiT.b, "dve")
                tvq = transpose_to(qb, qb.b, nt, [(c * 128, 128) for c in range(4)], None, None, "none")
                P.op("act", mk("copy", out=qT[0:64, 0:8:2, 0:nt], in_=tvq[0:64, 0:4, 0:nt]), reads=[tp.b], writes=[qT.b])
                P.op("act", mk("copy", out=qT[64:128, 1:8:2, 0:nt], in_=tvq[64:128, 0:4, 0:nt]), reads=[tp.b], writes=[qT.b])
                if not isS:
                    transpose_to(kb, kb.b, nt, [(c * 128, 128) for c in range(4)],
                                 lambda wm, n: kTv[:, 0:4, key0:key0 + nt], kT.b, "dve")

                yield
                ppv = pp[:].rearrange("p (g t) -> p g t", g=4)
                for g in range(4):
                    if isS:
                        P.op("pe", mk("matmul", out=ppv[:, g, 0:nt], lhsT=ucur[0:nt, g * 128:(g + 1) * 128],
                                                           rhs=band[0:nt, 1, g * 128:g * 128 + nt], start=True, stop=False),
                             reads=[ucur.b, band.b], writes=[pp.b])
                        P.op("pe", mk("matmul", out=ppv[:, g, 0:nt], lhsT=spb[0:15, g * 128:(g + 1) * 128],
                                                           rhs=band[0:15, 3, g * 128:g * 128 + nt], start=False, stop=True),
                             reads=[spb.b, band.b], writes=[pp.b])
                    elif i == 0:
                        P.op("pe", mk("matmul", out=ppv[:, g, 0:nt], lhsT=ucur[0:nt, g * 128:(g + 1) * 128],
                                                           rhs=band[0:nt, 0, g * 128:g * 128 + nt], start=True, stop=True),
                             reads=[ucur.b, band.b], writes=[pp.b])
                    else:
                        P.op("pe", mk("matmul", out=ppv[:, g, 0:nt], lhsT=ucur[0:nt, g * 128:(g + 1) * 128],
                                                           rhs=band[0:nt, 1, g * 128:g * 128 + nt], start=True, stop=False),
                             reads=[ucur.b, band.b], writes=[pp.b])
                        P.op("pe", mk("matmul", out=ppv[:, g, 0:nt], lhsT=uprev[:, g * 128:(g + 1) * 128],
                                                           rhs=band[:, 2, g * 128:g * 128 + nt], start=False, stop=True),
                             reads=[uprev.b, band.b], writes=[pp.b])
                P.op("act", mk("copy", out=plT[:, :, 0:nt], in_=ppv[:, :, 0:nt]), reads=[pp.b], writes=[plT.b])
                m = next_mm()
                for g in range(4):
                    P.op("pe", mk("matmul", out=m[0:nt, g * 128:(g + 1) * 128], lhsT=plT[:, g, 0:nt],
                                                            rhs=Wmix[:, g, :], start=True, stop=True),
                         reads=[plT.b, Wmix.b], writes=[m.b])
                P.op("dve", mk("tensor_tensor", out=gpt[0:nt, :], in0=m[0:nt, :], in1=szp[0:nt, :], op=ALU.mult),
                     reads=[m.b, szp.b], writes=[gpt.b])
                gpo = gpT[gi % 2]
                transpose_to(gpt, gpt.b, nt, [(c * 128, 128) for c in range(4)],
                             lambda wm, n: gpo[:, 0:4, 0:nt], gpo.b, "act")
                tix = seq["tile0"] + i
                P.dma("gp%d" % (gi % 2), mk("dma_start",
                    out=gp_scr[tix].rearrange("p (c t) -> p c t", c=4)[:, :, 0:nt], in_=gpo[:, :, 0:nt]),
                    reads=[gpo.b])

                yield
                ngk = (S + 511) // 512
                for gk in range(ngk):
                    k0 = gk * 512
                    gw = min(512, S - k0)
                    prev = None
                    for h in range(8):
                        if h % 2 == 0:
                            yield
                        m = next_mm()
                        bp = 32 * (h % 3)
                        P.op("pe", mk("matmul", out=
                            m[0:nt, 0:gw], lhsT=qiT[bp:bp + 32, h // 3, 0:nt], rhs=kiT[bp:bp + 32, k0:k0 + gw],
                            start=True, stop=True),
                            reads=[qiT.b, kiT.b], writes=[m.b])
                        rot["rel"] = (rot["rel"] + 1) % 3
                        r = rel[rot["rel"]]
                        P.op("act", mk("activation", out=r[0:nt, 0:gw], in_=m[0:nt, 0:gw], func=AF.Relu),
                             reads=[m.b], writes=[r.b])
                        if prev is not None:
                            ph, prr = prev
                            P.op("pe", mk("matmul", out=pp[0:nt, 0:gw], lhsT=Dg[0:nt, ph, 0:nt], rhs=prr[0:nt, 0:gw],
                                          start=(ph == 0), stop=False),
                                 reads=[Dg.b, prr.b], writes=[pp.b])
                        prev = (h, r)
                    ph, prr = prev
                    P.op("pe", mk("matmul", out=pp[0:nt, 0:gw], lhsT=Dg[0:nt, ph, 0:nt], rhs=prr[0:nt, 0:gw],
                                  start=False, stop=True),
                         reads=[Dg.b, prr.b], writes=[pp.b])
                    P.op("act", mk("copy", out=sview[0:nt, k0:k0 + gw], in_=pp[0:nt, 0:gw]), reads=[pp.b], writes=sbufs)
                return dict(i=i, gi=gi, S=S, ngk=ngk, qT=qT, sza=sza, kfi=kfi, vfi=vfi, tix=tix,
                            sview=sview, sbufs=sbufs)

            def stageA2(c):
                gi, S, sview, sbufs = c["gi"], c["S"], c["sview"], c["sbufs"]
                nch = (S + 127) // 128
                if isS:
                    mview = maskb[:, 0:S]
                    mbufs = [mb_bufs[0], mb_bufs[1]]
                else:
                    mview = maskb[:, (gi % 2) * MH:(gi % 2) * MH + S]
                    mbufs = [mb_bufs[gi % 2]]
                c["nch"], c["mview"], c["mbufs"] = nch, mview, mbufs
                yield
                need_topk = S > Ksel
                if not isS:
                    if need_topk:
                        P.op("dve", mk("tensor_reduce", out=bs_lo[0:nt, :], in_=sview[0:nt, 0:S - 64],
                                       axis=mybir.AxisListType.X, op=ALU.min),
                             reads=sbufs, writes=[bs_lo.b])
                    P.op("dve", mk("memset", ap=sview[0:64, S - 64:S], constant=NEG), writes=sbufs)
                else:
                    if need_topk:
                        P.op("dve", mk("tensor_reduce", out=bs_lo[0:nt, :], in_=sview[0:nt, 0:S],
                                       axis=mybir.AxisListType.X, op=ALU.min),
                             reads=sbufs, writes=[bs_lo.b])
                if need_topk:
                    P.op("dve", mk("tensor_reduce", out=bs_hi[0:nt, :], in_=sview[0:nt, 0:S],
                                   axis=mybir.AxisListType.X, op=ALU.max),
                         reads=sbufs, writes=[bs_hi.b])
                    P.op("dve", mk("tensor_tensor", out=bs_hi[0:nt, :], in0=bs_hi[0:nt, :], in1=bs_lo[0:nt, :],
                                   op=ALU.subtract),
                         reads=[bs_hi.b, bs_lo.b], writes=[bs_hi.b])
                    P.op("dve", mk("tensor_scalar", out=bs_ht[0:nt, :], in0=pw[0:nt, :], scalar1=bs_hi[0:nt, 0:1],
                                   scalar2=None, op0=ALU.mult),
                         reads=[pw.b, bs_hi.b], writes=[bs_ht.b])
                    P.op("dve", mk("tensor_tensor", out=bs_mid[0][0:nt, :], in0=bs_lo[0:nt, :], in1=bs_ht[0:nt, 0:1],
                                   op=ALU.add),
                         reads=[bs_lo.b, bs_ht.b], writes=[bs_mid[0].b])
                    for it in range(NI_BISECT):
                        yield
                        mc = bs_mid[it % 2]
                        mn = bs_mid[(it + 1) % 2]
                        P.op("dve", mk("tensor_scalar", out=mview[0:nt, 0:S], in0=sview[0:nt, 0:S],
                                       scalar1=mc[0:nt, 0:1], scalar2=None, op0=ALU.is_ge, op1=ALU.add,
                                       accum_out=bs_cnt[0:nt, :]),
                             reads=sbufs + [mc.b], writes=mbufs + [bs_cnt.b])
                        P.op("dve", mk("tensor_scalar", out=bs_u[0:nt, :], in0=bs_cnt[0:nt, :],
                                       scalar1=float(Ksel) - 0.5, scalar2=0.5, op0=ALU.is_ge, op1=ALU.subtract),
                             reads=[bs_cnt.b], writes=[bs_u.b])
                        P.op("dve", mk("scalar_tensor_tensor", out=mn[0:nt, :], in0=bs_u[0:nt, :],
                                       scalar=bs_ht[0:nt, it:it + 1], in1=mc[0:nt, :], op0=ALU.mult, op1=ALU.add),
                             reads=[bs_u.b, bs_ht.b, mc.b], writes=[mn.b])
                    mfin = bs_mid[NI_BISECT % 2]
                    P.op("dve", mk("scalar_tensor_tensor", out=bs_thr[0:nt, :],
                                   in0=bs_ht[0:nt, NI_BISECT - 1:NI_BISECT], scalar=-0.5, in1=mfin[0:nt, :],
                                   op0=ALU.mult, op1=ALU.add),
                         reads=[bs_ht.b, mfin.b], writes=[bs_thr.b])
                else:
                    P.op("dve", mk("memset", ap=bs_thr[0:nt, :], constant=-1.0e29), writes=[bs_thr.b])
                yield
                P.op("dve", mk("tensor_scalar", out=mview[0:nt, :], in0=sview[0:nt, 0:S], scalar1=bs_thr[0:nt, 0:1],
                               scalar2=None, op0=ALU.is_ge),
                     reads=sbufs + [bs_thr.b], writes=mbufs)

            def stageB1(c):
                i, gi, S, nch, ngk, qT, mview, mbufs, vfi = (c["i"], c["gi"], c["S"], c["nch"], c["ngk"], c["qT"],
                                                             c["mview"], c["mbufs"], c["vfi"])
                mTv = maskT[:, 0:nch * nt].rearrange("p (c t) -> p c t", t=nt)
                for gk in range(ngk):
                    yield
                    k0 = gk * 512
                    gw = min(512, S - k0)
                    blocks = []
                    cc_ = 0
                    while cc_ * 128 < gw:
                        blocks.append((k0 + cc_ * 128, min(128, gw - cc_ * 128)))
                        cc_ += 1
                    full = [b for b in blocks if b[1] == 128]
                    part = [b for b in blocks if b[1] < 128]
                    if full:
                        transpose_to(mview, mbufs[0], nt, full,
                                     lambda wm, n, k0=k0: mTv[:, k0 // 128:k0 // 128 + n, :], maskT.b, "actbias")
                    if part:
                        pc0, pw_ = part[0]
                        transpose_to(mview, mbufs[-1], nt, [(pc0, pw_)],
                                     lambda wm, n, pc0=pc0: mTv[0:wm, pc0 // 128:pc0 // 128 + 1, :],
                                     maskT.b, "actbias")
                halves = [(0, 8)] if not isS else [(0, 4), (4, 8)]
                for (h0, h1) in halves:
                    if isS:
                        hh = h0 // 4
                        nct = L0 // 128
                        for c4 in range(0, nct, 2):
                            n4 = min(2, nct - c4)
                            stg = cst[(c4 // 2) % 2]
                            P.dma("cst", mk("dma_start",
                                out=stg[:, 0:n4, :],
                                in_=ck[l].rearrange("(t p) f -> p t f", p=128)[:, c4:c4 + n4, hh * 256:(hh + 1) * 256]),
                                writes=[stg.b])
                            P.op("dve", mk("tensor_copy", out=cbf[:, 0:n4, :], in_=stg[:, 0:n4, :]),
                                 reads=[stg.b], writes=[cbf.b])
                            for j in range(n4):
                                c = c4 + j
                                transpose_to(cbf[:, j, :], cbf.b, 128, [(0, 128), (128, 128)],
                                             lambda wm, n, c=c: kTv[:, 0:2, c * 128:(c + 1) * 128], kT.b,
                                             "act" if j % 2 else "dve")
                            stg2 = cst[(c4 // 2 + 1) % 2]
                            P.dma("cst", mk("dma_start",
                                out=stg2[:, 0:n4, :],
                                in_=cv[l].rearrange("(t p) f -> p t f", p=128)[:, c4:c4 + n4, hh * 256:(hh + 1) * 256]),
                                writes=[stg2.b])
                            P.op("dve", mk("tensor_copy",
                                out=Vv[:, c4:c4 + n4, :, 0:64],
                                in_=stg2[:, 0:n4, :].rearrange("p t (h d) -> p t h d", d=64)),
                                reads=[stg2.b], writes=[Vg.b])
                        transpose_to(kb[:, hh * 256:(hh + 1) * 256], kb.b, nt, [(0, 128), (128, 128)],
                                     lambda wm, n: kTv[:, 0:2, L0:L0 + nt], kT.b, "act")
                        P.op("pool", mk("tensor_copy",
                            out=Vv[0:nt, NCS - 1, :, 0:64],
                            in_=vfi[0:nt, hh * 256:(hh + 1) * 256].rearrange("p (h d) -> p h d", d=64)),
                            reads=[vfi.b], writes=[Vg.b])
                    chunks = [(c, min(128, S - c * 128)) for c in range(nch)]
                    groups = []
                    cur = []
                    for cpair in chunks:
                        if cur and (len(cur) == 4 or cur[-1][1] != cpair[1]):
                            groups.append(cur)
                            cur = []
                        cur.append(cpair)
                    if cur:
                        groups.append(cur)
                    items = [(h, gidx) for h in range(h0, h1) for gidx in range(len(groups))]

                    def emit_L(h, gidx):
                        hl = h - h0
                        pr = hl // 2
                        grp = groups[gidx]
                        rc = grp[0][1]
                        ng = len(grp)
                        Lt = nxt("L", Lb)
                        Lv = Lt[:].rearrange("p (c t) -> p c t", t=128)
                        c0g = grp[0][0]
                        if nt == 128:
                            P.op("pe", mk("matmul", out=Lt[0:rc, 0:ng * 128], lhsT=ident[0:rc, 0:rc],
                                          rhs=maskT[0:rc, c0g * 128:(c0g + ng) * 128], start=True, stop=False),
                                 reads=[ident.b, maskT.b], writes=[Lt.b])
                        for j, (c, _) in enumerate(grp):
                            if nt != 128:
                                P.op("pe", mk("matmul", out=Lv[0:rc, j, 0:nt], lhsT=ident[0:rc, 0:rc],
                                              rhs=mTv[0:rc, c, :], start=True, stop=False),
                                     reads=[ident.b, maskT.b], writes=[Lt.b])
                            P.op("pe", mk("matmul", out=Lv[0:rc, j, 0:nt], lhsT=kTv[:, pr, c * 128:c * 128 + rc],
                                          rhs=qT[:, h, 0:nt], start=False, stop=(j == ng - 1 or nt != 128)),
                                 reads=[kT.b, qT.b], writes=[Lt.b])
                        rot["praw"] = (rot["praw"] + 1) % len(praw)
                        pr_t = praw[rot["praw"]]
                        prv = pr_t[:].rearrange("p (c t) -> p c t", t=128)
                        P.op("act", mk("activation", out=prv[0:rc, 0:ng, 0:nt], in_=Lv[0:rc, 0:ng, 0:nt],
                                       func=AF.Exp, scale=0.125),
                             reads=[Lt.b], writes=[pr_t.b])
                        return (h, gidx, pr_t, prv)

                    def emit_PV(h, gidx, pr_t, prv):
                        hl = h - h0
                        O = Ob[h // 4]
                        grp = groups[gidx]
                        rc = grp[0][1]
                        ng = len(grp)
                        for j, (c, _) in enumerate(grp):
                            first = (gidx == 0 and j == 0)
                            lastc = (gidx == len(groups) - 1 and j == ng - 1)
                            P.op("pe", mk("matmul", out=O[0:nt, h % 4, :], lhsT=prv[0:rc, j, 0:nt],
                                          rhs=Vv[0:rc, c, hl if isS else h, :], start=first, stop=lastc),
                                 reads=[pr_t.b, Vg.b], writes=[O.b])

                    pend = None
                    for (h, gidx) in items:
                        yield
                        curL = emit_L(h, gidx)
                        if pend is not None:
                            emit_PV(*pend)
                        pend = curL
                    if pend is not None:
                        emit_PV(*pend)

            def stageB2(c):
                gi, sza, tix = c["gi"], c["sza"], c["tix"]
                for ob in range(2):
                    O = Ob[ob]
                    P.op("dve", mk("reciprocal", out=rs[0:nt, ob * 4:ob * 4 + 4], in_=O[0:nt, :, 64]),
                         reads=[O.b], writes=[rs.b])
                    P.op("dve", mk("tensor_tensor",
                                   out=on[0:nt, ob * 256:(ob + 1) * 256].rearrange("p (h d) -> p h d", d=64),
                                   in0=O[0:nt, :, 0:64],
                                   in1=rs[0:nt, ob * 4:ob * 4 + 4].unsqueeze(2).broadcast_to([nt, 4, 64]),
                                   op=ALU.mult),
                         reads=[O.b, rs.b], writes=[on.b])
                P.op("pool", mk("tensor_tensor", out=gat[0:nt, :], in0=on[0:nt, :], in1=sza[0:nt, :], op=ALU.mult),
                     reads=[on.b, sza.b], writes=[gat.b])
                gao = gaT[gi % 2]
                transpose_to(gat, gat.b, nt, [(c * 128, 128) for c in range(4)],
                             lambda wm, n: gao[:, 0:4, 0:nt], gao.b, "act")
                P.dma("ga%d" % (gi % 2), mk("dma_start",
                    out=ga_scr[tix].rearrange("p (c t) -> p c t", c=4)[:, :, 0:nt], in_=gao[:, :, 0:nt]),
                    reads=[gao.b])

            load_x(seq, 0, gtile[0] % 2, xsrc)

            def drive(gens):
                res = [None] * len(gens)
                live = [g is not None for g in gens]
                while any(live):
                    for k, g in enumerate(gens):
                        if live[k]:
                            try:
                                next(g)
                            except StopIteration as ex:
                                res[k] = ex.value
                                live[k] = False
                return res

            ctx = {}
            for k in range(ntl + 2):
                gA1 = stageA1(k) if k < ntl else None
                gA2 = stageA2(ctx[k - 1]) if 0 <= k - 1 < ntl else None
                gB1 = stageB1(ctx[k - 2]) if 0 <= k - 2 < ntl else None
                r = drive([gA1, gA2, gB1])
                if gA1 is not None:
                    ctx[k] = r[0]
                if gB1 is not None:
                    stageB2(ctx[k - 2])
        P.barrier()
        st1.close()

        st2 = ExitStack()
        W2 = alloc(st2, "W2", [128, 8, C2], BF16)
        Wpo = alloc(st2, "Wpo", [128, 4, D], BF16)
        Wao = alloc(st2, "Wao", [128, 4, D], BF16)
        Wo = alloc(st2, "Wo", [128, 8, D], BF16)
        with ExitStack() as stl:
            stage = [alloc(stl, "stg%d" % i, [128, C2], F32) for i in range(2)]
            load_cast(lambda k: W2[:, k, :], lambda k: w_in[l, k * 128:(k + 1) * 128, C1:NCOL], 8, C2, stage, True, W2.b)
            load_cast(lambda k: Wpo[:, k, :], lambda k: w_po[l, k * 128:(k + 1) * 128, :], 4, D, stage, False, Wpo.b)
            load_cast(lambda k: Wao[:, k, :], lambda k: w_ao[l, k * 128:(k + 1) * 128, :], 4, D, stage, False, Wao.b)
            load_cast(lambda k: Wo[:, k, :], lambda k: w_o[l, k * 128:(k + 1) * 128, :], 8, D, stage, False, Wo.b)
            P.barrier()
        gpl = [alloc(st2, "gpl%d" % i, [128, 4, 128], BF16) for i in range(3)]
        gal = [alloc(st2, "gal%d" % i, [128, 4, 128], BF16) for i in range(3)]
        sgps = [alloc(st2, "sgp%d" % i, [128, D], BF16) for i in range(2)]
        sgas = [alloc(st2, "sga%d" % i, [128, D], BF16) for i in range(2)]
        xt.append(alloc(st2, "xt2", [128, D], F32))
        m1 = alloc(st2, "m1", [128, D], F32)
        t2 = alloc(st2, "t2", [128, 512], F32)
        mrg = alloc(st2, "mrg", [128, D], BF16)
        mT = alloc(st2, "mT", [128, 8, 128], BF16)
        yt = [alloc(st2, "yt%d" % i, [128, D], F32) for i in range(2)]

        prefetch = []
        if l + 1 < DEPTH:
            pst = [alloc(st2, "pst%d" % i, [128, C1], F32) for i in range(2)]
            load_gcol(l + 1)

            def mk_chunk(k):
                def emit():
                    sgt = pst[k % 2]
                    P.dma("pst", mk("dma_start", out=sgt[:, 0:C1], in_=w_in[l + 1, k * 128:(k + 1) * 128, 0:C1]),
                          writes=[sgt.b])
                    if k % 2 == 0:
                        P.op("dve", mk("tensor_scalar", out=W1[:, k, :], in0=sgt[:, 0:C1], scalar1=gcol[:, k:k + 1],
                                       scalar2=None, op0=ALU.mult), reads=[sgt.b, gcol.b], writes=[W1.b])
                    else:
                        P.op("act", mk("activation", out=W1[:, k, :], in_=sgt[:, 0:C1], func=AF.Copy,
                                       scale=gcol[:, k:k + 1]), reads=[sgt.b, gcol.b], writes=[W1.b])
                return emit
            prefetch = [mk_chunk(k) for k in range(8)]
        gtile2 = [0]
        for seq in seqs:
            nt = seq["nt"]
            ntl = seq["ntiles"]
            xsrc = x_src_of(seq)

            def loads2(i, gi):
                load_x(seq, i, gi % 3, xsrc)
                tix = seq["tile0"] + i
                P.dma("gpl%d" % (gi % 3), mk("dma_start",
                    out=gpl[gi % 3][:, :, 0:nt], in_=gp_scr[tix].rearrange("p (c t) -> p c t", c=4)[:, :, 0:nt]),
                    writes=[gpl[gi % 3].b])
                P.dma("gal%d" % (gi % 3), mk("dma_start",
                    out=gal[gi % 3][:, :, 0:nt], in_=ga_scr[tix].rearrange("p (c t) -> p c t", c=4)[:, :, 0:nt]),
                    writes=[gal[gi % 3].b])

            def stage2A(i):
                gi = gtile2[0]
                gtile2[0] += 1
                if prefetch and gi % 3 == 1:
                    prefetch.pop(0)()
                slot = gi % 3
                sgp = sgps[gi % 2]
                sga = sgas[gi % 2]
                if i + 1 < ntl:
                    loads2(i + 1, gi + 1)
                norm_hT(seq, slot)
                for hh in range(2):
                    m = proj(nt, W2, hh * 512, 512)
                    P.op("act", mk("activation", out=sgp[0:nt, hh * 512:(hh + 1) * 512], in_=m[0:nt, :],
                                                                   func=AF.Sigmoid),
                         reads=[m.b], writes=[sgp.b])
                for hh in range(2):
                    m = proj(nt, W2, 1024 + hh * 512, 512)
                    P.op("act", mk("activation", out=sga[0:nt, hh * 512:(hh + 1) * 512], in_=m[0:nt, :],
                                                                   func=AF.Sigmoid),
                         reads=[m.b], writes=[sga.b])
                return dict(i=i, gi=gi, slot=slot, sgp=sgp, sga=sga)

            def stage2B(c):
                i, gi, slot, sgp, sga = c["i"], c["gi"], c["slot"], c["sgp"], c["sga"]
                x = xt[slot]
                gp_, ga_ = gpl[slot], gal[slot]
                for hh in range(2):
                    m = next_mm()
                    for k in range(4):
                        P.op("pe", mk("matmul", out=m[0:nt, :], lhsT=gp_[:, k, 0:nt],
                                                                       rhs=Wpo[:, k, hh * 512:(hh + 1) * 512],
                                                                       start=(k == 0), stop=(k == 3)),
                             reads=[gp_.b, Wpo.b], writes=[m.b])
                    P.op("dve", mk("tensor_tensor", out=m1[0:nt, hh * 512:(hh + 1) * 512], in0=m[0:nt, :],
                                                                      in1=sgp[0:nt, hh * 512:(hh + 1) * 512], op=ALU.mult),
                         reads=[m.b, sgp.b], writes=[m1.b])
                for hh in range(2):
                    m = next_mm()
                    for k in range(4):
                        P.op("pe", mk("matmul", out=m[0:nt, :], lhsT=ga_[:, k, 0:nt],
                                                                       rhs=Wao[:, k, hh * 512:(hh + 1) * 512],
                                                                       start=(k == 0), stop=(k == 3)),
                             reads=[ga_.b, Wao.b], writes=[m.b])
                    P.op("dve", mk("tensor_tensor", out=t2[0:nt, :], in0=m[0:nt, :],
                                                                      in1=sga[0:nt, hh * 512:(hh + 1) * 512], op=ALU.mult),
                         reads=[m.b, sga.b], writes=[t2.b])
                    P.op("pool", mk("tensor_tensor", out=mrg[0:nt, hh * 512:(hh + 1) * 512],
                                                                  in0=m1[0:nt, hh * 512:(hh + 1) * 512], in1=t2[0:nt, :],
                                                                  op=ALU.add),
                         reads=[m1.b, t2.b], writes=[mrg.b])
                transpose_to(mrg, mrg.b, nt, [(c * 128, 128) for c in range(8)],
                             lambda wm, n: mT[:, 0:8, 0:nt], mT.b, "act")
                for hh in range(2):
                    m = next_mm()
                    for k in range(8):
                        P.op("pe", mk("matmul", out=m[0:nt, :], lhsT=mT[:, k, 0:nt],
                                                                       rhs=Wo[:, k, hh * 512:(hh + 1) * 512],
                                                                       start=(k == 0), stop=(k == 7)),
                             reads=[mT.b, Wo.b], writes=[m.b])
                    P.op("dve", mk("tensor_tensor", out=x[0:nt, hh * 512:(hh + 1) * 512],
                                                                      in0=x[0:nt, hh * 512:(hh + 1) * 512], in1=m[0:nt, :],
                                                                      op=ALU.add),
                         reads=[m.b, x.b], writes=[x.b])
                r0 = seq["tok0"] + i * 128
                if not last:
                    P.dma("x%d" % slot, mk("dma_start", out=xscr[r0:r0 + nt, :], in_=x[0:nt, :]), reads=[x.b])
                else:
                    y = yt[gi % 2]
                    P.op("act", mk("activation", out=xn[0:nt, 0:D], in_=x[0:nt, :], func=AF.Square,
                                                       accum_out=ssq[0:nt, :]),
                         reads=[x.b], writes=[xn.b, ssq.b])
                    P.op("act", mk("activation", out=rstd[0:nt, :], in_=ssq[0:nt, :], func=AF.Ln, scale=1.0 / D, bias=EPS),
                         reads=[ssq.b], writes=[rstd.b])
                    P.op("act", mk("activation", out=rstd[0:nt, :], in_=rstd[0:nt, :], func=AF.Exp, scale=-0.5),
                         reads=[rstd.b], writes=[rstd.b])
                    P.op("dve", mk("scalar_tensor_tensor", out=y[0:nt, :], in0=x[0:nt, :], scalar=rstd[0:nt, 0:1],
                                                                      in1=gfbc[0:nt, :], op0=ALU.mult, op1=ALU.mult),
                         reads=[x.b, rstd.b, gfbc.b], writes=[y.b])
                    yo = seq["y_out"]
                    P.dma("y%d" % slot, mk("dma_start", out=yo[i * 128:i * 128 + nt, :], in_=y[0:nt, :]),
                          reads=[y.b])
            loads2(0, gtile2[0])
            pend = None
            for i in range(ntl):
                cA = stage2A(i)
                if pend is not None:
                    stage2B(pend)
                pend = cA
            stage2B(pend)
        while prefetch:
            prefetch.pop(0)()
        P.barrier()
        xt.pop()
        st2.close()

    P.final_wait()
    P.emit()
    stack0.close()
    return nc


_CACHE = {}


def kernel(x_prompt, x_sample, cache_k, cache_v, cache_kidx, state_pool, norm_g, w_in, w_pool_mix,
           pool_scale, w_pool_out, w_attn_out, w_o, final_norm_g):
    NCORES = 8
    f = lambda a: np.ascontiguousarray(np.asarray(a, dtype=np.float32))
    x_prompt, x_sample = f(x_prompt), f(x_sample)
    cache_k, cache_v, cache_kidx, state_pool = f(cache_k), f(cache_v), f(cache_kidx), f(state_pool)
    BP, T, _ = x_prompt.shape
    BS, TS, _ = x_sample.shape
    DEPTH = cache_k.shape[0]
    L0 = cache_k.shape[2]
    assert BP % NCORES == 0 and BS == NCORES
    NP = BP // NCORES
    cfg = dict(NP=NP, T=T, TS=TS, L0=L0, DEPTH=DEPTH, KP=min(256, T // 4), KS=min(256, (L0 + TS) // 4))
    key = tuple(sorted(cfg.items()))
    if key not in _CACHE:
        _CACHE[key] = build(cfg)
    nc = _CACHE[key]

    rope = _rope_table(list(range(T)) + list(range(L0, L0 + TS)))
    bands = _band_tables().reshape(4, 128, 512)
    pw2 = np.tile((2.0 ** -(np.arange(NI_BISECT, dtype=np.float64) + 1)).astype(np.float32)[None, :], (128, 1))
    shared = dict(norm_g=f(norm_g), w_in=f(w_in), w_mix=f(w_pool_mix), pscale=f(pool_scale), w_po=f(w_pool_out),
                  w_ao=f(w_attn_out), w_o=f(w_o), gfin=f(final_norm_g).reshape(1, D), rope=rope, bands=bands, pw2=pw2)
    in_maps = []
    for c in range(NCORES):
        m = dict(shared)
        m["x_p"] = x_prompt[c * NP:(c + 1) * NP]
        m["x_s"] = x_sample[c]
        m["ck"] = cache_k[:, c].reshape(DEPTH, L0, 512)
        m["cv"] = cache_v[:, c].reshape(DEPTH, L0, 512)
        m["cki"] = cache_kidx[:, c]
        m["spool"] = state_pool[:, c]
        in_maps.append(m)
    res = run_bass_kernel_spmd(nc, in_maps, core_ids=list(range(NCORES)))
    R = res.results
    cat = lambda name, ax: np.concatenate([np.asarray(r[name]) for r in R], axis=ax)
    stk = lambda name, ax: np.stack([np.asarray(r[name]) for r in R], axis=ax)
    y_prompt = cat("y_p", 0)
    y_sample = stk("y_s", 0)
    nk_p = cat("ok_p", 1).reshape(DEPTH, BP, T, 8, 64)
    nv_p = cat("ov_p", 1).reshape(DEPTH, BP, T, 8, 64)
    nki_p = cat("oki_p", 1)
    npl_p = cat("opl_p", 1)
    nk_s = stk("ok_s", 1).reshape(DEPTH, BS, TS, 8, 64)
    nv_s = stk("ov_s", 1).reshape(DEPTH, BS, TS, 8, 64)
    nki_s = stk("oki_s", 1)
    npl_s = stk("opl_s", 1)
    return (y_prompt, y_sample, nk_p, nv_p, nki_p, npl_p, nk_s, nv_s, nki_s, npl_s)
```
